# Optimizing a Trainium2 kernel written in Bass

```python
import math
import jax, jax.numpy as jnp
from jax import lax
import numpy as np

D_MODEL = 1024
BATCH = 4
SEQ = 8192
DEPTH = 2

HEAD_DIM = 64
N_HEADS = D_MODEL // HEAD_DIM
HEADS_A = N_HEADS // 2
HEADS_B = N_HEADS - HEADS_A
KV_A = 2
KV_B = 2
G_A = HEADS_A // KV_A
G_B = HEADS_B // KV_B
Q_A = HEADS_A * HEAD_DIM
KVD_A = KV_A * HEAD_DIM
Q_B = HEADS_B * HEAD_DIM
KVD_B = KV_B * HEAD_DIM
IN_COLS = Q_A + 2 * KVD_A + Q_B + 2 * KVD_B
IN_SPLITS = (Q_A, Q_A + KVD_A, Q_A + 2 * KVD_A, Q_A + 2 * KVD_A + Q_B, Q_A + 2 * KVD_A + Q_B + KVD_B)
MIX_WIDTH = Q_A + Q_B

GRID_W = 64
ROPE_THETA = 10000.0
Q_BLOCK = 128
WINDOW = 128
N_BUCKETS = 32
MAX_DISTANCE = 128
D_FF = 2816
CONV_W = 3
ALPHA = (2.0 * DEPTH) ** 0.25
BETA = (8.0 * DEPTH) ** -0.25
RMS_EPS = 1e-6
LN_EPS = 1e-5

kernel_name = "hymba_axial_swa_convglu_deepnorm_encoder"


def rms_norm(x, g):
    xf = x.astype(jnp.float32)
    y = xf * lax.rsqrt(jnp.mean(xf * xf, axis=-1, keepdims=True) + RMS_EPS)
    return (y * g.astype(jnp.float32)).astype(x.dtype)


def layer_norm(x, g, b):
    xf = x.astype(jnp.float32)
    mu = jnp.mean(xf, axis=-1, keepdims=True)
    var = jnp.mean(jnp.square(xf - mu), axis=-1, keepdims=True)
    y = (xf - mu) * lax.rsqrt(var + LN_EPS)
    return (y * g.astype(jnp.float32) + b.astype(jnp.float32)).astype(x.dtype)


def axial_rope_tables(seq_len):
    rows_n = seq_len // GRID_W
    row = jnp.repeat(jnp.arange(rows_n, dtype=jnp.float32), GRID_W)
    col = jnp.tile(jnp.arange(GRID_W, dtype=jnp.float32), rows_n)
    half = HEAD_DIM // 2
    inv_freq = ROPE_THETA ** (-jnp.arange(0, half, 2, dtype=jnp.float32) / half)
    ang = jnp.concatenate([row[:, None] * inv_freq, col[:, None] * inv_freq], axis=-1)
    return jnp.cos(ang), jnp.sin(ang)


def apply_rope(x, cos, sin):
    xf = x.astype(jnp.float32).reshape(x.shape[:-1] + (HEAD_DIM // 2, 2))
    x0, x1 = xf[..., 0], xf[..., 1]
    c = cos[None, :, None, :]
    s = sin[None, :, None, :]
    out = jnp.stack([x0 * c - x1 * s, x0 * s + x1 * c], axis=-1).reshape(x.shape)
    return out.astype(x.dtype)


def global_gqa(q, k, v):
    B, S = q.shape[0], q.shape[1]
    nb = S // Q_BLOCK
    qb = q.reshape(B, nb, Q_BLOCK, KV_A, G_A, HEAD_DIM).transpose(1, 0, 2, 3, 4, 5)
    scale = HEAD_DIM ** -0.5

    def block(qi):
        s = jnp.einsum('bqhgd,bkhd->bhgqk', qi, k, preferred_element_type=jnp.float32) * scale
        p = jax.nn.softmax(s, axis=-1)
        return jnp.einsum('bhgqk,bkhd->bqhgd', p.astype(v.dtype), v)

    o = lax.map(block, qb)
    return o.transpose(1, 0, 2, 3, 4, 5).reshape(B, S, Q_A)


def t5_bucket(rel):
    half = N_BUCKETS // 2
    max_exact = half // 2
    bucket = jnp.where(rel > 0, half, 0)
    rp = jnp.abs(rel)
    rpf = jnp.maximum(rp, 1).astype(jnp.float32)
    large = max_exact + (jnp.log(rpf / max_exact) / math.log(MAX_DISTANCE / max_exact)
                         * (half - max_exact)).astype(jnp.int32)
    large = jnp.minimum(large, half - 1)
    return bucket + jnp.where(rp < max_exact, rp, large)


def window_gqa_sink(q, k, v, rel_bias, sink):
    B, S = q.shape[0], q.shape[1]
    nb = S // Q_BLOCK
    scale = HEAD_DIM ** -0.5
    qb = q.reshape(B, nb, Q_BLOCK, KV_B, G_B, HEAD_DIM)
    pad = ((0, 0), (Q_BLOCK, Q_BLOCK), (0, 0), (0, 0))
    kp = jnp.pad(k, pad).reshape(B, nb + 2, Q_BLOCK, KV_B, HEAD_DIM)
    vp = jnp.pad(v, pad).reshape(B, nb + 2, Q_BLOCK, KV_B, HEAD_DIM)
    kb = jnp.concatenate([kp[:, :-2], kp[:, 1:-1], kp[:, 2:]], axis=2)
    vb = jnp.concatenate([vp[:, :-2], vp[:, 1:-1], vp[:, 2:]], axis=2)
    qpos = jnp.arange(Q_BLOCK, dtype=jnp.int32)
    kpos = jnp.arange(3 * Q_BLOCK, dtype=jnp.int32) - Q_BLOCK
    rel = kpos[None, :] - qpos[:, None]
    bias = rel_bias.astype(jnp.float32)[t5_bucket(rel)]
    bias = bias.transpose(2, 0, 1).reshape(KV_B, G_B, Q_BLOCK, 3 * Q_BLOCK)
    kabs = jnp.arange(nb, dtype=jnp.int32)[:, None] * Q_BLOCK + kpos[None, :]
    valid = (jnp.abs(rel) <= WINDOW)[None] & ((kabs >= 0) & (kabs < S))[:, None, :]
    s = jnp.einsum('bnqhgd,bnkhd->bnhgqk', qb, kb, preferred_element_type=jnp.float32) * scale + bias
    s = jnp.where(valid[None, :, None, None], s, -jnp.inf)
    sink_logit = jnp.broadcast_to(sink.astype(jnp.float32).reshape(KV_B, G_B)[None, None, :, :, None, None],
                                  s.shape[:-1] + (1,))
    p = jax.nn.softmax(jnp.concatenate([s, sink_logit], axis=-1), axis=-1)[..., :-1]
    o = jnp.einsum('bnhgqk,bnkhd->bnqhgd', p.astype(v.dtype), vb)
    return o.reshape(B, S, Q_B)


def token_mixer(x, rel_bias, cos, sin, w_in, q_norm, k_norm, sink, out_norm_a, out_norm_b, w_out):
    B, S, _ = x.shape
    h = jnp.einsum('bsd,de->bse', x, w_in)
    qa, ka, va, qb, kb, vb = jnp.split(h, IN_SPLITS, axis=-1)
    qa = apply_rope(rms_norm(qa.reshape(B, S, HEADS_A, HEAD_DIM), q_norm), cos, sin)
    ka = apply_rope(rms_norm(ka.reshape(B, S, KV_A, HEAD_DIM), k_norm), cos, sin)
    va = va.reshape(B, S, KV_A, HEAD_DIM)
    ya = rms_norm(global_gqa(qa, ka, va), out_norm_a)
    yb = window_gqa_sink(qb.reshape(B, S, HEADS_B, HEAD_DIM), kb.reshape(B, S, KV_B, HEAD_DIM),
                         vb.reshape(B, S, KV_B, HEAD_DIM), rel_bias, sink)
    yb = rms_norm(yb, out_norm_b)
    return jnp.einsum('bse,ed->bsd', jnp.concatenate([ya, yb], axis=-1), w_out)


def conv_glu(x, w_gate, w_up, conv_w, conv_b, w_down):
    S = x.shape[1]
    g = jnp.einsum('bsd,df->bsf', x, w_gate)
    u = jnp.einsum('bsd,df->bsf', x, w_up)
    r = CONV_W // 2
    gp = jnp.pad(g, ((0, 0), (r, r), (0, 0)))
    gc = conv_b
    for j in range(CONV_W):
        gc = gc + gp[:, j:j + S] * conv_w[j]
    return jnp.einsum('bsf,fd->bsd', jax.nn.gelu(gc) * u, w_down)


def setup_inputs(seed: int = 0) -> dict:
    key = jax.random.key(seed)
    ks = jax.random.split(key, 20)
    f32 = jnp.float32
    L = DEPTH

    def nrm(k, shape, scale):
        return jax.random.normal(k, shape, f32) * scale

    return {
        "x": nrm(ks[0], (BATCH, SEQ, D_MODEL), 1.0),
        "rel_bias": nrm(ks[1], (N_BUCKETS, HEADS_B), 0.1),
        "w_in": nrm(ks[2], (L, D_MODEL, IN_COLS), D_MODEL ** -0.5),
        "q_norm": 1.0 + nrm(ks[3], (L, HEAD_DIM), 0.05),
        "k_norm": 1.0 + nrm(ks[4], (L, HEAD_DIM), 0.05),
        "sink": nrm(ks[5], (L, HEADS_B), 0.5),
        "out_norm_a": 1.0 + nrm(ks[6], (L, Q_A), 0.05),
        "out_norm_b": 1.0 + nrm(ks[7], (L, Q_B), 0.05),
        "w_out": nrm(ks[8], (L, MIX_WIDTH, D_MODEL), BETA * MIX_WIDTH ** -0.5),
        "ln1_g": 1.0 + nrm(ks[9], (L, D_MODEL), 0.05),
        "ln1_b": nrm(ks[10], (L, D_MODEL), 0.01),
        "w_gate": nrm(ks[11], (L, D_MODEL, D_FF), D_MODEL ** -0.5),
        "w_up": nrm(ks[12], (L, D_MODEL, D_FF), D_MODEL ** -0.5),
        "conv_w": nrm(ks[13], (L, CONV_W, D_FF), CONV_W ** -0.5),
        "conv_b": nrm(ks[14], (L, D_FF), 0.01),
        "w_down": nrm(ks[15], (L, D_FF, D_MODEL), BETA * D_FF ** -0.5),
        "ln2_g": 1.0 + nrm(ks[16], (L, D_MODEL), 0.05),
        "ln2_b": nrm(ks[17], (L, D_MODEL), 0.01),
    }


def reference(x, rel_bias, w_in, q_norm, k_norm, sink, out_norm_a, out_norm_b, w_out,
              ln1_g, ln1_b, w_gate, w_up, conv_w, conv_b, w_down, ln2_g, ln2_b):
    cos, sin = axial_rope_tables(x.shape[1])
    for l in range(DEPTH):
        mix = token_mixer(x, rel_bias, cos, sin, w_in[l], q_norm[l], k_norm[l], sink[l],
                          out_norm_a[l], out_norm_b[l], w_out[l])
        x = layer_norm(ALPHA * x + mix, ln1_g[l], ln1_b[l])
        ffn = conv_glu(x, w_gate[l], w_up[l], conv_w[l], conv_b[l], w_down[l])
        x = layer_norm(ALPHA * x + ffn, ln2_g[l], ln2_b[l])
    return x
```

```python
import math
from contextlib import ExitStack
import numpy as np
import concourse.bass as bass
import concourse.mybir as mybir
from concourse.bass_utils import run_bass_kernel_spmd

F32 = mybir.dt.float32
BF16 = mybir.dt.bfloat16
AF = mybir.ActivationFunctionType
ALU = mybir.AluOpType
AX = mybir.AxisListType

D = 1024
S = 8192
NT = 64
NB = 4
DFF = 2816
NFC = 22
ALPHA = 4.0 ** 0.25
RMS_EPS = 1e-6
LN_EPS = 1e-5
NEG = -30000.0
N_CORES = 8


class Buf:
    __slots__ = ("name", "w", "r", "dsem", "dcnt")

    def __init__(self, name):
        self.name = name
        self.w = []
        self.r = []
        self.dsem = None
        self.dcnt = 0


class Prog:
    CE = ("pe", "act", "dve", "pool")
    ENG = ("pe", "act", "dve", "pool", "sp")

    def __init__(self, nc, stack):
        self.nc = nc
        self.stack = stack
        self.ops = {e: [] for e in self.ENG}
        self.cnt = {e: 0 for e in self.CE}
        self.esem = {e: stack.enter_context(nc.semaphore("s_" + e)) for e in self.CE}
        self.seen = {e: {} for e in self.ENG}
        self.nsem = 4
        self.out_tokens = []

    def _mk_sem(self, name):
        self.nsem += 1
        return self.stack.enter_context(self.nc.semaphore(name))

    def _waits(self, eng, reads, writes, dwrites=()):
        deps = []
        for b in reads:
            deps.extend(b.w)
        for b in writes:
            deps.extend(b.w)
            deps.extend(b.r)
        for b in dwrites:
            deps.extend(b.r)
        best = {}
        for (s, v) in deps:
            k = id(s)
            if k not in best or best[k][1] < v:
                best[k] = (s, v)
        waits = []
        for k, (s, v) in best.items():
            if self.seen[eng].get(k, 0) >= v:
                continue
            self.seen[eng][k] = v
            waits.append((s, v))
        return waits

    @staticmethod
    def _compact(lst):
        best = {}
        for (s, v) in lst:
            if id(s) not in best or best[id(s)][1] < v:
                best[id(s)] = (s, v)
        return list(best.values())

    def _commit(self, tok, reads, writes, dwrites=()):
        for b in dwrites:
            b.w.append(tok)
            if len(b.w) > 24:
                b.w = self._compact(b.w)
        for b in reads:
            b.r.append(tok)
            if len(b.r) > 24:
                best = {}
                for (s, v) in b.r:
                    if id(s) not in best or best[id(s)][1] < v:
                        best[id(s)] = (s, v)
                b.r = list(best.values())
        for b in writes:
            b.w = [tok]
            b.r = []

    def op(self, eng, fn, reads=(), writes=()):
        waits = self._waits(eng, reads, writes)
        self.cnt[eng] += 1
        tok = (self.esem[eng], self.cnt[eng])
        self.ops[eng].append((fn, waits, (self.esem[eng], 1)))
        self._commit(tok, reads, writes)
        return tok

    def dma(self, q, fn, reads=(), writes=(), owner=None, is_out=False, dwrites=()):
        if owner is None:
            owner = writes[0] if writes else (dwrites[0] if dwrites else reads[0])
        if owner.dsem is None:
            owner.dsem = self._mk_sem("d_" + owner.name)
        waits = self._waits(q, reads, writes, dwrites)
        owner.dcnt += 16
        tok = (owner.dsem, owner.dcnt)
        self.ops[q].append((fn, waits, (owner.dsem, 16)))
        self._commit(tok, reads, writes, dwrites)
        if is_out:
            self.out_tokens.append(tok)
        return tok

    def emit(self):
        nc = self.nc
        ws = []
        for (s, v) in self.out_tokens:
            k = id(s)
            if self.seen["sp"].get(k, 0) >= v:
                continue
            self.seen["sp"][k] = v
            ws.append((s, v))
        if ws:
            self.ops["sp"].append((None, ws, None))
        ops = self.ops

        def replay(e, lst):
            for (fn, waits, inc) in lst:
                for (s, v) in waits:
                    e.wait_ge(s, v)
                if fn is not None:
                    ins = fn(e)
                    ins.then_inc(inc[0], inc[1])

        with nc.Block() as block:
            @block.tensor
            def _(e):
                replay(e, ops["pe"])

            @block.scalar
            def _(e):
                replay(e, ops["act"])

            @block.vector
            def _(e):
                replay(e, ops["dve"])

            @block.gpsimd
            def _(e):
                replay(e, ops["pool"])

            @block.sync
            def _(e):
                replay(e, ops["sp"])


class Builder:
    def __init__(self, n_layers, fused, dbg=False):
        self.fused = fused
        self.n_layers = n_layers
        self.dbg = dbg
        self.nc = bass.Bass("TRN2", target_bir_lowering=False)
        self.st = ExitStack()
        self.P = Prog(self.nc, self.st)
        self.sb_bytes = 0
        self._declare_io()
        self._alloc()

    def sb(self, name, shape, dt):
        n = 1
        for s in shape[1:]:
            n *= s
        self.sb_bytes += n * (2 if dt == BF16 else 4)
        return self.st.enter_context(self.nc.sbuf_tensor(name, list(shape), dt))

    def din(self, name, shape, dt=F32):
        return self.nc.dram_tensor(name, list(shape), dt, kind="ExternalInput").ap()

    def _declare_io(self):
        nc = self.nc
        L = self.n_layers
        self.x_in = self.din("xsrc", [S, D])
        self.W = []
        for l in range(L):
            w = dict(
                wq=self.din(f"wq{l}", [D, D]), wkv=self.din(f"wkv{l}", [D, 512]),
                wout=self.din(f"wout{l}", [D, D]), wg=self.din(f"wg{l}", [D, DFF]),
                wu=self.din(f"wu{l}", [D, DFF]), wd=self.din(f"wd{l}", [DFF, D]),
                nrm=self.din(f"nrm{l}", [1, 128]), gout=self.din(f"gout{l}", [128, 8]),
                lnp=self.din(f"lnp{l}", [128, 32]), cvp=self.din(f"cvp{l}", [128, 88]),
                sink=self.din(f"sink{l}", [1, 8]),
            )
            w["wq_s"] = nc.dram_tensor(f"wq_s{l}", [128, 8, D], BF16)
            w["wkv_s"] = nc.dram_tensor(f"wkv_s{l}", [128, 8, 512], BF16)
            w["wout_s"] = nc.dram_tensor(f"wout_s{l}", [8, 128, 8, 128], BF16)
            w["wgu_s"] = nc.dram_tensor(f"wgu_s{l}", [NFC, 128, 8, 256], BF16)
            w["wd_s"] = nc.dram_tensor(f"wd_s{l}", [NFC, 128, D], BF16)
            for k in ("wq_s", "wkv_s", "wout_s", "wgu_s", "wd_s"):
                w["B_" + k] = Buf(f"{k}{l}")
            self.W.append(w)
        self.biasT_in = self.din("biasT", [128, 3072])
        self.cs_in = self.din("cs", [128, NT, 64])
        self.jm_in = self.din("jm", [128, 4])
        self.ident_in = self.din("ident", [128, 128])
        self.out = nc.dram_tensor("out", [S // 2, D], F32, kind="ExternalOutput").ap()
        if self.fused:
            self.xmid = nc.dram_tensor("xmid", [S, D], BF16)
            self.B_xmid = Buf("xmid")
        self.B_out = Buf("out")
        self.dbg_out = {}

    def _alloc(self):
        nc, sb = self.nc, self.sb
        self.KAT = sb("KAT", [128, S], BF16)
        self.VA = sb("VA", [128, NT, 2, 65], BF16)
        self.KBT = sb("KBT", [128, 36 * 128], BF16)
        self.VB = sb("VB", [128, 36, 2, 65], BF16)
        self.B_KA = [Buf(f"KA{t}") for t in range(NT)]
        self.B_KB = [Buf(f"KB{t}") for t in range(NT)]
        self.B_vones = Buf("vones")
        self.ident_f = sb("ident_f", [128, 128], F32); self.B_idf = Buf("ident_f")
        self.ident_b = sb("ident_b", [128, 128], BF16); self.B_idb = Buf("ident_b")
        self.ones_b = sb("ones_b", [128, 128], BF16); self.B_ones = Buf("ones_b")
        self.onesf = sb("onesf", [128, 64], F32); self.B_onesf = Buf("onesf")
        self.mhalf = sb("mhalf", [128, 512], F32); self.B_mhalf = Buf("mhalf")
        self.zeros = sb("zeros", [128, 128], F32); self.B_zeros = Buf("zeros")
        self.biasT = sb("biasT_sb", [128, 3, 2, 512], BF16); self.B_biasT = Buf("biasT")
        self.jm = sb("jm_sb", [128, 4], F32); self.B_jm = Buf("jm")
        self.nrm = sb("nrm_sb", [128, 128], F32); self.B_nrm = Buf("nrm")
        self.gout = [sb(f"gout_sb{l}", [128, 8], F32) for l in range(self.n_layers)]
        self.B_gout = [Buf(f"gout{l}") for l in range(self.n_layers)]
        self.lnp = sb("lnp_sb", [128, 32], F32); self.B_lnp = Buf("lnp")
        self.cvp = sb("cvp_sb", [128, NFC, 4], F32); self.B_cvp = Buf("cvp")
        self.cvm = sb("cvm_sb", [128, NFC, 2], F32); self.B_cvm = Buf("cvm")
        self.esink = sb("esink", [128, 8], F32); self.B_esink = Buf("esink")
        self.esrow = sb("esrow", [128, 2, 512], F32); self.B_esrow = Buf("esrow")
        self.cst = [sb(f"cst{i}", [128, 64], F32) for i in range(2)]; self.B_cst = [Buf(f"cst{i}") for i in range(2)]
        self.xt = [sb(f"xt{i}", [128, D], BF16) for i in range(2)]; self.B_xt = [Buf(f"xt{i}") for i in range(2)]
        self.xTb = sb("xTb", [128, 8, 512], BF16); self.B_xTb = Buf("xTb")
        self.xTt = [sb(f"xTt{i}", [128, 8, 128], BF16) for i in range(2)]; self.B_xTt = [Buf(f"xTt{i}") for i in range(2)]
        self.wq = sb("wq_sb", [128, 8, 512], BF16); self.B_wq = Buf("wq")
        self.wkv = self.wq; self.B_wkv = self.B_wq
        self.fscr = sb("fscr", [128, 8], F32)
        self.t_sq = sb("t_sq", [128, 512], F32); self.B_tsq = Buf("t_sq")
        self.t_qn = sb("t_qn", [128, 512], F32); self.B_tqn = Buf("t_qn")
        self.t_ab = sb("t_ab", [128, 2, 256], F32); self.B_tab = Buf("t_ab")
        self.t_m = sb("t_m", [128, 4, 256], F32); self.B_tm = Buf("t_m")
        self.t_ss = sb("t_ss", [128, 16], F32); self.B_tss = Buf("t_ss")
        self.t_rs = sb("t_rs", [128, 16], F32); self.B_trs = Buf("t_rs")
        self.qbf = [sb(f"qbf{i}", [128, 512], BF16) for i in range(2)]; self.B_qbf = [Buf(f"qbf{i}") for i in range(2)]
        self.kvb = [sb(f"kvb{i}", [128, 256], BF16) for i in range(2)]; self.B_kvb = [Buf(f"kvb{i}") for i in range(2)]
        A1 = sb("A1", [128, 6144], BF16)
        self.QT = A1[:, 0:4096].rearrange("p (a n) -> p a n", a=8); self.B_QT = Buf("QT")
        self.PT = [A1[:, 4096 + i * 1024:5120 + i * 1024].rearrange("p (a n) -> p a n", a=2) for i in range(2)]
        self.B_PT = [Buf(f"PT{i}") for i in range(2)]
        self.hT = A1[:, 0:5632].rearrange("p (a n) -> p a n", a=11); self.B_hT = [Buf(f"hT{i}") for i in range(11)]
        A2 = sb("A2", [128, 14336], BF16)
        f32v = lambda a, b: A2[:, a:b].bitcast(F32)
        self.yT = A2[:, 0:4096].rearrange("p (a n) -> p a n", a=8); self.B_yT = Buf("yT")
        self.stt = [f32v(4096, 6144).rearrange("p (a n) -> p a n", a=2)]; self.B_stt = [Buf("stt0")]
        self.bcs = [f32v(6144 + i * 1024, 7168 + i * 1024) for i in range(2)]; self.B_bcs = [Buf(f"bcs{i}") for i in range(2)]
        self.rec = [f32v(8192 + i * 1024, 9216 + i * 1024) for i in range(2)]; self.B_rec = [Buf(f"rec{i}") for i in range(2)]
        self.rr = self.rec; self.B_rr = self.B_rec
        self.sqy = [A2[:, 10240 + i * 512:10752 + i * 512] for i in range(2)]; self.B_sqy = [Buf(f"sqy{i}") for i in range(2)]
        self.wgu = [A2[:, i * 2048:(i + 1) * 2048].rearrange("p (a n) -> p a n", a=8) for i in range(3)]
        self.B_wgu = [Buf(f"wgu{i}") for i in range(3)]
        self.gcs = [f32v(6144 + i * 1024, 7168 + i * 1024) for i in range(2)]; self.B_gcs = [Buf(f"gcs{i}") for i in range(2)]
        self.tgs = [f32v(8192 + i * 1024, 9216 + i * 1024) for i in range(2)]; self.B_tgs = [Buf(f"tgs{i}") for i in range(2)]
        self.y2 = [f32v(10240 + i * 1024, 11264 + i * 1024) for i in range(2)]; self.B_y2 = [Buf(f"y2_{i}") for i in range(2)]
        self.ot = [f32v(12288 + i * 1024, 13312 + i * 1024).rearrange("p (a n) -> p a n", a=4) for i in range(2)]
        self.B_ot = [Buf(f"ot{i}") for i in range(2)]
        self.att_bufs = [self.B_QT] + self.B_PT + [self.B_yT] + self.B_stt + self.B_bcs + self.B_rec + self.B_sqy
        self.ffn_bufs = self.B_hT + self.B_wgu + self.B_gcs + self.B_tgs + self.B_y2 + self.B_ot
        self.wo = [sb(f"wo{i}", [128, 8, 128], BF16) for i in range(2)]; self.B_wo = [Buf(f"wo{i}") for i in range(2)]
        self.z = sb("z", [128, 8, 512], F32); self.B_z = [Buf(f"z{m}") for m in range(8)]
        self.zb = [sb(f"zb{i}", [128, 512], BF16) for i in range(2)]; self.B_zb = [Buf(f"zb{i}") for i in range(2)]
        self.zq = [sb(f"zq{i}", [128, 512], BF16) for i in range(2)]; self.B_zq = [Buf(f"zq{i}") for i in range(2)]
        self.mean = sb("mean", [128, 512], F32); self.B_mean = Buf("mean")
        self.rstd = sb("rstd", [128, 512], F32); self.B_rstd = Buf("rstd")
        self.tmpa = [sb(f"tmpa{i}", [128, 512], F32) for i in range(2)]; self.B_tmpa = [Buf(f"tmpa{i}") for i in range(2)]
        self.tmpb = [sb(f"tmpb{i}", [128, 512], F32) for i in range(2)]; self.B_tmpb = [Buf(f"tmpb{i}") for i in range(2)]
        self.XB = [sb(f"XB{i}", [128, 8, 512], BF16) for i in range(2)]; self.B_XB = [Buf(f"XB{i}") for i in range(2)]
        self.XH = sb("XH", [128, 8, 2], BF16); self.B_XH = Buf("XH")
        self.LC = sb("LC", [128, 8, 8], BF16); self.B_LC = [Buf(f"LC{k}") for k in range(8)]
        self.HC = [sb(f"HC{i}", [128, 8, 2], BF16) for i in range(2)]; self.B_HC = [Buf(f"HC{i}") for i in range(2)]
        self.wdp = [sb(f"wdp{i}", [128, 384], BF16) for i in range(3)]; self.B_wdp = [Buf(f"wdp{i}") for i in range(3)]
        self.cvt = [self.z[:, 2 * i:2 * i + 2, :].rearrange("p a n -> p (a n)") for i in range(2)]
        self.B_cvt = [[self.B_z[2 * i], self.B_z[2 * i + 1]] for i in range(2)]
        self.cvo = [self.z[:, 4 + i, :].bitcast(BF16) for i in range(2)]
        self.B_cvo = [[self.B_z[4 + i]] for i in range(2)]
        self.ps = self.st.enter_context(nc.psum_tensor("ps", [128, 8, 512], F32))
        self.B_ps = [Buf(f"ps{i}") for i in range(8)]

    def bslot(self, tk):
        sl = (tk - (self.own_off - 2)) % NT
        return sl if sl < 36 else None

    def fence(self, bufs):
        self.P.op("pool", lambda e: e.memset(self.fscr[:], 0.0), writes=list(bufs))

    def psb(self, b):
        return self.ps[:, b, :].bitcast(BF16).rearrange("p (a n) -> p a n", a=8)

    def setup_consts(self):
        P = self.P
        P.dma("sp", lambda e: e.dma_start(out=self.ident_f[:], in_=self.ident_in), writes=[self.B_idf])
        P.op("act", lambda e: e.activation(out=self.ident_b[:], in_=self.ident_f[:], func=AF.Copy),
             reads=[self.B_idf], writes=[self.B_idb])
        P.op("pool", lambda e: e.memset(self.ones_b[:], 1.0), writes=[self.B_ones])
        P.op("pool", lambda e: e.memset(self.onesf[:], 1.0), writes=[self.B_onesf])
        P.op("pool", lambda e: e.memset(self.mhalf[:], -0.5), writes=[self.B_mhalf])
        P.op("pool", lambda e: e.memset(self.zeros[:], 0.0), writes=[self.B_zeros])
        P.op("pool", lambda e: e.memset(self.VA[:, :, :, 64:65], 1.0), writes=[self.B_vones])
        P.op("pool", lambda e: e.memset(self.VB[:, :, :, 64:65], 1.0), writes=[self.B_vones])
        P.dma("pool", lambda e: e.dma_start(out=self.biasT[:].rearrange("p a b n -> p (a b n)"), in_=self.biasT_in),
              writes=[self.B_biasT])
        P.dma("sp", lambda e: e.dma_start(out=self.jm[:], in_=self.jm_in), writes=[self.B_jm])

    def setup_layer(self, l, own_off):
        P = self.P
        w = self.W[l]
        P.dma("sp", lambda e: e.dma_start(out=self.nrm[:], in_=w["nrm"].partition_broadcast(128).rearrange("p a n -> p (a n)")), writes=[self.B_nrm])
        P.dma("sp", lambda e: e.dma_start(out=self.lnp[:], in_=w["lnp"]), writes=[self.B_lnp])
        P.dma("sp", lambda e: e.dma_start(out=self.cvp[:].rearrange("p c k -> p (c k)"), in_=w["cvp"]), writes=[self.B_cvp])
        P.dma("sp", lambda e: e.dma_start(out=self.esink[:], in_=w["sink"].partition_broadcast(128).rearrange("p a n -> p (a n)")), writes=[self.B_esink])
        P.op("act", lambda e: e.activation(out=self.esink[:], in_=self.esink[:], func=AF.Exp),
             reads=[self.B_esink], writes=[self.B_esink])
        for kvh in range(2):
            for c in range(4):
                h = kvh * 4 + c
                P.op("dve", lambda e, kvh=kvh, c=c, h=h: e.tensor_scalar(
                    self.esrow[64:65, kvh, c * 128:(c + 1) * 128], self.zeros[64:65, :],
                    self.esink[64:65, h:h + 1], None, ALU.add),
                    reads=[self.B_esink, self.B_zeros], writes=[self.B_esrow])
        jl = 2 if own_off == 0 else 3
        jr = 3 if own_off == 0 else 2
        P.op("dve", lambda e: e.tensor_scalar(self.cvm[:, :, 0], self.cvp[:, :, 0], self.jm[:, jl:jl + 1], None, ALU.mult),
             reads=[self.B_cvp, self.B_jm], writes=[self.B_cvm])
        P.op("dve", lambda e: e.tensor_scalar(self.cvm[:, :, 1], self.cvp[:, :, 2], self.jm[:, jr:jr + 1], None, ALU.mult),
             reads=[self.B_cvp, self.B_jm], writes=[self.B_cvm])

    def convert_weights(self, l):
        P = self.P
        w = self.W[l]
        steps = []

        def cast_dma(dst_ap, src_ap, B):
            P.dma("pool", lambda e: e.dma_start(out=dst_ap, in_=src_ap), dwrites=[B])

        for kc in range(8):
            steps.append(lambda kc=kc: cast_dma(w["wkv_s"].ap()[:, kc, :], w["wkv"][kc * 128:(kc + 1) * 128, :], w["B_wkv_s"]))
        for kc in range(8):
            steps.append(lambda kc=kc: cast_dma(w["wq_s"].ap()[:, kc, :], w["wq"][kc * 128:(kc + 1) * 128, :], w["B_wq_s"]))

        def wout_step(c):
            i = c % 2
            if c == 0:
                P.dma("sp", lambda e: e.dma_start(out=self.gout[l][:], in_=w["gout"]), writes=[self.B_gout[l]])
            P.dma("sp", lambda e: e.dma_start(out=self.cvt[i], in_=w["wout"][c * 128:(c + 1) * 128, :]),
                  writes=self.B_cvt[i])
            P.op("dve", lambda e: e.tensor_scalar(self.cvo[i], self.cvt[i], self.gout[l][:, c:c + 1], None, ALU.mult),
                 reads=self.B_cvt[i] + [self.B_gout[l]], writes=self.B_cvo[i])
            P.dma("sp", lambda e: e.dma_start(out=w["wout_s"].ap()[:, :, c, :].rearrange("m p n -> p m n"),
                                              in_=self.cvo[i].rearrange("p (m n) -> p m n", m=8)),
                  reads=self.B_cvo[i], dwrites=[w["B_wout_s"]], owner=w["B_wout_s"])
        for c in range(8):
            steps.append(lambda c=c: wout_step(c))
        for c in range(NFC):
            def gu(c=c):
                cast_dma(w["wgu_s"].ap()[c, :, :, 0:128],
                         w["wg"][:, c * 128:(c + 1) * 128].rearrange("(kc p) n -> p kc n", p=128), w["B_wgu_s"])
                cast_dma(w["wgu_s"].ap()[c, :, :, 128:256],
                         w["wu"][:, c * 128:(c + 1) * 128].rearrange("(kc p) n -> p kc n", p=128), w["B_wgu_s"])
            steps.append(gu)
        for c in range(NFC):
            steps.append(lambda c=c: cast_dma(w["wd_s"].ap()[c], w["wd"][c * 128:(c + 1) * 128, :], w["B_wd_s"]))
        return steps

    def load_xT(self, src, src_is_f32, B_src, t, dst_ap, B_dst, slot):
        P = self.P
        xt, B_xt = self.xt[slot], self.B_xt[slot]
        rd = [B_src] if B_src is not None else []
        if src_is_f32:
            P.dma("pool", lambda e: e.dma_start(out=xt[:], in_=src[t * 128:(t + 1) * 128, :]), reads=rd, writes=[B_xt])
        else:
            P.dma("sp", lambda e: e.dma_start(out=xt[:], in_=src[t * 128:(t + 1) * 128, :]), reads=rd, writes=[B_xt])
        bank = 6 + slot
        pv = self.psb(bank)

        def tr(e):
            for c in range(8):
                ins = e.transpose(pv[:, c, :], xt[:, c * 128:(c + 1) * 128], self.ident_b[:])
            return ins
        P.op("pe", tr, reads=[B_xt, self.B_idb], writes=[self.B_ps[bank]])
        P.op("act", lambda e: e.activation(out=dst_ap, in_=pv, func=AF.Copy), reads=[self.B_ps[bank]], writes=[B_dst])

    def norm_rope(self, src, B_src, nh, goff, cs, B_cs, scale, dst, B_dst):
        P = self.P
        W = nh * 64
        sq = self.t_sq[:, 0:W]
        P.op("act", lambda e: e.activation(out=sq, in_=src, func=AF.Square), reads=[B_src], writes=[self.B_tsq])
        ss = self.t_ss[:, 0:nh]
        P.op("dve", lambda e: e.tensor_reduce(out=ss, in_=sq.rearrange("p (h d) -> p h d", h=nh), axis=AX.X, op=ALU.add),
             reads=[self.B_tsq], writes=[self.B_tss])
        P.op("dve", lambda e: e.tensor_scalar(ss, ss, 1.0 / 64.0, RMS_EPS, ALU.mult, ALU.add),
             reads=[self.B_tss], writes=[self.B_tss])
        rs = self.t_rs[:, 0:nh]
        P.op("pool", lambda e: e.tensor_tensor(rs, ss, self.mhalf[:, 0:nh], ALU.pow),
             reads=[self.B_tss, self.B_mhalf], writes=[self.B_trs])
        if scale != 1.0:
            P.op("pool", lambda e: e.tensor_scalar(rs, rs, float(scale), None, ALU.mult),
                 reads=[self.B_trs], writes=[self.B_trs])
        qn = self.t_qn[:, 0:W].rearrange("p (h d) -> p h d", h=nh)
        P.op("dve", lambda e: e.tensor_tensor(qn, src.rearrange("p (h d) -> p h d", h=nh),
                                              rs.unsqueeze(2).to_broadcast([128, nh, 64]), ALU.mult),
             reads=[B_src, self.B_trs], writes=[self.B_tqn])
        x0 = qn[:, :, 0::2]
        x1 = qn[:, :, 1::2]
        ge = self.nrm[:, goff:goff + 32].unsqueeze(1).to_broadcast([128, nh, 32])
        go = self.nrm[:, goff + 32:goff + 64].unsqueeze(1).to_broadcast([128, nh, 32])
        cosb = cs[:, 0:32].unsqueeze(1).to_broadcast([128, nh, 32])
        sinb = cs[:, 32:64].unsqueeze(1).to_broadcast([128, nh, 32])
        a = self.t_ab[:, 0, 0:nh * 32].rearrange("p (h d) -> p h d", h=nh)
        b = self.t_ab[:, 1, 0:nh * 32].rearrange("p (h d) -> p h d", h=nh)
        P.op("dve", lambda e: e.tensor_tensor(a, x0, ge, ALU.mult), reads=[self.B_tqn, self.B_nrm], writes=[self.B_tab])
        P.op("pool", lambda e: e.tensor_tensor(b, x1, go, ALU.mult), reads=[self.B_tqn, self.B_nrm], writes=[self.B_tab])
        m = [self.t_m[:, i, 0:nh * 32].rearrange("p (h d) -> p h d", h=nh) for i in range(4)]
        P.op("dve", lambda e: e.tensor_tensor(m[0], a, cosb, ALU.mult), reads=[self.B_tab, B_cs], writes=[self.B_tm])
        P.op("pool", lambda e: e.tensor_tensor(m[1], b, sinb, ALU.mult), reads=[self.B_tab, B_cs], writes=[self.B_tm])
        P.op("dve", lambda e: e.tensor_tensor(m[2], a, sinb, ALU.mult), reads=[self.B_tab, B_cs], writes=[self.B_tm])
        P.op("pool", lambda e: e.tensor_tensor(m[3], b, cosb, ALU.mult), reads=[self.B_tab, B_cs], writes=[self.B_tm])
        d3 = dst.rearrange("p (h d) -> p h d", h=nh)
        P.op("dve", lambda e: e.tensor_tensor(d3[:, :, 0:32], m[0], m[1], ALU.subtract), reads=[self.B_tm], writes=[B_dst])
        P.op("pool", lambda e: e.tensor_tensor(d3[:, :, 32:64], m[2], m[3], ALU.add), reads=[self.B_tm], writes=[B_dst])

    def load_cs(self, t):
        i = t % 2
        self.P.dma("sp", lambda e: e.dma_start(out=self.cst[i][:], in_=self.cs_in[:, t, :]), writes=[self.B_cst[i]])
        return self.cst[i], self.B_cst[i]

    def kv_phase(self, l, src, src_is_f32, B_src, pending):
        P = self.P
        w = self.W[l]
        P.dma("sp", lambda e: e.dma_start(out=self.wkv[:], in_=w["wkv_s"].ap()), reads=[w["B_wkv_s"]], writes=[self.B_wkv])
        for t in range(NT):
            slot = t % 2
            self.load_xT(src, src_is_f32, B_src, t, self.xTt[slot][:], self.B_xTt[slot], slot)
            for _ in range(3):
                if pending:
                    pending.pop(0)()
            xT = self.xTt[slot]
            bk = 4 + slot
            pk = self.ps[:, bk, :]

            def mm(e, xT=xT, pk=pk):
                for c in range(8):
                    ins = e.matmul(pk, lhsT=xT[:, c, :], rhs=self.wkv[:, c, :], start=(c == 0), stop=(c == 7))
                return ins
            P.op("pe", mm, reads=[self.B_xTt[slot], self.B_wkv], writes=[self.B_ps[bk]])
            cs, B_cs = self.load_cs(t)
            kvb, B_kvb = self.kvb[slot], self.B_kvb[slot]
            self.norm_rope(pk[:, 0:128], self.B_ps[bk], 2, 64, cs, B_cs, 1.0, kvb[:, 0:128], B_kvb)
            P.op("act", lambda e, kvb=kvb, pk=pk: e.activation(out=kvb[:, 128:256], in_=pk[:, 256:384], func=AF.Copy),
                 reads=[self.B_ps[bk]], writes=[B_kvb])
            P.op("act", lambda e, t=t, pk=pk: e.activation(out=self.VA[:, t, :, 0:64],
                                                           in_=pk[:, 128:256].rearrange("p (h d) -> p h d", h=2), func=AF.Copy),
                 reads=[self.B_ps[bk], self.B_vones], writes=[self.B_KA[t]])
            sl = self.bslot(t)
            if sl is not None:
                P.op("act", lambda e, sl=sl, pk=pk: e.activation(out=self.VB[:, sl, :, 0:64],
                                                                 in_=pk[:, 384:512].rearrange("p (h d) -> p h d", h=2), func=AF.Copy),
                     reads=[self.B_ps[bk], self.B_vones], writes=[self.B_KB[t]])
            bt = 2 + slot
            pt = self.psb(bt)

            def trk(e, kvb=kvb, pt=pt):
                e.transpose(pt[:, 0, :], kvb[:, 0:128], self.ident_b[:])
                return e.transpose(pt[:, 1, :], kvb[:, 128:256], self.ident_b[:])
            P.op("pe", trk, reads=[B_kvb, self.B_idb], writes=[self.B_ps[bt]])
            P.op("dve", lambda e, t=t, pt=pt: e.tensor_copy(self.KAT[:, t * 128:(t + 1) * 128], pt[:, 0, :]),
                 reads=[self.B_ps[bt]], writes=[self.B_KA[t]])
            if sl is not None:
                P.op("dve", lambda e, sl=sl, pt=pt: e.tensor_copy(self.KBT[:, sl * 128:(sl + 1) * 128], pt[:, 1, :]),
                     reads=[self.B_ps[bt]], writes=[self.B_KB[t]])

    def att_block(self, l, src, src_is_f32, B_src, tiles, col0, ncol, x1_dst, B_x1):
        P = self.P
        w = self.W[l]
        nt = len(tiles)
        ntok = nt * 128
        for i, t in enumerate(tiles):
            self.load_xT(src, src_is_f32, B_src, t, self.xTb[:, :, i * 128:(i + 1) * 128], self.B_xTb, i % 2)
        for half in range(2):
            P.dma("sp", lambda e, half=half: e.dma_start(out=self.wq[:], in_=w["wq_s"].ap()[:, :, half * 512:(half + 1) * 512]),
                  reads=[w["B_wq_s"]], writes=[self.B_wq])
            for i, t in enumerate(tiles):
                bk = 4 + (i % 2)
                pq = self.ps[:, bk, :]

                def mm(e, i=i, pq=pq):
                    for c in range(8):
                        ins = e.matmul(pq, lhsT=self.xTb[:, c, i * 128:(i + 1) * 128], rhs=self.wq[:, c, :],
                                       start=(c == 0), stop=(c == 7))
                    return ins
                P.op("pe", mm, reads=[self.B_xTb, self.B_wq], writes=[self.B_ps[bk]])
                qb, B_qb = self.qbf[i % 2], self.B_qbf[i % 2]
                if half == 0:
                    cs, B_cs = self.load_cs(t)
                    self.norm_rope(pq, self.B_ps[bk], 8, 0, cs, B_cs, 0.125, qb[:, 0:512], B_qb)
                else:
                    P.op("act", lambda e, qb=qb, pq=pq: e.activation(out=qb[:, 0:512], in_=pq, func=AF.Copy),
                         reads=[self.B_ps[bk]], writes=[B_qb])
                bt = 2 + (i % 2)
                pt = self.psb(bt)

                def trq(e, qb=qb, pt=pt):
                    for c in range(4):
                        ins = e.transpose(pt[:, c, :], qb[:, c * 128:(c + 1) * 128], self.ident_b[:])
                    return ins
                P.op("pe", trq, reads=[B_qb, self.B_idb], writes=[self.B_ps[bt]])
                P.op("dve", lambda e, i=i, half=half, pt=pt: e.tensor_copy(
                    self.QT[:, half * 4:(half + 1) * 4, i * 128:(i + 1) * 128], pt[:, 0:4, :]),
                    reads=[self.B_ps[bt]], writes=[self.B_QT])
        for c in range(4):
            accb = (4, 5)

            def qk(kt, c=c):
                sb0 = 2 * (kt % 2)

                def f(e):
                    e.matmul(self.ps[:, sb0, 0:ntok], lhsT=self.KAT[0:64, kt * 128:(kt + 1) * 128],
                             rhs=self.QT[0:64, c, 0:ntok], start=True, stop=True)
                    return e.matmul(self.ps[:, sb0 + 1, 0:ntok], lhsT=self.KAT[64:128, kt * 128:(kt + 1) * 128],
                                    rhs=self.QT[64:128, c, 0:ntok], start=True, stop=True)
                P.op("pe", f, reads=[self.B_KA[kt], self.B_QT], writes=[self.B_ps[sb0], self.B_ps[sb0 + 1]])

            def ex(kt):
                sb0 = 2 * (kt % 2)
                pt = self.PT[kt % 2]
                P.op("act", lambda e: e.activation(out=pt[:, :, 0:ntok], in_=self.ps[:, sb0:sb0 + 2, 0:ntok], func=AF.Exp),
                     reads=[self.B_ps[sb0], self.B_ps[sb0 + 1]], writes=[self.B_PT[kt % 2]])

            def pv(kt):
                pt = self.PT[kt % 2]

                def f(e):
                    e.matmul(self.ps[0:65, accb[0], 0:ntok], lhsT=self.VA[:, kt, 0, :], rhs=pt[:, 0, 0:ntok],
                             start=(kt == 0), stop=(kt == NT - 1))
                    return e.matmul(self.ps[0:65, accb[1], 0:ntok], lhsT=self.VA[:, kt, 1, :], rhs=pt[:, 1, 0:ntok],
                                    start=(kt == 0), stop=(kt == NT - 1))
                P.op("pe", f, reads=[self.B_KA[kt], self.B_PT[kt % 2]], writes=[self.B_ps[accb[0]], self.B_ps[accb[1]]])

            qk(0)
            for kt in range(1, NT):
                ex(kt - 1)
                qk(kt)
                pv(kt - 1)
            ex(NT - 1)
            pv(NT - 1)
            for kvh in range(2):
                self.finalize_head(accb[kvh], kvh, None, self.yT[kvh * 64:(kvh + 1) * 64, c, 0:ntok], ntok)
        for i, t in enumerate(tiles):
            accb = (4, 5)
            nbrs = [((t - 1) % NT, 0), (t, 1), ((t + 1) % NT, 2)]
            for jj, (tk, jidx) in enumerate(nbrs):
                sb0 = 2 * (jj % 2)
                st = self.stt[0]
                B_st = self.B_stt[0]

                ks = self.bslot(tk)
                assert ks is not None

                def f(e, ks=ks, sb0=sb0, i=i):
                    e.matmul(self.ps[:, sb0, :], lhsT=self.KBT[0:64, ks * 128:(ks + 1) * 128],
                             rhs=self.QT[0:64, 4:8, i * 128:(i + 1) * 128], start=True, stop=True)
                    return e.matmul(self.ps[:, sb0 + 1, :], lhsT=self.KBT[64:128, ks * 128:(ks + 1) * 128],
                                    rhs=self.QT[64:128, 4:8, i * 128:(i + 1) * 128], start=True, stop=True)
                P.op("pe", f, reads=[self.B_KB[tk], self.B_QT], writes=[self.B_ps[sb0], self.B_ps[sb0 + 1]])
                for kvh in range(2):
                    P.op("dve", lambda e, kvh=kvh, sb0=sb0, st=st, jidx=jidx: e.scalar_tensor_tensor(
                        out=st[:, kvh, :], in0=self.ps[:, sb0 + kvh, :], scalar=0.125, in1=self.biasT[:, jidx, kvh, :],
                        op0=ALU.mult, op1=ALU.add),
                        reads=[self.B_ps[sb0 + kvh], self.B_biasT], writes=[B_st])
                jcol = None
                if jidx == 0 and t == 0:
                    jcol = 0
                elif jidx == 0 and t == 32:
                    jcol = 1
                elif jidx == 2 and t == 63:
                    jcol = 0
                elif jidx == 2 and t == 31:
                    jcol = 1
                if jcol is not None:
                    P.op("dve", lambda e, st=st, jcol=jcol: e.tensor_scalar(
                        st[:].rearrange("p a n -> p (a n)"), st[:].rearrange("p a n -> p (a n)"),
                        self.jm[:, jcol:jcol + 1], None, ALU.add),
                        reads=[B_st, self.B_jm], writes=[B_st])
                pt = self.PT[jj % 2]
                P.op("act", lambda e, pt=pt, st=st: e.activation(out=pt[:], in_=st[:], func=AF.Exp),
                     reads=[B_st], writes=[self.B_PT[jj % 2]])

                def g(e, ks=ks, pt=pt, jj=jj):
                    e.matmul(self.ps[0:65, accb[0], :], lhsT=self.VB[:, ks, 0, :], rhs=pt[:, 0, :],
                             start=(jj == 0), stop=(jj == 2))
                    return e.matmul(self.ps[0:65, accb[1], :], lhsT=self.VB[:, ks, 1, :], rhs=pt[:, 1, :],
                                    start=(jj == 0), stop=(jj == 2))
                P.op("pe", g, reads=[self.B_KB[tk], self.B_PT[jj % 2]], writes=[self.B_ps[accb[0]], self.B_ps[accb[1]]])
            for kvh in range(2):
                self.finalize_head(accb[kvh], kvh, kvh, self.yT[kvh * 64:(kvh + 1) * 64, 4:8, i * 128:(i + 1) * 128], 512)
        self.out_stage(l, col0, ncol)
        self.layer_norm(0, col0, ncol, lambda m: x1_dst[:, m, :], B_x1, BF16)

    def finalize_head(self, bank, kvh, sink_kvh, y_dst, n):
        P = self.P
        i = kvh
        rec, B_rec = self.rec[i], self.B_rec[i]
        accv = self.ps[:, bank, 0:n]
        if sink_kvh is not None:
            P.op("dve", lambda e: e.tensor_tensor(rec[64:65, 0:n], self.ps[64:65, bank, 0:n], self.esrow[64:65, sink_kvh, 0:n], ALU.add),
                 reads=[self.B_ps[bank], self.B_esrow], writes=[B_rec])
            P.op("dve", lambda e: e.reciprocal(rec[64:65, 0:n], rec[64:65, 0:n]), reads=[B_rec], writes=[B_rec])
        else:
            P.op("dve", lambda e: e.reciprocal(rec[64:65, 0:n], self.ps[64:65, bank, 0:n]), reads=[self.B_ps[bank]], writes=[B_rec])
        bb = 6 + i
        P.op("pe", lambda e: e.matmul(self.ps[0:64, bb, 0:n], lhsT=self.onesf[64:65, 0:64], rhs=rec[64:65, 0:n],
                                      start=True, stop=True),
             reads=[B_rec, self.B_onesf], writes=[self.B_ps[bb]])
        bcs, B_bcs = self.bcs[i], self.B_bcs[i]
        P.op("act", lambda e: e.activation(out=bcs[0:64, 0:n], in_=self.ps[0:64, bb, 0:n], func=AF.Copy),
             reads=[self.B_ps[bb]], writes=[B_bcs])
        if len(y_dst.shape) == 3:
            in0 = self.ps[0:64, bank, 0:n].rearrange("p (a n) -> p a n", a=4)
            in1 = bcs[0:64, 0:n].rearrange("p (a n) -> p a n", a=4)
        else:
            in0 = self.ps[0:64, bank, 0:n]
            in1 = bcs[0:64, 0:n]
        P.op("dve", lambda e: e.tensor_tensor(y_dst, in0, in1, ALU.mult),
             reads=[self.B_ps[bank], B_bcs], writes=[self.B_yT])

    def out_stage(self, l, col0, ncol):
        P = self.P
        w = self.W[l]
        cs = slice(col0, col0 + ncol)
        for g in range(2):
            bank = g
            for cc in range(4):
                c = g * 4 + cc
                sq, B_sq = self.sqy[cc % 2], self.B_sqy[cc % 2]
                P.op("act", lambda e, sq=sq, c=c: e.activation(out=sq[:, 0:ncol], in_=self.yT[:, c, cs], func=AF.Square),
                     reads=[self.B_yT], writes=[B_sq])
                P.op("pe", lambda e, sq=sq, cc=cc, bank=bank: e.matmul(self.ps[:, bank, 0:ncol], lhsT=self.ones_b[:], rhs=sq[:, 0:ncol],
                                                                    start=(cc == 0), stop=(cc == 3)),
                     reads=[B_sq, self.B_ones], writes=[self.B_ps[bank]])
            rr, B_rr = self.rr[g], self.B_rr[g]
            P.op("dve", lambda e, rr=rr, bank=bank: e.tensor_scalar(rr[:, 0:ncol], self.ps[:, bank, 0:ncol], 1.0 / 512.0, RMS_EPS,
                                                                   ALU.mult, ALU.add),
                 reads=[self.B_ps[bank]], writes=[B_rr])
            P.op("pool", lambda e, rr=rr: e.tensor_tensor(rr[:, 0:ncol], rr[:, 0:ncol], self.mhalf[:, 0:ncol], ALU.pow),
                 reads=[B_rr, self.B_mhalf], writes=[B_rr])
        for m in range(8):
            wo, B_wo = self.wo[m % 2], self.B_wo[m % 2]
            P.dma("sp", lambda e, wo=wo, m=m: e.dma_start(out=wo[:], in_=w["wout_s"].ap()[m]),
                  reads=[w["B_wout_s"]], writes=[B_wo])
            ba = 2 + 2 * (m % 2)

            def mm(e, wo=wo, ba=ba, m=m):
                for g in range(2):
                    for cc in range(4):
                        c = g * 4 + cc
                        ins = e.matmul(self.ps[:, ba + g, 0:ncol], lhsT=wo[:, c, :], rhs=self.yT[:, c, cs],
                                       start=(cc == 0), stop=(cc == 3))
                return ins
            P.op("pe", mm, reads=[B_wo, self.B_yT], writes=[self.B_ps[ba], self.B_ps[ba + 1]])
            ta, B_ta = self.tmpa[m % 2], self.B_tmpa[m % 2]
            tb, B_tb = self.tmpb[m % 2], self.B_tmpb[m % 2]
            P.op("dve", lambda e, ta=ta, ba=ba: e.tensor_tensor(ta[:, 0:ncol], self.ps[:, ba, 0:ncol], self.rr[0][:, 0:ncol], ALU.mult),
                 reads=[self.B_ps[ba], self.B_rr[0]], writes=[B_ta])
            P.op("dve", lambda e, tb=tb, ba=ba: e.tensor_tensor(tb[:, 0:ncol], self.ps[:, ba + 1, 0:ncol], self.rr[1][:, 0:ncol], ALU.mult),
                 reads=[self.B_ps[ba + 1], self.B_rr[1]], writes=[B_tb])
            P.op("pool", lambda e, ta=ta, tb=tb: e.tensor_tensor(ta[:, 0:ncol], ta[:, 0:ncol], tb[:, 0:ncol], ALU.add),
                 reads=[B_ta, B_tb], writes=[B_ta])
            P.op("dve", lambda e, ta=ta, m=m: e.scalar_tensor_tensor(out=self.z[:, m, 0:ncol], in0=self.xTb[:, m, cs], scalar=ALPHA,
                                                                     in1=ta[:, 0:ncol], op0=ALU.mult, op1=ALU.add),
                 reads=[self.B_xTb, B_ta], writes=[self.B_z[m]])

    def layer_norm(self, which, col0_unused, ncol, dst_fn, B_dst, out_dt):
        P = self.P
        gcol = 0 if which == 0 else 16
        bcol = gcol + 8
        for m in range(8):
            zb, B_zb = self.zb[m % 2], self.B_zb[m % 2]
            zq, B_zq = self.zq[m % 2], self.B_zq[m % 2]
            P.op("act", lambda e, zb=zb, m=m: e.activation(out=zb[:, 0:ncol], in_=self.z[:, m, 0:ncol], func=AF.Copy),
                 reads=[self.B_z[m]], writes=[B_zb])
            P.op("act", lambda e, zq=zq, m=m: e.activation(out=zq[:, 0:ncol], in_=self.z[:, m, 0:ncol], func=AF.Square),
                 reads=[self.B_z[m]], writes=[B_zq])
            P.op("pe", lambda e, zb=zb, m=m: e.matmul(self.ps[:, 0, 0:ncol], lhsT=self.ones_b[:], rhs=zb[:, 0:ncol],
                                                      start=(m == 0), stop=(m == 7)),
                 reads=[B_zb, self.B_ones], writes=[self.B_ps[0]])
            P.op("pe", lambda e, zq=zq, m=m: e.matmul(self.ps[:, 1, 0:ncol], lhsT=self.ones_b[:], rhs=zq[:, 0:ncol],
                                                      start=(m == 0), stop=(m == 7)),
                 reads=[B_zq, self.B_ones], writes=[self.B_ps[1]])
        mean, rstd = self.mean, self.rstd
        P.op("dve", lambda e: e.tensor_scalar(mean[:, 0:ncol], self.ps[:, 0, 0:ncol], 1.0 / D, None, ALU.mult),
             reads=[self.B_ps[0]], writes=[self.B_mean])
        P.op("pool", lambda e: e.tensor_tensor(rstd[:, 0:ncol], mean[:, 0:ncol], mean[:, 0:ncol], ALU.mult),
             reads=[self.B_mean], writes=[self.B_rstd])
        P.op("dve", lambda e: e.scalar_tensor_tensor(out=rstd[:, 0:ncol], in0=self.ps[:, 1, 0:ncol], scalar=1.0 / D,
                                                     in1=rstd[:, 0:ncol], op0=ALU.mult, op1=ALU.subtract),
             reads=[self.B_ps[1], self.B_rstd], writes=[self.B_rstd])
        P.op("dve", lambda e: e.tensor_scalar(rstd[:, 0:ncol], rstd[:, 0:ncol], LN_EPS, None, ALU.add),
             reads=[self.B_rstd], writes=[self.B_rstd])
        P.op("pool", lambda e: e.tensor_tensor(rstd[:, 0:ncol], rstd[:, 0:ncol], self.mhalf[:, 0:ncol], ALU.pow),
             reads=[self.B_rstd, self.B_mhalf], writes=[self.B_rstd])
        for m in range(8):
            ta, B_ta = self.tmpa[m % 2], self.B_tmpa[m % 2]
            P.op("dve", lambda e, ta=ta, m=m: e.tensor_tensor(ta[:, 0:ncol], self.z[:, m, 0:ncol], mean[:, 0:ncol], ALU.subtract),
                 reads=[self.B_z[m], self.B_mean], writes=[B_ta])
            P.op("pool", lambda e, ta=ta: e.tensor_tensor(ta[:, 0:ncol], ta[:, 0:ncol], rstd[:, 0:ncol], ALU.mult),
                 reads=[B_ta, self.B_rstd], writes=[B_ta])
            dst = dst_fn(m)
            Bd = B_dst(m) if callable(B_dst) else B_dst
            P.op("act", lambda e, ta=ta, m=m, dst=dst: e.activation(out=dst, in_=ta[:, 0:ncol], func=AF.Identity,
                                                                    scale=self.lnp[:, gcol + m:gcol + m + 1],
                                                                    bias=self.lnp[:, bcol + m:bcol + m + 1]),
                 reads=[B_ta, self.B_lnp], writes=[Bd])

    def ffn_block(self, l, k, X, B_X, hl_ap, B_hl, hr_ap, B_hr, first, last, dst, dst_dt, B_dstbuf, row0):
        P = self.P
        w = self.W[l]
        HC, B_HC = self.HC[k % 2], self.B_HC[k % 2]
        P.op("pool", lambda e: e.tensor_copy(HC[:, :, 0:1], hl_ap), reads=[B_hl], writes=[B_HC])
        P.op("pool", lambda e: e.tensor_copy(HC[:, :, 1:2], hr_ap), reads=[B_hr], writes=[B_HC])
        for m in range(8):
            P.op("act", lambda e, m=m: e.activation(out=self.z[:, m, :], in_=X[:, m, :], func=AF.Copy, scale=ALPHA),
                 reads=[B_X], writes=[self.B_z[m]])
        for hh in range(2):
            for cc in range(11):
                c = hh * 11 + cc
                wg, B_wg = self.wgu[c % 3], self.B_wgu[c % 3]
                P.dma("sp", lambda e, wg=wg, c=c: e.dma_start(out=wg[:], in_=w["wgu_s"].ap()[c]),
                      reads=[w["B_wgu_s"]], writes=[B_wg])
                gb = c % 2
                ub = 2 + c % 2
                hcol = (c % 2) * 2

                def mm(e, wg=wg, gb=gb, ub=ub, hcol=hcol):
                    for kc in range(8):
                        e.matmul(self.ps[:, gb, :], lhsT=wg[:, kc, 0:128], rhs=X[:, kc, :], start=(kc == 0), stop=(kc == 7))
                    for kc in range(8):
                        e.matmul(self.ps[:, 7, hcol:hcol + 2], lhsT=wg[:, kc, 0:128], rhs=HC[:, kc, :], start=(kc == 0), stop=(kc == 7))
                    for kc in range(8):
                        ins = e.matmul(self.ps[:, ub, :], lhsT=wg[:, kc, 128:256], rhs=X[:, kc, :], start=(kc == 0), stop=(kc == 7))
                    return ins
                P.op("pe", mm, reads=[B_wg, B_X, B_HC], writes=[self.B_ps[gb], self.B_ps[ub], self.B_ps[7]])
                gc, B_gc = self.gcs[c % 2], self.B_gcs[c % 2]
                G = self.ps[:, gb, :]
                Gh = self.ps[:, 7, hcol:hcol + 2]
                w0 = self.cvp[:, c, 0:1]
                w1 = self.cvp[:, c, 1:2]
                w2 = self.cvp[:, c, 2:3]
                cb = self.cvp[:, c, 3:4]
                w0e = self.cvm[:, c, 0:1] if first else w0
                w2e = self.cvm[:, c, 1:2] if last else w2
                P.op("act", lambda e, gc=gc, G=G, w1=w1, cb=cb: e.activation(out=gc[:], in_=G, func=AF.Identity, scale=w1, bias=cb),
                     reads=[self.B_ps[gb], self.B_cvp], writes=[B_gc])
                P.op("dve", lambda e, gc=gc, G=G, w0=w0: e.scalar_tensor_tensor(out=gc[:, 1:512], in0=G[:, 0:511], scalar=w0,
                                                                                 in1=gc[:, 1:512], op0=ALU.mult, op1=ALU.add),
                     reads=[self.B_ps[gb], B_gc, self.B_cvp], writes=[B_gc])
                P.op("dve", lambda e, gc=gc, G=G, w2=w2: e.scalar_tensor_tensor(out=gc[:, 0:511], in0=G[:, 1:512], scalar=w2,
                                                                                 in1=gc[:, 0:511], op0=ALU.mult, op1=ALU.add),
                     reads=[self.B_ps[gb], B_gc, self.B_cvp], writes=[B_gc])
                P.op("dve", lambda e, gc=gc, Gh=Gh, w0e=w0e: e.scalar_tensor_tensor(out=gc[:, 0:1], in0=Gh[:, 0:1], scalar=w0e,
                                                                                     in1=gc[:, 0:1], op0=ALU.mult, op1=ALU.add),
                     reads=[self.B_ps[7], B_gc, self.B_cvp, self.B_cvm], writes=[B_gc])
                P.op("dve", lambda e, gc=gc, Gh=Gh, w2e=w2e: e.scalar_tensor_tensor(out=gc[:, 511:512], in0=Gh[:, 1:2], scalar=w2e,
                                                                                     in1=gc[:, 511:512], op0=ALU.mult, op1=ALU.add),
                     reads=[self.B_ps[7], B_gc, self.B_cvp, self.B_cvm], writes=[B_gc])
                tg, B_tg = self.tgs[c % 2], self.B_tgs[c % 2]
                P.op("act", lambda e, tg=tg, gc=gc: e.activation(out=tg[:], in_=gc[:], func=AF.Gelu_apprx_tanh),
                     reads=[B_gc], writes=[B_tg])
                P.op("dve", lambda e, tg=tg, ub=ub, cc=cc: e.tensor_tensor(self.hT[:, cc, :], self.ps[:, ub, :], tg[:], ALU.mult),
                     reads=[self.B_ps[ub], B_tg], writes=[self.B_hT[cc]])
            di = 0
            for grp in ((0, 1, 2), (3, 4, 5), (6, 7)):
                ng = len(grp)
                for cc in range(11):
                    c = hh * 11 + cc
                    wd, B_wd = self.wdp[di % 3], self.B_wdp[di % 3]
                    di += 1
                    P.dma("sp", lambda e, wd=wd, c=c, grp=grp, ng=ng: e.dma_start(
                        out=wd[:, 0:ng * 128], in_=w["wd_s"].ap()[c, :, grp[0] * 128:(grp[0] + ng) * 128]),
                        reads=[w["B_wd_s"]], writes=[B_wd])

                    def mm(e, wd=wd, cc=cc, ng=ng):
                        for mi in range(ng):
                            ins = e.matmul(self.ps[:, 4 + mi, :], lhsT=wd[:, mi * 128:(mi + 1) * 128], rhs=self.hT[:, cc, :],
                                           start=(cc == 0), stop=(cc == 10))
                        return ins
                    P.op("pe", mm, reads=[B_wd, self.B_hT[cc]], writes=[self.B_ps[4 + mi] for mi in range(ng)])
                for mi, m in enumerate(grp):
                    eng = "dve" if mi % 2 == 0 else "dve"
                    P.op(eng, lambda e, mi=mi, m=m: e.tensor_tensor(self.z[:, m, :], self.ps[:, 4 + mi, :], self.z[:, m, :], ALU.add),
                         reads=[self.B_ps[4 + mi], self.B_z[m]], writes=[self.B_z[m]])
        is_f32 = (dst_dt == F32)

        def emit_out(m, y2, B_y2):
            bank = 2 + (m % 2)
            if is_f32:
                pv = self.ps[:, bank, :].rearrange("p (a n) -> p a n", a=4)
                idn, B_idn = self.ident_f, self.B_idf
            else:
                pv = self.psb(bank)[:, 0:4, :]
                idn, B_idn = self.ident_b, self.B_idb

            def tr(e):
                for i in range(4):
                    ins = e.transpose(pv[:, i, :], y2[:, i * 128:(i + 1) * 128], idn[:])
                return ins
            P.op("pe", tr, reads=[B_y2, B_idn], writes=[self.B_ps[bank]])
            if is_f32:
                ot, B_ot = self.ot[m % 2], self.B_ot[m % 2]
                otv = ot[:]
            else:
                ot, B_ot = self.ot[m % 2], self.B_ot[m % 2]
                otv = ot[:].rearrange("p a n -> p (a n)").bitcast(BF16)[:, 0:512].rearrange("p (a n) -> p a n", a=4)
            P.op("act", lambda e: e.activation(out=otv, in_=pv, func=AF.Copy), reads=[self.B_ps[bank]], writes=[B_ot])
            dview = dst[row0:row0 + 512, m * 128:(m + 1) * 128].rearrange("(a p) n -> p a n", p=128)
            P.dma("sp", lambda e: e.dma_start(out=dview, in_=otv), reads=[B_ot], dwrites=[B_dstbuf], owner=B_ot,
                  is_out=True)

        self._ln_out_queue = []

        def dst_fn(m):
            y2 = self.y2[m % 2]
            if is_f32:
                return y2[:]
            return y2[:].bitcast(BF16)[:, 0:512]

        self.layer_norm_with_out(1, 512, dst_fn, lambda m: self.B_y2[m % 2], emit_out, is_f32)

    def layer_norm_with_out(self, which, ncol, dst_fn, B_dst_fn, emit_out, is_f32):
        P = self.P
        gcol = 0 if which == 0 else 16
        bcol = gcol + 8
        for m in range(8):
            zb, B_zb = self.zb[m % 2], self.B_zb[m % 2]
            zq, B_zq = self.zq[m % 2], self.B_zq[m % 2]
            P.op("act", lambda e, zb=zb, m=m: e.activation(out=zb[:, 0:ncol], in_=self.z[:, m, 0:ncol], func=AF.Copy),
                 reads=[self.B_z[m]], writes=[B_zb])
            P.op("act", lambda e, zq=zq, m=m: e.activation(out=zq[:, 0:ncol], in_=self.z[:, m, 0:ncol], func=AF.Square),
                 reads=[self.B_z[m]], writes=[B_zq])
            P.op("pe", lambda e, zb=zb, m=m: e.matmul(self.ps[:, 0, 0:ncol], lhsT=self.ones_b[:], rhs=zb[:, 0:ncol],
                                                      start=(m == 0), stop=(m == 7)),
                 reads=[B_zb, self.B_ones], writes=[self.B_ps[0]])
            P.op("pe", lambda e, zq=zq, m=m: e.matmul(self.ps[:, 1, 0:ncol], lhsT=self.ones_b[:], rhs=zq[:, 0:ncol],
                                                      start=(m == 0), stop=(m == 7)),
                 reads=[B_zq, self.B_ones], writes=[self.B_ps[1]])
        mean, rstd = self.mean, self.rstd
        P.op("dve", lambda e: e.tensor_scalar(mean[:, 0:ncol], self.ps[:, 0, 0:ncol], 1.0 / D, None, ALU.mult),
             reads=[self.B_ps[0]], writes=[self.B_mean])
        P.op("pool", lambda e: e.tensor_tensor(rstd[:, 0:ncol], mean[:, 0:ncol], mean[:, 0:ncol], ALU.mult),
             reads=[self.B_mean], writes=[self.B_rstd])
        P.op("dve", lambda e: e.scalar_tensor_tensor(out=rstd[:, 0:ncol], in0=self.ps[:, 1, 0:ncol], scalar=1.0 / D,
                                                     in1=rstd[:, 0:ncol], op0=ALU.mult, op1=ALU.subtract),
             reads=[self.B_ps[1], self.B_rstd], writes=[self.B_rstd])
        P.op("dve", lambda e: e.tensor_scalar(rstd[:, 0:ncol], rstd[:, 0:ncol], LN_EPS, None, ALU.add),
             reads=[self.B_rstd], writes=[self.B_rstd])
        P.op("pool", lambda e: e.tensor_tensor(rstd[:, 0:ncol], rstd[:, 0:ncol], self.mhalf[:, 0:ncol], ALU.pow),
             reads=[self.B_rstd, self.B_mhalf], writes=[self.B_rstd])
        for m in range(8):
            ta, B_ta = self.tmpa[m % 2], self.B_tmpa[m % 2]
            P.op("dve", lambda e, ta=ta, m=m: e.tensor_tensor(ta[:, 0:ncol], self.z[:, m, 0:ncol], mean[:, 0:ncol], ALU.subtract),
                 reads=[self.B_z[m], self.B_mean], writes=[B_ta])
            P.op("pool", lambda e, ta=ta: e.tensor_tensor(ta[:, 0:ncol], ta[:, 0:ncol], rstd[:, 0:ncol], ALU.mult),
                 reads=[B_ta, self.B_rstd], writes=[B_ta])
            dst = dst_fn(m)
            Bd = B_dst_fn(m)
            P.op("act", lambda e, ta=ta, m=m, dst=dst: e.activation(out=dst, in_=ta[:, 0:ncol], func=AF.Identity,
                                                                    scale=self.lnp[:, gcol + m:gcol + m + 1],
                                                                    bias=self.lnp[:, bcol + m:bcol + m + 1]),
                 reads=[B_ta, self.B_lnp], writes=[Bd])
            emit_out(m, dst, Bd)

    def half_layer(self, l, own_off, src, src_is_f32, B_src, dst, dst_dt, B_dstbuf, pending):
        P = self.P
        self.own_off = own_off
        self.setup_layer(l, own_off)
        self.kv_phase(l, src, src_is_f32, B_src, pending)
        while pending:
            pending.pop(0)()
        tl = (own_off - 1) % NT
        tr_ = (own_off + 32) % NT
        self.att_block(l, src, src_is_f32, B_src, [tl, tr_], 127, 2, self.XH, self.B_XH)
        for k in range(8):
            tiles = [own_off + 4 * k + i for i in range(4)]
            XB, B_XB = self.XB[k % 2], self.B_XB[k % 2]
            self.att_block(l, src, src_is_f32, B_src, tiles, 0, 512, XB, B_XB)
            P.op("pool", lambda e, XB=XB, k=k: e.tensor_copy(self.LC[:, :, k:k + 1], XB[:, :, 511:512]),
                 reads=[B_XB], writes=[self.B_LC[k]])
            if k >= 1:
                self._ffn(l, k - 1, dst, dst_dt, B_dstbuf)
        self._ffn(l, 7, dst, dst_dt, B_dstbuf)

    def _ffn(self, l, k, dst, dst_dt, B_dstbuf):
        X, B_X = self.XB[k % 2], self.B_XB[k % 2]
        if k == 0:
            hl, B_hl = self.XH[:, :, 0:1], self.B_XH
        else:
            hl, B_hl = self.LC[:, :, k - 1:k], self.B_LC[k - 1]
        if k == 7:
            hr, B_hr = self.XH[:, :, 1:2], self.B_XH
        else:
            hr, B_hr = self.XB[(k + 1) % 2][:, :, 0:1], self.B_XB[(k + 1) % 2]
        self.fence(self.att_bufs + self.ffn_bufs)
        self.ffn_block(l, k, X, B_X, hl, B_hl, hr, B_hr, k == 0, k == 7, dst, dst_dt, B_dstbuf, k * 512)
        self.fence(self.att_bufs + self.ffn_bufs)

    def build(self):
        self.setup_consts()
        if not self.fused:
            pending = self.convert_weights(0)
            for _ in range(8):
                pending.pop(0)()
            self.half_layer(0, 0, self.x_in, True, None, self.out, F32, self.B_out, pending)
        else:
            pending = self.convert_weights(0) + self.convert_weights(1)
            for _ in range(8):
                pending.pop(0)()
            xm = self.xmid.ap()
            self.half_layer(0, 32, self.x_in, True, None, xm[S // 2:S, :], BF16, self.B_xmid, pending)
            self.half_layer(0, 0, self.x_in, True, None, xm[0:S // 2, :], BF16, self.B_xmid, pending)
            self.half_layer(1, 0, xm, False, self.B_xmid, self.out, F32, self.B_out, pending)
        self.P.emit()
        self.st.close()
        return self.nc


HPERM = [0, 4, 1, 5, 2, 6, 3, 7]


def _t5_bucket(rel):
    half = 16
    max_exact = 8
    bucket = np.where(rel > 0, half, 0)
    rp = np.abs(rel)
    rpf = np.maximum(rp, 1).astype(np.float32)
    large = max_exact + (np.log(rpf / np.float32(max_exact)) / np.float32(math.log(128 / max_exact))
                         * np.float32(half - max_exact)).astype(np.int32)
    large = np.minimum(large, half - 1)
    return bucket + np.where(rp < max_exact, rp, large)


def _bias_table(rel_bias):
    k = np.arange(128)[:, None, None]
    j = np.arange(3)[None, :, None]
    q = np.arange(128)[None, None, :]
    rel = (j - 1) * 128 + k - q
    idx = _t5_bucket(rel)
    tab = np.asarray(rel_bias, np.float32)[idx]
    tab = np.where((np.abs(rel) <= 128)[..., None], tab, np.float32(NEG))
    tab = np.ascontiguousarray(tab.transpose(0, 1, 3, 2))
    return tab.reshape(128, 3 * 8 * 128).astype(np.float32)


def _rope_table(half):
    tok = (np.arange(S) + half * (S // 2)) % S
    row = (tok // 64).astype(np.float32)
    col = (tok % 64).astype(np.float32)
    inv = (np.float32(10000.0) ** (-np.arange(0, 32, 2, dtype=np.float32) / np.float32(32))).astype(np.float32)
    ang = np.concatenate([row[:, None] * inv, col[:, None] * inv], axis=-1).astype(np.float32)
    cs = np.concatenate([np.cos(ang), np.sin(ang)], axis=-1).astype(np.float32)
    return np.ascontiguousarray(cs.reshape(NT, 128, 64).transpose(1, 0, 2))


def _layer_params(inp, l):
    f = lambda a: np.ascontiguousarray(np.asarray(a, np.float32))
    w_in = np.asarray(inp["w_in"][l], np.float32)
    qa = w_in[:, 0:512].reshape(D, 8, 64)[:, HPERM, :].reshape(D, 512)
    qb = w_in[:, 768:1280].reshape(D, 8, 64)[:, HPERM, :].reshape(D, 512)
    wq = np.concatenate([qa, qb], axis=1)
    wkv = np.concatenate([w_in[:, 512:640], w_in[:, 640:768], w_in[:, 1280:1408], w_in[:, 1408:1536]], axis=1)
    w_out = np.asarray(inp["w_out"][l], np.float32)
    wo = np.concatenate([w_out[0:512].reshape(8, 64, D)[HPERM].reshape(512, D),
                         w_out[512:1024].reshape(8, 64, D)[HPERM].reshape(512, D)], axis=0)
    ga = np.asarray(inp["out_norm_a"][l], np.float32).reshape(8, 64)[HPERM].reshape(512)
    gb = np.asarray(inp["out_norm_b"][l], np.float32).reshape(8, 64)[HPERM].reshape(512)
    gout = np.concatenate([ga, gb]).reshape(8, 128).T
    qn = np.asarray(inp["q_norm"][l], np.float32)
    kn = np.asarray(inp["k_norm"][l], np.float32)
    nrm = np.concatenate([qn[0::2], qn[1::2], kn[0::2], kn[1::2]])[None, :]
    fm = lambda v: np.asarray(v, np.float32).reshape(8, 128).T
    lnp = np.concatenate([fm(inp["ln1_g"][l]), fm(inp["ln1_b"][l]), fm(inp["ln2_g"][l]), fm(inp["ln2_b"][l])], axis=1)
    cw = np.asarray(inp["conv_w"][l], np.float32)
    cb = np.asarray(inp["conv_b"][l], np.float32)
    cv = np.stack([cw[0], cw[1], cw[2], cb], axis=-1).reshape(NFC, 128, 4).transpose(1, 0, 2).reshape(128, NFC * 4)
    sink = np.asarray(inp["sink"][l], np.float32)[None, :]
    return dict(wq=f(wq), wkv=f(wkv), wout=f(wo), wg=f(inp["w_gate"][l]), wu=f(inp["w_up"][l]), wd=f(inp["w_down"][l]),
                nrm=f(nrm), gout=f(gout), lnp=f(lnp), cvp=f(cv), sink=f(sink))


def _core_consts(inp, half):
    jm = np.zeros((128, 4), np.float32)
    if half == 0:
        jm[:, 0] = NEG; jm[:, 1] = 0.0; jm[:, 2] = 0.0; jm[:, 3] = 1.0
    else:
        jm[:, 0] = 0.0; jm[:, 1] = NEG; jm[:, 2] = 1.0; jm[:, 3] = 0.0
    return dict(biasT=_bias_table(inp["rel_bias"]), cs=_rope_table(half), jm=jm, ident=np.eye(128, dtype=np.float32))


def _layout_x(xb, half):
    if half == 0:
        return np.ascontiguousarray(xb)
    return np.ascontiguousarray(np.concatenate([xb[S // 2:], xb[:S // 2]], axis=0))


_NC_CACHE = {}


def _get_nc(n_layers, fused):
    key = (n_layers, fused)
    if key not in _NC_CACHE:
        _NC_CACHE[key] = Builder(n_layers, fused).build()
    return _NC_CACHE[key]


FUSED = True


def kernel(**inp):
    x = np.asarray(inp["x"], np.float32)
    B = x.shape[0]
    lp = [_layer_params(inp, l) for l in range(2)]
    cc = [_core_consts(inp, h) for h in range(2)]
    if FUSED:
        nc = _get_nc(2, True)
        in_maps = []
        for core in range(N_CORES):
            b, h = core // 2, core % 2
            m = {"xsrc": _layout_x(x[b], h)}
            for l in range(2):
                for k, v in lp[l].items():
                    m[f"{k}{l}"] = v
            m.update(cc[h])
            in_maps.append(m)
        res = run_bass_kernel_spmd(nc, in_maps, core_ids=list(range(N_CORES)))
        out = np.empty((B, S, D), np.float32)
        for core in range(N_CORES):
            b, h = core // 2, core % 2
            out[b, h * (S // 2):(h + 1) * (S // 2)] = res.results[core]["out"]
        return out
    nc = _get_nc(1, False)
    cur = x
    for l in range(2):
        in_maps = []
        for core in range(N_CORES):
            b, h = core // 2, core % 2
            m = {"xsrc": _layout_x(cur[b], h)}
            for k, v in lp[l].items():
                m[f"{k}0"] = v
            m.update(cc[h])
            in_maps.append(m)
        res = run_bass_kernel_spmd(nc, in_maps, core_ids=list(range(N_CORES)))
        nxt = np.empty((B, S, D), np.float32)
        for core in range(N_CORES):
            b, h = core // 2, core % 2
            nxt[b, h * (S // 2):(h + 1) * (S // 2)] = res.results[core]["out"]
        cur = nxt
    return cur
```

```python
import math
from contextlib import ExitStack
import numpy as np
import concourse.bass as bass
import concourse.mybir as mybir
from concourse.bass_utils import run_bass_kernel_spmd

F32 = mybir.dt.float32
BF16 = mybir.dt.bfloat16
AF = mybir.ActivationFunctionType
ALU = mybir.AluOpType
AX = mybir.AxisListType

D = 1024
S = 8192
NT = 64
NB = 4
DFF = 2816
NFC = 22
ALPHA = 4.0 ** 0.25
RMS_EPS = 1e-6
LN_EPS = 1e-5
NEG = -30000.0
N_CORES = 8


class Buf:
    __slots__ = ("name", "w", "r", "dsem", "dcnt")

    def __init__(self, name):
        self.name = name
        self.w = []
        self.r = []
        self.dsem = None
        self.dcnt = 0


class Prog:
    CE = ("pe", "act", "dve", "pool")
    ENG = ("pe", "act", "dve", "pool", "sp")

    def __init__(self, nc, stack):
        self.nc = nc
        self.stack = stack
        self.ops = {e: [] for e in self.ENG}
        self.cnt = {e: 0 for e in self.CE}
        self.esem = {e: stack.enter_context(nc.semaphore("s_" + e)) for e in self.CE}
        self.seen = {e: {} for e in self.ENG}
        self.nsem = 4
        self.out_tokens = []

    def _mk_sem(self, name):
        self.nsem += 1
        return self.stack.enter_context(self.nc.semaphore(name))

    def _waits(self, eng, reads, writes, dwrites=()):
        deps = []
        for b in reads:
            deps.extend(b.w)
        for b in writes:
            deps.extend(b.w)
            deps.extend(b.r)
        for b in dwrites:
            deps.extend(b.r)
        best = {}
        for (s, v) in deps:
            k = id(s)
            if k not in best or best[k][1] < v:
                best[k] = (s, v)
        waits = []
        own = self.esem.get(eng) if eng == "pe" else None
        for k, (s, v) in best.items():
            if s is own:
                continue
            if self.seen[eng].get(k, 0) >= v:
                continue
            self.seen[eng][k] = v
            waits.append((s, v))
        return waits

    @staticmethod
    def _compact(lst):
        best = {}
        for (s, v) in lst:
            if id(s) not in best or best[id(s)][1] < v:
                best[id(s)] = (s, v)
        return list(best.values())

    def _commit(self, tok, reads, writes, dwrites=()):
        for b in dwrites:
            b.w.append(tok)
            if len(b.w) > 24:
                b.w = self._compact(b.w)
        for b in reads:
            b.r.append(tok)
            if len(b.r) > 24:
                best = {}
                for (s, v) in b.r:
                    if id(s) not in best or best[id(s)][1] < v:
                        best[id(s)] = (s, v)
                b.r = list(best.values())
        for b in writes:
            b.w = [tok]
            b.r = []

    def op(self, eng, fn, reads=(), writes=()):
        waits = self._waits(eng, reads, writes)
        self.cnt[eng] += 1
        tok = (self.esem[eng], self.cnt[eng])
        self.ops[eng].append((fn, waits, (self.esem[eng], 1)))
        self._commit(tok, reads, writes)
        return tok

    def dma(self, q, fn, reads=(), writes=(), owner=None, is_out=False, dwrites=()):
        if owner is None:
            owner = writes[0] if writes else (dwrites[0] if dwrites else reads[0])
        if owner.dsem is None:
            owner.dsem = self._mk_sem("d_" + owner.name)
        waits = self._waits(q, reads, writes, dwrites)
        owner.dcnt += 16
        tok = (owner.dsem, owner.dcnt)
        self.ops[q].append((fn, waits, (owner.dsem, 16)))
        self._commit(tok, reads, writes, dwrites)
        if is_out:
            self.out_tokens.append(tok)
        return tok

    def emit(self):
        nc = self.nc
        ws = []
        for (s, v) in self.out_tokens:
            k = id(s)
            if self.seen["sp"].get(k, 0) >= v:
                continue
            self.seen["sp"][k] = v
            ws.append((s, v))
        if ws:
            self.ops["sp"].append((None, ws, None))
        ops = self.ops

        def replay(e, lst):
            for (fn, waits, inc) in lst:
                for (s, v) in waits:
                    e.wait_ge(s, v)
                if fn is not None:
                    ins = fn(e)
                    ins.then_inc(inc[0], inc[1])

        with nc.Block() as block:
            @block.tensor
            def _(e):
                replay(e, ops["pe"])

            @block.scalar
            def _(e):
                replay(e, ops["act"])

            @block.vector
            def _(e):
                replay(e, ops["dve"])

            @block.gpsimd
            def _(e):
                replay(e, ops["pool"])

            @block.sync
            def _(e):
                replay(e, ops["sp"])


class Builder:
    def __init__(self, n_layers, fused, dbg=False):
        self.fused = fused
        self.n_layers = n_layers
        self.dbg = dbg
        self.nc = bass.Bass("TRN2", target_bir_lowering=False)
        self.st = ExitStack()
        self.P = Prog(self.nc, self.st)
        self.sb_bytes = 0
        self._declare_io()
        self._alloc()

    def sb(self, name, shape, dt):
        n = 1
        for s in shape[1:]:
            n *= s
        self.sb_bytes += n * (2 if dt == BF16 else 4)
        return self.st.enter_context(self.nc.sbuf_tensor(name, list(shape), dt))

    def din(self, name, shape, dt=F32):
        return self.nc.dram_tensor(name, list(shape), dt, kind="ExternalInput").ap()

    def _declare_io(self):
        nc = self.nc
        L = self.n_layers
        self.x_in = self.din("xsrc", [S, D])
        self.W = []
        for l in range(L):
            w = dict(
                wq=self.din(f"wq{l}", [D, D]), wkv=self.din(f"wkv{l}", [D, 512]),
                wout=self.din(f"wout{l}", [D, D]), wg=self.din(f"wg{l}", [D, DFF]),
                wu=self.din(f"wu{l}", [D, DFF]), wd=self.din(f"wd{l}", [DFF, D]),
                nrm=self.din(f"nrm{l}", [1, 128]), gout=self.din(f"gout{l}", [128, 8]),
                lnp=self.din(f"lnp{l}", [128, 32]), cvp=self.din(f"cvp{l}", [128, 88]),
                sink=self.din(f"sink{l}", [1, 8]),
            )
            w["wq_s"] = nc.dram_tensor(f"wq_s{l}", [128, 8, D], BF16)
            w["wkv_s"] = nc.dram_tensor(f"wkv_s{l}", [128, 8, 512], BF16)
            w["wout_s"] = nc.dram_tensor(f"wout_s{l}", [8, 128, 8, 128], BF16)
            w["wgu_s"] = nc.dram_tensor(f"wgu_s{l}", [NFC, 128, 8, 256], BF16)
            w["wd_s"] = nc.dram_tensor(f"wd_s{l}", [NFC, 128, D], BF16)
            for k in ("wq_s", "wkv_s", "wout_s", "wgu_s", "wd_s"):
                w["B_" + k] = Buf(f"{k}{l}")
            self.W.append(w)
        self.biasT_in = self.din("biasT", [128, 3072])
        self.cs_in = self.din("cs", [128, NT, 64])
        self.jm_in = self.din("jm", [128, 4])
        self.ident_in = self.din("ident", [128, 128])
        self.out = nc.dram_tensor("out", [S // 2, D], F32, kind="ExternalOutput").ap()
        if self.fused:
            self.xmid = nc.dram_tensor("xmid", [S, D], BF16)
            self.B_xmid = Buf("xmid")
        self.B_out = Buf("out")
        self.dbg_out = {}

    def _alloc(self):
        nc, sb = self.nc, self.sb
        self.KAT = sb("KAT", [128, S], BF16)
        self.VA = sb("VA", [128, NT, 2, 128], BF16)
        self.KBT = sb("KBT", [128, 36 * 128], BF16)
        self.VB = sb("VB", [128, 36, 2, 65], BF16)
        self.B_KA = [Buf(f"KA{t}") for t in range(NT)]
        self.B_KB = [Buf(f"KB{t}") for t in range(NT)]
        self.B_vones = Buf("vones")
        self.ident_f = sb("ident_f", [128, 128], F32); self.B_idf = Buf("ident_f")
        self.ident_b = sb("ident_b", [128, 128], BF16); self.B_idb = Buf("ident_b")
        self.ones_b = sb("ones_b", [128, 128], BF16); self.B_ones = Buf("ones_b")
        self.zeros = sb("zeros", [128, 128], F32); self.B_zeros = Buf("zeros")
        self.onesf = sb("onesf", [128, 64], F32); self.B_onesf = Buf("onesf")
        self.epsr = sb("epsr", [128, 4], F32); self.B_eps = Buf("epsr")
        self.biasT = sb("biasT_sb", [128, 3, 2, 512], BF16); self.B_biasT = Buf("biasT")
        self.jm = sb("jm_sb", [128, 4], F32); self.B_jm = Buf("jm")
        self.nrm = sb("nrm_sb", [128, 128], F32); self.B_nrm = Buf("nrm")
        self.gout = [sb(f"gout_sb{l}", [128, 8], F32) for l in range(self.n_layers)]
        self.B_gout = [Buf(f"gout{l}") for l in range(self.n_layers)]
        self.lnp = sb("lnp_sb", [128, 32], F32); self.B_lnp = Buf("lnp")
        self.cvp = sb("cvp_sb", [128, NFC, 4], F32); self.B_cvp = Buf("cvp")
        self.cvm = sb("cvm_sb", [128, NFC, 2], F32); self.B_cvm = Buf("cvm")
        self.esink = sb("esink", [128, 8], F32); self.B_esink = Buf("esink")
        self.esrow = sb("esrow", [128, 2, 512], F32); self.B_esrow = Buf("esrow")
        self.cst = [sb(f"cst{i}", [128, 64], F32) for i in range(2)]; self.B_cst = [Buf(f"cst{i}") for i in range(2)]
        self.xt = [sb(f"xt{i}", [128, D], BF16) for i in range(2)]; self.B_xt = [Buf(f"xt{i}") for i in range(2)]
        self.xTb = sb("xTb", [128, 8, 512], BF16); self.B_xTb = Buf("xTb")
        self.xTt = [sb(f"xTt{i}", [128, 8, 128], BF16) for i in range(2)]; self.B_xTt = [Buf(f"xTt{i}") for i in range(2)]
        self.wq = sb("wq_sb", [128, 8, 512], BF16); self.B_wq = Buf("wq")
        self.wkv = self.wq; self.B_wkv = self.B_wq
        self.fscr = sb("fscr", [128, 8], F32)
        self.t_sq = sb("t_sq", [128, 512], F32); self.B_tsq = Buf("t_sq")
        self.t_qn = sb("t_qn", [128, 512], F32); self.B_tqn = Buf("t_qn")
        self.t_ab = sb("t_ab", [128, 2, 256], F32); self.B_tab = Buf("t_ab")
        self.t_m = sb("t_m", [128, 4, 256], F32); self.B_tm = Buf("t_m")
        self.t_ss = sb("t_ss", [128, 16], F32); self.B_tss = Buf("t_ss")
        self.t_rs = sb("t_rs", [128, 16], F32); self.B_trs = Buf("t_rs")
        self.tq = dict(sq=self.t_sq, qn=self.t_qn, ab=self.t_ab, m=self.t_m, ss=self.t_ss, rs=self.t_rs,
                       B=[self.B_tsq, self.B_tqn, self.B_tab, self.B_tm, self.B_tss, self.B_trs])
        self.tk = []
        for i in range(2):
            self.tk.append(dict(sq=sb(f"k_sq{i}", [128, 128], F32), qn=sb(f"k_qn{i}", [128, 128], F32),
                                ab=sb(f"k_ab{i}", [128, 2, 64], F32), m=sb(f"k_m{i}", [128, 4, 64], F32),
                                ss=sb(f"k_ss{i}", [128, 4], F32), rs=sb(f"k_rs{i}", [128, 4], F32),
                                B=[Buf(f"k_t{i}_{j}") for j in range(6)]))
        self.qbf = [sb(f"qbf{i}", [128, 512], BF16) for i in range(2)]; self.B_qbf = [Buf(f"qbf{i}") for i in range(2)]
        self.kvb = [sb(f"kvb{i}", [128, 256], BF16) for i in range(2)]; self.B_kvb = [Buf(f"kvb{i}") for i in range(2)]
        A1 = sb("A1", [128, 6144], BF16)
        self.QT = A1[:, 0:4096].rearrange("p (a n) -> p a n", a=8); self.B_QT = Buf("QT")
        self.PT = [A1[:, 4096 + i * 1024:5120 + i * 1024].rearrange("p (a n) -> p a n", a=2) for i in range(2)]
        self.B_PT = [Buf(f"PT{i}") for i in range(2)]
        self.hT = A1[:, 0:5632].rearrange("p (a n) -> p a n", a=11); self.B_hT = [Buf(f"hT{i}") for i in range(11)]
        A2 = sb("A2", [128, 10240], BF16)
        f32v = lambda a, b: A2[:, a:b].bitcast(F32)
        self.yT = A2[:, 0:4096].rearrange("p (a n) -> p a n", a=8); self.B_yT = Buf("yT")
        self.stt = [f32v(4096 + i * 2048, 6144 + i * 2048).rearrange("p (a n) -> p a n", a=2) for i in range(2)]
        self.B_stt = [Buf(f"stt{i}") for i in range(2)]
        self.dsb = [f32v(8192 + i * 1024, 9216 + i * 1024) for i in range(2)]; self.B_dsb = [Buf(f"dsb{i}") for i in range(2)]
        self.rr = self.dsb; self.B_rr = self.B_dsb
        self.wgu = [A2[:, i * 2048:(i + 1) * 2048].rearrange("p (a n) -> p a n", a=8) for i in range(3)]
        self.B_wgu = [Buf(f"wgu{i}") for i in range(3)]
        self.gcs = [f32v(6144 + i * 1024, 7168 + i * 1024) for i in range(2)]; self.B_gcs = [Buf(f"gcs{i}") for i in range(2)]
        self.tgs = [f32v(8192 + i * 1024, 9216 + i * 1024) for i in range(2)]; self.B_tgs = [Buf(f"tgs{i}") for i in range(2)]
        self.y2 = self.gcs; self.B_y2 = self.B_gcs
        self.ot = [t.rearrange("p (a n) -> p a n", a=4) for t in self.tgs]; self.B_ot = self.B_tgs
        self.att_bufs = [self.B_QT] + self.B_PT + [self.B_yT] + self.B_stt + self.B_dsb
        self.ffn_bufs = self.B_hT + self.B_wgu + self.B_gcs + self.B_tgs
        self.wo = [sb(f"wo{i}", [128, 8, 128], BF16) for i in range(2)]; self.B_wo = [Buf(f"wo{i}") for i in range(2)]
        self.z = sb("z", [128, 8, 512], F32); self.B_z = [Buf(f"z{m}") for m in range(8)]
        self.zb = [sb(f"zb{i}", [128, 512], BF16) for i in range(2)]; self.B_zb = [Buf(f"zb{i}") for i in range(2)]
        self.zq = [sb(f"zq{i}", [128, 512], BF16) for i in range(2)]; self.B_zq = [Buf(f"zq{i}") for i in range(2)]
        self.sqy = self.zq; self.B_sqy = self.B_zq
        self.mean = self.t_ab[:].rearrange("p a n -> p (a n)"); self.B_mean = self.B_tab
        self.rstd = self.t_m[:, 0:2, :].rearrange("p a n -> p (a n)"); self.B_rstd = self.B_tm
        self.tmpa = [self.t_sq, self.t_qn]; self.B_tmpa = [self.B_tsq, self.B_tqn]
        self.XB = [sb(f"XB{i}", [128, 8, 512], BF16) for i in range(2)]; self.B_XB = [Buf(f"XB{i}") for i in range(2)]
        self.XH = sb("XH", [128, 8, 2], BF16); self.B_XH = Buf("XH")
        self.LC = sb("LC", [128, 8, 8], BF16); self.B_LC = [Buf(f"LC{k}") for k in range(8)]
        self.HC = [sb(f"HC{i}", [128, 8, 2], BF16) for i in range(2)]; self.B_HC = [Buf(f"HC{i}") for i in range(2)]
        self.wdp = [sb(f"wdp{i}", [128, 384], BF16) for i in range(4)]; self.B_wdp = [Buf(f"wdp{i}") for i in range(4)]
        self.cvt = [self.z[:, 2 * i:2 * i + 2, :].rearrange("p a n -> p (a n)") for i in range(2)]
        self.B_cvt = [[self.B_z[2 * i], self.B_z[2 * i + 1]] for i in range(2)]
        self.cvo = [self.z[:, 4 + i, :].bitcast(BF16) for i in range(2)]
        self.B_cvo = [[self.B_z[4 + i]] for i in range(2)]
        self.ps = self.st.enter_context(nc.psum_tensor("ps", [128, 8, 512], F32))
        self.B_ps = [Buf(f"ps{i}") for i in range(8)]
        self.B_gh = [Buf(f"gh{i}") for i in range(3)]

    def bslot(self, tk):
        sl = (tk - (self.own_off - 2)) % NT
        return sl if sl < 36 else None

    def fence(self, bufs):
        self.P.op("pool", lambda e: e.memset(self.fscr[:], 0.0), writes=list(bufs))

    def psb(self, b):
        return self.ps[:, b, :].bitcast(BF16).rearrange("p (a n) -> p a n", a=8)

    def setup_consts(self):
        P = self.P
        P.dma("sp", lambda e: e.dma_start(out=self.ident_f[:], in_=self.ident_in), writes=[self.B_idf])
        P.op("act", lambda e: e.activation(out=self.ident_b[:], in_=self.ident_f[:], func=AF.Copy),
             reads=[self.B_idf], writes=[self.B_idb])
        P.op("pool", lambda e: e.memset(self.ones_b[:], 1.0), writes=[self.B_ones])
        P.op("pool", lambda e: e.memset(self.zeros[:], 0.0), writes=[self.B_zeros])
        P.op("pool", lambda e: e.memset(self.onesf[:], 1.0), writes=[self.B_onesf])
        P.op("pool", lambda e: e.memset(self.epsr[:, 0:1], RMS_EPS), writes=[self.B_eps])
        P.op("pool", lambda e: e.memset(self.epsr[:, 1:2], math.log(0.125)), writes=[self.B_eps])
        P.op("pool", lambda e: e.memset(self.epsr[:, 2:3], 0.0), writes=[self.B_eps])
        P.op("pool", lambda e: e.memset(self.epsr[:, 3:4], LN_EPS), writes=[self.B_eps])
        P.op("pool", lambda e: e.memset(self.VA[:, :, :, 64:128], 1.0), writes=[self.B_vones])
        P.op("pool", lambda e: e.memset(self.VB[:, :, :, 64:65], 1.0), writes=[self.B_vones])
        P.dma("pool", lambda e: e.dma_start(out=self.biasT[:].rearrange("p a b n -> p (a b n)"), in_=self.biasT_in),
              writes=[self.B_biasT])
        P.dma("sp", lambda e: e.dma_start(out=self.jm[:], in_=self.jm_in), writes=[self.B_jm])

    def setup_layer(self, l, own_off):
        P = self.P
        w = self.W[l]
        P.dma("sp", lambda e: e.dma_start(out=self.nrm[:], in_=w["nrm"].partition_broadcast(128).rearrange("p a n -> p (a n)")), writes=[self.B_nrm])
        P.dma("sp", lambda e: e.dma_start(out=self.lnp[:], in_=w["lnp"]), writes=[self.B_lnp])
        P.dma("sp", lambda e: e.dma_start(out=self.cvp[:].rearrange("p c k -> p (c k)"), in_=w["cvp"]), writes=[self.B_cvp])
        P.dma("sp", lambda e: e.dma_start(out=self.esink[:], in_=w["sink"].partition_broadcast(128).rearrange("p a n -> p (a n)")), writes=[self.B_esink])
        P.op("act", lambda e: e.activation(out=self.esink[:], in_=self.esink[:], func=AF.Exp),
             reads=[self.B_esink], writes=[self.B_esink])
        for kvh in range(2):
            for c in range(4):
                h = kvh * 4 + c
                P.op("dve", lambda e, kvh=kvh, c=c, h=h: e.tensor_scalar(
                    self.esrow[:, kvh, c * 128:(c + 1) * 128], self.zeros[:, :],
                    self.esink[:, h:h + 1], None, ALU.add),
                    reads=[self.B_esink, self.B_zeros], writes=[self.B_esrow])
        jl = 2 if own_off == 0 else 3
        jr = 3 if own_off == 0 else 2
        P.op("dve", lambda e: e.tensor_scalar(self.cvm[:, :, 0], self.cvp[:, :, 0], self.jm[:, jl:jl + 1], None, ALU.mult),
             reads=[self.B_cvp, self.B_jm], writes=[self.B_cvm])
        P.op("dve", lambda e: e.tensor_scalar(self.cvm[:, :, 1], self.cvp[:, :, 2], self.jm[:, jr:jr + 1], None, ALU.mult),
             reads=[self.B_cvp, self.B_jm], writes=[self.B_cvm])

    def convert_weights(self, l):
        P = self.P
        w = self.W[l]
        steps = []

        def cast_dma(dst_ap, src_ap, B):
            P.dma("pool", lambda e: e.dma_start(out=dst_ap, in_=src_ap), dwrites=[B])

        for kc in range(8):
            steps.append(lambda kc=kc: cast_dma(w["wkv_s"].ap()[:, kc, :], w["wkv"][kc * 128:(kc + 1) * 128, :], w["B_wkv_s"]))
        for kc in range(8):
            steps.append(lambda kc=kc: cast_dma(w["wq_s"].ap()[:, kc, :], w["wq"][kc * 128:(kc + 1) * 128, :], w["B_wq_s"]))

        def wout_step(c):
            i = c % 2
            if c == 0:
                P.dma("sp", lambda e: e.dma_start(out=self.gout[l][:], in_=w["gout"]), writes=[self.B_gout[l]])
            P.dma("sp", lambda e: e.dma_start(out=self.cvt[i], in_=w["wout"][c * 128:(c + 1) * 128, :]),
                  writes=self.B_cvt[i])
            P.op("dve", lambda e: e.tensor_scalar(self.cvo[i], self.cvt[i], self.gout[l][:, c:c + 1], None, ALU.mult),
                 reads=self.B_cvt[i] + [self.B_gout[l]], writes=self.B_cvo[i])
            P.dma("sp", lambda e: e.dma_start(out=w["wout_s"].ap()[:, :, c, :].rearrange("m p n -> p m n"),
                                              in_=self.cvo[i].rearrange("p (m n) -> p m n", m=8)),
                  reads=self.B_cvo[i], dwrites=[w["B_wout_s"]], owner=w["B_wout_s"])
        for c in range(8):
            steps.append(lambda c=c: wout_step(c))
        for c in range(NFC):
            def gu(c=c):
                cast_dma(w["wgu_s"].ap()[c, :, :, 0:128],
                         w["wg"][:, c * 128:(c + 1) * 128].rearrange("(kc p) n -> p kc n", p=128), w["B_wgu_s"])
                cast_dma(w["wgu_s"].ap()[c, :, :, 128:256],
                         w["wu"][:, c * 128:(c + 1) * 128].rearrange("(kc p) n -> p kc n", p=128), w["B_wgu_s"])
            steps.append(gu)
        for c in range(NFC):
            steps.append(lambda c=c: cast_dma(w["wd_s"].ap()[c], w["wd"][c * 128:(c + 1) * 128, :], w["B_wd_s"]))
        return steps

    def load_xT(self, src, src_is_f32, B_src, t, dst_ap, B_dst, slot):
        P = self.P
        xt, B_xt = self.xt[slot], self.B_xt[slot]
        rd = [B_src] if B_src is not None else []
        if src_is_f32:
            P.dma("pool", lambda e: e.dma_start(out=xt[:], in_=src[t * 128:(t + 1) * 128, :]), reads=rd, writes=[B_xt])
        else:
            P.dma("sp", lambda e: e.dma_start(out=xt[:], in_=src[t * 128:(t + 1) * 128, :]), reads=rd, writes=[B_xt])
        bank = 6 + slot
        pv = self.psb(bank)

        def tr(e):
            for c in range(8):
                ins = e.transpose(pv[:, c, :], xt[:, c * 128:(c + 1) * 128], self.ident_b[:])
            return ins
        P.op("pe", tr, reads=[B_xt, self.B_idb], writes=[self.B_ps[bank]])
        P.op("act", lambda e: e.activation(out=dst_ap, in_=pv, func=AF.Copy), reads=[self.B_ps[bank]], writes=[B_dst])

    def norm_rope(self, src, B_src, nh, goff, cs, B_cs, scale, dst, B_dst, T=None):
        P = self.P
        W = nh * 64
        if T is None:
            T = self.tq
        B_tsq, B_tqn, B_tab, B_tm, B_tss, B_trs = T["B"]
        sq = T["sq"][:, 0:W]
        P.op("act", lambda e: e.activation(out=sq, in_=src, func=AF.Square), reads=[B_src], writes=[B_tsq])
        ss = T["ss"][:, 0:nh]
        P.op("dve", lambda e: e.tensor_reduce(out=ss, in_=sq.rearrange("p (h d) -> p h d", h=nh), axis=AX.X, op=ALU.add),
             reads=[B_tsq], writes=[B_tss])
        P.op("act", lambda e: e.activation(out=ss, in_=ss, func=AF.Ln, scale=1.0 / 64.0, bias=self.epsr[:, 0:1]),
             reads=[B_tss, self.B_eps], writes=[B_tss])
        rs = T["rs"][:, 0:nh]
        bcol = 1 if scale != 1.0 else 2
        P.op("act", lambda e: e.activation(out=rs, in_=ss, func=AF.Exp, scale=-0.5, bias=self.epsr[:, bcol:bcol + 1]),
             reads=[B_tss, self.B_eps], writes=[B_trs])
        qn = T["qn"][:, 0:W].rearrange("p (h d) -> p h d", h=nh)
        P.op("dve", lambda e: e.tensor_tensor(qn, src.rearrange("p (h d) -> p h d", h=nh),
                                              rs.unsqueeze(2).to_broadcast([128, nh, 64]), ALU.mult),
             reads=[B_src, B_trs], writes=[B_tqn])
        x0 = qn[:, :, 0::2]
        x1 = qn[:, :, 1::2]
        ge = self.nrm[:, goff:goff + 32].unsqueeze(1).to_broadcast([128, nh, 32])
        go = self.nrm[:, goff + 32:goff + 64].unsqueeze(1).to_broadcast([128, nh, 32])
        cosb = cs[:, 0:32].unsqueeze(1).to_broadcast([128, nh, 32])
        sinb = cs[:, 32:64].unsqueeze(1).to_broadcast([128, nh, 32])
        a = T["ab"][:, 0, 0:nh * 32].rearrange("p (h d) -> p h d", h=nh)
        b = T["ab"][:, 1, 0:nh * 32].rearrange("p (h d) -> p h d", h=nh)
        P.op("dve", lambda e: e.tensor_tensor(a, x0, ge, ALU.mult), reads=[B_tqn, self.B_nrm], writes=[B_tab])
        P.op("dve", lambda e: e.tensor_tensor(b, x1, go, ALU.mult), reads=[B_tqn, self.B_nrm], writes=[B_tab])
        m = [T["m"][:, i, 0:nh * 32].rearrange("p (h d) -> p h d", h=nh) for i in range(4)]
        P.op("dve", lambda e: e.tensor_tensor(m[0], a, cosb, ALU.mult), reads=[B_tab, B_cs], writes=[B_tm])
        P.op("dve", lambda e: e.tensor_tensor(m[1], b, sinb, ALU.mult), reads=[B_tab, B_cs], writes=[B_tm])
        P.op("dve", lambda e: e.tensor_tensor(m[2], a, sinb, ALU.mult), reads=[B_tab, B_cs], writes=[B_tm])
        P.op("dve", lambda e: e.tensor_tensor(m[3], b, cosb, ALU.mult), reads=[B_tab, B_cs], writes=[B_tm])
        d3 = dst.rearrange("p (h d) -> p h d", h=nh)
        P.op("dve", lambda e: e.tensor_tensor(d3[:, :, 0:32], m[0], m[1], ALU.subtract), reads=[B_tm], writes=[B_dst])
        P.op("dve", lambda e: e.tensor_tensor(d3[:, :, 32:64], m[2], m[3], ALU.add), reads=[B_tm], writes=[B_dst])

    def load_cs(self, t):
        i = t % 2
        self.P.dma("sp", lambda e: e.dma_start(out=self.cst[i][:], in_=self.cs_in[:, t, :]), writes=[self.B_cst[i]])
        return self.cst[i], self.B_cst[i]

    def kv_phase(self, l, src, src_is_f32, B_src, pending):
        P = self.P
        w = self.W[l]
        P.dma("sp", lambda e: e.dma_start(out=self.wkv[:], in_=w["wkv_s"].ap()), reads=[w["B_wkv_s"]], writes=[self.B_wkv])
        def stage1(t):
            slot = t % 2
            self.load_xT(src, src_is_f32, B_src, t, self.xTt[slot][:], self.B_xTt[slot], slot)
            for _ in range(3):
                if pending:
                    pending.pop(0)()
            xT = self.xTt[slot]
            bk = 4 + slot
            pk = self.ps[:, bk, :]

            def mm(e, xT=xT, pk=pk):
                for c in range(8):
                    ins = e.matmul(pk, lhsT=xT[:, c, :], rhs=self.wkv[:, c, :], start=(c == 0), stop=(c == 7))
                return ins
            P.op("pe", mm, reads=[self.B_xTt[slot], self.B_wkv], writes=[self.B_ps[bk]])

        def stage2(t):
            slot = t % 2
            bk = 4 + slot
            pk = self.ps[:, bk, :]
            cs, B_cs = self.load_cs(t)
            kvb, B_kvb = self.kvb[slot], self.B_kvb[slot]
            self.norm_rope(pk[:, 0:128], self.B_ps[bk], 2, 64, cs, B_cs, 1.0, kvb[:, 0:128], B_kvb, T=self.tk[slot])
            P.op("act", lambda e, kvb=kvb, pk=pk: e.activation(out=kvb[:, 128:256], in_=pk[:, 256:384], func=AF.Copy),
                 reads=[self.B_ps[bk]], writes=[B_kvb])
            P.op("act", lambda e, t=t, pk=pk: e.activation(out=self.VA[:, t, :, 0:64],
                                                           in_=pk[:, 128:256].rearrange("p (h d) -> p h d", h=2), func=AF.Copy),
                 reads=[self.B_ps[bk], self.B_vones], writes=[self.B_KA[t]])
            sl = self.bslot(t)
            if sl is not None:
                P.op("act", lambda e, sl=sl, pk=pk: e.activation(out=self.VB[:, sl, :, 0:64],
                                                                 in_=pk[:, 384:512].rearrange("p (h d) -> p h d", h=2), func=AF.Copy),
                     reads=[self.B_ps[bk], self.B_vones], writes=[self.B_KB[t]])
            bt = 2 + slot
            pt = self.psb(bt)

            def trk(e, kvb=kvb, pt=pt):
                e.transpose(pt[:, 0, :], kvb[:, 0:128], self.ident_b[:])
                return e.transpose(pt[:, 1, :], kvb[:, 128:256], self.ident_b[:])
            P.op("pe", trk, reads=[B_kvb, self.B_idb], writes=[self.B_ps[bt]])
            P.op("dve", lambda e, t=t, pt=pt: e.tensor_copy(self.KAT[:, t * 128:(t + 1) * 128], pt[:, 0, :]),
                 reads=[self.B_ps[bt]], writes=[self.B_KA[t]])
            if sl is not None:
                P.op("dve", lambda e, sl=sl, pt=pt: e.tensor_copy(self.KBT[:, sl * 128:(sl + 1) * 128], pt[:, 1, :]),
                     reads=[self.B_ps[bt]], writes=[self.B_KB[t]])

        stage1(0)
        for t in range(NT):
            if t + 1 < NT:
                stage1(t + 1)
            stage2(t)

    def att_block(self, l, src, src_is_f32, B_src, tiles, col0, ncol, x1_dst, B_x1):
        P = self.P
        w = self.W[l]
        nt = len(tiles)
        ntok = nt * 128
        for i, t in enumerate(tiles):
            self.load_xT(src, src_is_f32, B_src, t, self.xTb[:, :, i * 128:(i + 1) * 128], self.B_xTb, i % 2)
        for half in range(2):
            P.dma("sp", lambda e, half=half: e.dma_start(out=self.wq[:], in_=w["wq_s"].ap()[:, :, half * 512:(half + 1) * 512]),
                  reads=[w["B_wq_s"]], writes=[self.B_wq])
            for i, t in enumerate(tiles):
                bk = 4 + i
                pq = self.ps[:, bk, :]

                def mm(e, i=i, pq=pq):
                    for c in range(8):
                        ins = e.matmul(pq, lhsT=self.xTb[:, c, i * 128:(i + 1) * 128], rhs=self.wq[:, c, :],
                                       start=(c == 0), stop=(c == 7))
                    return ins
                P.op("pe", mm, reads=[self.B_xTb, self.B_wq], writes=[self.B_ps[bk]])
            for i, t in enumerate(tiles):
                bk = 4 + i
                pq = self.ps[:, bk, :]
                qb, B_qb = self.qbf[i % 2], self.B_qbf[i % 2]
                if half == 0:
                    cs, B_cs = self.load_cs(t)
                    self.norm_rope(pq, self.B_ps[bk], 8, 0, cs, B_cs, 0.125, qb[:, 0:512], B_qb)
                else:
                    P.op("act", lambda e, qb=qb, pq=pq: e.activation(out=qb[:, 0:512], in_=pq, func=AF.Copy),
                         reads=[self.B_ps[bk]], writes=[B_qb])
                bt = 2 + (i % 2)
                pt = self.psb(bt)

                def trq(e, qb=qb, pt=pt):
                    for c in range(4):
                        ins = e.transpose(pt[:, c, :], qb[:, c * 128:(c + 1) * 128], self.ident_b[:])
                    return ins
                P.op("pe", trq, reads=[B_qb, self.B_idb], writes=[self.B_ps[bt]])
                P.op("dve", lambda e, i=i, half=half, pt=pt: e.tensor_copy(
                    self.QT[:, half * 4:(half + 1) * 4, i * 128:(i + 1) * 128], pt[:, 0:4, :]),
                    reads=[self.B_ps[bt]], writes=[self.B_QT])
        for c in range(4):
            accb = (4, 5) if c % 2 == 0 else (6, 7)

            def qk(kt, c=c):
                sb0 = 2 * (kt % 2)

                def f(e):
                    e.matmul(self.ps[:, sb0, 0:ntok], lhsT=self.KAT[0:64, kt * 128:(kt + 1) * 128],
                             rhs=self.QT[0:64, c, 0:ntok], start=True, stop=True)
                    return e.matmul(self.ps[:, sb0 + 1, 0:ntok], lhsT=self.KAT[64:128, kt * 128:(kt + 1) * 128],
                                    rhs=self.QT[64:128, c, 0:ntok], start=True, stop=True)
                P.op("pe", f, reads=[self.B_KA[kt], self.B_QT], writes=[self.B_ps[sb0], self.B_ps[sb0 + 1]])

            def ex(kt):
                sb0 = 2 * (kt % 2)
                pt = self.PT[kt % 2]
                P.op("act", lambda e: e.activation(out=pt[:, :, 0:ntok], in_=self.ps[:, sb0:sb0 + 2, 0:ntok], func=AF.Exp),
                     reads=[self.B_ps[sb0], self.B_ps[sb0 + 1]], writes=[self.B_PT[kt % 2]])

            def pv(kt, accb=accb):
                pt = self.PT[kt % 2]

                def f(e):
                    e.matmul(self.ps[:, accb[0], 0:ntok], lhsT=self.VA[:, kt, 0, :], rhs=pt[:, 0, 0:ntok],
                             start=(kt == 0), stop=(kt == NT - 1))
                    return e.matmul(self.ps[:, accb[1], 0:ntok], lhsT=self.VA[:, kt, 1, :], rhs=pt[:, 1, 0:ntok],
                                    start=(kt == 0), stop=(kt == NT - 1))
                P.op("pe", f, reads=[self.B_KA[kt], self.B_PT[kt % 2]], writes=[self.B_ps[accb[0]], self.B_ps[accb[1]]])

            qk(0)
            qk(1)
            for kt in range(NT):
                ex(kt)
                pv(kt)
                if kt + 2 < NT:
                    qk(kt + 2)
            for kvh in range(2):
                self.finalize_head(accb[kvh], kvh, None, self.yT[kvh * 64:(kvh + 1) * 64, c, 0:ntok], ntok, kvh)
        for i, t in enumerate(tiles):
            accb = (4, 5)
            nbrs = [((t - 1) % NT, 0), (t, 1), ((t + 1) % NT, 2)]
            for jj, (tk, jidx) in enumerate(nbrs):
                sb0 = 2 * (jj % 2)
                st = self.stt[jj % 2]
                B_st = self.B_stt[jj % 2]

                ks = self.bslot(tk)
                assert ks is not None

                def f(e, ks=ks, sb0=sb0, i=i):
                    e.matmul(self.ps[:, sb0, :], lhsT=self.KBT[0:64, ks * 128:(ks + 1) * 128],
                             rhs=self.QT[0:64, 4:8, i * 128:(i + 1) * 128], start=True, stop=True)
                    return e.matmul(self.ps[:, sb0 + 1, :], lhsT=self.KBT[64:128, ks * 128:(ks + 1) * 128],
                                    rhs=self.QT[64:128, 4:8, i * 128:(i + 1) * 128], start=True, stop=True)
                P.op("pe", f, reads=[self.B_KB[tk], self.B_QT], writes=[self.B_ps[sb0], self.B_ps[sb0 + 1]])
                for kvh in range(2):
                    P.op("dve", lambda e, kvh=kvh, sb0=sb0, st=st, jidx=jidx: e.scalar_tensor_tensor(
                        out=st[:, kvh, :], in0=self.ps[:, sb0 + kvh, :], scalar=0.125, in1=self.biasT[:, jidx, kvh, :],
                        op0=ALU.mult, op1=ALU.add),
                        reads=[self.B_ps[sb0 + kvh], self.B_biasT], writes=[B_st])
                jcol = None
                if jidx == 0 and t == 0:
                    jcol = 0
                elif jidx == 0 and t == 32:
                    jcol = 1
                elif jidx == 2 and t == 63:
                    jcol = 0
                elif jidx == 2 and t == 31:
                    jcol = 1
                if jcol is not None:
                    P.op("dve", lambda e, st=st, jcol=jcol: e.tensor_scalar(
                        st[:].rearrange("p a n -> p (a n)"), st[:].rearrange("p a n -> p (a n)"),
                        self.jm[:, jcol:jcol + 1], None, ALU.add),
                        reads=[B_st, self.B_jm], writes=[B_st])
                pt = self.PT[jj % 2]
                P.op("act", lambda e, pt=pt, st=st: e.activation(out=pt[:], in_=st[:], func=AF.Exp),
                     reads=[B_st], writes=[self.B_PT[jj % 2]])

                def g(e, ks=ks, pt=pt, jj=jj, accb=accb):
                    e.matmul(self.ps[0:65, accb[0], :], lhsT=self.VB[:, ks, 0, :], rhs=pt[:, 0, :],
                             start=(jj == 0), stop=(jj == 2))
                    return e.matmul(self.ps[0:65, accb[1], :], lhsT=self.VB[:, ks, 1, :], rhs=pt[:, 1, :],
                                    start=(jj == 0), stop=(jj == 2))
                P.op("pe", g, reads=[self.B_KB[tk], self.B_PT[jj % 2]], writes=[self.B_ps[accb[0]], self.B_ps[accb[1]]])
            for kvh in range(2):
                self.finalize_head_b(accb[kvh], kvh, self.yT[kvh * 64:(kvh + 1) * 64, 4:8, i * 128:(i + 1) * 128])
        self.out_stage(l, col0, ncol)
        self.layer_norm(0, col0, ncol, lambda m: x1_dst[:, m, :], B_x1, BF16)

    def finalize_head(self, bank, kvh, sink_kvh, y_dst, n, slot):
        P = self.P
        dsb, B_dsb = self.dsb[slot], self.B_dsb[slot]
        if sink_kvh is not None:
            P.op("dve", lambda e: e.tensor_tensor(dsb[0:64, 0:n], self.ps[64:128, bank, 0:n], self.esrow[64:128, sink_kvh, 0:n], ALU.add),
                 reads=[self.B_ps[bank], self.B_esrow], writes=[B_dsb])
        else:
            P.op("act", lambda e: e.activation(out=dsb[0:64, 0:n], in_=self.ps[64:128, bank, 0:n], func=AF.Copy),
                 reads=[self.B_ps[bank]], writes=[B_dsb])
        if sink_kvh is not None:
            P.op("act", lambda e: e.activation(out=dsb[0:64, 0:n], in_=dsb[0:64, 0:n], func=AF.Ln), reads=[B_dsb], writes=[B_dsb])
            P.op("act", lambda e: e.activation(out=dsb[0:64, 0:n], in_=dsb[0:64, 0:n], func=AF.Exp, scale=-1.0),
                 reads=[B_dsb], writes=[B_dsb])
        else:
            P.op("dve", lambda e: e.reciprocal(dsb[0:64, 0:n], dsb[0:64, 0:n]), reads=[B_dsb], writes=[B_dsb])
        if len(y_dst.shape) == 3:
            in0 = self.ps[0:64, bank, 0:n].rearrange("p (a n) -> p a n", a=4)
            in1 = dsb[0:64, 0:n].rearrange("p (a n) -> p a n", a=4)
        else:
            in0 = self.ps[0:64, bank, 0:n]
            in1 = dsb[0:64, 0:n]
        P.op("dve", lambda e: e.tensor_tensor(y_dst, in0, in1, ALU.mult),
             reads=[self.B_ps[bank], B_dsb], writes=[self.B_yT])

    def finalize_head_b(self, bank, kvh, y_dst):
        P = self.P
        n = 512
        dsb, B_dsb = self.dsb[kvh], self.B_dsb[kvh]
        P.op("dve", lambda e: e.tensor_tensor(dsb[64:65, 0:n], self.ps[64:65, bank, 0:n], self.esrow[64:65, kvh, 0:n], ALU.add),
             reads=[self.B_ps[bank], self.B_esrow], writes=[B_dsb])
        P.op("dve", lambda e: e.reciprocal(dsb[64:65, 0:n], dsb[64:65, 0:n]), reads=[B_dsb], writes=[B_dsb])
        bb = 6 + kvh
        P.op("pe", lambda e: e.matmul(self.ps[0:64, bb, 0:n], lhsT=self.onesf[64:65, 0:64], rhs=dsb[64:65, 0:n],
                                      start=True, stop=True),
             reads=[B_dsb, self.B_onesf], writes=[self.B_ps[bb]])
        P.op("act", lambda e: e.activation(out=dsb[0:64, 0:n], in_=self.ps[0:64, bb, 0:n], func=AF.Copy),
             reads=[self.B_ps[bb]], writes=[B_dsb])
        in0 = self.ps[0:64, bank, 0:n].rearrange("p (a n) -> p a n", a=4)
        in1 = dsb[0:64, 0:n].rearrange("p (a n) -> p a n", a=4)
        P.op("dve", lambda e: e.tensor_tensor(y_dst, in0, in1, ALU.mult),
             reads=[self.B_ps[bank], B_dsb], writes=[self.B_yT])

    def rsqrt_act(self, dst, src_ap, B_src, B_dst, scale, eps_col, ncol):
        P = self.P
        P.op("act", lambda e: e.activation(out=dst, in_=src_ap, func=AF.Ln, scale=scale, bias=self.epsr[:, eps_col:eps_col + 1]),
             reads=[B_src, self.B_eps], writes=[B_dst])
        P.op("act", lambda e: e.activation(out=dst, in_=dst, func=AF.Exp, scale=-0.5), reads=[B_dst], writes=[B_dst])

    def out_stage(self, l, col0, ncol):
        P = self.P
        w = self.W[l]
        cs = slice(col0, col0 + ncol)
        for g in range(2):
            bank = g
            for cc in range(4):
                c = g * 4 + cc
                sq, B_sq = self.sqy[cc % 2], self.B_sqy[cc % 2]
                P.op("act", lambda e, sq=sq, c=c: e.activation(out=sq[:, 0:ncol], in_=self.yT[:, c, cs], func=AF.Square),
                     reads=[self.B_yT], writes=[B_sq])
                P.op("pe", lambda e, sq=sq, cc=cc, bank=bank: e.matmul(self.ps[:, bank, 0:ncol], lhsT=self.ones_b[:], rhs=sq[:, 0:ncol],
                                                                    start=(cc == 0), stop=(cc == 3)),
                     reads=[B_sq, self.B_ones], writes=[self.B_ps[bank]])
            rr, B_rr = self.rr[g], self.B_rr[g]
            self.rsqrt_act(rr[:, 0:ncol], self.ps[:, bank, 0:ncol], self.B_ps[bank], B_rr, 1.0 / 512.0, 0, ncol)
            for cc in range(4):
                c = g * 4 + cc
                P.op("dve", lambda e, c=c, rr=rr: e.tensor_tensor(self.yT[:, c, cs], self.yT[:, c, cs], rr[:, 0:ncol], ALU.mult),
                     reads=[self.B_yT, B_rr], writes=[self.B_yT])
        for m in range(8):
            wo, B_wo = self.wo[m % 2], self.B_wo[m % 2]
            P.dma("sp", lambda e, wo=wo, m=m: e.dma_start(out=wo[:], in_=w["wout_s"].ap()[m]),
                  reads=[w["B_wout_s"]], writes=[B_wo])
            ba = 2 + (m % 2)

            def mm(e, wo=wo, ba=ba, m=m):
                for c in range(8):
                    ins = e.matmul(self.ps[:, ba, 0:ncol], lhsT=wo[:, c, :], rhs=self.yT[:, c, cs],
                                   start=(c == 0), stop=(c == 7))
                return ins
            P.op("pe", mm, reads=[B_wo, self.B_yT], writes=[self.B_ps[ba]])
            P.op("dve", lambda e, ba=ba, m=m: e.scalar_tensor_tensor(out=self.z[:, m, 0:ncol], in0=self.xTb[:, m, cs], scalar=ALPHA,
                                                                     in1=self.ps[:, ba, 0:ncol], op0=ALU.mult, op1=ALU.add),
                 reads=[self.B_xTb, self.B_ps[ba]], writes=[self.B_z[m]])

    def layer_norm(self, which, col0_unused, ncol, dst_fn, B_dst, out_dt):
        P = self.P
        gcol = 0 if which == 0 else 16
        bcol = gcol + 8
        for m in range(8):
            zb, B_zb = self.zb[m % 2], self.B_zb[m % 2]
            zq, B_zq = self.zq[m % 2], self.B_zq[m % 2]
            P.op("act", lambda e, zb=zb, m=m: e.activation(out=zb[:, 0:ncol], in_=self.z[:, m, 0:ncol], func=AF.Copy),
                 reads=[self.B_z[m]], writes=[B_zb])
            P.op("act", lambda e, zq=zq, m=m: e.activation(out=zq[:, 0:ncol], in_=self.z[:, m, 0:ncol], func=AF.Square),
                 reads=[self.B_z[m]], writes=[B_zq])
            P.op("pe", lambda e, zb=zb, m=m: e.matmul(self.ps[:, 0, 0:ncol], lhsT=self.ones_b[:], rhs=zb[:, 0:ncol],
                                                      start=(m == 0), stop=(m == 7)),
                 reads=[B_zb, self.B_ones], writes=[self.B_ps[0]])
            P.op("pe", lambda e, zq=zq, m=m: e.matmul(self.ps[:, 1, 0:ncol], lhsT=self.ones_b[:], rhs=zq[:, 0:ncol],
                                                      start=(m == 0), stop=(m == 7)),
                 reads=[B_zq, self.B_ones], writes=[self.B_ps[1]])
        mean, rstd = self.mean, self.rstd
        P.op("dve", lambda e: e.tensor_scalar(mean[:, 0:ncol], self.ps[:, 0, 0:ncol], 1.0 / D, None, ALU.mult),
             reads=[self.B_ps[0]], writes=[self.B_mean])
        P.op("dve", lambda e: e.tensor_tensor(rstd[:, 0:ncol], mean[:, 0:ncol], mean[:, 0:ncol], ALU.mult),
             reads=[self.B_mean], writes=[self.B_rstd])
        P.op("dve", lambda e: e.scalar_tensor_tensor(out=rstd[:, 0:ncol], in0=self.ps[:, 1, 0:ncol], scalar=1.0 / D,
                                                     in1=rstd[:, 0:ncol], op0=ALU.mult, op1=ALU.subtract),
             reads=[self.B_ps[1], self.B_rstd], writes=[self.B_rstd])
        self.rsqrt_act(rstd[:, 0:ncol], rstd[:, 0:ncol], self.B_rstd, self.B_rstd, 1.0, 3, ncol)
        for m in range(8):
            ta, B_ta = self.tmpa[m % 2], self.B_tmpa[m % 2]
            P.op("dve", lambda e, ta=ta, m=m: e.tensor_tensor(ta[:, 0:ncol], self.z[:, m, 0:ncol], mean[:, 0:ncol], ALU.subtract),
                 reads=[self.B_z[m], self.B_mean], writes=[B_ta])
            P.op("dve", lambda e, ta=ta: e.tensor_tensor(ta[:, 0:ncol], ta[:, 0:ncol], rstd[:, 0:ncol], ALU.mult),
                 reads=[B_ta, self.B_rstd], writes=[B_ta])
            dst = dst_fn(m)
            Bd = B_dst(m) if callable(B_dst) else B_dst
            P.op("act", lambda e, ta=ta, m=m, dst=dst: e.activation(out=dst, in_=ta[:, 0:ncol], func=AF.Identity,
                                                                    scale=self.lnp[:, gcol + m:gcol + m + 1],
                                                                    bias=self.lnp[:, bcol + m:bcol + m + 1]),
                 reads=[B_ta, self.B_lnp], writes=[Bd])

    def ffn_block(self, l, k, X, B_X, hl_ap, B_hl, hr_ap, B_hr, first, last, dst, dst_dt, B_dstbuf, row0):
        P = self.P
        w = self.W[l]
        HC, B_HC = self.HC[k % 2], self.B_HC[k % 2]
        P.op("dve", lambda e: e.tensor_copy(HC[:, :, 0:1], hl_ap), reads=[B_hl], writes=[B_HC])
        P.op("dve", lambda e: e.tensor_copy(HC[:, :, 1:2], hr_ap), reads=[B_hr], writes=[B_HC])
        for m in range(8):
            P.op("act", lambda e, m=m: e.activation(out=self.z[:, m, :], in_=X[:, m, :], func=AF.Copy, scale=ALPHA),
                 reads=[B_X], writes=[self.B_z[m]])
        for hh in range(2):
            for cc in range(11):
                c = hh * 11 + cc
                wg, B_wg = self.wgu[c % 3], self.B_wgu[c % 3]
                P.dma("sp", lambda e, wg=wg, c=c: e.dma_start(out=wg[:], in_=w["wgu_s"].ap()[c]),
                      reads=[w["B_wgu_s"]], writes=[B_wg])
                gb = (0, 1, 4)[c % 3]
                ub = (2, 3, 5)[c % 3]
                hcol = (c % 3) * 2

                def mm(e, wg=wg, gb=gb, ub=ub, hcol=hcol):
                    for kc in range(8):
                        e.matmul(self.ps[:, gb, :], lhsT=wg[:, kc, 0:128], rhs=X[:, kc, :], start=(kc == 0), stop=(kc == 7))
                    for kc in range(8):
                        e.matmul(self.ps[:, 7, hcol:hcol + 2], lhsT=wg[:, kc, 0:128], rhs=HC[:, kc, :], start=(kc == 0), stop=(kc == 7))
                    for kc in range(8):
                        ins = e.matmul(self.ps[:, ub, :], lhsT=wg[:, kc, 128:256], rhs=X[:, kc, :], start=(kc == 0), stop=(kc == 7))
                    return ins
                P.op("pe", mm, reads=[B_wg, B_X, B_HC], writes=[self.B_ps[gb], self.B_ps[ub], self.B_ps[7]])
                gc, B_gc = self.gcs[c % 2], self.B_gcs[c % 2]
                G = self.ps[:, gb, :]
                Gh = self.ps[:, 7, hcol:hcol + 2]
                w0 = self.cvp[:, c, 0:1]
                w1 = self.cvp[:, c, 1:2]
                w2 = self.cvp[:, c, 2:3]
                cb = self.cvp[:, c, 3:4]
                w0e = self.cvm[:, c, 0:1] if first else w0
                w2e = self.cvm[:, c, 1:2] if last else w2
                P.op("act", lambda e, gc=gc, G=G, w1=w1, cb=cb: e.activation(out=gc[:], in_=G, func=AF.Identity, scale=w1, bias=cb),
                     reads=[self.B_ps[gb], self.B_cvp], writes=[B_gc])
                P.op("dve", lambda e, gc=gc, G=G, w0=w0: e.scalar_tensor_tensor(out=gc[:, 1:512], in0=G[:, 0:511], scalar=w0,
                                                                                 in1=gc[:, 1:512], op0=ALU.mult, op1=ALU.add),
                     reads=[self.B_ps[gb], B_gc, self.B_cvp], writes=[B_gc])
                P.op("dve", lambda e, gc=gc, G=G, w2=w2: e.scalar_tensor_tensor(out=gc[:, 0:511], in0=G[:, 1:512], scalar=w2,
                                                                                 in1=gc[:, 0:511], op0=ALU.mult, op1=ALU.add),
                     reads=[self.B_ps[gb], B_gc, self.B_cvp], writes=[B_gc])
                P.op("dve", lambda e, gc=gc, Gh=Gh, w0e=w0e: e.scalar_tensor_tensor(out=gc[:, 0:1], in0=Gh[:, 0:1], scalar=w0e,
                                                                                     in1=gc[:, 0:1], op0=ALU.mult, op1=ALU.add),
                     reads=[self.B_ps[7], B_gc, self.B_cvp, self.B_cvm], writes=[B_gc])
                P.op("dve", lambda e, gc=gc, Gh=Gh, w2e=w2e: e.scalar_tensor_tensor(out=gc[:, 511:512], in0=Gh[:, 1:2], scalar=w2e,
                                                                                     in1=gc[:, 511:512], op0=ALU.mult, op1=ALU.add),
                     reads=[self.B_ps[7], B_gc, self.B_cvp, self.B_cvm], writes=[B_gc])
                tg, B_tg = self.tgs[c % 2], self.B_tgs[c % 2]
                P.op("act", lambda e, tg=tg, gc=gc: e.activation(out=tg[:], in_=gc[:], func=AF.Gelu_apprx_tanh),
                     reads=[B_gc], writes=[B_tg])
                P.op("dve", lambda e, tg=tg, ub=ub, cc=cc: e.tensor_tensor(self.hT[:, cc, :], self.ps[:, ub, :], tg[:], ALU.mult),
                     reads=[self.B_ps[ub], B_tg], writes=[self.B_hT[cc]])
            di = 0
            for grp in ((0, 1, 2), (3, 4, 5), (6, 7)):
                ng = len(grp)
                for cc in range(11):
                    c = hh * 11 + cc
                    wd, B_wd = self.wdp[di % 4], self.B_wdp[di % 4]
                    di += 1
                    P.dma("sp", lambda e, wd=wd, c=c, grp=grp, ng=ng: e.dma_start(
                        out=wd[:, 0:ng * 128], in_=w["wd_s"].ap()[c, :, grp[0] * 128:(grp[0] + ng) * 128]),
                        reads=[w["B_wd_s"]], writes=[B_wd])

                    def mm(e, wd=wd, cc=cc, ng=ng):
                        for mi in range(ng):
                            ins = e.matmul(self.ps[:, 4 + mi, :], lhsT=wd[:, mi * 128:(mi + 1) * 128], rhs=self.hT[:, cc, :],
                                           start=(cc == 0), stop=(cc == 10))
                        return ins
                    P.op("pe", mm, reads=[B_wd, self.B_hT[cc]], writes=[self.B_ps[4 + mi] for mi in range(ng)])
                for mi, m in enumerate(grp):
                    eng = "dve" if mi % 2 == 0 else "dve"
                    P.op(eng, lambda e, mi=mi, m=m: e.tensor_tensor(self.z[:, m, :], self.ps[:, 4 + mi, :], self.z[:, m, :], ALU.add),
                         reads=[self.B_ps[4 + mi], self.B_z[m]], writes=[self.B_z[m]])
        is_f32 = (dst_dt == F32)

        def emit_out(m, y2, B_y2):
            bank = 2 + (m % 2)
            if is_f32:
                pv = self.ps[:, bank, :].rearrange("p (a n) -> p a n", a=4)
                idn, B_idn = self.ident_f, self.B_idf
            else:
                pv = self.psb(bank)[:, 0:4, :]
                idn, B_idn = self.ident_b, self.B_idb

            def tr(e):
                for i in range(4):
                    ins = e.transpose(pv[:, i, :], y2[:, i * 128:(i + 1) * 128], idn[:])
                return ins
            P.op("pe", tr, reads=[B_y2, B_idn], writes=[self.B_ps[bank]])
            if is_f32:
                ot, B_ot = self.ot[m % 2], self.B_ot[m % 2]
                otv = ot[:]
            else:
                ot, B_ot = self.ot[m % 2], self.B_ot[m % 2]
                otv = ot[:].rearrange("p a n -> p (a n)").bitcast(BF16)[:, 0:512].rearrange("p (a n) -> p a n", a=4)
            P.op("act", lambda e: e.activation(out=otv, in_=pv, func=AF.Copy), reads=[self.B_ps[bank]], writes=[B_ot])
            dview = dst[row0:row0 + 512, m * 128:(m + 1) * 128].rearrange("(a p) n -> p a n", p=128)
            P.dma("sp", lambda e: e.dma_start(out=dview, in_=otv), reads=[B_ot], dwrites=[B_dstbuf], owner=B_ot,
                  is_out=True)

        self._ln_out_queue = []

        def dst_fn(m):
            y2 = self.y2[m % 2]
            if is_f32:
                return y2[:]
            return y2[:].bitcast(BF16)[:, 0:512]

        self.layer_norm_with_out(1, 512, dst_fn, lambda m: self.B_y2[m % 2], emit_out, is_f32)

    def layer_norm_with_out(self, which, ncol, dst_fn, B_dst_fn, emit_out, is_f32):
        P = self.P
        gcol = 0 if which == 0 else 16
        bcol = gcol + 8
        for m in range(8):
            zb, B_zb = self.zb[m % 2], self.B_zb[m % 2]
            zq, B_zq = self.zq[m % 2], self.B_zq[m % 2]
            P.op("act", lambda e, zb=zb, m=m: e.activation(out=zb[:, 0:ncol], in_=self.z[:, m, 0:ncol], func=AF.Copy),
                 reads=[self.B_z[m]], writes=[B_zb])
            P.op("act", lambda e, zq=zq, m=m: e.activation(out=zq[:, 0:ncol], in_=self.z[:, m, 0:ncol], func=AF.Square),
                 reads=[self.B_z[m]], writes=[B_zq])
            P.op("pe", lambda e, zb=zb, m=m: e.matmul(self.ps[:, 0, 0:ncol], lhsT=self.ones_b[:], rhs=zb[:, 0:ncol],
                                                      start=(m == 0), stop=(m == 7)),
                 reads=[B_zb, self.B_ones], writes=[self.B_ps[0]])
            P.op("pe", lambda e, zq=zq, m=m: e.matmul(self.ps[:, 1, 0:ncol], lhsT=self.ones_b[:], rhs=zq[:, 0:ncol],
                                                      start=(m == 0), stop=(m == 7)),
                 reads=[B_zq, self.B_ones], writes=[self.B_ps[1]])
        mean, rstd = self.mean, self.rstd
        P.op("dve", lambda e: e.tensor_scalar(mean[:, 0:ncol], self.ps[:, 0, 0:ncol], 1.0 / D, None, ALU.mult),
             reads=[self.B_ps[0]], writes=[self.B_mean])
        P.op("dve", lambda e: e.tensor_tensor(rstd[:, 0:ncol], mean[:, 0:ncol], mean[:, 0:ncol], ALU.mult),
             reads=[self.B_mean], writes=[self.B_rstd])
        P.op("dve", lambda e: e.scalar_tensor_tensor(out=rstd[:, 0:ncol], in0=self.ps[:, 1, 0:ncol], scalar=1.0 / D,
                                                     in1=rstd[:, 0:ncol], op0=ALU.mult, op1=ALU.subtract),
             reads=[self.B_ps[1], self.B_rstd], writes=[self.B_rstd])
        self.rsqrt_act(rstd[:, 0:ncol], rstd[:, 0:ncol], self.B_rstd, self.B_rstd, 1.0, 3, ncol)
        for m in range(8):
            ta, B_ta = self.tmpa[m % 2], self.B_tmpa[m % 2]
            P.op("dve", lambda e, ta=ta, m=m: e.tensor_tensor(ta[:, 0:ncol], self.z[:, m, 0:ncol], mean[:, 0:ncol], ALU.subtract),
                 reads=[self.B_z[m], self.B_mean], writes=[B_ta])
            P.op("dve", lambda e, ta=ta: e.tensor_tensor(ta[:, 0:ncol], ta[:, 0:ncol], rstd[:, 0:ncol], ALU.mult),
                 reads=[B_ta, self.B_rstd], writes=[B_ta])
            dst = dst_fn(m)
            Bd = B_dst_fn(m)
            P.op("act", lambda e, ta=ta, m=m, dst=dst: e.activation(out=dst, in_=ta[:, 0:ncol], func=AF.Identity,
                                                                    scale=self.lnp[:, gcol + m:gcol + m + 1],
                                                                    bias=self.lnp[:, bcol + m:bcol + m + 1]),
                 reads=[B_ta, self.B_lnp], writes=[Bd])
            emit_out(m, dst, Bd)

    def half_layer(self, l, own_off, src, src_is_f32, B_src, dst, dst_dt, B_dstbuf, pending):
        P = self.P
        self.own_off = own_off
        self.setup_layer(l, own_off)
        self.kv_phase(l, src, src_is_f32, B_src, pending)
        while pending:
            pending.pop(0)()
        tl = (own_off - 1) % NT
        tr_ = (own_off + 32) % NT
        self.att_block(l, src, src_is_f32, B_src, [tl, tr_], 127, 2, self.XH, self.B_XH)
        for k in range(8):
            tiles = [own_off + 4 * k + i for i in range(4)]
            XB, B_XB = self.XB[k % 2], self.B_XB[k % 2]
            self.att_block(l, src, src_is_f32, B_src, tiles, 0, 512, XB, B_XB)
            P.op("dve", lambda e, XB=XB, k=k: e.tensor_copy(self.LC[:, :, k:k + 1], XB[:, :, 511:512]),
                 reads=[B_XB], writes=[self.B_LC[k]])
            if k >= 1:
                self._ffn(l, k - 1, dst, dst_dt, B_dstbuf)
        self._ffn(l, 7, dst, dst_dt, B_dstbuf)

    def _ffn(self, l, k, dst, dst_dt, B_dstbuf):
        X, B_X = self.XB[k % 2], self.B_XB[k % 2]
        if k == 0:
            hl, B_hl = self.XH[:, :, 0:1], self.B_XH
        else:
            hl, B_hl = self.LC[:, :, k - 1:k], self.B_LC[k - 1]
        if k == 7:
            hr, B_hr = self.XH[:, :, 1:2], self.B_XH
        else:
            hr, B_hr = self.XB[(k + 1) % 2][:, :, 0:1], self.B_XB[(k + 1) % 2]
        self.fence(self.att_bufs + self.ffn_bufs)
        self.ffn_block(l, k, X, B_X, hl, B_hl, hr, B_hr, k == 0, k == 7, dst, dst_dt, B_dstbuf, k * 512)
        self.fence(self.att_bufs + self.ffn_bufs)

    def build(self):
        self.setup_consts()
        if not self.fused:
            pending = self.convert_weights(0)
            for _ in range(8):
                pending.pop(0)()
            self.half_layer(0, 0, self.x_in, True, None, self.out, F32, self.B_out, pending)
        else:
            pending = self.convert_weights(0) + self.convert_weights(1)
            for _ in range(8):
                pending.pop(0)()
            xm = self.xmid.ap()
            self.half_layer(0, 32, self.x_in, True, None, xm[S // 2:S, :], BF16, self.B_xmid, pending)
            self.half_layer(0, 0, self.x_in, True, None, xm[0:S // 2, :], BF16, self.B_xmid, pending)
            self.half_layer(1, 0, xm, False, self.B_xmid, self.out, F32, self.B_out, pending)
        self.P.emit()
        self.st.close()
        return self.nc


HPERM = [0, 4, 1, 5, 2, 6, 3, 7]


def _t5_bucket(rel):
    half = 16
    max_exact = 8
    bucket = np.where(rel > 0, half, 0)
    rp = np.abs(rel)
    rpf = np.maximum(rp, 1).astype(np.float32)
    large = max_exact + (np.log(rpf / np.float32(max_exact)) / np.float32(math.log(128 / max_exact))
                         * np.float32(half - max_exact)).astype(np.int32)
    large = np.minimum(large, half - 1)
    return bucket + np.where(rp < max_exact, rp, large)


def _bias_table(rel_bias):
    k = np.arange(128)[:, None, None]
    j = np.arange(3)[None, :, None]
    q = np.arange(128)[None, None, :]
    rel = (j - 1) * 128 + k - q
    idx = _t5_bucket(rel)
    tab = np.asarray(rel_bias, np.float32)[idx]
    tab = np.where((np.abs(rel) <= 128)[..., None], tab, np.float32(NEG))
    tab = np.ascontiguousarray(tab.transpose(0, 1, 3, 2))
    return tab.reshape(128, 3 * 8 * 128).astype(np.float32)


def _rope_table(half):
    tok = (np.arange(S) + half * (S // 2)) % S
    row = (tok // 64).astype(np.float32)
    col = (tok % 64).astype(np.float32)
    inv = (np.float32(10000.0) ** (-np.arange(0, 32, 2, dtype=np.float32) / np.float32(32))).astype(np.float32)
    ang = np.concatenate([row[:, None] * inv, col[:, None] * inv], axis=-1).astype(np.float32)
    cs = np.concatenate([np.cos(ang), np.sin(ang)], axis=-1).astype(np.float32)
    return np.ascontiguousarray(cs.reshape(NT, 128, 64).transpose(1, 0, 2))


def _layer_params(inp, l):
    f = lambda a: np.ascontiguousarray(np.asarray(a, np.float32))
    w_in = np.asarray(inp["w_in"][l], np.float32)
    qa = w_in[:, 0:512].reshape(D, 8, 64)[:, HPERM, :].reshape(D, 512)
    qb = w_in[:, 768:1280].reshape(D, 8, 64)[:, HPERM, :].reshape(D, 512)
    wq = np.concatenate([qa, qb], axis=1)
    wkv = np.concatenate([w_in[:, 512:640], w_in[:, 640:768], w_in[:, 1280:1408], w_in[:, 1408:1536]], axis=1)
    w_out = np.asarray(inp["w_out"][l], np.float32)
    wo = np.concatenate([w_out[0:512].reshape(8, 64, D)[HPERM].reshape(512, D),
                         w_out[512:1024].reshape(8, 64, D)[HPERM].reshape(512, D)], axis=0)
    ga = np.asarray(inp["out_norm_a"][l], np.float32).reshape(8, 64)[HPERM].reshape(512)
    gb = np.asarray(inp["out_norm_b"][l], np.float32).reshape(8, 64)[HPERM].reshape(512)
    gout = np.concatenate([ga, gb]).reshape(8, 128).T
    qn = np.asarray(inp["q_norm"][l], np.float32)
    kn = np.asarray(inp["k_norm"][l], np.float32)
    nrm = np.concatenate([qn[0::2], qn[1::2], kn[0::2], kn[1::2]])[None, :]
    fm = lambda v: np.asarray(v, np.float32).reshape(8, 128).T
    lnp = np.concatenate([fm(inp["ln1_g"][l]), fm(inp["ln1_b"][l]), fm(inp["ln2_g"][l]), fm(inp["ln2_b"][l])], axis=1)
    cw = np.asarray(inp["conv_w"][l], np.float32)
    cb = np.asarray(inp["conv_b"][l], np.float32)
    cv = np.stack([cw[0], cw[1], cw[2], cb], axis=-1).reshape(NFC, 128, 4).transpose(1, 0, 2).reshape(128, NFC * 4)
    sink = np.asarray(inp["sink"][l], np.float32)[None, :]
    return dict(wq=f(wq), wkv=f(wkv), wout=f(wo), wg=f(inp["w_gate"][l]), wu=f(inp["w_up"][l]), wd=f(inp["w_down"][l]),
                nrm=f(nrm), gout=f(gout), lnp=f(lnp), cvp=f(cv), sink=f(sink))


def _core_consts(inp, half):
    jm = np.zeros((128, 4), np.float32)
    if half == 0:
        jm[:, 0] = NEG; jm[:, 1] = 0.0; jm[:, 2] = 0.0; jm[:, 3] = 1.0
    else:
        jm[:, 0] = 0.0; jm[:, 1] = NEG; jm[:, 2] = 1.0; jm[:, 3] = 0.0
    return dict(biasT=_bias_table(inp["rel_bias"]), cs=_rope_table(half), jm=jm, ident=np.eye(128, dtype=np.float32))


def _layout_x(xb, half):
    if half == 0:
        return np.ascontiguousarray(xb)
    return np.ascontiguousarray(np.concatenate([xb[S // 2:], xb[:S // 2]], axis=0))


_NC_CACHE = {}


def _get_nc(n_layers, fused):
    key = (n_layers, fused)
    if key not in _NC_CACHE:
        _NC_CACHE[key] = Builder(n_layers, fused).build()
    return _NC_CACHE[key]


FUSED = True


def kernel(**inp):
    x = np.asarray(inp["x"], np.float32)
    B = x.shape[0]
    lp = [_layer_params(inp, l) for l in range(2)]
    cc = [_core_consts(inp, h) for h in range(2)]
    if FUSED:
        nc = _get_nc(2, True)
        in_maps = []
        for core in range(N_CORES):
            b, h = core // 2, core % 2
            m = {"xsrc": _layout_x(x[b], h)}
            for l in range(2):
                for k, v in lp[l].items():
                    m[f"{k}{l}"] = v
            m.update(cc[h])
            in_maps.append(m)
        res = run_bass_kernel_spmd(nc, in_maps, core_ids=list(range(N_CORES)))
        out = np.empty((B, S, D), np.float32)
        for core in range(N_CORES):
            b, h = core // 2, core % 2
            out[b, h * (S // 2):(h + 1) * (S // 2)] = res.results[core]["out"]
        return out
    nc = _get_nc(1, False)
    cur = x
    for l in range(2):
        in_maps = []
        for core in range(N_CORES):
            b, h = core // 2, core % 2
            m = {"xsrc": _layout_x(cur[b], h)}
            for k, v in lp[l].items():
                m[f"{k}0"] = v
            m.update(cc[h])
            in_maps.append(m)
        res = run_bass_kernel_spmd(nc, in_maps, core_ids=list(range(N_CORES)))
        nxt = np.empty((B, S, D), np.float32)
        for core in range(N_CORES):
            b, h = core // 2, core % 2
            nxt[b, h * (S // 2):(h + 1) * (S // 2)] = res.results[core]["out"]
        cur = nxt
    return cur
```

```python
import math
from contextlib import ExitStack
import numpy as np
import concourse.bass as bass
import concourse.mybir as mybir
from concourse.bass_utils import run_bass_kernel_spmd

F32 = mybir.dt.float32
BF16 = mybir.dt.bfloat16
AF = mybir.ActivationFunctionType
ALU = mybir.AluOpType
AX = mybir.AxisListType

D = 1024
S = 8192
NT = 64
NB = 4
DFF = 2816
NFC = 22
ALPHA = 4.0 ** 0.25
RMS_EPS = 1e-6
LN_EPS = 1e-5
NEG = -30000.0
N_CORES = 8


class Buf:
    __slots__ = ("name", "w", "r", "dsem", "dcnt")

    def __init__(self, name):
        self.name = name
        self.w = []
        self.r = []
        self.dsem = None
        self.dcnt = 0


class Prog:
    CE = ("pe", "act", "dve", "pool")
    ENG = ("pe", "act", "dve", "pool", "sp")

    def __init__(self, nc, stack):
        self.nc = nc
        self.stack = stack
        self.ops = {e: [] for e in self.ENG}
        self.cnt = {e: 0 for e in self.CE}
        self.esem = {e: stack.enter_context(nc.semaphore("s_" + e)) for e in self.CE}
        self.seen = {e: {} for e in self.ENG}
        self.nsem = 4
        self.out_tokens = []

    def _mk_sem(self, name):
        self.nsem += 1
        return self.stack.enter_context(self.nc.semaphore(name))

    def _waits(self, eng, reads, writes, dwrites=()):
        deps = []
        for b in reads:
            deps.extend(b.w)
        for b in writes:
            deps.extend(b.w)
            deps.extend(b.r)
        for b in dwrites:
            deps.extend(b.r)
        best = {}
        for (s, v) in deps:
            k = id(s)
            if k not in best or best[k][1] < v:
                best[k] = (s, v)
        waits = []
        own = self.esem.get(eng) if eng == "pe" else None
        for k, (s, v) in best.items():
            if s is own:
                continue
            if self.seen[eng].get(k, 0) >= v:
                continue
            self.seen[eng][k] = v
            waits.append((s, v))
        return waits

    @staticmethod
    def _compact(lst):
        best = {}
        for (s, v) in lst:
            if id(s) not in best or best[id(s)][1] < v:
                best[id(s)] = (s, v)
        return list(best.values())

    def _commit(self, tok, reads, writes, dwrites=()):
        for b in dwrites:
            b.w.append(tok)
            if len(b.w) > 24:
                b.w = self._compact(b.w)
        for b in reads:
            b.r.append(tok)
            if len(b.r) > 24:
                best = {}
                for (s, v) in b.r:
                    if id(s) not in best or best[id(s)][1] < v:
                        best[id(s)] = (s, v)
                b.r = list(best.values())
        for b in writes:
            b.w = [tok]
            b.r = []

    def op(self, eng, fn, reads=(), writes=()):
        waits = self._waits(eng, reads, writes)
        self.cnt[eng] += 1
        tok = (self.esem[eng], self.cnt[eng])
        self.ops[eng].append((fn, waits, (self.esem[eng], 1)))
        self._commit(tok, reads, writes)
        return tok

    def dma(self, q, fn, reads=(), writes=(), owner=None, is_out=False, dwrites=()):
        if owner is None:
            owner = writes[0] if writes else (dwrites[0] if dwrites else reads[0])
        if owner.dsem is None:
            owner.dsem = self._mk_sem("d_" + owner.name)
        waits = self._waits(q, reads, writes, dwrites)
        owner.dcnt += 16
        tok = (owner.dsem, owner.dcnt)
        self.ops[q].append((fn, waits, (owner.dsem, 16)))
        self._commit(tok, reads, writes, dwrites)
        if is_out:
            self.out_tokens.append(tok)
        return tok

    def emit(self):
        nc = self.nc
        ws = []
        for (s, v) in self.out_tokens:
            k = id(s)
            if self.seen["sp"].get(k, 0) >= v:
                continue
            self.seen["sp"][k] = v
            ws.append((s, v))
        if ws:
            self.ops["sp"].append((None, ws, None))
        ops = self.ops

        def replay(e, lst):
            for (fn, waits, inc) in lst:
                for (s, v) in waits:
                    e.wait_ge(s, v)
                if fn is not None:
                    ins = fn(e)
                    ins.then_inc(inc[0], inc[1])

        with nc.Block() as block:
            @block.tensor
            def _(e):
                replay(e, ops["pe"])

            @block.scalar
            def _(e):
                replay(e, ops["act"])

            @block.vector
            def _(e):
                replay(e, ops["dve"])

            @block.gpsimd
            def _(e):
                replay(e, ops["pool"])

            @block.sync
            def _(e):
                replay(e, ops["sp"])


class Builder:
    def __init__(self, n_layers, fused, dbg=False):
        self.fused = fused
        self.n_layers = n_layers
        self.dbg = dbg
        self.nc = bass.Bass("TRN2", target_bir_lowering=False)
        self.st = ExitStack()
        self.P = Prog(self.nc, self.st)
        self.sb_bytes = 0
        self._declare_io()
        self._alloc()

    def sb(self, name, shape, dt):
        n = 1
        for s in shape[1:]:
            n *= s
        self.sb_bytes += n * (2 if dt == BF16 else 4)
        return self.st.enter_context(self.nc.sbuf_tensor(name, list(shape), dt))

    def din(self, name, shape, dt=F32):
        return self.nc.dram_tensor(name, list(shape), dt, kind="ExternalInput").ap()

    def _declare_io(self):
        nc = self.nc
        L = self.n_layers
        self.x_in = self.din("xsrc", [S, D])
        self.W = []
        for l in range(L):
            w = dict(
                wq=self.din(f"wq{l}", [D, D]), wkv=self.din(f"wkv{l}", [D, 512]),
                wout=self.din(f"wout{l}", [D, D]), wg=self.din(f"wg{l}", [D, DFF]),
                wu=self.din(f"wu{l}", [D, DFF]), wd=self.din(f"wd{l}", [DFF, D]),
                nrm=self.din(f"nrm{l}", [1, 128]), gout=self.din(f"gout{l}", [128, 8]),
                lnp=self.din(f"lnp{l}", [128, 32]), cvp=self.din(f"cvp{l}", [128, 88]),
                sink=self.din(f"sink{l}", [1, 8]),
            )
            w["wq_s"] = nc.dram_tensor(f"wq_s{l}", [128, 8, D], BF16)
            w["wkv_s"] = nc.dram_tensor(f"wkv_s{l}", [128, 8, 512], BF16)
            w["wout_s"] = nc.dram_tensor(f"wout_s{l}", [8, 128, 8, 128], BF16)
            w["wgu_s"] = nc.dram_tensor(f"wgu_s{l}", [NFC, 128, 8, 256], BF16)
            w["wd_s"] = nc.dram_tensor(f"wd_s{l}", [NFC, 128, D], BF16)
            for k in ("wq_s", "wkv_s", "wout_s", "wgu_s", "wd_s"):
                w["B_" + k] = Buf(f"{k}{l}")
            self.W.append(w)
        self.biasT_in = self.din("biasT", [128, 3072])
        self.cs_in = self.din("cs", [128, NT, 64])
        self.jm_in = self.din("jm", [128, 4])
        self.ident_in = self.din("ident", [128, 128])
        self.out = nc.dram_tensor("out", [S // 2, D], F32, kind="ExternalOutput").ap()
        if self.fused:
            self.xmid = nc.dram_tensor("xmid", [S, D], BF16)
            self.B_xmid = Buf("xmid")
        self.B_out = Buf("out")
        self.dbg_out = {}

    def _alloc(self):
        nc, sb = self.nc, self.sb
        self.KAT = sb("KAT", [128, S], BF16)
        self.VA = sb("VA", [128, NT, 2, 128], BF16)
        self.KBT = sb("KBT", [128, 36 * 128], BF16)
        self.VB = sb("VB", [128, 36, 2, 65], BF16)
        self.B_KA = [Buf(f"KA{t}") for t in range(NT)]
        self.B_KB = [Buf(f"KB{t}") for t in range(NT)]
        self.B_vones = Buf("vones")
        self.ident_f = sb("ident_f", [128, 128], F32); self.B_idf = Buf("ident_f")
        self.ident_b = sb("ident_b", [128, 128], BF16); self.B_idb = Buf("ident_b")
        self.ones_b = sb("ones_b", [128, 128], BF16); self.B_ones = Buf("ones_b")
        self.zeros = sb("zeros", [128, 128], F32); self.B_zeros = Buf("zeros")
        self.onesf = sb("onesf", [128, 64], F32); self.B_onesf = Buf("onesf")
        self.epsr = sb("epsr", [128, 4], F32); self.B_eps = Buf("epsr")
        self.biasT = sb("biasT_sb", [128, 3, 2, 512], BF16); self.B_biasT = Buf("biasT")
        self.jm = sb("jm_sb", [128, 4], F32); self.B_jm = Buf("jm")
        self.nrm = sb("nrm_sb", [128, 128], F32); self.B_nrm = Buf("nrm")
        self.gout = [sb(f"gout_sb{l}", [128, 8], F32) for l in range(self.n_layers)]
        self.B_gout = [Buf(f"gout{l}") for l in range(self.n_layers)]
        self.lnp = sb("lnp_sb", [128, 32], F32); self.B_lnp = Buf("lnp")
        self.cvp = sb("cvp_sb", [128, NFC, 4], F32); self.B_cvp = Buf("cvp")
        self.cvm = sb("cvm_sb", [128, NFC, 2], F32); self.B_cvm = Buf("cvm")
        self.esink = sb("esink", [128, 8], F32); self.B_esink = Buf("esink")
        self.esrow = sb("esrow", [128, 2, 512], F32); self.B_esrow = Buf("esrow")
        self.cst = [sb(f"cst{i}", [128, 64], F32) for i in range(2)]; self.B_cst = [Buf(f"cst{i}") for i in range(2)]
        self.xt = [sb(f"xt{i}", [128, D], BF16) for i in range(2)]; self.B_xt = [Buf(f"xt{i}") for i in range(2)]
        self.xTb = sb("xTb", [128, 8, 512], BF16); self.B_xTb = Buf("xTb")
        self.xTt = [sb(f"xTt{i}", [128, 8, 128], BF16) for i in range(2)]; self.B_xTt = [Buf(f"xTt{i}") for i in range(2)]
        self.wq = sb("wq_sb", [128, 8, 512], BF16); self.B_wq = Buf("wq")
        self.wkv = self.wq; self.B_wkv = self.B_wq
        self.fscr = sb("fscr", [128, 8], F32)
        self.t_sq = sb("t_sq", [128, 512], F32); self.B_tsq = Buf("t_sq")
        self.t_qn = sb("t_qn", [128, 512], F32); self.B_tqn = Buf("t_qn")
        self.t_ab = sb("t_ab", [128, 2, 256], F32); self.B_tab = Buf("t_ab")
        self.t_m = sb("t_m", [128, 4, 256], F32); self.B_tm = Buf("t_m")
        self.t_ss = sb("t_ss", [128, 16], F32); self.B_tss = Buf("t_ss")
        self.t_rs = sb("t_rs", [128, 16], F32); self.B_trs = Buf("t_rs")
        self.tq = dict(sq=self.t_sq, qn=self.t_qn, ab=self.t_ab, m=self.t_m, ss=self.t_ss, rs=self.t_rs,
                       B=[self.B_tsq, self.B_tqn, self.B_tab, self.B_tm, self.B_tss, self.B_trs])
        self.tk = []
        for i in range(2):
            self.tk.append(dict(sq=sb(f"k_sq{i}", [128, 128], F32), qn=sb(f"k_qn{i}", [128, 128], F32),
                                ab=sb(f"k_ab{i}", [128, 2, 64], F32), m=sb(f"k_m{i}", [128, 4, 64], F32),
                                ss=sb(f"k_ss{i}", [128, 4], F32), rs=sb(f"k_rs{i}", [128, 4], F32),
                                B=[Buf(f"k_t{i}_{j}") for j in range(6)]))
        self.qbf = [sb(f"qbf{i}", [128, 512], BF16) for i in range(2)]; self.B_qbf = [Buf(f"qbf{i}") for i in range(2)]
        self.kvb = [sb(f"kvb{i}", [128, 256], BF16) for i in range(2)]; self.B_kvb = [Buf(f"kvb{i}") for i in range(2)]
        A1 = sb("A1", [128, 6144], BF16)
        self.QT = A1[:, 0:4096].rearrange("p (a n) -> p a n", a=8); self.B_QT = Buf("QT")
        self.PT = [A1[:, 4096 + i * 1024:5120 + i * 1024].rearrange("p (a n) -> p a n", a=2) for i in range(2)]
        self.B_PT = [Buf(f"PT{i}") for i in range(2)]
        self.hT = A1[:, 0:5632].rearrange("p (a n) -> p a n", a=11); self.B_hT = [Buf(f"hT{i}") for i in range(11)]
        A2 = sb("A2", [128, 10240], BF16)
        f32v = lambda a, b: A2[:, a:b].bitcast(F32)
        self.yT = A2[:, 0:4096].rearrange("p (a n) -> p a n", a=8); self.B_yT = Buf("yT")
        self.stt = [f32v(4096 + i * 2048, 6144 + i * 2048).rearrange("p (a n) -> p a n", a=2) for i in range(2)]
        self.B_stt = [Buf(f"stt{i}") for i in range(2)]
        self.dsb = [f32v(8192 + i * 1024, 9216 + i * 1024) for i in range(2)]; self.B_dsb = [Buf(f"dsb{i}") for i in range(2)]
        self.rr = self.dsb; self.B_rr = self.B_dsb
        self.wgu = [A2[:, i * 2048:(i + 1) * 2048].rearrange("p (a n) -> p a n", a=8) for i in range(3)]
        self.B_wgu = [Buf(f"wgu{i}") for i in range(3)]
        self.gcs = [f32v(6144 + i * 1024, 7168 + i * 1024) for i in range(2)]; self.B_gcs = [Buf(f"gcs{i}") for i in range(2)]
        self.tgs = [f32v(8192 + i * 1024, 9216 + i * 1024) for i in range(2)]; self.B_tgs = [Buf(f"tgs{i}") for i in range(2)]
        self.y2 = self.gcs; self.B_y2 = self.B_gcs
        self.ot = [t.rearrange("p (a n) -> p a n", a=4) for t in self.tgs]; self.B_ot = self.B_tgs
        self.att_bufs = [self.B_QT] + self.B_PT + [self.B_yT] + self.B_stt + self.B_dsb
        self.ffn_bufs = self.B_hT + self.B_wgu + self.B_gcs + self.B_tgs
        self.wo = [sb(f"wo{i}", [128, 8, 128], BF16) for i in range(2)]; self.B_wo = [Buf(f"wo{i}") for i in range(2)]
        self.z = sb("z", [128, 8, 512], F32); self.B_z = [Buf(f"z{m}") for m in range(8)]
        self.zb = [sb(f"zb{i}", [128, 512], BF16) for i in range(2)]; self.B_zb = [Buf(f"zb{i}") for i in range(2)]
        self.zq = [sb(f"zq{i}", [128, 512], BF16) for i in range(2)]; self.B_zq = [Buf(f"zq{i}") for i in range(2)]
        self.sqy = self.zq; self.B_sqy = self.B_zq
        self.mean = self.t_ab[:].rearrange("p a n -> p (a n)"); self.B_mean = self.B_tab
        self.rstd = self.t_m[:, 0:2, :].rearrange("p a n -> p (a n)"); self.B_rstd = self.B_tm
        self.tmpa = [self.t_sq, self.t_qn]; self.B_tmpa = [self.B_tsq, self.B_tqn]
        self.XB = [sb(f"XB{i}", [128, 8, 512], BF16) for i in range(2)]; self.B_XB = [Buf(f"XB{i}") for i in range(2)]
        self.XH = sb("XH", [128, 8, 2], BF16); self.B_XH = Buf("XH")
        self.LC = sb("LC", [128, 8, 8], BF16); self.B_LC = [Buf(f"LC{k}") for k in range(8)]
        self.HC = [sb(f"HC{i}", [128, 8, 2], BF16) for i in range(2)]; self.B_HC = [Buf(f"HC{i}") for i in range(2)]
        self.wdp = [sb(f"wdp{i}", [128, 384], BF16) for i in range(4)]; self.B_wdp = [Buf(f"wdp{i}") for i in range(4)]
        self.cvt = [self.z[:, 2 * i:2 * i + 2, :].rearrange("p a n -> p (a n)") for i in range(2)]
        self.B_cvt = [[self.B_z[2 * i], self.B_z[2 * i + 1]] for i in range(2)]
        self.cvo = [self.z[:, 4 + i, :].bitcast(BF16) for i in range(2)]
        self.B_cvo = [[self.B_z[4 + i]] for i in range(2)]
        self.ps = self.st.enter_context(nc.psum_tensor("ps", [128, 8, 512], F32))
        self.B_ps = [Buf(f"ps{i}") for i in range(8)]
        self.B_gh = [Buf(f"gh{i}") for i in range(3)]

    def bslot(self, tk):
        sl = (tk - (self.own_off - 2)) % NT
        return sl if sl < 36 else None

    def fence(self, bufs):
        self.P.op("pool", lambda e: e.memset(self.fscr[:], 0.0), writes=list(bufs))

    def psb(self, b):
        return self.ps[:, b, :].bitcast(BF16).rearrange("p (a n) -> p a n", a=8)

    def setup_consts(self):
        P = self.P
        P.dma("sp", lambda e: e.dma_start(out=self.ident_f[:], in_=self.ident_in), writes=[self.B_idf])
        P.op("act", lambda e: e.activation(out=self.ident_b[:], in_=self.ident_f[:], func=AF.Copy),
             reads=[self.B_idf], writes=[self.B_idb])
        P.op("pool", lambda e: e.memset(self.ones_b[:], 1.0), writes=[self.B_ones])
        P.op("pool", lambda e: e.memset(self.zeros[:], 0.0), writes=[self.B_zeros])
        P.op("pool", lambda e: e.memset(self.onesf[:], 1.0), writes=[self.B_onesf])
        P.op("pool", lambda e: e.memset(self.epsr[:, 0:1], RMS_EPS), writes=[self.B_eps])
        P.op("pool", lambda e: e.memset(self.epsr[:, 1:2], math.log(0.125)), writes=[self.B_eps])
        P.op("pool", lambda e: e.memset(self.epsr[:, 2:3], 0.0), writes=[self.B_eps])
        P.op("pool", lambda e: e.memset(self.epsr[:, 3:4], LN_EPS), writes=[self.B_eps])
        P.op("pool", lambda e: e.memset(self.VA[:, :, :, 64:128], 1.0), writes=[self.B_vones])
        P.op("pool", lambda e: e.memset(self.VB[:, :, :, 64:65], 1.0), writes=[self.B_vones])
        P.dma("pool", lambda e: e.dma_start(out=self.biasT[:].rearrange("p a b n -> p (a b n)"), in_=self.biasT_in),
              writes=[self.B_biasT])
        P.dma("sp", lambda e: e.dma_start(out=self.jm[:], in_=self.jm_in), writes=[self.B_jm])

    def setup_layer(self, l, own_off):
        P = self.P
        w = self.W[l]
        P.dma("sp", lambda e: e.dma_start(out=self.nrm[:], in_=w["nrm"].partition_broadcast(128).rearrange("p a n -> p (a n)")), writes=[self.B_nrm])
        P.dma("sp", lambda e: e.dma_start(out=self.lnp[:], in_=w["lnp"]), writes=[self.B_lnp])
        P.dma("sp", lambda e: e.dma_start(out=self.cvp[:].rearrange("p c k -> p (c k)"), in_=w["cvp"]), writes=[self.B_cvp])
        P.dma("sp", lambda e: e.dma_start(out=self.esink[:], in_=w["sink"].partition_broadcast(128).rearrange("p a n -> p (a n)")), writes=[self.B_esink])
        P.op("act", lambda e: e.activation(out=self.esink[:], in_=self.esink[:], func=AF.Exp),
             reads=[self.B_esink], writes=[self.B_esink])
        for kvh in range(2):
            for c in range(4):
                h = kvh * 4 + c
                P.op("dve", lambda e, kvh=kvh, c=c, h=h: e.tensor_scalar(
                    self.esrow[:, kvh, c * 128:(c + 1) * 128], self.zeros[:, :],
                    self.esink[:, h:h + 1], None, ALU.add),
                    reads=[self.B_esink, self.B_zeros], writes=[self.B_esrow])
        jl = 2 if own_off == 0 else 3
        jr = 3 if own_off == 0 else 2
        P.op("dve", lambda e: e.tensor_scalar(self.cvm[:, :, 0], self.cvp[:, :, 0], self.jm[:, jl:jl + 1], None, ALU.mult),
             reads=[self.B_cvp, self.B_jm], writes=[self.B_cvm])
        P.op("dve", lambda e: e.tensor_scalar(self.cvm[:, :, 1], self.cvp[:, :, 2], self.jm[:, jr:jr + 1], None, ALU.mult),
             reads=[self.B_cvp, self.B_jm], writes=[self.B_cvm])

    def convert_weights(self, l):
        P = self.P
        w = self.W[l]
        steps = []

        def cast_dma(dst_ap, src_ap, B):
            P.dma("pool", lambda e: e.dma_start(out=dst_ap, in_=src_ap), dwrites=[B])

        for kc in range(8):
            steps.append(lambda kc=kc: cast_dma(w["wkv_s"].ap()[:, kc, :], w["wkv"][kc * 128:(kc + 1) * 128, :], w["B_wkv_s"]))
        for kc in range(8):
            steps.append(lambda kc=kc: cast_dma(w["wq_s"].ap()[:, kc, :], w["wq"][kc * 128:(kc + 1) * 128, :], w["B_wq_s"]))

        def wout_step(c):
            i = c % 2
            if c == 0:
                P.dma("sp", lambda e: e.dma_start(out=self.gout[l][:], in_=w["gout"]), writes=[self.B_gout[l]])
            P.dma("sp", lambda e: e.dma_start(out=self.cvt[i], in_=w["wout"][c * 128:(c + 1) * 128, :]),
                  writes=self.B_cvt[i])
            P.op("dve", lambda e: e.tensor_scalar(self.cvo[i], self.cvt[i], self.gout[l][:, c:c + 1], None, ALU.mult),
                 reads=self.B_cvt[i] + [self.B_gout[l]], writes=self.B_cvo[i])
            P.dma("sp", lambda e: e.dma_start(out=w["wout_s"].ap()[:, :, c, :].rearrange("m p n -> p m n"),
                                              in_=self.cvo[i].rearrange("p (m n) -> p m n", m=8)),
                  reads=self.B_cvo[i], dwrites=[w["B_wout_s"]], owner=w["B_wout_s"])
        for c in range(8):
            steps.append(lambda c=c: wout_step(c))
        for c in range(NFC):
            def gu(c=c):
                cast_dma(w["wgu_s"].ap()[c, :, :, 0:128],
                         w["wg"][:, c * 128:(c + 1) * 128].rearrange("(kc p) n -> p kc n", p=128), w["B_wgu_s"])
                cast_dma(w["wgu_s"].ap()[c, :, :, 128:256],
                         w["wu"][:, c * 128:(c + 1) * 128].rearrange("(kc p) n -> p kc n", p=128), w["B_wgu_s"])
            steps.append(gu)
        for c in range(NFC):
            steps.append(lambda c=c: cast_dma(w["wd_s"].ap()[c], w["wd"][c * 128:(c + 1) * 128, :], w["B_wd_s"]))
        return steps

    def load_xT(self, src, src_is_f32, B_src, t, dst_ap, B_dst, slot):
        P = self.P
        xt, B_xt = self.xt[slot], self.B_xt[slot]
        rd = [B_src] if B_src is not None else []
        if src_is_f32:
            P.dma("pool", lambda e: e.dma_start(out=xt[:], in_=src[t * 128:(t + 1) * 128, :]), reads=rd, writes=[B_xt])
        else:
            P.dma("sp", lambda e: e.dma_start(out=xt[:], in_=src[t * 128:(t + 1) * 128, :]), reads=rd, writes=[B_xt])
        bank = 6 + slot
        pv = self.psb(bank)

        def tr(e):
            for c in range(8):
                ins = e.transpose(pv[:, c, :], xt[:, c * 128:(c + 1) * 128], self.ident_b[:])
            return ins
        P.op("pe", tr, reads=[B_xt, self.B_idb], writes=[self.B_ps[bank]])
        P.op("act", lambda e: e.activation(out=dst_ap, in_=pv, func=AF.Copy), reads=[self.B_ps[bank]], writes=[B_dst])

    def norm_rope(self, src, B_src, nh, goff, cs, B_cs, scale, dst, B_dst, T=None):
        P = self.P
        W = nh * 64
        if T is None:
            T = self.tq
        B_tsq, B_tqn, B_tab, B_tm, B_tss, B_trs = T["B"]
        sq = T["sq"][:, 0:W]
        P.op("act", lambda e: e.activation(out=sq, in_=src, func=AF.Square), reads=[B_src], writes=[B_tsq])
        ss = T["ss"][:, 0:nh]
        P.op("dve", lambda e: e.tensor_reduce(out=ss, in_=sq.rearrange("p (h d) -> p h d", h=nh), axis=AX.X, op=ALU.add),
             reads=[B_tsq], writes=[B_tss])
        P.op("act", lambda e: e.activation(out=ss, in_=ss, func=AF.Ln, scale=1.0 / 64.0, bias=self.epsr[:, 0:1]),
             reads=[B_tss, self.B_eps], writes=[B_tss])
        rs = T["rs"][:, 0:nh]
        bcol = 1 if scale != 1.0 else 2
        P.op("act", lambda e: e.activation(out=rs, in_=ss, func=AF.Exp, scale=-0.5, bias=self.epsr[:, bcol:bcol + 1]),
             reads=[B_tss, self.B_eps], writes=[B_trs])
        qn = T["qn"][:, 0:W].rearrange("p (h d) -> p h d", h=nh)
        P.op("dve", lambda e: e.tensor_tensor(qn, src.rearrange("p (h d) -> p h d", h=nh),
                                              rs.unsqueeze(2).to_broadcast([128, nh, 64]), ALU.mult),
             reads=[B_src, B_trs], writes=[B_tqn])
        x0 = qn[:, :, 0::2]
        x1 = qn[:, :, 1::2]
        ge = self.nrm[:, goff:goff + 32].unsqueeze(1).to_broadcast([128, nh, 32])
        go = self.nrm[:, goff + 32:goff + 64].unsqueeze(1).to_broadcast([128, nh, 32])
        cosb = cs[:, 0:32].unsqueeze(1).to_broadcast([128, nh, 32])
        sinb = cs[:, 32:64].unsqueeze(1).to_broadcast([128, nh, 32])
        a = T["ab"][:, 0, 0:nh * 32].rearrange("p (h d) -> p h d", h=nh)
        b = T["ab"][:, 1, 0:nh * 32].rearrange("p (h d) -> p h d", h=nh)
        P.op("dve", lambda e: e.tensor_tensor(a, x0, ge, ALU.mult), reads=[B_tqn, self.B_nrm], writes=[B_tab])
        P.op("dve", lambda e: e.tensor_tensor(b, x1, go, ALU.mult), reads=[B_tqn, self.B_nrm], writes=[B_tab])
        m = [T["m"][:, i, 0:nh * 32].rearrange("p (h d) -> p h d", h=nh) for i in range(4)]
        P.op("dve", lambda e: e.tensor_tensor(m[0], a, cosb, ALU.mult), reads=[B_tab, B_cs], writes=[B_tm])
        P.op("dve", lambda e: e.tensor_tensor(m[1], b, sinb, ALU.mult), reads=[B_tab, B_cs], writes=[B_tm])
        P.op("dve", lambda e: e.tensor_tensor(m[2], a, sinb, ALU.mult), reads=[B_tab, B_cs], writes=[B_tm])
        P.op("dve", lambda e: e.tensor_tensor(m[3], b, cosb, ALU.mult), reads=[B_tab, B_cs], writes=[B_tm])
        d3 = dst.rearrange("p (h d) -> p h d", h=nh)
        P.op("dve", lambda e: e.tensor_tensor(d3[:, :, 0:32], m[0], m[1], ALU.subtract), reads=[B_tm], writes=[B_dst])
        P.op("dve", lambda e: e.tensor_tensor(d3[:, :, 32:64], m[2], m[3], ALU.add), reads=[B_tm], writes=[B_dst])

    def load_cs(self, t):
        i = t % 2
        self.P.dma("sp", lambda e: e.dma_start(out=self.cst[i][:], in_=self.cs_in[:, t, :]), writes=[self.B_cst[i]])
        return self.cst[i], self.B_cst[i]

    def kv_phase(self, l, src, src_is_f32, B_src, pending):
        P = self.P
        w = self.W[l]
        P.dma("sp", lambda e: e.dma_start(out=self.wkv[:], in_=w["wkv_s"].ap()), reads=[w["B_wkv_s"]], writes=[self.B_wkv])
        def stage1(t):
            slot = t % 2
            self.load_xT(src, src_is_f32, B_src, t, self.xTt[slot][:], self.B_xTt[slot], slot)
            for _ in range(3):
                if pending:
                    pending.pop(0)()
            xT = self.xTt[slot]
            bk = 4 + slot
            pk = self.ps[:, bk, :]

            def mm(e, xT=xT, pk=pk):
                for c in range(8):
                    ins = e.matmul(pk, lhsT=xT[:, c, :], rhs=self.wkv[:, c, :], start=(c == 0), stop=(c == 7))
                return ins
            P.op("pe", mm, reads=[self.B_xTt[slot], self.B_wkv], writes=[self.B_ps[bk]])

        def stage2(t):
            slot = t % 2
            bk = 4 + slot
            pk = self.ps[:, bk, :]
            cs, B_cs = self.load_cs(t)
            kvb, B_kvb = self.kvb[slot], self.B_kvb[slot]
            self.norm_rope(pk[:, 0:128], self.B_ps[bk], 2, 64, cs, B_cs, 1.0, kvb[:, 0:128], B_kvb, T=self.tk[slot])
            P.op("act", lambda e, kvb=kvb, pk=pk: e.activation(out=kvb[:, 128:256], in_=pk[:, 256:384], func=AF.Copy),
                 reads=[self.B_ps[bk]], writes=[B_kvb])
            P.op("act", lambda e, t=t, pk=pk: e.activation(out=self.VA[:, t, :, 0:64],
                                                           in_=pk[:, 128:256].rearrange("p (h d) -> p h d", h=2), func=AF.Copy),
                 reads=[self.B_ps[bk], self.B_vones], writes=[self.B_KA[t]])
            sl = self.bslot(t)
            if sl is not None:
                P.op("act", lambda e, sl=sl, pk=pk: e.activation(out=self.VB[:, sl, :, 0:64],
                                                                 in_=pk[:, 384:512].rearrange("p (h d) -> p h d", h=2), func=AF.Copy),
                     reads=[self.B_ps[bk], self.B_vones], writes=[self.B_KB[t]])
            bt = 2 + slot
            pt = self.psb(bt)

            def trk(e, kvb=kvb, pt=pt):
                e.transpose(pt[:, 0, :], kvb[:, 0:128], self.ident_b[:])
                return e.transpose(pt[:, 1, :], kvb[:, 128:256], self.ident_b[:])
            P.op("pe", trk, reads=[B_kvb, self.B_idb], writes=[self.B_ps[bt]])
            P.op("dve", lambda e, t=t, pt=pt: e.tensor_copy(self.KAT[:, t * 128:(t + 1) * 128], pt[:, 0, :]),
                 reads=[self.B_ps[bt]], writes=[self.B_KA[t]])
            if sl is not None:
                P.op("dve", lambda e, sl=sl, pt=pt: e.tensor_copy(self.KBT[:, sl * 128:(sl + 1) * 128], pt[:, 1, :]),
                     reads=[self.B_ps[bt]], writes=[self.B_KB[t]])

        stage1(0)
        for t in range(NT):
            if t + 1 < NT:
                stage1(t + 1)
            stage2(t)

    def att_block(self, l, src, src_is_f32, B_src, tiles, col0, ncol, x1_dst, B_x1):
        P = self.P
        w = self.W[l]
        nt = len(tiles)
        ntok = nt * 128
        for i, t in enumerate(tiles):
            self.load_xT(src, src_is_f32, B_src, t, self.xTb[:, :, i * 128:(i + 1) * 128], self.B_xTb, i % 2)
        for half in range(2):
            P.dma("sp", lambda e, half=half: e.dma_start(out=self.wq[:], in_=w["wq_s"].ap()[:, :, half * 512:(half + 1) * 512]),
                  reads=[w["B_wq_s"]], writes=[self.B_wq])
            for i, t in enumerate(tiles):
                bk = 4 + i
                pq = self.ps[:, bk, :]

                def mm(e, i=i, pq=pq):
                    for c in range(8):
                        ins = e.matmul(pq, lhsT=self.xTb[:, c, i * 128:(i + 1) * 128], rhs=self.wq[:, c, :],
                                       start=(c == 0), stop=(c == 7))
                    return ins
                P.op("pe", mm, reads=[self.B_xTb, self.B_wq], writes=[self.B_ps[bk]])
            for i, t in enumerate(tiles):
                bk = 4 + i
                pq = self.ps[:, bk, :]
                qb, B_qb = self.qbf[i % 2], self.B_qbf[i % 2]
                if half == 0:
                    cs, B_cs = self.load_cs(t)
                    self.norm_rope(pq, self.B_ps[bk], 8, 0, cs, B_cs, 0.125, qb[:, 0:512], B_qb)
                else:
                    P.op("act", lambda e, qb=qb, pq=pq: e.activation(out=qb[:, 0:512], in_=pq, func=AF.Copy),
                         reads=[self.B_ps[bk]], writes=[B_qb])
                bt = 2 + (i % 2)
                pt = self.psb(bt)

                def trq(e, qb=qb, pt=pt):
                    for c in range(4):
                        ins = e.transpose(pt[:, c, :], qb[:, c * 128:(c + 1) * 128], self.ident_b[:])
                    return ins
                P.op("pe", trq, reads=[B_qb, self.B_idb], writes=[self.B_ps[bt]])
                P.op("dve", lambda e, i=i, half=half, pt=pt: e.tensor_copy(
                    self.QT[:, half * 4:(half + 1) * 4, i * 128:(i + 1) * 128], pt[:, 0:4, :]),
                    reads=[self.B_ps[bt]], writes=[self.B_QT])
        for c in range(4):
            accb = (4, 5) if c % 2 == 0 else (6, 7)

            def qk(kt, c=c):
                sb0 = 2 * (kt % 2)

                def f(e):
                    e.matmul(self.ps[:, sb0, 0:ntok], lhsT=self.KAT[0:64, kt * 128:(kt + 1) * 128],
                             rhs=self.QT[0:64, c, 0:ntok], start=True, stop=True)
                    return e.matmul(self.ps[:, sb0 + 1, 0:ntok], lhsT=self.KAT[64:128, kt * 128:(kt + 1) * 128],
                                    rhs=self.QT[64:128, c, 0:ntok], start=True, stop=True)
                P.op("pe", f, reads=[self.B_KA[kt], self.B_QT], writes=[self.B_ps[sb0], self.B_ps[sb0 + 1]])

            def ex(kt):
                sb0 = 2 * (kt % 2)
                pt = self.PT[kt % 2]
                P.op("act", lambda e: e.activation(out=pt[:, :, 0:ntok], in_=self.ps[:, sb0:sb0 + 2, 0:ntok], func=AF.Exp),
                     reads=[self.B_ps[sb0], self.B_ps[sb0 + 1]], writes=[self.B_PT[kt % 2]])

            def pv(kt, accb=accb):
                pt = self.PT[kt % 2]

                def f(e):
                    e.matmul(self.ps[:, accb[0], 0:ntok], lhsT=self.VA[:, kt, 0, :], rhs=pt[:, 0, 0:ntok],
                             start=(kt == 0), stop=(kt == NT - 1))
                    return e.matmul(self.ps[:, accb[1], 0:ntok], lhsT=self.VA[:, kt, 1, :], rhs=pt[:, 1, 0:ntok],
                                    start=(kt == 0), stop=(kt == NT - 1))
                P.op("pe", f, reads=[self.B_KA[kt], self.B_PT[kt % 2]], writes=[self.B_ps[accb[0]], self.B_ps[accb[1]]])

            qk(0)
            qk(1)
            for kt in range(NT):
                ex(kt)
                pv(kt)
                if kt + 2 < NT:
                    qk(kt + 2)
            for kvh in range(2):
                self.finalize_head(accb[kvh], kvh, None, self.yT[kvh * 64:(kvh + 1) * 64, c, 0:ntok], ntok, kvh)
        def b_main(i, t):
            accb = (4, 5) if i % 2 == 0 else (6, 7)
            nbrs = [((t - 1) % NT, 0), (t, 1), ((t + 1) % NT, 2)]
            for jj, (tk, jidx) in enumerate(nbrs):
                sb0 = 2 * (jj % 2)
                st = self.stt[jj % 2]
                B_st = self.B_stt[jj % 2]

                ks = self.bslot(tk)
                assert ks is not None

                def f(e, ks=ks, sb0=sb0, i=i):
                    e.matmul(self.ps[:, sb0, :], lhsT=self.KBT[0:64, ks * 128:(ks + 1) * 128],
                             rhs=self.QT[0:64, 4:8, i * 128:(i + 1) * 128], start=True, stop=True)
                    return e.matmul(self.ps[:, sb0 + 1, :], lhsT=self.KBT[64:128, ks * 128:(ks + 1) * 128],
                                    rhs=self.QT[64:128, 4:8, i * 128:(i + 1) * 128], start=True, stop=True)
                P.op("pe", f, reads=[self.B_KB[tk], self.B_QT], writes=[self.B_ps[sb0], self.B_ps[sb0 + 1]])
                for kvh in range(2):
                    P.op("dve", lambda e, kvh=kvh, sb0=sb0, st=st, jidx=jidx: e.scalar_tensor_tensor(
                        out=st[:, kvh, :], in0=self.ps[:, sb0 + kvh, :], scalar=0.125, in1=self.biasT[:, jidx, kvh, :],
                        op0=ALU.mult, op1=ALU.add),
                        reads=[self.B_ps[sb0 + kvh], self.B_biasT], writes=[B_st])
                jcol = None
                if jidx == 0 and t == 0:
                    jcol = 0
                elif jidx == 0 and t == 32:
                    jcol = 1
                elif jidx == 2 and t == 63:
                    jcol = 0
                elif jidx == 2 and t == 31:
                    jcol = 1
                if jcol is not None:
                    P.op("dve", lambda e, st=st, jcol=jcol: e.tensor_scalar(
                        st[:].rearrange("p a n -> p (a n)"), st[:].rearrange("p a n -> p (a n)"),
                        self.jm[:, jcol:jcol + 1], None, ALU.add),
                        reads=[B_st, self.B_jm], writes=[B_st])
                pt = self.PT[jj % 2]
                P.op("act", lambda e, pt=pt, st=st: e.activation(out=pt[:], in_=st[:], func=AF.Exp),
                     reads=[B_st], writes=[self.B_PT[jj % 2]])

                def g(e, ks=ks, pt=pt, jj=jj, accb=accb):
                    e.matmul(self.ps[0:65, accb[0], :], lhsT=self.VB[:, ks, 0, :], rhs=pt[:, 0, :],
                             start=(jj == 0), stop=(jj == 2))
                    return e.matmul(self.ps[0:65, accb[1], :], lhsT=self.VB[:, ks, 1, :], rhs=pt[:, 1, :],
                                    start=(jj == 0), stop=(jj == 2))
                P.op("pe", g, reads=[self.B_KB[tk], self.B_PT[jj % 2]], writes=[self.B_ps[accb[0]], self.B_ps[accb[1]]])

        def b_fin(i):
            accb = (4, 5) if i % 2 == 0 else (6, 7)
            for kvh in range(2):
                self.finalize_head_b(accb[kvh], kvh, self.yT[kvh * 64:(kvh + 1) * 64, 4:8, i * 128:(i + 1) * 128])

        for i, t in enumerate(tiles):
            b_main(i, t)
            if i >= 1:
                b_fin(i - 1)
        b_fin(nt - 1)
        self.out_stage(l, col0, ncol)
        self.layer_norm(0, col0, ncol, lambda m: x1_dst[:, m, :], B_x1, BF16)

    def finalize_head(self, bank, kvh, sink_kvh, y_dst, n, slot):
        P = self.P
        dsb, B_dsb = self.dsb[slot], self.B_dsb[slot]
        if sink_kvh is not None:
            P.op("dve", lambda e: e.tensor_tensor(dsb[0:64, 0:n], self.ps[64:128, bank, 0:n], self.esrow[64:128, sink_kvh, 0:n], ALU.add),
                 reads=[self.B_ps[bank], self.B_esrow], writes=[B_dsb])
        else:
            P.op("act", lambda e: e.activation(out=dsb[0:64, 0:n], in_=self.ps[64:128, bank, 0:n], func=AF.Copy),
                 reads=[self.B_ps[bank]], writes=[B_dsb])
        if sink_kvh is not None:
            P.op("act", lambda e: e.activation(out=dsb[0:64, 0:n], in_=dsb[0:64, 0:n], func=AF.Ln), reads=[B_dsb], writes=[B_dsb])
            P.op("act", lambda e: e.activation(out=dsb[0:64, 0:n], in_=dsb[0:64, 0:n], func=AF.Exp, scale=-1.0),
                 reads=[B_dsb], writes=[B_dsb])
        else:
            P.op("dve", lambda e: e.reciprocal(dsb[0:64, 0:n], dsb[0:64, 0:n]), reads=[B_dsb], writes=[B_dsb])
        if len(y_dst.shape) == 3:
            in0 = self.ps[0:64, bank, 0:n].rearrange("p (a n) -> p a n", a=4)
            in1 = dsb[0:64, 0:n].rearrange("p (a n) -> p a n", a=4)
        else:
            in0 = self.ps[0:64, bank, 0:n]
            in1 = dsb[0:64, 0:n]
        P.op("dve", lambda e: e.tensor_tensor(y_dst, in0, in1, ALU.mult),
             reads=[self.B_ps[bank], B_dsb], writes=[self.B_yT])

    def finalize_head_b(self, bank, kvh, y_dst):
        P = self.P
        n = 512
        dsb, B_dsb = self.dsb[kvh], self.B_dsb[kvh]
        P.op("dve", lambda e: e.tensor_tensor(dsb[64:65, 0:n], self.ps[64:65, bank, 0:n], self.esrow[64:65, kvh, 0:n], ALU.add),
             reads=[self.B_ps[bank], self.B_esrow], writes=[B_dsb])
        P.op("act", lambda e: e.activation(out=dsb[64:65, 0:n], in_=dsb[64:65, 0:n], func=AF.Ln), reads=[B_dsb], writes=[B_dsb])
        P.op("act", lambda e: e.activation(out=dsb[64:65, 0:n], in_=dsb[64:65, 0:n], func=AF.Exp, scale=-1.0),
             reads=[B_dsb], writes=[B_dsb])
        bb = 2 + kvh
        P.op("pe", lambda e: e.matmul(self.ps[0:64, bb, 0:n], lhsT=self.onesf[64:65, 0:64], rhs=dsb[64:65, 0:n],
                                      start=True, stop=True),
             reads=[B_dsb, self.B_onesf], writes=[self.B_ps[bb]])
        P.op("act", lambda e: e.activation(out=dsb[0:64, 0:n], in_=self.ps[0:64, bb, 0:n], func=AF.Copy),
             reads=[self.B_ps[bb]], writes=[B_dsb])
        in0 = self.ps[0:64, bank, 0:n].rearrange("p (a n) -> p a n", a=4)
        in1 = dsb[0:64, 0:n].rearrange("p (a n) -> p a n", a=4)
        P.op("dve", lambda e: e.tensor_tensor(y_dst, in0, in1, ALU.mult),
             reads=[self.B_ps[bank], B_dsb], writes=[self.B_yT])

    def rsqrt_act(self, dst, src_ap, B_src, B_dst, scale, eps_col, ncol):
        P = self.P
        P.op("act", lambda e: e.activation(out=dst, in_=src_ap, func=AF.Ln, scale=scale, bias=self.epsr[:, eps_col:eps_col + 1]),
             reads=[B_src, self.B_eps], writes=[B_dst])
        P.op("act", lambda e: e.activation(out=dst, in_=dst, func=AF.Exp, scale=-0.5), reads=[B_dst], writes=[B_dst])

    def out_stage(self, l, col0, ncol):
        P = self.P
        w = self.W[l]
        cs = slice(col0, col0 + ncol)
        for g in range(2):
            bank = g
            for cc in range(4):
                c = g * 4 + cc
                sq, B_sq = self.sqy[cc % 2], self.B_sqy[cc % 2]
                P.op("act", lambda e, sq=sq, c=c: e.activation(out=sq[:, 0:ncol], in_=self.yT[:, c, cs], func=AF.Square),
                     reads=[self.B_yT], writes=[B_sq])
                P.op("pe", lambda e, sq=sq, cc=cc, bank=bank: e.matmul(self.ps[:, bank, 0:ncol], lhsT=self.ones_b[:], rhs=sq[:, 0:ncol],
                                                                    start=(cc == 0), stop=(cc == 3)),
                     reads=[B_sq, self.B_ones], writes=[self.B_ps[bank]])
            rr, B_rr = self.rr[g], self.B_rr[g]
            self.rsqrt_act(rr[:, 0:ncol], self.ps[:, bank, 0:ncol], self.B_ps[bank], B_rr, 1.0 / 512.0, 0, ncol)
            for cc in range(4):
                c = g * 4 + cc
                P.op("dve", lambda e, c=c, rr=rr: e.tensor_tensor(self.yT[:, c, cs], self.yT[:, c, cs], rr[:, 0:ncol], ALU.mult),
                     reads=[self.B_yT, B_rr], writes=[self.B_yT])
        for m in range(8):
            wo, B_wo = self.wo[m % 2], self.B_wo[m % 2]
            P.dma("sp", lambda e, wo=wo, m=m: e.dma_start(out=wo[:], in_=w["wout_s"].ap()[m]),
                  reads=[w["B_wout_s"]], writes=[B_wo])
            ba = 2 + (m % 2)

            def mm(e, wo=wo, ba=ba, m=m):
                for c in range(8):
                    ins = e.matmul(self.ps[:, ba, 0:ncol], lhsT=wo[:, c, :], rhs=self.yT[:, c, cs],
                                   start=(c == 0), stop=(c == 7))
                return ins
            P.op("pe", mm, reads=[B_wo, self.B_yT], writes=[self.B_ps[ba]])
            P.op("dve", lambda e, ba=ba, m=m: e.scalar_tensor_tensor(out=self.z[:, m, 0:ncol], in0=self.xTb[:, m, cs], scalar=ALPHA,
                                                                     in1=self.ps[:, ba, 0:ncol], op0=ALU.mult, op1=ALU.add),
                 reads=[self.B_xTb, self.B_ps[ba]], writes=[self.B_z[m]])

    def layer_norm(self, which, col0_unused, ncol, dst_fn, B_dst, out_dt):
        P = self.P
        gcol = 0 if which == 0 else 16
        bcol = gcol + 8
        for m in range(8):
            zb, B_zb = self.zb[m % 2], self.B_zb[m % 2]
            zq, B_zq = self.zq[m % 2], self.B_zq[m % 2]
            P.op("act", lambda e, zb=zb, m=m: e.activation(out=zb[:, 0:ncol], in_=self.z[:, m, 0:ncol], func=AF.Copy),
                 reads=[self.B_z[m]], writes=[B_zb])
            P.op("act", lambda e, zq=zq, m=m: e.activation(out=zq[:, 0:ncol], in_=self.z[:, m, 0:ncol], func=AF.Square),
                 reads=[self.B_z[m]], writes=[B_zq])
            P.op("pe", lambda e, zb=zb, m=m: e.matmul(self.ps[:, 0, 0:ncol], lhsT=self.ones_b[:], rhs=zb[:, 0:ncol],
                                                      start=(m == 0), stop=(m == 7)),
                 reads=[B_zb, self.B_ones], writes=[self.B_ps[0]])
            P.op("pe", lambda e, zq=zq, m=m: e.matmul(self.ps[:, 1, 0:ncol], lhsT=self.ones_b[:], rhs=zq[:, 0:ncol],
                                                      start=(m == 0), stop=(m == 7)),
                 reads=[B_zq, self.B_ones], writes=[self.B_ps[1]])
        mean, rstd = self.mean, self.rstd
        P.op("dve", lambda e: e.tensor_scalar(mean[:, 0:ncol], self.ps[:, 0, 0:ncol], 1.0 / D, None, ALU.mult),
             reads=[self.B_ps[0]], writes=[self.B_mean])
        P.op("dve", lambda e: e.tensor_tensor(rstd[:, 0:ncol], mean[:, 0:ncol], mean[:, 0:ncol], ALU.mult),
             reads=[self.B_mean], writes=[self.B_rstd])
        P.op("dve", lambda e: e.scalar_tensor_tensor(out=rstd[:, 0:ncol], in0=self.ps[:, 1, 0:ncol], scalar=1.0 / D,
                                                     in1=rstd[:, 0:ncol], op0=ALU.mult, op1=ALU.subtract),
             reads=[self.B_ps[1], self.B_rstd], writes=[self.B_rstd])
        self.rsqrt_act(rstd[:, 0:ncol], rstd[:, 0:ncol], self.B_rstd, self.B_rstd, 1.0, 3, ncol)
        for m in range(8):
            ta, B_ta = self.tmpa[m % 2], self.B_tmpa[m % 2]
            P.op("dve", lambda e, ta=ta, m=m: e.tensor_tensor(ta[:, 0:ncol], self.z[:, m, 0:ncol], mean[:, 0:ncol], ALU.subtract),
                 reads=[self.B_z[m], self.B_mean], writes=[B_ta])
            P.op("dve", lambda e, ta=ta: e.tensor_tensor(ta[:, 0:ncol], ta[:, 0:ncol], rstd[:, 0:ncol], ALU.mult),
                 reads=[B_ta, self.B_rstd], writes=[B_ta])
            dst = dst_fn(m)
            Bd = B_dst(m) if callable(B_dst) else B_dst
            P.op("act", lambda e, ta=ta, m=m, dst=dst: e.activation(out=dst, in_=ta[:, 0:ncol], func=AF.Identity,
                                                                    scale=self.lnp[:, gcol + m:gcol + m + 1],
                                                                    bias=self.lnp[:, bcol + m:bcol + m + 1]),
                 reads=[B_ta, self.B_lnp], writes=[Bd])

    def ffn_block(self, l, k, X, B_X, hl_ap, B_hl, hr_ap, B_hr, first, last, dst, dst_dt, B_dstbuf, row0):
        P = self.P
        w = self.W[l]
        HC, B_HC = self.HC[k % 2], self.B_HC[k % 2]
        P.op("dve", lambda e: e.tensor_copy(HC[:, :, 0:1], hl_ap), reads=[B_hl], writes=[B_HC])
        P.op("dve", lambda e: e.tensor_copy(HC[:, :, 1:2], hr_ap), reads=[B_hr], writes=[B_HC])
        for m in range(8):
            P.op("act", lambda e, m=m: e.activation(out=self.z[:, m, :], in_=X[:, m, :], func=AF.Copy, scale=ALPHA),
                 reads=[B_X], writes=[self.B_z[m]])
        for hh in range(2):
            for cc in range(11):
                c = hh * 11 + cc
                wg, B_wg = self.wgu[c % 3], self.B_wgu[c % 3]
                P.dma("sp", lambda e, wg=wg, c=c: e.dma_start(out=wg[:], in_=w["wgu_s"].ap()[c]),
                      reads=[w["B_wgu_s"]], writes=[B_wg])
                gb = (0, 1, 4)[c % 3]
                ub = (2, 3, 5)[c % 3]
                hcol = (c % 3) * 2

                def mm(e, wg=wg, gb=gb, ub=ub, hcol=hcol):
                    for kc in range(8):
                        e.matmul(self.ps[:, gb, :], lhsT=wg[:, kc, 0:128], rhs=X[:, kc, :], start=(kc == 0), stop=(kc == 7))
                    for kc in range(8):
                        e.matmul(self.ps[:, 7, hcol:hcol + 2], lhsT=wg[:, kc, 0:128], rhs=HC[:, kc, :], start=(kc == 0), stop=(kc == 7))
                    for kc in range(8):
                        ins = e.matmul(self.ps[:, ub, :], lhsT=wg[:, kc, 128:256], rhs=X[:, kc, :], start=(kc == 0), stop=(kc == 7))
                    return ins
                P.op("pe", mm, reads=[B_wg, B_X, B_HC], writes=[self.B_ps[gb], self.B_ps[ub], self.B_ps[7]])
                gc, B_gc = self.gcs[c % 2], self.B_gcs[c % 2]
                G = self.ps[:, gb, :]
                Gh = self.ps[:, 7, hcol:hcol + 2]
                w0 = self.cvp[:, c, 0:1]
                w1 = self.cvp[:, c, 1:2]
                w2 = self.cvp[:, c, 2:3]
                cb = self.cvp[:, c, 3:4]
                w0e = self.cvm[:, c, 0:1] if first else w0
                w2e = self.cvm[:, c, 1:2] if last else w2
                P.op("act", lambda e, gc=gc, G=G, w1=w1, cb=cb: e.activation(out=gc[:], in_=G, func=AF.Identity, scale=w1, bias=cb),
                     reads=[self.B_ps[gb], self.B_cvp], writes=[B_gc])
                P.op("act", lambda e, gc=gc, Gh=Gh, w0e=w0e: e.activation(out=gc[:, 0:1], in_=Gh[:, 0:1], func=AF.Identity,
                                                                           scale=w0e, bias=gc[:, 0:1]),
                     reads=[self.B_ps[7], B_gc, self.B_cvp, self.B_cvm], writes=[B_gc])
                P.op("act", lambda e, gc=gc, Gh=Gh, w2e=w2e: e.activation(out=gc[:, 511:512], in_=Gh[:, 1:2], func=AF.Identity,
                                                                           scale=w2e, bias=gc[:, 511:512]),
                     reads=[self.B_ps[7], B_gc, self.B_cvp, self.B_cvm], writes=[B_gc])
                P.op("dve", lambda e, gc=gc, G=G, w0=w0: e.scalar_tensor_tensor(out=gc[:, 1:512], in0=G[:, 0:511], scalar=w0,
                                                                                 in1=gc[:, 1:512], op0=ALU.mult, op1=ALU.add),
                     reads=[self.B_ps[gb], B_gc, self.B_cvp], writes=[B_gc])
                P.op("dve", lambda e, gc=gc, G=G, w2=w2: e.scalar_tensor_tensor(out=gc[:, 0:511], in0=G[:, 1:512], scalar=w2,
                                                                                 in1=gc[:, 0:511], op0=ALU.mult, op1=ALU.add),
                     reads=[self.B_ps[gb], B_gc, self.B_cvp], writes=[B_gc])
                tg, B_tg = self.tgs[c % 2], self.B_tgs[c % 2]
                P.op("act", lambda e, tg=tg, gc=gc: e.activation(out=tg[:], in_=gc[:], func=AF.Gelu_apprx_tanh),
                     reads=[B_gc], writes=[B_tg])
                P.op("dve", lambda e, tg=tg, ub=ub, cc=cc: e.tensor_tensor(self.hT[:, cc, :], self.ps[:, ub, :], tg[:], ALU.mult),
                     reads=[self.B_ps[ub], B_tg], writes=[self.B_hT[cc]])
            di = 0
            for grp in ((0, 1, 2), (3, 4, 5), (6, 7)):
                ng = len(grp)
                for cc in range(11):
                    c = hh * 11 + cc
                    wd, B_wd = self.wdp[di % 4], self.B_wdp[di % 4]
                    di += 1
                    P.dma("sp", lambda e, wd=wd, c=c, grp=grp, ng=ng: e.dma_start(
                        out=wd[:, 0:ng * 128], in_=w["wd_s"].ap()[c, :, grp[0] * 128:(grp[0] + ng) * 128]),
                        reads=[w["B_wd_s"]], writes=[B_wd])

                    def mm(e, wd=wd, cc=cc, ng=ng):
                        for mi in range(ng):
                            ins = e.matmul(self.ps[:, 4 + mi, :], lhsT=wd[:, mi * 128:(mi + 1) * 128], rhs=self.hT[:, cc, :],
                                           start=(cc == 0), stop=(cc == 10))
                        return ins
                    P.op("pe", mm, reads=[B_wd, self.B_hT[cc]], writes=[self.B_ps[4 + mi] for mi in range(ng)])
                for mi, m in enumerate(grp):
                    eng = "dve" if mi % 2 == 0 else "dve"
                    P.op(eng, lambda e, mi=mi, m=m: e.tensor_tensor(self.z[:, m, :], self.ps[:, 4 + mi, :], self.z[:, m, :], ALU.add),
                         reads=[self.B_ps[4 + mi], self.B_z[m]], writes=[self.B_z[m]])
        is_f32 = (dst_dt == F32)

        def emit_out(m, y2, B_y2):
            bank = 2 + (m % 2)
            if is_f32:
                pv = self.ps[:, bank, :].rearrange("p (a n) -> p a n", a=4)
                idn, B_idn = self.ident_f, self.B_idf
            else:
                pv = self.psb(bank)[:, 0:4, :]
                idn, B_idn = self.ident_b, self.B_idb

            def tr(e):
                for i in range(4):
                    ins = e.transpose(pv[:, i, :], y2[:, i * 128:(i + 1) * 128], idn[:])
                return ins
            P.op("pe", tr, reads=[B_y2, B_idn], writes=[self.B_ps[bank]])
            if is_f32:
                ot, B_ot = self.ot[m % 2], self.B_ot[m % 2]
                otv = ot[:]
            else:
                ot, B_ot = self.ot[m % 2], self.B_ot[m % 2]
                otv = ot[:].rearrange("p a n -> p (a n)").bitcast(BF16)[:, 0:512].rearrange("p (a n) -> p a n", a=4)
            P.op("act", lambda e: e.activation(out=otv, in_=pv, func=AF.Copy), reads=[self.B_ps[bank]], writes=[B_ot])
            dview = dst[row0:row0 + 512, m * 128:(m + 1) * 128].rearrange("(a p) n -> p a n", p=128)
            P.dma("sp", lambda e: e.dma_start(out=dview, in_=otv), reads=[B_ot], dwrites=[B_dstbuf], owner=B_ot,
                  is_out=True)

        self._ln_out_queue = []

        def dst_fn(m):
            y2 = self.y2[m % 2]
            if is_f32:
                return y2[:]
            return y2[:].bitcast(BF16)[:, 0:512]

        self.layer_norm_with_out(1, 512, dst_fn, lambda m: self.B_y2[m % 2], emit_out, is_f32)

    def layer_norm_with_out(self, which, ncol, dst_fn, B_dst_fn, emit_out, is_f32):
        P = self.P
        gcol = 0 if which == 0 else 16
        bcol = gcol + 8
        for m in range(8):
            zb, B_zb = self.zb[m % 2], self.B_zb[m % 2]
            zq, B_zq = self.zq[m % 2], self.B_zq[m % 2]
            P.op("act", lambda e, zb=zb, m=m: e.activation(out=zb[:, 0:ncol], in_=self.z[:, m, 0:ncol], func=AF.Copy),
                 reads=[self.B_z[m]], writes=[B_zb])
            P.op("act", lambda e, zq=zq, m=m: e.activation(out=zq[:, 0:ncol], in_=self.z[:, m, 0:ncol], func=AF.Square),
                 reads=[self.B_z[m]], writes=[B_zq])
            P.op("pe", lambda e, zb=zb, m=m: e.matmul(self.ps[:, 0, 0:ncol], lhsT=self.ones_b[:], rhs=zb[:, 0:ncol],
                                                      start=(m == 0), stop=(m == 7)),
                 reads=[B_zb, self.B_ones], writes=[self.B_ps[0]])
            P.op("pe", lambda e, zq=zq, m=m: e.matmul(self.ps[:, 1, 0:ncol], lhsT=self.ones_b[:], rhs=zq[:, 0:ncol],
                                                      start=(m == 0), stop=(m == 7)),
                 reads=[B_zq, self.B_ones], writes=[self.B_ps[1]])
        mean, rstd = self.mean, self.rstd
        P.op("dve", lambda e: e.tensor_scalar(mean[:, 0:ncol], self.ps[:, 0, 0:ncol], 1.0 / D, None, ALU.mult),
             reads=[self.B_ps[0]], writes=[self.B_mean])
        P.op("dve", lambda e: e.tensor_tensor(rstd[:, 0:ncol], mean[:, 0:ncol], mean[:, 0:ncol], ALU.mult),
             reads=[self.B_mean], writes=[self.B_rstd])
        P.op("dve", lambda e: e.scalar_tensor_tensor(out=rstd[:, 0:ncol], in0=self.ps[:, 1, 0:ncol], scalar=1.0 / D,
                                                     in1=rstd[:, 0:ncol], op0=ALU.mult, op1=ALU.subtract),
             reads=[self.B_ps[1], self.B_rstd], writes=[self.B_rstd])
        self.rsqrt_act(rstd[:, 0:ncol], rstd[:, 0:ncol], self.B_rstd, self.B_rstd, 1.0, 3, ncol)
        for m in range(8):
            ta, B_ta = self.tmpa[m % 2], self.B_tmpa[m % 2]
            P.op("dve", lambda e, ta=ta, m=m: e.tensor_tensor(ta[:, 0:ncol], self.z[:, m, 0:ncol], mean[:, 0:ncol], ALU.subtract),
                 reads=[self.B_z[m], self.B_mean], writes=[B_ta])
            P.op("dve", lambda e, ta=ta: e.tensor_tensor(ta[:, 0:ncol], ta[:, 0:ncol], rstd[:, 0:ncol], ALU.mult),
                 reads=[B_ta, self.B_rstd], writes=[B_ta])
            dst = dst_fn(m)
            Bd = B_dst_fn(m)
            P.op("act", lambda e, ta=ta, m=m, dst=dst: e.activation(out=dst, in_=ta[:, 0:ncol], func=AF.Identity,
                                                                    scale=self.lnp[:, gcol + m:gcol + m + 1],
                                                                    bias=self.lnp[:, bcol + m:bcol + m + 1]),
                 reads=[B_ta, self.B_lnp], writes=[Bd])
            emit_out(m, dst, Bd)

    def half_layer(self, l, own_off, src, src_is_f32, B_src, dst, dst_dt, B_dstbuf, pending):
        P = self.P
        self.own_off = own_off
        self.setup_layer(l, own_off)
        self.kv_phase(l, src, src_is_f32, B_src, pending)
        while pending:
            pending.pop(0)()
        tl = (own_off - 1) % NT
        tr_ = (own_off + 32) % NT
        self.att_block(l, src, src_is_f32, B_src, [tl, tr_], 127, 2, self.XH, self.B_XH)
        for k in range(8):
            tiles = [own_off + 4 * k + i for i in range(4)]
            XB, B_XB = self.XB[k % 2], self.B_XB[k % 2]
            self.att_block(l, src, src_is_f32, B_src, tiles, 0, 512, XB, B_XB)
            P.op("dve", lambda e, XB=XB, k=k: e.tensor_copy(self.LC[:, :, k:k + 1], XB[:, :, 511:512]),
                 reads=[B_XB], writes=[self.B_LC[k]])
            if k >= 1:
                self._ffn(l, k - 1, dst, dst_dt, B_dstbuf)
        self._ffn(l, 7, dst, dst_dt, B_dstbuf)

    def _ffn(self, l, k, dst, dst_dt, B_dstbuf):
        X, B_X = self.XB[k % 2], self.B_XB[k % 2]
        if k == 0:
            hl, B_hl = self.XH[:, :, 0:1], self.B_XH
        else:
            hl, B_hl = self.LC[:, :, k - 1:k], self.B_LC[k - 1]
        if k == 7:
            hr, B_hr = self.XH[:, :, 1:2], self.B_XH
        else:
            hr, B_hr = self.XB[(k + 1) % 2][:, :, 0:1], self.B_XB[(k + 1) % 2]
        self.fence(self.att_bufs + self.ffn_bufs)
        self.ffn_block(l, k, X, B_X, hl, B_hl, hr, B_hr, k == 0, k == 7, dst, dst_dt, B_dstbuf, k * 512)
        self.fence(self.att_bufs + self.ffn_bufs)

    def build(self):
        self.setup_consts()
        if not self.fused:
            pending = self.convert_weights(0)
            for _ in range(8):
                pending.pop(0)()
            self.half_layer(0, 0, self.x_in, True, None, self.out, F32, self.B_out, pending)
        else:
            pending = self.convert_weights(0) + self.convert_weights(1)
            for _ in range(8):
                pending.pop(0)()
            xm = self.xmid.ap()
            self.half_layer(0, 32, self.x_in, True, None, xm[S // 2:S, :], BF16, self.B_xmid, pending)
            self.half_layer(0, 0, self.x_in, True, None, xm[0:S // 2, :], BF16, self.B_xmid, pending)
            self.half_layer(1, 0, xm, False, self.B_xmid, self.out, F32, self.B_out, pending)
        self.P.emit()
        self.st.close()
        return self.nc


HPERM = [0, 4, 1, 5, 2, 6, 3, 7]


def _t5_bucket(rel):
    half = 16
    max_exact = 8
    bucket = np.where(rel > 0, half, 0)
    rp = np.abs(rel)
    rpf = np.maximum(rp, 1).astype(np.float32)
    large = max_exact + (np.log(rpf / np.float32(max_exact)) / np.float32(math.log(128 / max_exact))
                         * np.float32(half - max_exact)).astype(np.int32)
    large = np.minimum(large, half - 1)
    return bucket + np.where(rp < max_exact, rp, large)


def _bias_table(rel_bias):
    k = np.arange(128)[:, None, None]
    j = np.arange(3)[None, :, None]
    q = np.arange(128)[None, None, :]
    rel = (j - 1) * 128 + k - q
    idx = _t5_bucket(rel)
    tab = np.asarray(rel_bias, np.float32)[idx]
    tab = np.where((np.abs(rel) <= 128)[..., None], tab, np.float32(NEG))
    tab = np.ascontiguousarray(tab.transpose(0, 1, 3, 2))
    return tab.reshape(128, 3 * 8 * 128).astype(np.float32)


def _rope_table(half):
    tok = (np.arange(S) + half * (S // 2)) % S
    row = (tok // 64).astype(np.float32)
    col = (tok % 64).astype(np.float32)
    inv = (np.float32(10000.0) ** (-np.arange(0, 32, 2, dtype=np.float32) / np.float32(32))).astype(np.float32)
    ang = np.concatenate([row[:, None] * inv, col[:, None] * inv], axis=-1).astype(np.float32)
    cs = np.concatenate([np.cos(ang), np.sin(ang)], axis=-1).astype(np.float32)
    return np.ascontiguousarray(cs.reshape(NT, 128, 64).transpose(1, 0, 2))


def _layer_params(inp, l):
    f = lambda a: np.ascontiguousarray(np.asarray(a, np.float32))
    w_in = np.asarray(inp["w_in"][l], np.float32)
    qa = w_in[:, 0:512].reshape(D, 8, 64)[:, HPERM, :].reshape(D, 512)
    qb = w_in[:, 768:1280].reshape(D, 8, 64)[:, HPERM, :].reshape(D, 512)
    wq = np.concatenate([qa, qb], axis=1)
    wkv = np.concatenate([w_in[:, 512:640], w_in[:, 640:768], w_in[:, 1280:1408], w_in[:, 1408:1536]], axis=1)
    w_out = np.asarray(inp["w_out"][l], np.float32)
    wo = np.concatenate([w_out[0:512].reshape(8, 64, D)[HPERM].reshape(512, D),
                         w_out[512:1024].reshape(8, 64, D)[HPERM].reshape(512, D)], axis=0)
    ga = np.asarray(inp["out_norm_a"][l], np.float32).reshape(8, 64)[HPERM].reshape(512)
    gb = np.asarray(inp["out_norm_b"][l], np.float32).reshape(8, 64)[HPERM].reshape(512)
    gout = np.concatenate([ga, gb]).reshape(8, 128).T
    qn = np.asarray(inp["q_norm"][l], np.float32)
    kn = np.asarray(inp["k_norm"][l], np.float32)
    nrm = np.concatenate([qn[0::2], qn[1::2], kn[0::2], kn[1::2]])[None, :]
    fm = lambda v: np.asarray(v, np.float32).reshape(8, 128).T
    lnp = np.concatenate([fm(inp["ln1_g"][l]), fm(inp["ln1_b"][l]), fm(inp["ln2_g"][l]), fm(inp["ln2_b"][l])], axis=1)
    cw = np.asarray(inp["conv_w"][l], np.float32)
    cb = np.asarray(inp["conv_b"][l], np.float32)
    cv = np.stack([cw[0], cw[1], cw[2], cb], axis=-1).reshape(NFC, 128, 4).transpose(1, 0, 2).reshape(128, NFC * 4)
    sink = np.asarray(inp["sink"][l], np.float32)[None, :]
    return dict(wq=f(wq), wkv=f(wkv), wout=f(wo), wg=f(inp["w_gate"][l]), wu=f(inp["w_up"][l]), wd=f(inp["w_down"][l]),
                nrm=f(nrm), gout=f(gout), lnp=f(lnp), cvp=f(cv), sink=f(sink))


def _core_consts(inp, half):
    jm = np.zeros((128, 4), np.float32)
    if half == 0:
        jm[:, 0] = NEG; jm[:, 1] = 0.0; jm[:, 2] = 0.0; jm[:, 3] = 1.0
    else:
        jm[:, 0] = 0.0; jm[:, 1] = NEG; jm[:, 2] = 1.0; jm[:, 3] = 0.0
    return dict(biasT=_bias_table(inp["rel_bias"]), cs=_rope_table(half), jm=jm, ident=np.eye(128, dtype=np.float32))


def _layout_x(xb, half):
    if half == 0:
        return np.ascontiguousarray(xb)
    return np.ascontiguousarray(np.concatenate([xb[S // 2:], xb[:S // 2]], axis=0))


_NC_CACHE = {}


def _get_nc(n_layers, fused):
    key = (n_layers, fused)
    if key not in _NC_CACHE:
        _NC_CACHE[key] = Builder(n_layers, fused).build()
    return _NC_CACHE[key]


FUSED = True


def kernel(**inp):
    x = np.asarray(inp["x"], np.float32)
    B = x.shape[0]
    lp = [_layer_params(inp, l) for l in range(2)]
    cc = [_core_consts(inp, h) for h in range(2)]
    if FUSED:
        nc = _get_nc(2, True)
        in_maps = []
        for core in range(N_CORES):
            b, h = core // 2, core % 2
            m = {"xsrc": _layout_x(x[b], h)}
            for l in range(2):
                for k, v in lp[l].items():
                    m[f"{k}{l}"] = v
            m.update(cc[h])
            in_maps.append(m)
        res = run_bass_kernel_spmd(nc, in_maps, core_ids=list(range(N_CORES)))
        out = np.empty((B, S, D), np.float32)
        for core in range(N_CORES):
            b, h = core // 2, core % 2
            out[b, h * (S // 2):(h + 1) * (S // 2)] = res.results[core]["out"]
        return out
    nc = _get_nc(1, False)
    cur = x
    for l in range(2):
        in_maps = []
        for core in range(N_CORES):
            b, h = core // 2, core % 2
            m = {"xsrc": _layout_x(cur[b], h)}
            for k, v in lp[l].items():
                m[f"{k}0"] = v
            m.update(cc[h])
            in_maps.append(m)
        res = run_bass_kernel_spmd(nc, in_maps, core_ids=list(range(N_CORES)))
        nxt = np.empty((B, S, D), np.float32)
        for core in range(N_CORES):
            b, h = core // 2, core % 2
            nxt[b, h * (S // 2):(h + 1) * (S // 2)] = res.results[core]["out"]
        cur = nxt
    return cur
```

```python
import math
from contextlib import ExitStack
import numpy as np
import concourse.bass as bass
import concourse.mybir as mybir
from concourse.bass_utils import run_bass_kernel_spmd

F32 = mybir.dt.float32
BF16 = mybir.dt.bfloat16
AF = mybir.ActivationFunctionType
ALU = mybir.AluOpType
AX = mybir.AxisListType

D = 1024
S = 8192
NT = 64
NB = 4
DFF = 2816
NFC = 22
ALPHA = 4.0 ** 0.25
RMS_EPS = 1e-6
LN_EPS = 1e-5
NEG = -30000.0
N_CORES = 8


class Buf:
    __slots__ = ("name", "w", "r", "dsem", "dcnt")

    def __init__(self, name):
        self.name = name
        self.w = []
        self.r = []
        self.dsem = None
        self.dcnt = 0


class Prog:
    CE = ("pe", "act", "dve", "pool")
    ENG = ("pe", "act", "dve", "pool", "sp")

    def __init__(self, nc, stack):
        self.nc = nc
        self.stack = stack
        self.ops = {e: [] for e in self.ENG}
        self.cnt = {e: 0 for e in self.CE}
        self.esem = {e: stack.enter_context(nc.semaphore("s_" + e)) for e in self.CE}
        self.seen = {e: {} for e in self.ENG}
        self.nsem = 4
        self.out_tokens = []

    def _mk_sem(self, name):
        self.nsem += 1
        return self.stack.enter_context(self.nc.semaphore(name))

    def _waits(self, eng, reads, writes, dwrites=()):
        deps = []
        for b in reads:
            deps.extend(b.w)
        for b in writes:
            deps.extend(b.w)
            deps.extend(b.r)
        for b in dwrites:
            deps.extend(b.r)
        best = {}
        for (s, v) in deps:
            k = id(s)
            if k not in best or best[k][1] < v:
                best[k] = (s, v)
        waits = []
        own = self.esem.get(eng) if eng == "pe" else None
        for k, (s, v) in best.items():
            if s is own:
                continue
            if self.seen[eng].get(k, 0) >= v:
                continue
            self.seen[eng][k] = v
            waits.append((s, v))
        return waits

    @staticmethod
    def _compact(lst):
        best = {}
        for (s, v) in lst:
            if id(s) not in best or best[id(s)][1] < v:
                best[id(s)] = (s, v)
        return list(best.values())

    def _commit(self, tok, reads, writes, dwrites=()):
        for b in dwrites:
            b.w.append(tok)
            if len(b.w) > 24:
                b.w = self._compact(b.w)
        for b in reads:
            b.r.append(tok)
            if len(b.r) > 24:
                best = {}
                for (s, v) in b.r:
                    if id(s) not in best or best[id(s)][1] < v:
                        best[id(s)] = (s, v)
                b.r = list(best.values())
        for b in writes:
            b.w = [tok]
            b.r = []

    def op(self, eng, fn, reads=(), writes=()):
        waits = self._waits(eng, reads, writes)
        self.cnt[eng] += 1
        tok = (self.esem[eng], self.cnt[eng])
        self.ops[eng].append((fn, waits, (self.esem[eng], 1)))
        self._commit(tok, reads, writes)
        return tok

    def dma(self, q, fn, reads=(), writes=(), owner=None, is_out=False, dwrites=()):
        if owner is None:
            owner = writes[0] if writes else (dwrites[0] if dwrites else reads[0])
        if owner.dsem is None:
            owner.dsem = self._mk_sem("d_" + owner.name)
        waits = self._waits(q, reads, writes, dwrites)
        owner.dcnt += 16
        tok = (owner.dsem, owner.dcnt)
        self.ops[q].append((fn, waits, (owner.dsem, 16)))
        self._commit(tok, reads, writes, dwrites)
        if is_out:
            self.out_tokens.append(tok)
        return tok

    def emit(self):
        nc = self.nc
        ws = []
        for (s, v) in self.out_tokens:
            k = id(s)
            if self.seen["sp"].get(k, 0) >= v:
                continue
            self.seen["sp"][k] = v
            ws.append((s, v))
        if ws:
            self.ops["sp"].append((None, ws, None))
        ops = self.ops

        def replay(e, lst):
            for (fn, waits, inc) in lst:
                for (s, v) in waits:
                    e.wait_ge(s, v)
                if fn is not None:
                    ins = fn(e)
                    ins.then_inc(inc[0], inc[1])

        with nc.Block() as block:
            @block.tensor
            def _(e):
                replay(e, ops["pe"])

            @block.scalar
            def _(e):
                replay(e, ops["act"])

            @block.vector
            def _(e):
                replay(e, ops["dve"])

            @block.gpsimd
            def _(e):
                replay(e, ops["pool"])

            @block.sync
            def _(e):
                replay(e, ops["sp"])


class Builder:
    def __init__(self, n_layers, fused, dbg=False):
        self.fused = fused
        self.n_layers = n_layers
        self.dbg = dbg
        self.nc = bass.Bass("TRN2", target_bir_lowering=False)
        self.st = ExitStack()
        self.P = Prog(self.nc, self.st)
        self.sb_bytes = 0
        self._declare_io()
        self._alloc()

    def sb(self, name, shape, dt):
        n = 1
        for s in shape[1:]:
            n *= s
        self.sb_bytes += n * (2 if dt == BF16 else 4)
        return self.st.enter_context(self.nc.sbuf_tensor(name, list(shape), dt))

    def din(self, name, shape, dt=F32):
        return self.nc.dram_tensor(name, list(shape), dt, kind="ExternalInput").ap()

    def _declare_io(self):
        nc = self.nc
        L = self.n_layers
        self.x_in = self.din("xsrc", [S, D])
        self.W = []
        for l in range(L):
            w = dict(
                wq=self.din(f"wq{l}", [D, D]), wkv=self.din(f"wkv{l}", [D, 512]),
                wout=self.din(f"wout{l}", [D, D]), wg=self.din(f"wg{l}", [D, DFF]),
                wu=self.din(f"wu{l}", [D, DFF]), wd=self.din(f"wd{l}", [DFF, D]),
                nrm=self.din(f"nrm{l}", [1, 128]), gout=self.din(f"gout{l}", [128, 8]),
                lnp=self.din(f"lnp{l}", [128, 32]), cvp=self.din(f"cvp{l}", [128, 88]),
                sink=self.din(f"sink{l}", [1, 8]),
            )
            w["wq_s"] = nc.dram_tensor(f"wq_s{l}", [128, 8, D], BF16)
            w["wkv_s"] = nc.dram_tensor(f"wkv_s{l}", [128, 8, 512], BF16)
            w["wout_s"] = nc.dram_tensor(f"wout_s{l}", [8, 128, 8, 128], BF16)
            w["wgu_s"] = nc.dram_tensor(f"wgu_s{l}", [NFC, 128, 8, 256], BF16)
            w["wd_s"] = nc.dram_tensor(f"wd_s{l}", [NFC, 128, D], BF16)
            for k in ("wq_s", "wkv_s", "wout_s", "wgu_s", "wd_s"):
                w["B_" + k] = Buf(f"{k}{l}")
            self.W.append(w)
        self.biasT_in = self.din("biasT", [128, 3072])
        self.cs_in = self.din("cs", [128, NT, 64])
        self.jm_in = self.din("jm", [128, 4])
        self.ident_in = self.din("ident", [128, 128])
        self.out = nc.dram_tensor("out", [S // 2, D], F32, kind="ExternalOutput").ap()
        if self.fused:
            self.xmid = nc.dram_tensor("xmid", [S, D], BF16)
            self.B_xmid = Buf("xmid")
        self.B_out = Buf("out")
        self.dbg_out = {}

    def _alloc(self):
        nc, sb = self.nc, self.sb
        self.KAT = sb("KAT", [128, S], BF16)
        self.VA = sb("VA", [128, NT, 2, 128], BF16)
        self.KBT = sb("KBT", [128, 36 * 128], BF16)
        self.VB = sb("VB", [128, 36, 2, 65], BF16)
        self.B_KA = [Buf(f"KA{t}") for t in range(NT)]
        self.B_KB = [Buf(f"KB{t}") for t in range(NT)]
        self.B_vones = Buf("vones")
        self.ident_f = sb("ident_f", [128, 128], F32); self.B_idf = Buf("ident_f")
        self.ident_b = sb("ident_b", [128, 128], BF16); self.B_idb = Buf("ident_b")
        self.ones_b = sb("ones_b", [128, 128], BF16); self.B_ones = Buf("ones_b")
        self.zeros = sb("zeros", [128, 128], F32); self.B_zeros = Buf("zeros")
        self.onesf = sb("onesf", [128, 64], F32); self.B_onesf = Buf("onesf")
        self.epsr = sb("epsr", [128, 4], F32); self.B_eps = Buf("epsr")
        self.biasT = sb("biasT_sb", [128, 3, 2, 512], BF16); self.B_biasT = Buf("biasT")
        self.jm = sb("jm_sb", [128, 4], F32); self.B_jm = Buf("jm")
        self.nrm = sb("nrm_sb", [128, 128], F32); self.B_nrm = Buf("nrm")
        self.gout = [sb(f"gout_sb{l}", [128, 8], F32) for l in range(self.n_layers)]
        self.B_gout = [Buf(f"gout{l}") for l in range(self.n_layers)]
        self.lnp = sb("lnp_sb", [128, 32], F32); self.B_lnp = Buf("lnp")
        self.cvp = sb("cvp_sb", [128, NFC, 4], F32); self.B_cvp = Buf("cvp")
        self.cvm = sb("cvm_sb", [128, NFC, 2], F32); self.B_cvm = Buf("cvm")
        self.esink = sb("esink", [128, 8], F32); self.B_esink = Buf("esink")
        self.esrow = sb("esrow", [128, 2, 512], F32); self.B_esrow = Buf("esrow")
        self.cst = [sb(f"cst{i}", [128, 64], F32) for i in range(2)]; self.B_cst = [Buf(f"cst{i}") for i in range(2)]
        self.xt = [sb(f"xt{i}", [128, D], BF16) for i in range(2)]; self.B_xt = [Buf(f"xt{i}") for i in range(2)]
        self.xTb = sb("xTb", [128, 8, 512], BF16); self.B_xTb = Buf("xTb")
        self.xTt = [sb(f"xTt{i}", [128, 8, 128], BF16) for i in range(2)]; self.B_xTt = [Buf(f"xTt{i}") for i in range(2)]
        self.wq = sb("wq_sb", [128, 8, 512], BF16); self.B_wq = Buf("wq")
        self.wkv = self.wq; self.B_wkv = self.B_wq
        self.fscr = sb("fscr", [128, 8], F32)
        self.t_sq = sb("t_sq", [128, 512], F32); self.B_tsq = Buf("t_sq")
        self.t_qn = sb("t_qn", [128, 512], F32); self.B_tqn = Buf("t_qn")
        self.t_ab = sb("t_ab", [128, 2, 256], F32); self.B_tab = Buf("t_ab")
        self.t_m = sb("t_m", [128, 4, 256], F32); self.B_tm = Buf("t_m")
        self.t_ss = sb("t_ss", [128, 16], F32); self.B_tss = Buf("t_ss")
        self.t_rs = sb("t_rs", [128, 16], F32); self.B_trs = Buf("t_rs")
        self.tq = dict(sq=self.t_sq, qn=self.t_qn, ab=self.t_ab, m=self.t_m, ss=self.t_ss, rs=self.t_rs,
                       B=[self.B_tsq, self.B_tqn, self.B_tab, self.B_tm, self.B_tss, self.B_trs])
        self.tk = []
        for i in range(2):
            self.tk.append(dict(sq=sb(f"k_sq{i}", [128, 128], F32), qn=sb(f"k_qn{i}", [128, 128], F32),
                                ab=sb(f"k_ab{i}", [128, 2, 64], F32), m=sb(f"k_m{i}", [128, 4, 64], F32),
                                ss=sb(f"k_ss{i}", [128, 4], F32), rs=sb(f"k_rs{i}", [128, 4], F32),
                                B=[Buf(f"k_t{i}_{j}") for j in range(6)]))
        self.qbf = [sb(f"qbf{i}", [128, 512], BF16) for i in range(2)]; self.B_qbf = [Buf(f"qbf{i}") for i in range(2)]
        self.kvb = [sb(f"kvb{i}", [128, 256], BF16) for i in range(2)]; self.B_kvb = [Buf(f"kvb{i}") for i in range(2)]
        self.B_kvb2 = [Buf(f"kvbB{i}") for i in range(2)]
        A1 = sb("A1", [128, 6144], BF16)
        self.QT = A1[:, 0:4096].rearrange("p (a n) -> p a n", a=8); self.B_QT = Buf("QT")
        self.PT = [A1[:, 4096 + i * 1024:5120 + i * 1024].rearrange("p (a n) -> p a n", a=2) for i in range(2)]
        self.B_PT = [Buf(f"PT{i}") for i in range(2)]
        self.hT = A1[:, 0:5632].rearrange("p (a n) -> p a n", a=11); self.B_hT = [Buf(f"hT{i}") for i in range(11)]
        A2 = sb("A2", [128, 10240], BF16)
        f32v = lambda a, b: A2[:, a:b].bitcast(F32)
        self.yT = A2[:, 0:4096].rearrange("p (a n) -> p a n", a=8); self.B_yT = Buf("yT")
        self.stt = [f32v(4096 + i * 2048, 6144 + i * 2048).rearrange("p (a n) -> p a n", a=2) for i in range(2)]
        self.B_stt = [Buf(f"stt{i}") for i in range(2)]
        self.dsb = [f32v(8192 + i * 1024, 9216 + i * 1024) for i in range(2)]; self.B_dsb = [Buf(f"dsb{i}") for i in range(2)]
        self.rr = self.dsb; self.B_rr = self.B_dsb
        self.wgu = [A2[:, i * 2048:(i + 1) * 2048].rearrange("p (a n) -> p a n", a=8) for i in range(3)]
        self.B_wgu = [Buf(f"wgu{i}") for i in range(3)]
        self.gcs = [f32v(6144 + i * 1024, 7168 + i * 1024) for i in range(2)]; self.B_gcs = [Buf(f"gcs{i}") for i in range(2)]
        self.tgs = [f32v(8192 + i * 1024, 9216 + i * 1024) for i in range(2)]; self.B_tgs = [Buf(f"tgs{i}") for i in range(2)]
        self.y2 = self.gcs; self.B_y2 = self.B_gcs
        self.ot = [t.rearrange("p (a n) -> p a n", a=4) for t in self.tgs]; self.B_ot = self.B_tgs
        self.att_bufs = [self.B_QT] + self.B_PT + [self.B_yT] + self.B_stt + self.B_dsb
        self.ffn_bufs = self.B_hT + self.B_wgu + self.B_gcs + self.B_tgs
        self.wo = [sb(f"wo{i}", [128, 8, 128], BF16) for i in range(2)]; self.B_wo = [Buf(f"wo{i}") for i in range(2)]
        self.z = sb("z", [128, 8, 512], F32); self.B_z = [Buf(f"z{m}") for m in range(8)]
        self.zb = [sb(f"zb{i}", [128, 512], BF16) for i in range(2)]; self.B_zb = [Buf(f"zb{i}") for i in range(2)]
        self.zq = [sb(f"zq{i}", [128, 512], BF16) for i in range(2)]; self.B_zq = [Buf(f"zq{i}") for i in range(2)]
        self.sqy = self.zq; self.B_sqy = self.B_zq
        self.mean = self.t_ab[:].rearrange("p a n -> p (a n)"); self.B_mean = self.B_tab
        self.rstd = self.t_m[:, 0:2, :].rearrange("p a n -> p (a n)"); self.B_rstd = self.B_tm
        self.tmpa = [self.t_sq, self.t_qn]; self.B_tmpa = [self.B_tsq, self.B_tqn]
        self.XB = [sb(f"XB{i}", [128, 8, 512], BF16) for i in range(2)]; self.B_XB = [Buf(f"XB{i}") for i in range(2)]
        self.XH = sb("XH", [128, 8, 2], BF16); self.B_XH = Buf("XH")
        self.LC = sb("LC", [128, 8, 8], BF16); self.B_LC = [Buf(f"LC{k}") for k in range(8)]
        self.HC = [sb(f"HC{i}", [128, 8, 2], BF16) for i in range(2)]; self.B_HC = [Buf(f"HC{i}") for i in range(2)]
        self.wdp = [sb(f"wdp{i}", [128, 384], BF16) for i in range(8)]; self.B_wdp = [Buf(f"wdp{i}") for i in range(8)]
        self.cvt = [self.z[:, 2 * i:2 * i + 2, :].rearrange("p a n -> p (a n)") for i in range(2)]
        self.B_cvt = [[self.B_z[2 * i], self.B_z[2 * i + 1]] for i in range(2)]
        self.cvo = [self.z[:, 4 + i, :].bitcast(BF16) for i in range(2)]
        self.B_cvo = [[self.B_z[4 + i]] for i in range(2)]
        self.ps = self.st.enter_context(nc.psum_tensor("ps", [128, 8, 512], F32))
        self.B_ps = [Buf(f"ps{i}") for i in range(8)]
        self.B_gh = [Buf(f"gh{i}") for i in range(3)]

    def bslot(self, tk):
        sl = (tk - (self.own_off - 2)) % NT
        return sl if sl < 36 else None

    def fence(self, bufs):
        self.P.op("pool", lambda e: e.memset(self.fscr[:], 0.0), writes=list(bufs))

    def psb(self, b):
        return self.ps[:, b, :].bitcast(BF16).rearrange("p (a n) -> p a n", a=8)

    def setup_consts(self):
        P = self.P
        P.dma("sp", lambda e: e.dma_start(out=self.ident_f[:], in_=self.ident_in), writes=[self.B_idf])
        P.op("act", lambda e: e.activation(out=self.ident_b[:], in_=self.ident_f[:], func=AF.Copy),
             reads=[self.B_idf], writes=[self.B_idb])
        P.op("pool", lambda e: e.memset(self.ones_b[:], 1.0), writes=[self.B_ones])
        P.op("pool", lambda e: e.memset(self.zeros[:], 0.0), writes=[self.B_zeros])
        P.op("pool", lambda e: e.memset(self.onesf[:], 1.0), writes=[self.B_onesf])
        P.op("pool", lambda e: e.memset(self.epsr[:, 0:1], RMS_EPS), writes=[self.B_eps])
        P.op("pool", lambda e: e.memset(self.epsr[:, 1:2], math.log(0.125)), writes=[self.B_eps])
        P.op("pool", lambda e: e.memset(self.epsr[:, 2:3], 0.0), writes=[self.B_eps])
        P.op("pool", lambda e: e.memset(self.epsr[:, 3:4], LN_EPS), writes=[self.B_eps])
        P.op("pool", lambda e: e.memset(self.VA[:, :, :, 64:128], 1.0), writes=[self.B_vones])
        P.op("pool", lambda e: e.memset(self.VB[:, :, :, 64:65], 1.0), writes=[self.B_vones])
        P.dma("pool", lambda e: e.dma_start(out=self.biasT[:].rearrange("p a b n -> p (a b n)"), in_=self.biasT_in),
              writes=[self.B_biasT])
        P.dma("sp", lambda e: e.dma_start(out=self.jm[:], in_=self.jm_in), writes=[self.B_jm])

    def setup_layer(self, l, own_off):
        P = self.P
        w = self.W[l]
        P.dma("sp", lambda e: e.dma_start(out=self.nrm[:], in_=w["nrm"].partition_broadcast(128).rearrange("p a n -> p (a n)")), writes=[self.B_nrm])
        P.dma("sp", lambda e: e.dma_start(out=self.lnp[:], in_=w["lnp"]), writes=[self.B_lnp])
        P.dma("sp", lambda e: e.dma_start(out=self.cvp[:].rearrange("p c k -> p (c k)"), in_=w["cvp"]), writes=[self.B_cvp])
        P.dma("sp", lambda e: e.dma_start(out=self.esink[:], in_=w["sink"].partition_broadcast(128).rearrange("p a n -> p (a n)")), writes=[self.B_esink])
        P.op("act", lambda e: e.activation(out=self.esink[:], in_=self.esink[:], func=AF.Exp),
             reads=[self.B_esink], writes=[self.B_esink])
        for kvh in range(2):
            for c in range(4):
                h = kvh * 4 + c
                P.op("dve", lambda e, kvh=kvh, c=c, h=h: e.tensor_scalar(
                    self.esrow[:, kvh, c * 128:(c + 1) * 128], self.zeros[:, :],
                    self.esink[:, h:h + 1], None, ALU.add),
                    reads=[self.B_esink, self.B_zeros], writes=[self.B_esrow])
        jl = 2 if own_off == 0 else 3
        jr = 3 if own_off == 0 else 2
        P.op("dve", lambda e: e.tensor_scalar(self.cvm[:, :, 0], self.cvp[:, :, 0], self.jm[:, jl:jl + 1], None, ALU.mult),
             reads=[self.B_cvp, self.B_jm], writes=[self.B_cvm])
        P.op("dve", lambda e: e.tensor_scalar(self.cvm[:, :, 1], self.cvp[:, :, 2], self.jm[:, jr:jr + 1], None, ALU.mult),
             reads=[self.B_cvp, self.B_jm], writes=[self.B_cvm])

    def convert_weights(self, l):
        P = self.P
        w = self.W[l]
        steps = []

        def cast_dma(dst_ap, src_ap, B):
            P.dma("pool", lambda e: e.dma_start(out=dst_ap, in_=src_ap), dwrites=[B])

        for kc in range(8):
            steps.append(lambda kc=kc: cast_dma(w["wkv_s"].ap()[:, kc, :], w["wkv"][kc * 128:(kc + 1) * 128, :], w["B_wkv_s"]))
        for kc in range(8):
            steps.append(lambda kc=kc: cast_dma(w["wq_s"].ap()[:, kc, :], w["wq"][kc * 128:(kc + 1) * 128, :], w["B_wq_s"]))

        def wout_step(c):
            i = c % 2
            if c == 0:
                P.dma("sp", lambda e: e.dma_start(out=self.gout[l][:], in_=w["gout"]), writes=[self.B_gout[l]])
            P.dma("sp", lambda e: e.dma_start(out=self.cvt[i], in_=w["wout"][c * 128:(c + 1) * 128, :]),
                  writes=self.B_cvt[i])
            P.op("dve", lambda e: e.tensor_scalar(self.cvo[i], self.cvt[i], self.gout[l][:, c:c + 1], None, ALU.mult),
                 reads=self.B_cvt[i] + [self.B_gout[l]], writes=self.B_cvo[i])
            P.dma("sp", lambda e: e.dma_start(out=w["wout_s"].ap()[:, :, c, :].rearrange("m p n -> p m n"),
                                              in_=self.cvo[i].rearrange("p (m n) -> p m n", m=8)),
                  reads=self.B_cvo[i], dwrites=[w["B_wout_s"]], owner=w["B_wout_s"])
        for c in range(8):
            steps.append(lambda c=c: wout_step(c))
        for c in range(NFC):
            def gu(c=c):
                cast_dma(w["wgu_s"].ap()[c, :, :, 0:128],
                         w["wg"][:, c * 128:(c + 1) * 128].rearrange("(kc p) n -> p kc n", p=128), w["B_wgu_s"])
                cast_dma(w["wgu_s"].ap()[c, :, :, 128:256],
                         w["wu"][:, c * 128:(c + 1) * 128].rearrange("(kc p) n -> p kc n", p=128), w["B_wgu_s"])
            steps.append(gu)
        for c in range(NFC):
            steps.append(lambda c=c: cast_dma(w["wd_s"].ap()[c], w["wd"][c * 128:(c + 1) * 128, :], w["B_wd_s"]))
        return steps

    def load_xT(self, src, src_is_f32, B_src, t, dst_ap, B_dst, slot):
        P = self.P
        xt, B_xt = self.xt[slot], self.B_xt[slot]
        rd = [B_src] if B_src is not None else []
        if src_is_f32:
            P.dma("pool", lambda e: e.dma_start(out=xt[:], in_=src[t * 128:(t + 1) * 128, :]), reads=rd, writes=[B_xt])
        else:
            P.dma("sp", lambda e: e.dma_start(out=xt[:], in_=src[t * 128:(t + 1) * 128, :]), reads=rd, writes=[B_xt])
        bank = 6 + slot
        pv = self.psb(bank)

        def tr(e):
            for c in range(8):
                ins = e.transpose(pv[:, c, :], xt[:, c * 128:(c + 1) * 128], self.ident_b[:])
            return ins
        P.op("pe", tr, reads=[B_xt, self.B_idb], writes=[self.B_ps[bank]])
        P.op("act", lambda e: e.activation(out=dst_ap, in_=pv, func=AF.Copy), reads=[self.B_ps[bank]], writes=[B_dst])

    def norm_rope(self, src, B_src, nh, goff, cs, B_cs, scale, dst, B_dst, T=None):
        P = self.P
        W = nh * 64
        if T is None:
            T = self.tq
        B_tsq, B_tqn, B_tab, B_tm, B_tss, B_trs = T["B"]
        sq = T["sq"][:, 0:W]
        P.op("act", lambda e: e.activation(out=sq, in_=src, func=AF.Square), reads=[B_src], writes=[B_tsq])
        ss = T["ss"][:, 0:nh]
        P.op("dve", lambda e: e.tensor_reduce(out=ss, in_=sq.rearrange("p (h d) -> p h d", h=nh), axis=AX.X, op=ALU.add),
             reads=[B_tsq], writes=[B_tss])
        P.op("act", lambda e: e.activation(out=ss, in_=ss, func=AF.Ln, scale=1.0 / 64.0, bias=self.epsr[:, 0:1]),
             reads=[B_tss, self.B_eps], writes=[B_tss])
        rs = T["rs"][:, 0:nh]
        bcol = 1 if scale != 1.0 else 2
        P.op("act", lambda e: e.activation(out=rs, in_=ss, func=AF.Exp, scale=-0.5, bias=self.epsr[:, bcol:bcol + 1]),
             reads=[B_tss, self.B_eps], writes=[B_trs])
        qn = T["qn"][:, 0:W].rearrange("p (h d) -> p h d", h=nh)
        P.op("dve", lambda e: e.tensor_tensor(qn, src.rearrange("p (h d) -> p h d", h=nh),
                                              rs.unsqueeze(2).to_broadcast([128, nh, 64]), ALU.mult),
             reads=[B_src, B_trs], writes=[B_tqn])
        x0 = qn[:, :, 0::2]
        x1 = qn[:, :, 1::2]
        ge = self.nrm[:, goff:goff + 32].unsqueeze(1).to_broadcast([128, nh, 32])
        go = self.nrm[:, goff + 32:goff + 64].unsqueeze(1).to_broadcast([128, nh, 32])
        cosb = cs[:, 0:32].unsqueeze(1).to_broadcast([128, nh, 32])
        sinb = cs[:, 32:64].unsqueeze(1).to_broadcast([128, nh, 32])
        a = T["ab"][:, 0, 0:nh * 32].rearrange("p (h d) -> p h d", h=nh)
        b = T["ab"][:, 1, 0:nh * 32].rearrange("p (h d) -> p h d", h=nh)
        P.op("dve", lambda e: e.tensor_tensor(a, x0, ge, ALU.mult), reads=[B_tqn, self.B_nrm], writes=[B_tab])
        P.op("dve", lambda e: e.tensor_tensor(b, x1, go, ALU.mult), reads=[B_tqn, self.B_nrm], writes=[B_tab])
        m = [T["m"][:, i, 0:nh * 32].rearrange("p (h d) -> p h d", h=nh) for i in range(4)]
        P.op("dve", lambda e: e.tensor_tensor(m[0], a, cosb, ALU.mult), reads=[B_tab, B_cs], writes=[B_tm])
        P.op("dve", lambda e: e.tensor_tensor(m[1], b, sinb, ALU.mult), reads=[B_tab, B_cs], writes=[B_tm])
        P.op("dve", lambda e: e.tensor_tensor(m[2], a, sinb, ALU.mult), reads=[B_tab, B_cs], writes=[B_tm])
        P.op("dve", lambda e: e.tensor_tensor(m[3], b, cosb, ALU.mult), reads=[B_tab, B_cs], writes=[B_tm])
        d3 = dst.rearrange("p (h d) -> p h d", h=nh)
        P.op("dve", lambda e: e.tensor_tensor(d3[:, :, 0:32], m[0], m[1], ALU.subtract), reads=[B_tm], writes=[B_dst])
        P.op("dve", lambda e: e.tensor_tensor(d3[:, :, 32:64], m[2], m[3], ALU.add), reads=[B_tm], writes=[B_dst])

    def load_cs(self, t):
        i = t % 2
        self.P.dma("sp", lambda e: e.dma_start(out=self.cst[i][:], in_=self.cs_in[:, t, :]), writes=[self.B_cst[i]])
        return self.cst[i], self.B_cst[i]

    def kv_phase(self, l, src, src_is_f32, B_src, pending):
        P = self.P
        w = self.W[l]
        P.dma("sp", lambda e: e.dma_start(out=self.wkv[:], in_=w["wkv_s"].ap()), reads=[w["B_wkv_s"]], writes=[self.B_wkv])
        def stage1(t):
            slot = t % 2
            self.load_xT(src, src_is_f32, B_src, t, self.xTt[slot][:], self.B_xTt[slot], slot)
            for _ in range(3):
                if pending:
                    pending.pop(0)()
            xT = self.xTt[slot]
            bk = 4 + slot
            pk = self.ps[:, bk, :]

            def mm(e, xT=xT, pk=pk):
                for c in range(8):
                    ins = e.matmul(pk, lhsT=xT[:, c, :], rhs=self.wkv[:, c, :], start=(c == 0), stop=(c == 7))
                return ins
            P.op("pe", mm, reads=[self.B_xTt[slot], self.B_wkv], writes=[self.B_ps[bk]])

        def stage2a(t):
            slot = t % 2
            bk = 4 + slot
            pk = self.ps[:, bk, :]
            kvb, B_kvb, B_kvb2 = self.kvb[slot], self.B_kvb[slot], self.B_kvb2[slot]
            P.op("act", lambda e: e.activation(out=kvb[:, 128:256], in_=pk[:, 256:384], func=AF.Copy),
                 reads=[self.B_ps[bk]], writes=[B_kvb2])
            P.op("act", lambda e: e.activation(out=self.VA[:, t, :, 0:64],
                                               in_=pk[:, 128:256].rearrange("p (h d) -> p h d", h=2), func=AF.Copy),
                 reads=[self.B_ps[bk], self.B_vones], writes=[self.B_KA[t]])
            sl = self.bslot(t)
            if sl is not None:
                P.op("act", lambda e: e.activation(out=self.VB[:, sl, :, 0:64],
                                                   in_=pk[:, 384:512].rearrange("p (h d) -> p h d", h=2), func=AF.Copy),
                     reads=[self.B_ps[bk], self.B_vones], writes=[self.B_KB[t]])
            cs, B_cs = self.load_cs(t)
            self.norm_rope(pk[:, 0:128], self.B_ps[bk], 2, 64, cs, B_cs, 1.0, kvb[:, 0:128], B_kvb, T=self.tk[slot])

        def stage2b(t):
            slot = t % 2
            kvb, B_kvb, B_kvb2 = self.kvb[slot], self.B_kvb[slot], self.B_kvb2[slot]
            sl = self.bslot(t)
            bt = 2 + slot
            pt = self.psb(bt)

            def trk(e):
                e.transpose(pt[:, 0, :], kvb[:, 0:128], self.ident_b[:])
                return e.transpose(pt[:, 1, :], kvb[:, 128:256], self.ident_b[:])
            P.op("pe", trk, reads=[B_kvb, B_kvb2, self.B_idb], writes=[self.B_ps[bt]])
            P.op("dve", lambda e: e.tensor_copy(self.KAT[:, t * 128:(t + 1) * 128], pt[:, 0, :]),
                 reads=[self.B_ps[bt]], writes=[self.B_KA[t]])
            if sl is not None:
                P.op("dve", lambda e: e.tensor_copy(self.KBT[:, sl * 128:(sl + 1) * 128], pt[:, 1, :]),
                     reads=[self.B_ps[bt]], writes=[self.B_KB[t]])

        stage1(0)
        for t in range(NT):
            stage2a(t)
            if t + 1 < NT:
                stage1(t + 1)
            stage2b(t)

    def att_block(self, l, src, src_is_f32, B_src, tiles, col0, ncol, x1_dst, B_x1):
        P = self.P
        w = self.W[l]
        nt = len(tiles)
        ntok = nt * 128
        for i, t in enumerate(tiles):
            self.load_xT(src, src_is_f32, B_src, t, self.xTb[:, :, i * 128:(i + 1) * 128], self.B_xTb, i % 2)
        for half in range(2):
            P.dma("sp", lambda e, half=half: e.dma_start(out=self.wq[:], in_=w["wq_s"].ap()[:, :, half * 512:(half + 1) * 512]),
                  reads=[w["B_wq_s"]], writes=[self.B_wq])
            for i, t in enumerate(tiles):
                bk = 4 + i
                pq = self.ps[:, bk, :]

                def mm(e, i=i, pq=pq):
                    for c in range(8):
                        ins = e.matmul(pq, lhsT=self.xTb[:, c, i * 128:(i + 1) * 128], rhs=self.wq[:, c, :],
                                       start=(c == 0), stop=(c == 7))
                    return ins
                P.op("pe", mm, reads=[self.B_xTb, self.B_wq], writes=[self.B_ps[bk]])
            for i, t in enumerate(tiles):
                bk = 4 + i
                pq = self.ps[:, bk, :]
                qb, B_qb = self.qbf[i % 2], self.B_qbf[i % 2]
                if half == 0:
                    cs, B_cs = self.load_cs(t)
                    self.norm_rope(pq, self.B_ps[bk], 8, 0, cs, B_cs, 0.125, qb[:, 0:512], B_qb)
                else:
                    P.op("act", lambda e, qb=qb, pq=pq: e.activation(out=qb[:, 0:512], in_=pq, func=AF.Copy),
                         reads=[self.B_ps[bk]], writes=[B_qb])
                bt = 2 + (i % 2)
                pt = self.psb(bt)

                def trq(e, qb=qb, pt=pt):
                    for c in range(4):
                        ins = e.transpose(pt[:, c, :], qb[:, c * 128:(c + 1) * 128], self.ident_b[:])
                    return ins
                P.op("pe", trq, reads=[B_qb, self.B_idb], writes=[self.B_ps[bt]])
                P.op("dve", lambda e, i=i, half=half, pt=pt: e.tensor_copy(
                    self.QT[:, half * 4:(half + 1) * 4, i * 128:(i + 1) * 128], pt[:, 0:4, :]),
                    reads=[self.B_ps[bt]], writes=[self.B_QT])
        for c in range(4):
            accb = (4, 5) if c % 2 == 0 else (6, 7)

            def qk(kt, c=c):
                sb0 = 2 * (kt % 2)

                def f(e):
                    e.matmul(self.ps[:, sb0, 0:ntok], lhsT=self.KAT[0:64, kt * 128:(kt + 1) * 128],
                             rhs=self.QT[0:64, c, 0:ntok], start=True, stop=True)
                    return e.matmul(self.ps[:, sb0 + 1, 0:ntok], lhsT=self.KAT[64:128, kt * 128:(kt + 1) * 128],
                                    rhs=self.QT[64:128, c, 0:ntok], start=True, stop=True)
                P.op("pe", f, reads=[self.B_KA[kt], self.B_QT], writes=[self.B_ps[sb0], self.B_ps[sb0 + 1]])

            def ex(kt):
                sb0 = 2 * (kt % 2)
                pt = self.PT[kt % 2]
                P.op("act", lambda e: e.activation(out=pt[:, :, 0:ntok], in_=self.ps[:, sb0:sb0 + 2, 0:ntok], func=AF.Exp),
                     reads=[self.B_ps[sb0], self.B_ps[sb0 + 1]], writes=[self.B_PT[kt % 2]])

            def pv(kt, accb=accb):
                pt = self.PT[kt % 2]

                def f(e):
                    e.matmul(self.ps[:, accb[0], 0:ntok], lhsT=self.VA[:, kt, 0, :], rhs=pt[:, 0, 0:ntok],
                             start=(kt == 0), stop=(kt == NT - 1))
                    return e.matmul(self.ps[:, accb[1], 0:ntok], lhsT=self.VA[:, kt, 1, :], rhs=pt[:, 1, 0:ntok],
                                    start=(kt == 0), stop=(kt == NT - 1))
                P.op("pe", f, reads=[self.B_KA[kt], self.B_PT[kt % 2]], writes=[self.B_ps[accb[0]], self.B_ps[accb[1]]])

            qk(0)
            qk(1)
            for kt in range(NT):
                ex(kt)
                pv(kt)
                if kt + 2 < NT:
                    qk(kt + 2)
            for kvh in range(2):
                self.finalize_head(accb[kvh], kvh, None, self.yT[kvh * 64:(kvh + 1) * 64, c, 0:ntok], ntok, kvh)
        def b_main(i, t):
            accb = (4, 5) if i % 2 == 0 else (6, 7)
            nbrs = [((t - 1) % NT, 0), (t, 1), ((t + 1) % NT, 2)]

            def qk(jj):
                tk, jidx = nbrs[jj]
                sb0 = 2 * (jj % 2)
                ks = self.bslot(tk)
                assert ks is not None

                def f(e):
                    e.matmul(self.ps[:, sb0, :], lhsT=self.KBT[0:64, ks * 128:(ks + 1) * 128],
                             rhs=self.QT[0:64, 4:8, i * 128:(i + 1) * 128], start=True, stop=True)
                    return e.matmul(self.ps[:, sb0 + 1, :], lhsT=self.KBT[64:128, ks * 128:(ks + 1) * 128],
                                    rhs=self.QT[64:128, 4:8, i * 128:(i + 1) * 128], start=True, stop=True)
                P.op("pe", f, reads=[self.B_KB[tk], self.B_QT], writes=[self.B_ps[sb0], self.B_ps[sb0 + 1]])

            def chain(jj):
                tk, jidx = nbrs[jj]
                sb0 = 2 * (jj % 2)
                st = self.stt[jj % 2]
                B_st = self.B_stt[jj % 2]
                ks = self.bslot(tk)
                for kvh in range(2):
                    P.op("dve", lambda e, kvh=kvh: e.scalar_tensor_tensor(
                        out=st[:, kvh, :], in0=self.ps[:, sb0 + kvh, :], scalar=0.125, in1=self.biasT[:, jidx, kvh, :],
                        op0=ALU.mult, op1=ALU.add),
                        reads=[self.B_ps[sb0 + kvh], self.B_biasT], writes=[B_st])
                jcol = None
                if jidx == 0 and t == 0:
                    jcol = 0
                elif jidx == 0 and t == 32:
                    jcol = 1
                elif jidx == 2 and t == 63:
                    jcol = 0
                elif jidx == 2 and t == 31:
                    jcol = 1
                if jcol is not None:
                    P.op("dve", lambda e: e.tensor_scalar(
                        st[:].rearrange("p a n -> p (a n)"), st[:].rearrange("p a n -> p (a n)"),
                        self.jm[:, jcol:jcol + 1], None, ALU.add),
                        reads=[B_st, self.B_jm], writes=[B_st])
                pt = self.PT[jj % 2]
                P.op("act", lambda e: e.activation(out=pt[:], in_=st[:], func=AF.Exp),
                     reads=[B_st], writes=[self.B_PT[jj % 2]])

                def g(e):
                    e.matmul(self.ps[0:65, accb[0], :], lhsT=self.VB[:, ks, 0, :], rhs=pt[:, 0, :],
                             start=(jj == 0), stop=(jj == 2))
                    return e.matmul(self.ps[0:65, accb[1], :], lhsT=self.VB[:, ks, 1, :], rhs=pt[:, 1, :],
                                    start=(jj == 0), stop=(jj == 2))
                P.op("pe", g, reads=[self.B_KB[tk], self.B_PT[jj % 2]], writes=[self.B_ps[accb[0]], self.B_ps[accb[1]]])

            qk(0)
            qk(1)
            chain(0)
            qk(2)
            chain(1)
            chain(2)

        def b_fin(i):
            accb = (4, 5) if i % 2 == 0 else (6, 7)
            for kvh in range(2):
                self.finalize_head_b(accb[kvh], kvh, self.yT[kvh * 64:(kvh + 1) * 64, 4:8, i * 128:(i + 1) * 128])

        for i, t in enumerate(tiles):
            b_main(i, t)
            if i >= 1:
                b_fin(i - 1)
        b_fin(nt - 1)
        self.out_stage(l, col0, ncol)
        self.layer_norm(0, col0, ncol, lambda m: x1_dst[:, m, :], B_x1, BF16)

    def finalize_head(self, bank, kvh, sink_kvh, y_dst, n, slot):
        P = self.P
        dsb, B_dsb = self.dsb[slot], self.B_dsb[slot]
        if sink_kvh is not None:
            P.op("dve", lambda e: e.tensor_tensor(dsb[0:64, 0:n], self.ps[64:128, bank, 0:n], self.esrow[64:128, sink_kvh, 0:n], ALU.add),
                 reads=[self.B_ps[bank], self.B_esrow], writes=[B_dsb])
        else:
            P.op("act", lambda e: e.activation(out=dsb[0:64, 0:n], in_=self.ps[64:128, bank, 0:n], func=AF.Copy),
                 reads=[self.B_ps[bank]], writes=[B_dsb])
        if sink_kvh is not None:
            P.op("act", lambda e: e.activation(out=dsb[0:64, 0:n], in_=dsb[0:64, 0:n], func=AF.Ln), reads=[B_dsb], writes=[B_dsb])
            P.op("act", lambda e: e.activation(out=dsb[0:64, 0:n], in_=dsb[0:64, 0:n], func=AF.Exp, scale=-1.0),
                 reads=[B_dsb], writes=[B_dsb])
        else:
            P.op("dve", lambda e: e.reciprocal(dsb[0:64, 0:n], dsb[0:64, 0:n]), reads=[B_dsb], writes=[B_dsb])
        if len(y_dst.shape) == 3:
            in0 = self.ps[0:64, bank, 0:n].rearrange("p (a n) -> p a n", a=4)
            in1 = dsb[0:64, 0:n].rearrange("p (a n) -> p a n", a=4)
        else:
            in0 = self.ps[0:64, bank, 0:n]
            in1 = dsb[0:64, 0:n]
        P.op("dve", lambda e: e.tensor_tensor(y_dst, in0, in1, ALU.mult),
             reads=[self.B_ps[bank], B_dsb], writes=[self.B_yT])

    def finalize_head_b(self, bank, kvh, y_dst):
        P = self.P
        n = 512
        dsb, B_dsb = self.dsb[kvh], self.B_dsb[kvh]
        P.op("dve", lambda e: e.tensor_tensor(dsb[64:65, 0:n], self.ps[64:65, bank, 0:n], self.esrow[64:65, kvh, 0:n], ALU.add),
             reads=[self.B_ps[bank], self.B_esrow], writes=[B_dsb])
        P.op("act", lambda e: e.activation(out=dsb[64:65, 0:n], in_=dsb[64:65, 0:n], func=AF.Ln), reads=[B_dsb], writes=[B_dsb])
        P.op("act", lambda e: e.activation(out=dsb[64:65, 0:n], in_=dsb[64:65, 0:n], func=AF.Exp, scale=-1.0),
             reads=[B_dsb], writes=[B_dsb])
        bb = 2 + kvh
        P.op("pe", lambda e: e.matmul(self.ps[0:64, bb, 0:n], lhsT=self.onesf[64:65, 0:64], rhs=dsb[64:65, 0:n],
                                      start=True, stop=True),
             reads=[B_dsb, self.B_onesf], writes=[self.B_ps[bb]])
        P.op("act", lambda e: e.activation(out=dsb[0:64, 0:n], in_=self.ps[0:64, bb, 0:n], func=AF.Copy),
             reads=[self.B_ps[bb]], writes=[B_dsb])
        in0 = self.ps[0:64, bank, 0:n].rearrange("p (a n) -> p a n", a=4)
        in1 = dsb[0:64, 0:n].rearrange("p (a n) -> p a n", a=4)
        P.op("dve", lambda e: e.tensor_tensor(y_dst, in0, in1, ALU.mult),
             reads=[self.B_ps[bank], B_dsb], writes=[self.B_yT])

    def rsqrt_act(self, dst, src_ap, B_src, B_dst, scale, eps_col, ncol):
        P = self.P
        P.op("act", lambda e: e.activation(out=dst, in_=src_ap, func=AF.Ln, scale=scale, bias=self.epsr[:, eps_col:eps_col + 1]),
             reads=[B_src, self.B_eps], writes=[B_dst])
        P.op("act", lambda e: e.activation(out=dst, in_=dst, func=AF.Exp, scale=-0.5), reads=[B_dst], writes=[B_dst])

    def out_stage(self, l, col0, ncol):
        P = self.P
        w = self.W[l]
        cs = slice(col0, col0 + ncol)
        for g in range(2):
            bank = g
            for cc in range(4):
                c = g * 4 + cc
                sq, B_sq = self.sqy[cc % 2], self.B_sqy[cc % 2]
                P.op("act", lambda e, sq=sq, c=c: e.activation(out=sq[:, 0:ncol], in_=self.yT[:, c, cs], func=AF.Square),
                     reads=[self.B_yT], writes=[B_sq])
                P.op("pe", lambda e, sq=sq, cc=cc, bank=bank: e.matmul(self.ps[:, bank, 0:ncol], lhsT=self.ones_b[:], rhs=sq[:, 0:ncol],
                                                                    start=(cc == 0), stop=(cc == 3)),
                     reads=[B_sq, self.B_ones], writes=[self.B_ps[bank]])
            rr, B_rr = self.rr[g], self.B_rr[g]
            self.rsqrt_act(rr[:, 0:ncol], self.ps[:, bank, 0:ncol], self.B_ps[bank], B_rr, 1.0 / 512.0, 0, ncol)
            for cc in range(4):
                c = g * 4 + cc
                P.op("dve", lambda e, c=c, rr=rr: e.tensor_tensor(self.yT[:, c, cs], self.yT[:, c, cs], rr[:, 0:ncol], ALU.mult),
                     reads=[self.B_yT, B_rr], writes=[self.B_yT])
        for m in range(8):
            wo, B_wo = self.wo[m % 2], self.B_wo[m % 2]
            P.dma("sp", lambda e, wo=wo, m=m: e.dma_start(out=wo[:], in_=w["wout_s"].ap()[m]),
                  reads=[w["B_wout_s"]], writes=[B_wo])
            ba = 2 + (m % 2)

            def mm(e, wo=wo, ba=ba, m=m):
                for c in range(8):
                    ins = e.matmul(self.ps[:, ba, 0:ncol], lhsT=wo[:, c, :], rhs=self.yT[:, c, cs],
                                   start=(c == 0), stop=(c == 7))
                return ins
            P.op("pe", mm, reads=[B_wo, self.B_yT], writes=[self.B_ps[ba]])
            P.op("dve", lambda e, ba=ba, m=m: e.scalar_tensor_tensor(out=self.z[:, m, 0:ncol], in0=self.xTb[:, m, cs], scalar=ALPHA,
                                                                     in1=self.ps[:, ba, 0:ncol], op0=ALU.mult, op1=ALU.add),
                 reads=[self.B_xTb, self.B_ps[ba]], writes=[self.B_z[m]])

    def layer_norm(self, which, col0_unused, ncol, dst_fn, B_dst, out_dt):
        P = self.P
        gcol = 0 if which == 0 else 16
        bcol = gcol + 8
        for m in range(8):
            zb, B_zb = self.zb[m % 2], self.B_zb[m % 2]
            zq, B_zq = self.zq[m % 2], self.B_zq[m % 2]
            P.op("act", lambda e, zb=zb, m=m: e.activation(out=zb[:, 0:ncol], in_=self.z[:, m, 0:ncol], func=AF.Copy),
                 reads=[self.B_z[m]], writes=[B_zb])
            P.op("act", lambda e, zq=zq, m=m: e.activation(out=zq[:, 0:ncol], in_=self.z[:, m, 0:ncol], func=AF.Square),
                 reads=[self.B_z[m]], writes=[B_zq])
            P.op("pe", lambda e, zb=zb, m=m: e.matmul(self.ps[:, 0, 0:ncol], lhsT=self.ones_b[:], rhs=zb[:, 0:ncol],
                                                      start=(m == 0), stop=(m == 7)),
                 reads=[B_zb, self.B_ones], writes=[self.B_ps[0]])
            P.op("pe", lambda e, zq=zq, m=m: e.matmul(self.ps[:, 1, 0:ncol], lhsT=self.ones_b[:], rhs=zq[:, 0:ncol],
                                                      start=(m == 0), stop=(m == 7)),
                 reads=[B_zq, self.B_ones], writes=[self.B_ps[1]])
        mean, rstd = self.mean, self.rstd
        P.op("dve", lambda e: e.tensor_scalar(mean[:, 0:ncol], self.ps[:, 0, 0:ncol], 1.0 / D, None, ALU.mult),
             reads=[self.B_ps[0]], writes=[self.B_mean])
        P.op("dve", lambda e: e.tensor_tensor(rstd[:, 0:ncol], mean[:, 0:ncol], mean[:, 0:ncol], ALU.mult),
             reads=[self.B_mean], writes=[self.B_rstd])
        P.op("dve", lambda e: e.scalar_tensor_tensor(out=rstd[:, 0:ncol], in0=self.ps[:, 1, 0:ncol], scalar=1.0 / D,
                                                     in1=rstd[:, 0:ncol], op0=ALU.mult, op1=ALU.subtract),
             reads=[self.B_ps[1], self.B_rstd], writes=[self.B_rstd])
        self.rsqrt_act(rstd[:, 0:ncol], rstd[:, 0:ncol], self.B_rstd, self.B_rstd, 1.0, 3, ncol)
        for m in range(8):
            ta, B_ta = self.tmpa[m % 2], self.B_tmpa[m % 2]
            P.op("dve", lambda e, ta=ta, m=m: e.tensor_tensor(ta[:, 0:ncol], self.z[:, m, 0:ncol], mean[:, 0:ncol], ALU.subtract),
                 reads=[self.B_z[m], self.B_mean], writes=[B_ta])
            P.op("dve", lambda e, ta=ta: e.tensor_tensor(ta[:, 0:ncol], ta[:, 0:ncol], rstd[:, 0:ncol], ALU.mult),
                 reads=[B_ta, self.B_rstd], writes=[B_ta])
            dst = dst_fn(m)
            Bd = B_dst(m) if callable(B_dst) else B_dst
            P.op("act", lambda e, ta=ta, m=m, dst=dst: e.activation(out=dst, in_=ta[:, 0:ncol], func=AF.Identity,
                                                                    scale=self.lnp[:, gcol + m:gcol + m + 1],
                                                                    bias=self.lnp[:, bcol + m:bcol + m + 1]),
                 reads=[B_ta, self.B_lnp], writes=[Bd])

    def ffn_block(self, l, k, X, B_X, hl_ap, B_hl, hr_ap, B_hr, first, last, dst, dst_dt, B_dstbuf, row0):
        P = self.P
        w = self.W[l]
        HC, B_HC = self.HC[k % 2], self.B_HC[k % 2]
        P.op("dve", lambda e: e.tensor_copy(HC[:, :, 0:1], hl_ap), reads=[B_hl], writes=[B_HC])
        P.op("dve", lambda e: e.tensor_copy(HC[:, :, 1:2], hr_ap), reads=[B_hr], writes=[B_HC])
        for m in range(8):
            P.op("act", lambda e, m=m: e.activation(out=self.z[:, m, :], in_=X[:, m, :], func=AF.Copy, scale=ALPHA),
                 reads=[B_X], writes=[self.B_z[m]])
        for hh in range(2):
            for cc in range(11):
                c = hh * 11 + cc
                wg, B_wg = self.wgu[c % 3], self.B_wgu[c % 3]
                P.dma("sp", lambda e, wg=wg, c=c: e.dma_start(out=wg[:], in_=w["wgu_s"].ap()[c]),
                      reads=[w["B_wgu_s"]], writes=[B_wg])
                gb = (0, 1, 4)[c % 3]
                ub = (2, 3, 5)[c % 3]
                hcol = (c % 3) * 2

                def mm(e, wg=wg, gb=gb, ub=ub, hcol=hcol):
                    for kc in range(8):
                        e.matmul(self.ps[:, gb, :], lhsT=wg[:, kc, 0:128], rhs=X[:, kc, :], start=(kc == 0), stop=(kc == 7))
                    for kc in range(8):
                        e.matmul(self.ps[:, 7, hcol:hcol + 2], lhsT=wg[:, kc, 0:128], rhs=HC[:, kc, :], start=(kc == 0), stop=(kc == 7))
                    for kc in range(8):
                        ins = e.matmul(self.ps[:, ub, :], lhsT=wg[:, kc, 128:256], rhs=X[:, kc, :], start=(kc == 0), stop=(kc == 7))
                    return ins
                P.op("pe", mm, reads=[B_wg, B_X, B_HC], writes=[self.B_ps[gb], self.B_ps[ub], self.B_ps[7]])
                gc, B_gc = self.gcs[c % 2], self.B_gcs[c % 2]
                G = self.ps[:, gb, :]
                Gh = self.ps[:, 7, hcol:hcol + 2]
                w0 = self.cvp[:, c, 0:1]
                w1 = self.cvp[:, c, 1:2]
                w2 = self.cvp[:, c, 2:3]
                cb = self.cvp[:, c, 3:4]
                w0e = self.cvm[:, c, 0:1] if first else w0
                w2e = self.cvm[:, c, 1:2] if last else w2
                P.op("act", lambda e, gc=gc, G=G, w1=w1, cb=cb: e.activation(out=gc[:], in_=G, func=AF.Identity, scale=w1, bias=cb),
                     reads=[self.B_ps[gb], self.B_cvp], writes=[B_gc])
                P.op("act", lambda e, gc=gc, Gh=Gh, w0e=w0e: e.activation(out=gc[:, 0:1], in_=Gh[:, 0:1], func=AF.Identity,
                                                                           scale=w0e, bias=gc[:, 0:1]),
                     reads=[self.B_ps[7], B_gc, self.B_cvp, self.B_cvm], writes=[B_gc])
                P.op("act", lambda e, gc=gc, Gh=Gh, w2e=w2e: e.activation(out=gc[:, 511:512], in_=Gh[:, 1:2], func=AF.Identity,
                                                                           scale=w2e, bias=gc[:, 511:512]),
                     reads=[self.B_ps[7], B_gc, self.B_cvp, self.B_cvm], writes=[B_gc])
                P.op("dve", lambda e, gc=gc, G=G, w0=w0: e.scalar_tensor_tensor(out=gc[:, 1:512], in0=G[:, 0:511], scalar=w0,
                                                                                 in1=gc[:, 1:512], op0=ALU.mult, op1=ALU.add),
                     reads=[self.B_ps[gb], B_gc, self.B_cvp], writes=[B_gc])
                P.op("dve", lambda e, gc=gc, G=G, w2=w2: e.scalar_tensor_tensor(out=gc[:, 0:511], in0=G[:, 1:512], scalar=w2,
                                                                                 in1=gc[:, 0:511], op0=ALU.mult, op1=ALU.add),
                     reads=[self.B_ps[gb], B_gc, self.B_cvp], writes=[B_gc])
                if cc >= 1:
                    self._ffn_tail(c - 1, cc - 1)
            self._ffn_tail(hh * 11 + 10, 10)
            di = 0
            for grp in ((0, 1, 2), (3, 4, 5), (6, 7)):
                ng = len(grp)
                for cc in range(11):
                    c = hh * 11 + cc
                    wd, B_wd = self.wdp[di % 8], self.B_wdp[di % 8]
                    di += 1
                    P.dma("sp", lambda e, wd=wd, c=c, grp=grp, ng=ng: e.dma_start(
                        out=wd[:, 0:ng * 128], in_=w["wd_s"].ap()[c, :, grp[0] * 128:(grp[0] + ng) * 128]),
                        reads=[w["B_wd_s"]], writes=[B_wd])

                    def mm(e, wd=wd, cc=cc, ng=ng):
                        for mi in range(ng):
                            ins = e.matmul(self.ps[:, 4 + mi, :], lhsT=wd[:, mi * 128:(mi + 1) * 128], rhs=self.hT[:, cc, :],
                                           start=(cc == 0), stop=(cc == 10))
                        return ins
                    P.op("pe", mm, reads=[B_wd, self.B_hT[cc]], writes=[self.B_ps[4 + mi] for mi in range(ng)])
                for mi, m in enumerate(grp):
                    eng = "dve" if mi % 2 == 0 else "dve"
                    P.op(eng, lambda e, mi=mi, m=m: e.tensor_tensor(self.z[:, m, :], self.ps[:, 4 + mi, :], self.z[:, m, :], ALU.add),
                         reads=[self.B_ps[4 + mi], self.B_z[m]], writes=[self.B_z[m]])
        is_f32 = (dst_dt == F32)

        def emit_out(m, y2, B_y2):
            bank = 2 + (m % 2)
            if is_f32:
                pv = self.ps[:, bank, :].rearrange("p (a n) -> p a n", a=4)
                idn, B_idn = self.ident_f, self.B_idf
            else:
                pv = self.psb(bank)[:, 0:4, :]
                idn, B_idn = self.ident_b, self.B_idb

            def tr(e):
                for i in range(4):
                    ins = e.transpose(pv[:, i, :], y2[:, i * 128:(i + 1) * 128], idn[:])
                return ins
            P.op("pe", tr, reads=[B_y2, B_idn], writes=[self.B_ps[bank]])
            if is_f32:
                ot, B_ot = self.ot[m % 2], self.B_ot[m % 2]
                otv = ot[:]
            else:
                ot, B_ot = self.ot[m % 2], self.B_ot[m % 2]
                otv = ot[:].rearrange("p a n -> p (a n)").bitcast(BF16)[:, 0:512].rearrange("p (a n) -> p a n", a=4)
            P.op("act", lambda e: e.activation(out=otv, in_=pv, func=AF.Copy), reads=[self.B_ps[bank]], writes=[B_ot])
            dview = dst[row0:row0 + 512, m * 128:(m + 1) * 128].rearrange("(a p) n -> p a n", p=128)
            P.dma("sp", lambda e: e.dma_start(out=dview, in_=otv), reads=[B_ot], dwrites=[B_dstbuf], owner=B_ot,
                  is_out=True)

        self._ln_out_queue = []

        def dst_fn(m):
            y2 = self.y2[m % 2]
            if is_f32:
                return y2[:]
            return y2[:].bitcast(BF16)[:, 0:512]

        self.layer_norm_with_out(1, 512, dst_fn, lambda m: self.B_y2[m % 2], emit_out, is_f32)

    def _ffn_tail(self, c, cc):
        P = self.P
        ub = (2, 3, 5)[c % 3]
        gc, B_gc = self.gcs[c % 2], self.B_gcs[c % 2]
        tg, B_tg = self.tgs[c % 2], self.B_tgs[c % 2]
        P.op("act", lambda e: e.activation(out=tg[:], in_=gc[:], func=AF.Gelu_apprx_tanh), reads=[B_gc], writes=[B_tg])
        P.op("dve", lambda e: e.tensor_tensor(self.hT[:, cc, :], self.ps[:, ub, :], tg[:], ALU.mult),
             reads=[self.B_ps[ub], B_tg], writes=[self.B_hT[cc]])

    def layer_norm_with_out(self, which, ncol, dst_fn, B_dst_fn, emit_out, is_f32):
        P = self.P
        gcol = 0 if which == 0 else 16
        bcol = gcol + 8
        for m in range(8):
            zb, B_zb = self.zb[m % 2], self.B_zb[m % 2]
            zq, B_zq = self.zq[m % 2], self.B_zq[m % 2]
            P.op("act", lambda e, zb=zb, m=m: e.activation(out=zb[:, 0:ncol], in_=self.z[:, m, 0:ncol], func=AF.Copy),
                 reads=[self.B_z[m]], writes=[B_zb])
            P.op("act", lambda e, zq=zq, m=m: e.activation(out=zq[:, 0:ncol], in_=self.z[:, m, 0:ncol], func=AF.Square),
                 reads=[self.B_z[m]], writes=[B_zq])
            P.op("pe", lambda e, zb=zb, m=m: e.matmul(self.ps[:, 0, 0:ncol], lhsT=self.ones_b[:], rhs=zb[:, 0:ncol],
                                                      start=(m == 0), stop=(m == 7)),
                 reads=[B_zb, self.B_ones], writes=[self.B_ps[0]])
            P.op("pe", lambda e, zq=zq, m=m: e.matmul(self.ps[:, 1, 0:ncol], lhsT=self.ones_b[:], rhs=zq[:, 0:ncol],
                                                      start=(m == 0), stop=(m == 7)),
                 reads=[B_zq, self.B_ones], writes=[self.B_ps[1]])
        mean, rstd = self.mean, self.rstd
        P.op("dve", lambda e: e.tensor_scalar(mean[:, 0:ncol], self.ps[:, 0, 0:ncol], 1.0 / D, None, ALU.mult),
             reads=[self.B_ps[0]], writes=[self.B_mean])
        P.op("dve", lambda e: e.tensor_tensor(rstd[:, 0:ncol], mean[:, 0:ncol], mean[:, 0:ncol], ALU.mult),
             reads=[self.B_mean], writes=[self.B_rstd])
        P.op("dve", lambda e: e.scalar_tensor_tensor(out=rstd[:, 0:ncol], in0=self.ps[:, 1, 0:ncol], scalar=1.0 / D,
                                                     in1=rstd[:, 0:ncol], op0=ALU.mult, op1=ALU.subtract),
             reads=[self.B_ps[1], self.B_rstd], writes=[self.B_rstd])
        self.rsqrt_act(rstd[:, 0:ncol], rstd[:, 0:ncol], self.B_rstd, self.B_rstd, 1.0, 3, ncol)
        for m in range(8):
            ta, B_ta = self.tmpa[m % 2], self.B_tmpa[m % 2]
            P.op("dve", lambda e, ta=ta, m=m: e.tensor_tensor(ta[:, 0:ncol], self.z[:, m, 0:ncol], mean[:, 0:ncol], ALU.subtract),
                 reads=[self.B_z[m], self.B_mean], writes=[B_ta])
            P.op("dve", lambda e, ta=ta: e.tensor_tensor(ta[:, 0:ncol], ta[:, 0:ncol], rstd[:, 0:ncol], ALU.mult),
                 reads=[B_ta, self.B_rstd], writes=[B_ta])
            dst = dst_fn(m)
            Bd = B_dst_fn(m)
            P.op("act", lambda e, ta=ta, m=m, dst=dst: e.activation(out=dst, in_=ta[:, 0:ncol], func=AF.Identity,
                                                                    scale=self.lnp[:, gcol + m:gcol + m + 1],
                                                                    bias=self.lnp[:, bcol + m:bcol + m + 1]),
                 reads=[B_ta, self.B_lnp], writes=[Bd])
            emit_out(m, dst, Bd)

    def half_layer(self, l, own_off, src, src_is_f32, B_src, dst, dst_dt, B_dstbuf, pending):
        P = self.P
        self.own_off = own_off
        self.setup_layer(l, own_off)
        self.kv_phase(l, src, src_is_f32, B_src, pending)
        while pending:
            pending.pop(0)()
        tl = (own_off - 1) % NT
        tr_ = (own_off + 32) % NT
        self.att_block(l, src, src_is_f32, B_src, [tl, tr_], 127, 2, self.XH, self.B_XH)
        for k in range(8):
            tiles = [own_off + 4 * k + i for i in range(4)]
            XB, B_XB = self.XB[k % 2], self.B_XB[k % 2]
            self.att_block(l, src, src_is_f32, B_src, tiles, 0, 512, XB, B_XB)
            P.op("dve", lambda e, XB=XB, k=k: e.tensor_copy(self.LC[:, :, k:k + 1], XB[:, :, 511:512]),
                 reads=[B_XB], writes=[self.B_LC[k]])
            if k >= 1:
                self._ffn(l, k - 1, dst, dst_dt, B_dstbuf)
        self._ffn(l, 7, dst, dst_dt, B_dstbuf)

    def _ffn(self, l, k, dst, dst_dt, B_dstbuf):
        X, B_X = self.XB[k % 2], self.B_XB[k % 2]
        if k == 0:
            hl, B_hl = self.XH[:, :, 0:1], self.B_XH
        else:
            hl, B_hl = self.LC[:, :, k - 1:k], self.B_LC[k - 1]
        if k == 7:
            hr, B_hr = self.XH[:, :, 1:2], self.B_XH
        else:
            hr, B_hr = self.XB[(k + 1) % 2][:, :, 0:1], self.B_XB[(k + 1) % 2]
        self.fence(self.att_bufs + self.ffn_bufs)
        self.ffn_block(l, k, X, B_X, hl, B_hl, hr, B_hr, k == 0, k == 7, dst, dst_dt, B_dstbuf, k * 512)
        self.fence(self.att_bufs + self.ffn_bufs)

    def build(self):
        self.setup_consts()
        if not self.fused:
            pending = self.convert_weights(0)
            for _ in range(8):
                pending.pop(0)()
            self.half_layer(0, 0, self.x_in, True, None, self.out, F32, self.B_out, pending)
        else:
            pending = self.convert_weights(0) + self.convert_weights(1)
            for _ in range(8):
                pending.pop(0)()
            xm = self.xmid.ap()
            self.half_layer(0, 32, self.x_in, True, None, xm[S // 2:S, :], BF16, self.B_xmid, pending)
            self.half_layer(0, 0, self.x_in, True, None, xm[0:S // 2, :], BF16, self.B_xmid, pending)
            self.half_layer(1, 0, xm, False, self.B_xmid, self.out, F32, self.B_out, pending)
        self.P.emit()
        self.st.close()
        return self.nc


HPERM = [0, 4, 1, 5, 2, 6, 3, 7]


def _t5_bucket(rel):
    half = 16
    max_exact = 8
    bucket = np.where(rel > 0, half, 0)
    rp = np.abs(rel)
    rpf = np.maximum(rp, 1).astype(np.float32)
    large = max_exact + (np.log(rpf / np.float32(max_exact)) / np.float32(math.log(128 / max_exact))
                         * np.float32(half - max_exact)).astype(np.int32)
    large = np.minimum(large, half - 1)
    return bucket + np.where(rp < max_exact, rp, large)


def _bias_table(rel_bias):
    k = np.arange(128)[:, None, None]
    j = np.arange(3)[None, :, None]
    q = np.arange(128)[None, None, :]
    rel = (j - 1) * 128 + k - q
    idx = _t5_bucket(rel)
    tab = np.asarray(rel_bias, np.float32)[idx]
    tab = np.where((np.abs(rel) <= 128)[..., None], tab, np.float32(NEG))
    tab = np.ascontiguousarray(tab.transpose(0, 1, 3, 2))
    return tab.reshape(128, 3 * 8 * 128).astype(np.float32)


def _rope_table(half):
    tok = (np.arange(S) + half * (S // 2)) % S
    row = (tok // 64).astype(np.float32)
    col = (tok % 64).astype(np.float32)
    inv = (np.float32(10000.0) ** (-np.arange(0, 32, 2, dtype=np.float32) / np.float32(32))).astype(np.float32)
    ang = np.concatenate([row[:, None] * inv, col[:, None] * inv], axis=-1).astype(np.float32)
    cs = np.concatenate([np.cos(ang), np.sin(ang)], axis=-1).astype(np.float32)
    return np.ascontiguousarray(cs.reshape(NT, 128, 64).transpose(1, 0, 2))


def _layer_params(inp, l):
    f = lambda a: np.ascontiguousarray(np.asarray(a, np.float32))
    w_in = np.asarray(inp["w_in"][l], np.float32)
    qa = w_in[:, 0:512].reshape(D, 8, 64)[:, HPERM, :].reshape(D, 512)
    qb = w_in[:, 768:1280].reshape(D, 8, 64)[:, HPERM, :].reshape(D, 512)
    wq = np.concatenate([qa, qb], axis=1)
    wkv = np.concatenate([w_in[:, 512:640], w_in[:, 640:768], w_in[:, 1280:1408], w_in[:, 1408:1536]], axis=1)
    w_out = np.asarray(inp["w_out"][l], np.float32)
    wo = np.concatenate([w_out[0:512].reshape(8, 64, D)[HPERM].reshape(512, D),
                         w_out[512:1024].reshape(8, 64, D)[HPERM].reshape(512, D)], axis=0)
    ga = np.asarray(inp["out_norm_a"][l], np.float32).reshape(8, 64)[HPERM].reshape(512)
    gb = np.asarray(inp["out_norm_b"][l], np.float32).reshape(8, 64)[HPERM].reshape(512)
    gout = np.concatenate([ga, gb]).reshape(8, 128).T
    qn = np.asarray(inp["q_norm"][l], np.float32)
    kn = np.asarray(inp["k_norm"][l], np.float32)
    nrm = np.concatenate([qn[0::2], qn[1::2], kn[0::2], kn[1::2]])[None, :]
    fm = lambda v: np.asarray(v, np.float32).reshape(8, 128).T
    lnp = np.concatenate([fm(inp["ln1_g"][l]), fm(inp["ln1_b"][l]), fm(inp["ln2_g"][l]), fm(inp["ln2_b"][l])], axis=1)
    cw = np.asarray(inp["conv_w"][l], np.float32)
    cb = np.asarray(inp["conv_b"][l], np.float32)
    cv = np.stack([cw[0], cw[1], cw[2], cb], axis=-1).reshape(NFC, 128, 4).transpose(1, 0, 2).reshape(128, NFC * 4)
    sink = np.asarray(inp["sink"][l], np.float32)[None, :]
    return dict(wq=f(wq), wkv=f(wkv), wout=f(wo), wg=f(inp["w_gate"][l]), wu=f(inp["w_up"][l]), wd=f(inp["w_down"][l]),
                nrm=f(nrm), gout=f(gout), lnp=f(lnp), cvp=f(cv), sink=f(sink))


def _core_consts(inp, half):
    jm = np.zeros((128, 4), np.float32)
    if half == 0:
        jm[:, 0] = NEG; jm[:, 1] = 0.0; jm[:, 2] = 0.0; jm[:, 3] = 1.0
    else:
        jm[:, 0] = 0.0; jm[:, 1] = NEG; jm[:, 2] = 1.0; jm[:, 3] = 0.0
    return dict(biasT=_bias_table(inp["rel_bias"]), cs=_rope_table(half), jm=jm, ident=np.eye(128, dtype=np.float32))


def _layout_x(xb, half):
    if half == 0:
        return np.ascontiguousarray(xb)
    return np.ascontiguousarray(np.concatenate([xb[S // 2:], xb[:S // 2]], axis=0))


_NC_CACHE = {}


def _get_nc(n_layers, fused):
    key = (n_layers, fused)
    if key not in _NC_CACHE:
        _NC_CACHE[key] = Builder(n_layers, fused).build()
    return _NC_CACHE[key]


FUSED = True


def kernel(**inp):
    x = np.asarray(inp["x"], np.float32)
    B = x.shape[0]
    lp = [_layer_params(inp, l) for l in range(2)]
    cc = [_core_consts(inp, h) for h in range(2)]
    if FUSED:
        nc = _get_nc(2, True)
        in_maps = []
        for core in range(N_CORES):
            b, h = core // 2, core % 2
            m = {"xsrc": _layout_x(x[b], h)}
            for l in range(2):
                for k, v in lp[l].items():
                    m[f"{k}{l}"] = v
            m.update(cc[h])
            in_maps.append(m)
        res = run_bass_kernel_spmd(nc, in_maps, core_ids=list(range(N_CORES)))
        out = np.empty((B, S, D), np.float32)
        for core in range(N_CORES):
            b, h = core // 2, core % 2
            out[b, h * (S // 2):(h + 1) * (S // 2)] = res.results[core]["out"]
        return out
    nc = _get_nc(1, False)
    cur = x
    for l in range(2):
        in_maps = []
        for core in range(N_CORES):
            b, h = core // 2, core % 2
            m = {"xsrc": _layout_x(cur[b], h)}
            for k, v in lp[l].items():
                m[f"{k}0"] = v
            m.update(cc[h])
            in_maps.append(m)
        res = run_bass_kernel_spmd(nc, in_maps, core_ids=list(range(N_CORES)))
        nxt = np.empty((B, S, D), np.float32)
        for core in range(N_CORES):
            b, h = core // 2, core % 2
            nxt[b, h * (S // 2):(h + 1) * (S // 2)] = res.results[core]["out"]
        cur = nxt
    return cur
```

```python
import math
from contextlib import ExitStack
import numpy as np
import concourse.bass as bass
import concourse.mybir as mybir
from concourse.bass_utils import run_bass_kernel_spmd

F32 = mybir.dt.float32
BF16 = mybir.dt.bfloat16
AF = mybir.ActivationFunctionType
ALU = mybir.AluOpType
AX = mybir.AxisListType

D = 1024
S = 8192
NT = 64
NB = 4
DFF = 2816
NFC = 22
ALPHA = 4.0 ** 0.25
RMS_EPS = 1e-6
LN_EPS = 1e-5
NEG = -30000.0
N_CORES = 8


class Buf:
    __slots__ = ("name", "w", "r", "dsem", "dcnt")

    def __init__(self, name):
        self.name = name
        self.w = []
        self.r = []
        self.dsem = None
        self.dcnt = 0


class Prog:
    CE = ("pe", "act", "dve", "pool")
    ENG = ("pe", "act", "dve", "pool", "sp")

    def __init__(self, nc, stack):
        self.nc = nc
        self.stack = stack
        self.ops = {e: [] for e in self.ENG}
        self.cnt = {e: 0 for e in self.CE}
        self.esem = {e: stack.enter_context(nc.semaphore("s_" + e)) for e in self.CE}
        self.seen = {e: {} for e in self.ENG}
        self.nsem = 4
        self.out_tokens = []

    def _mk_sem(self, name):
        self.nsem += 1
        return self.stack.enter_context(self.nc.semaphore(name))

    def _waits(self, eng, reads, writes, dwrites=()):
        deps = []
        for b in reads:
            deps.extend(b.w)
        for b in writes:
            deps.extend(b.w)
            deps.extend(b.r)
        for b in dwrites:
            deps.extend(b.r)
        best = {}
        for (s, v) in deps:
            k = id(s)
            if k not in best or best[k][1] < v:
                best[k] = (s, v)
        waits = []
        own = self.esem.get(eng) if eng == "pe" else None
        for k, (s, v) in best.items():
            if s is own:
                continue
            if self.seen[eng].get(k, 0) >= v:
                continue
            self.seen[eng][k] = v
            waits.append((s, v))
        return waits

    @staticmethod
    def _compact(lst):
        best = {}
        for (s, v) in lst:
            if id(s) not in best or best[id(s)][1] < v:
                best[id(s)] = (s, v)
        return list(best.values())

    def _commit(self, tok, reads, writes, dwrites=()):
        for b in dwrites:
            b.w.append(tok)
            if len(b.w) > 24:
                b.w = self._compact(b.w)
        for b in reads:
            b.r.append(tok)
            if len(b.r) > 24:
                best = {}
                for (s, v) in b.r:
                    if id(s) not in best or best[id(s)][1] < v:
                        best[id(s)] = (s, v)
                b.r = list(best.values())
        for b in writes:
            b.w = [tok]
            b.r = []

    def op(self, eng, fn, reads=(), writes=()):
        waits = self._waits(eng, reads, writes)
        self.cnt[eng] += 1
        tok = (self.esem[eng], self.cnt[eng])
        self.ops[eng].append((fn, waits, (self.esem[eng], 1)))
        self._commit(tok, reads, writes)
        return tok

    def dma(self, q, fn, reads=(), writes=(), owner=None, is_out=False, dwrites=()):
        if owner is None:
            owner = writes[0] if writes else (dwrites[0] if dwrites else reads[0])
        if owner.dsem is None:
            owner.dsem = self._mk_sem("d_" + owner.name)
        waits = self._waits(q, reads, writes, dwrites)
        owner.dcnt += 16
        tok = (owner.dsem, owner.dcnt)
        self.ops[q].append((fn, waits, (owner.dsem, 16)))
        self._commit(tok, reads, writes, dwrites)
        if is_out:
            self.out_tokens.append(tok)
        return tok

    def emit(self):
        nc = self.nc
        ws = []
        for (s, v) in self.out_tokens:
            k = id(s)
            if self.seen["sp"].get(k, 0) >= v:
                continue
            self.seen["sp"][k] = v
            ws.append((s, v))
        if ws:
            self.ops["sp"].append((None, ws, None))
        ops = self.ops

        def replay(e, lst):
            for (fn, waits, inc) in lst:
                for (s, v) in waits:
                    e.wait_ge(s, v)
                if fn is not None:
                    ins = fn(e)
                    ins.then_inc(inc[0], inc[1])

        with nc.Block() as block:
            @block.tensor
            def _(e):
                replay(e, ops["pe"])

            @block.scalar
            def _(e):
                replay(e, ops["act"])

            @block.vector
            def _(e):
                replay(e, ops["dve"])

            @block.gpsimd
            def _(e):
                replay(e, ops["pool"])

            @block.sync
            def _(e):
                replay(e, ops["sp"])


class Builder:
    def __init__(self, n_layers, fused, dbg=False):
        self.fused = fused
        self.n_layers = n_layers
        self.dbg = dbg
        self.nc = bass.Bass("TRN2", target_bir_lowering=False)
        self.st = ExitStack()
        self.P = Prog(self.nc, self.st)
        self.sb_bytes = 0
        self._declare_io()
        self._alloc()

    def sb(self, name, shape, dt):
        n = 1
        for s in shape[1:]:
            n *= s
        self.sb_bytes += n * (2 if dt == BF16 else 4)
        return self.st.enter_context(self.nc.sbuf_tensor(name, list(shape), dt))

    def din(self, name, shape, dt=F32):
        return self.nc.dram_tensor(name, list(shape), dt, kind="ExternalInput").ap()

    def _declare_io(self):
        nc = self.nc
        L = self.n_layers
        self.x_in = self.din("xsrc", [S, D])
        self.W = []
        for l in range(L):
            w = dict(
                wq=self.din(f"wq{l}", [D, D]), wkv=self.din(f"wkv{l}", [D, 512]),
                wout=self.din(f"wout{l}", [D, D]), wg=self.din(f"wg{l}", [D, DFF]),
                wu=self.din(f"wu{l}", [D, DFF]), wd=self.din(f"wd{l}", [DFF, D]),
                nrm=self.din(f"nrm{l}", [1, 128]), gout=self.din(f"gout{l}", [128, 8]),
                lnp=self.din(f"lnp{l}", [128, 32]), cvp=self.din(f"cvp{l}", [128, 88]),
                sink=self.din(f"sink{l}", [1, 8]),
            )
            w["wq_s"] = nc.dram_tensor(f"wq_s{l}", [128, 8, D], BF16)
            w["wkv_s"] = nc.dram_tensor(f"wkv_s{l}", [128, 8, 512], BF16)
            w["wout_s"] = nc.dram_tensor(f"wout_s{l}", [8, 128, 8, 128], BF16)
            w["wgu_s"] = nc.dram_tensor(f"wgu_s{l}", [NFC, 128, 8, 256], BF16)
            w["wd_s"] = nc.dram_tensor(f"wd_s{l}", [NFC, 128, D], BF16)
            for k in ("wq_s", "wkv_s", "wout_s", "wgu_s", "wd_s"):
                w["B_" + k] = Buf(f"{k}{l}")
            self.W.append(w)
        self.biasT_in = self.din("biasT", [128, 3072])
        self.cs_in = self.din("cs", [128, NT, 64])
        self.jm_in = self.din("jm", [128, 4])
        self.ident_in = self.din("ident", [128, 128])
        self.out = nc.dram_tensor("out", [S // 2, D], F32, kind="ExternalOutput").ap()
        if self.fused:
            self.xmid = nc.dram_tensor("xmid", [S, D], BF16)
            self.B_xmid = Buf("xmid")
        self.B_out = Buf("out")
        self.dbg_out = {}

    def _alloc(self):
        nc, sb = self.nc, self.sb
        self.KAT = sb("KAT", [128, S], BF16)
        self.VA = sb("VA", [128, NT, 2, 128], BF16)
        self.KBT = sb("KBT", [128, 36 * 128], BF16)
        self.VB = sb("VB", [128, 36, 2, 65], BF16)
        self.B_KA = [Buf(f"KA{t}") for t in range(NT)]
        self.B_KB = [Buf(f"KB{t}") for t in range(NT)]
        self.B_vones = Buf("vones")
        self.ident_f = sb("ident_f", [128, 128], F32); self.B_idf = Buf("ident_f")
        self.ident_b = sb("ident_b", [128, 128], BF16); self.B_idb = Buf("ident_b")
        self.ones_b = sb("ones_b", [128, 128], BF16); self.B_ones = Buf("ones_b")
        self.zeros = sb("zeros", [128, 128], F32); self.B_zeros = Buf("zeros")
        self.onesf = sb("onesf", [128, 64], F32); self.B_onesf = Buf("onesf")
        self.epsr = sb("epsr", [128, 4], F32); self.B_eps = Buf("epsr")
        self.biasT = sb("biasT_sb", [128, 3, 2, 512], BF16); self.B_biasT = Buf("biasT")
        self.jm = sb("jm_sb", [128, 4], F32); self.B_jm = Buf("jm")
        self.nrm = sb("nrm_sb", [128, 128], F32); self.B_nrm = Buf("nrm")
        self.gout = [sb(f"gout_sb{l}", [128, 8], F32) for l in range(self.n_layers)]
        self.B_gout = [Buf(f"gout{l}") for l in range(self.n_layers)]
        self.lnp = sb("lnp_sb", [128, 32], F32); self.B_lnp = Buf("lnp")
        self.cvp = sb("cvp_sb", [128, NFC, 4], F32); self.B_cvp = Buf("cvp")
        self.cvm = sb("cvm_sb", [128, NFC, 2], F32); self.B_cvm = Buf("cvm")
        self.esink = sb("esink", [128, 8], F32); self.B_esink = Buf("esink")
        self.esrow = sb("esrow", [128, 2, 512], F32); self.B_esrow = Buf("esrow")
        self.cst = [sb(f"cst{i}", [128, 64], F32) for i in range(2)]; self.B_cst = [Buf(f"cst{i}") for i in range(2)]
        self.xt = [sb(f"xt{i}", [128, D], BF16) for i in range(2)]; self.B_xt = [Buf(f"xt{i}") for i in range(2)]
        self.xTb = sb("xTb", [128, 8, 512], BF16); self.B_xTb = Buf("xTb")
        self.xTt = [sb(f"xTt{i}", [128, 8, 128], BF16) for i in range(2)]; self.B_xTt = [Buf(f"xTt{i}") for i in range(2)]
        self.wq = sb("wq_sb", [128, 8, 512], BF16); self.B_wq = Buf("wq")
        self.wkv = self.wq; self.B_wkv = self.B_wq
        self.fscr = sb("fscr", [128, 8], F32)
        self.t_sq = sb("t_sq", [128, 512], F32); self.B_tsq = Buf("t_sq")
        self.t_qn = sb("t_qn", [128, 512], F32); self.B_tqn = Buf("t_qn")
        self.t_ab = sb("t_ab", [128, 2, 256], F32); self.B_tab = Buf("t_ab")
        self.t_m = sb("t_m", [128, 4, 256], F32); self.B_tm = Buf("t_m")
        self.t_ss = sb("t_ss", [128, 16], F32); self.B_tss = Buf("t_ss")
        self.t_rs = sb("t_rs", [128, 16], F32); self.B_trs = Buf("t_rs")
        self.tq = dict(sq=self.t_sq, qn=self.t_qn, ab=self.t_ab, m=self.t_m, ss=self.t_ss, rs=self.t_rs,
                       B=[self.B_tsq, self.B_tqn, self.B_tab, self.B_tm, self.B_tss, self.B_trs])
        self.tk = []
        for i in range(2):
            self.tk.append(dict(sq=sb(f"k_sq{i}", [128, 128], F32), qn=sb(f"k_qn{i}", [128, 128], F32),
                                ab=sb(f"k_ab{i}", [128, 2, 64], F32), m=sb(f"k_m{i}", [128, 4, 64], F32),
                                ss=sb(f"k_ss{i}", [128, 4], F32), rs=sb(f"k_rs{i}", [128, 4], F32),
                                B=[Buf(f"k_t{i}_{j}") for j in range(6)]))
        self.qbf = [sb(f"qbf{i}", [128, 512], BF16) for i in range(2)]; self.B_qbf = [Buf(f"qbf{i}") for i in range(2)]
        self.kvb = [sb(f"kvb{i}", [128, 256], BF16) for i in range(2)]; self.B_kvb = [Buf(f"kvb{i}") for i in range(2)]
        self.B_kvb2 = [Buf(f"kvbB{i}") for i in range(2)]
        A1 = sb("A1", [128, 6144], BF16)
        self.QT = A1[:, 0:4096].rearrange("p (a n) -> p a n", a=8); self.B_QT = Buf("QT")
        self.PT = [A1[:, 4096 + i * 1024:5120 + i * 1024].rearrange("p (a n) -> p a n", a=2) for i in range(2)]
        self.B_PT = [Buf(f"PT{i}") for i in range(2)]
        self.hT = A1[:, 0:5632].rearrange("p (a n) -> p a n", a=11); self.B_hT = [Buf(f"hT{i}") for i in range(11)]
        A2 = sb("A2", [128, 10240], BF16)
        f32v = lambda a, b: A2[:, a:b].bitcast(F32)
        self.yT = A2[:, 0:4096].rearrange("p (a n) -> p a n", a=8); self.B_yT = Buf("yT")
        self.stt = [f32v(4096 + i * 2048, 6144 + i * 2048).rearrange("p (a n) -> p a n", a=2) for i in range(2)]
        self.B_stt = [Buf(f"stt{i}") for i in range(2)]
        self.dsb = [f32v(8192 + i * 1024, 9216 + i * 1024) for i in range(2)]; self.B_dsb = [Buf(f"dsb{i}") for i in range(2)]
        self.rr = self.dsb; self.B_rr = self.B_dsb
        self.wgu = [A2[:, i * 2048:(i + 1) * 2048].rearrange("p (a n) -> p a n", a=8) for i in range(3)]
        self.B_wgu = [Buf(f"wgu{i}") for i in range(3)]
        self.gcs = [f32v(6144 + i * 1024, 7168 + i * 1024) for i in range(2)]; self.B_gcs = [Buf(f"gcs{i}") for i in range(2)]
        self.tgs = [f32v(8192 + i * 1024, 9216 + i * 1024) for i in range(2)]; self.B_tgs = [Buf(f"tgs{i}") for i in range(2)]
        self.y2 = self.gcs; self.B_y2 = self.B_gcs
        self.ot = [t.rearrange("p (a n) -> p a n", a=4) for t in self.tgs]; self.B_ot = self.B_tgs
        self.att_bufs = [self.B_QT] + self.B_PT + [self.B_yT] + self.B_stt + self.B_dsb
        self.ffn_bufs = self.B_hT + self.B_wgu + self.B_gcs + self.B_tgs
        self.wo = [sb(f"wo{i}", [128, 8, 128], BF16) for i in range(2)]; self.B_wo = [Buf(f"wo{i}") for i in range(2)]
        self.z = sb("z", [128, 8, 512], F32); self.B_z = [Buf(f"z{m}") for m in range(8)]
        self.zb = [sb(f"zb{i}", [128, 512], BF16) for i in range(2)]; self.B_zb = [Buf(f"zb{i}") for i in range(2)]
        self.zq = [sb(f"zq{i}", [128, 512], BF16) for i in range(2)]; self.B_zq = [Buf(f"zq{i}") for i in range(2)]
        self.sqy = self.zq; self.B_sqy = self.B_zq
        self.mean = self.t_ab[:].rearrange("p a n -> p (a n)"); self.B_mean = self.B_tab
        self.rstd = self.t_m[:, 0:2, :].rearrange("p a n -> p (a n)"); self.B_rstd = self.B_tm
        self.tmpa = [self.t_sq, self.t_qn]; self.B_tmpa = [self.B_tsq, self.B_tqn]
        self.XB = [sb(f"XB{i}", [128, 8, 512], BF16) for i in range(2)]; self.B_XB = [Buf(f"XB{i}") for i in range(2)]
        self.XH = sb("XH", [128, 8, 2], BF16); self.B_XH = Buf("XH")
        self.LC = sb("LC", [128, 8, 8], BF16); self.B_LC = [Buf(f"LC{k}") for k in range(8)]
        self.HC = [sb(f"HC{i}", [128, 8, 2], BF16) for i in range(2)]; self.B_HC = [Buf(f"HC{i}") for i in range(2)]
        self.ed = [sb(f"ed{i}", [128, 2], F32) for i in range(3)]; self.B_ed = [Buf(f"ed{i}") for i in range(3)]
        self.wdp = [sb(f"wdp{i}", [128, 384], BF16) for i in range(8)]; self.B_wdp = [Buf(f"wdp{i}") for i in range(8)]
        self.cvt = [self.z[:, 2 * i:2 * i + 2, :].rearrange("p a n -> p (a n)") for i in range(2)]
        self.B_cvt = [[self.B_z[2 * i], self.B_z[2 * i + 1]] for i in range(2)]
        self.cvo = [self.z[:, 4 + i, :].bitcast(BF16) for i in range(2)]
        self.B_cvo = [[self.B_z[4 + i]] for i in range(2)]
        self.ps = self.st.enter_context(nc.psum_tensor("ps", [128, 8, 512], F32))
        self.B_ps = [Buf(f"ps{i}") for i in range(8)]
        self.B_gh = [Buf(f"gh{i}") for i in range(3)]

    def bslot(self, tk):
        sl = (tk - (self.own_off - 2)) % NT
        return sl if sl < 36 else None

    def fence(self, bufs):
        self.P.op("pool", lambda e: e.memset(self.fscr[:], 0.0), writes=list(bufs))

    def psb(self, b):
        return self.ps[:, b, :].bitcast(BF16).rearrange("p (a n) -> p a n", a=8)

    def setup_consts(self):
        P = self.P
        P.dma("sp", lambda e: e.dma_start(out=self.ident_f[:], in_=self.ident_in), writes=[self.B_idf])
        P.op("act", lambda e: e.activation(out=self.ident_b[:], in_=self.ident_f[:], func=AF.Copy),
             reads=[self.B_idf], writes=[self.B_idb])
        P.op("pool", lambda e: e.memset(self.ones_b[:], 1.0), writes=[self.B_ones])
        P.op("pool", lambda e: e.memset(self.zeros[:], 0.0), writes=[self.B_zeros])
        P.op("pool", lambda e: e.memset(self.onesf[:], 1.0), writes=[self.B_onesf])
        P.op("pool", lambda e: e.memset(self.epsr[:, 0:1], RMS_EPS), writes=[self.B_eps])
        P.op("pool", lambda e: e.memset(self.epsr[:, 1:2], math.log(0.125)), writes=[self.B_eps])
        P.op("pool", lambda e: e.memset(self.epsr[:, 2:3], 0.0), writes=[self.B_eps])
        P.op("pool", lambda e: e.memset(self.epsr[:, 3:4], LN_EPS), writes=[self.B_eps])
        P.op("pool", lambda e: e.memset(self.VA[:, :, :, 64:128], 1.0), writes=[self.B_vones])
        P.op("pool", lambda e: e.memset(self.VB[:, :, :, 64:65], 1.0), writes=[self.B_vones])
        P.dma("pool", lambda e: e.dma_start(out=self.biasT[:].rearrange("p a b n -> p (a b n)"), in_=self.biasT_in),
              writes=[self.B_biasT])
        P.dma("sp", lambda e: e.dma_start(out=self.jm[:], in_=self.jm_in), writes=[self.B_jm])

    def setup_layer(self, l, own_off):
        P = self.P
        w = self.W[l]
        P.dma("sp", lambda e: e.dma_start(out=self.nrm[:], in_=w["nrm"].partition_broadcast(128).rearrange("p a n -> p (a n)")), writes=[self.B_nrm])
        P.dma("sp", lambda e: e.dma_start(out=self.lnp[:], in_=w["lnp"]), writes=[self.B_lnp])
        P.dma("sp", lambda e: e.dma_start(out=self.cvp[:].rearrange("p c k -> p (c k)"), in_=w["cvp"]), writes=[self.B_cvp])
        P.dma("sp", lambda e: e.dma_start(out=self.esink[:], in_=w["sink"].partition_broadcast(128).rearrange("p a n -> p (a n)")), writes=[self.B_esink])
        P.op("act", lambda e: e.activation(out=self.esink[:], in_=self.esink[:], func=AF.Exp),
             reads=[self.B_esink], writes=[self.B_esink])
        for kvh in range(2):
            for c in range(4):
                h = kvh * 4 + c
                P.op("dve", lambda e, kvh=kvh, c=c, h=h: e.tensor_scalar(
                    self.esrow[:, kvh, c * 128:(c + 1) * 128], self.zeros[:, :],
                    self.esink[:, h:h + 1], None, ALU.add),
                    reads=[self.B_esink, self.B_zeros], writes=[self.B_esrow])
        jl = 2 if own_off == 0 else 3
        jr = 3 if own_off == 0 else 2
        P.op("dve", lambda e: e.tensor_scalar(self.cvm[:, :, 0], self.cvp[:, :, 0], self.jm[:, jl:jl + 1], None, ALU.mult),
             reads=[self.B_cvp, self.B_jm], writes=[self.B_cvm])
        P.op("dve", lambda e: e.tensor_scalar(self.cvm[:, :, 1], self.cvp[:, :, 2], self.jm[:, jr:jr + 1], None, ALU.mult),
             reads=[self.B_cvp, self.B_jm], writes=[self.B_cvm])

    def convert_weights(self, l):
        P = self.P
        w = self.W[l]
        steps = []

        def cast_dma(dst_ap, src_ap, B):
            P.dma("pool", lambda e: e.dma_start(out=dst_ap, in_=src_ap), dwrites=[B])

        for kc in range(8):
            steps.append(lambda kc=kc: cast_dma(w["wkv_s"].ap()[:, kc, :], w["wkv"][kc * 128:(kc + 1) * 128, :], w["B_wkv_s"]))
        for kc in range(8):
            steps.append(lambda kc=kc: cast_dma(w["wq_s"].ap()[:, kc, :], w["wq"][kc * 128:(kc + 1) * 128, :], w["B_wq_s"]))

        def wout_step(c):
            i = c % 2
            if c == 0:
                P.dma("sp", lambda e: e.dma_start(out=self.gout[l][:], in_=w["gout"]), writes=[self.B_gout[l]])
            P.dma("sp", lambda e: e.dma_start(out=self.cvt[i], in_=w["wout"][c * 128:(c + 1) * 128, :]),
                  writes=self.B_cvt[i])
            P.op("dve", lambda e: e.tensor_scalar(self.cvo[i], self.cvt[i], self.gout[l][:, c:c + 1], None, ALU.mult),
                 reads=self.B_cvt[i] + [self.B_gout[l]], writes=self.B_cvo[i])
            P.dma("sp", lambda e: e.dma_start(out=w["wout_s"].ap()[:, :, c, :].rearrange("m p n -> p m n"),
                                              in_=self.cvo[i].rearrange("p (m n) -> p m n", m=8)),
                  reads=self.B_cvo[i], dwrites=[w["B_wout_s"]], owner=w["B_wout_s"])
        for c in range(8):
            steps.append(lambda c=c: wout_step(c))
        for c in range(NFC):
            def gu(c=c):
                cast_dma(w["wgu_s"].ap()[c, :, :, 0:128],
                         w["wg"][:, c * 128:(c + 1) * 128].rearrange("(kc p) n -> p kc n", p=128), w["B_wgu_s"])
                cast_dma(w["wgu_s"].ap()[c, :, :, 128:256],
                         w["wu"][:, c * 128:(c + 1) * 128].rearrange("(kc p) n -> p kc n", p=128), w["B_wgu_s"])
            steps.append(gu)
        for c in range(NFC):
            steps.append(lambda c=c: cast_dma(w["wd_s"].ap()[c], w["wd"][c * 128:(c + 1) * 128, :], w["B_wd_s"]))
        return steps

    def load_xT(self, src, src_is_f32, B_src, t, dst_ap, B_dst, slot):
        P = self.P
        xt, B_xt = self.xt[slot], self.B_xt[slot]
        rd = [B_src] if B_src is not None else []
        if src_is_f32:
            P.dma("pool", lambda e: e.dma_start(out=xt[:], in_=src[t * 128:(t + 1) * 128, :]), reads=rd, writes=[B_xt])
        else:
            P.dma("sp", lambda e: e.dma_start(out=xt[:], in_=src[t * 128:(t + 1) * 128, :]), reads=rd, writes=[B_xt])
        bank = 6 + slot
        pv = self.psb(bank)

        def tr(e):
            for c in range(8):
                ins = e.transpose(pv[:, c, :], xt[:, c * 128:(c + 1) * 128], self.ident_b[:])
            return ins
        P.op("pe", tr, reads=[B_xt, self.B_idb], writes=[self.B_ps[bank]])
        P.op("act", lambda e: e.activation(out=dst_ap, in_=pv, func=AF.Copy), reads=[self.B_ps[bank]], writes=[B_dst])

    def norm_rope(self, src, B_src, nh, goff, cs, B_cs, scale, dst, B_dst, T=None):
        P = self.P
        W = nh * 64
        if T is None:
            T = self.tq
        B_tsq, B_tqn, B_tab, B_tm, B_tss, B_trs = T["B"]
        sq = T["sq"][:, 0:W]
        P.op("act", lambda e: e.activation(out=sq, in_=src, func=AF.Square), reads=[B_src], writes=[B_tsq])
        ss = T["ss"][:, 0:nh]
        P.op("dve", lambda e: e.tensor_reduce(out=ss, in_=sq.rearrange("p (h d) -> p h d", h=nh), axis=AX.X, op=ALU.add),
             reads=[B_tsq], writes=[B_tss])
        P.op("act", lambda e: e.activation(out=ss, in_=ss, func=AF.Ln, scale=1.0 / 64.0, bias=self.epsr[:, 0:1]),
             reads=[B_tss, self.B_eps], writes=[B_tss])
        rs = T["rs"][:, 0:nh]
        bcol = 1 if scale != 1.0 else 2
        P.op("act", lambda e: e.activation(out=rs, in_=ss, func=AF.Exp, scale=-0.5, bias=self.epsr[:, bcol:bcol + 1]),
             reads=[B_tss, self.B_eps], writes=[B_trs])
        qn = T["qn"][:, 0:W].rearrange("p (h d) -> p h d", h=nh)
        P.op("dve", lambda e: e.tensor_tensor(qn, src.rearrange("p (h d) -> p h d", h=nh),
                                              rs.unsqueeze(2).to_broadcast([128, nh, 64]), ALU.mult),
             reads=[B_src, B_trs], writes=[B_tqn])
        x0 = qn[:, :, 0::2]
        x1 = qn[:, :, 1::2]
        ge = self.nrm[:, goff:goff + 32].unsqueeze(1).to_broadcast([128, nh, 32])
        go = self.nrm[:, goff + 32:goff + 64].unsqueeze(1).to_broadcast([128, nh, 32])
        cosb = cs[:, 0:32].unsqueeze(1).to_broadcast([128, nh, 32])
        sinb = cs[:, 32:64].unsqueeze(1).to_broadcast([128, nh, 32])
        a = T["ab"][:, 0, 0:nh * 32].rearrange("p (h d) -> p h d", h=nh)
        b = T["ab"][:, 1, 0:nh * 32].rearrange("p (h d) -> p h d", h=nh)
        P.op("dve", lambda e: e.tensor_tensor(a, x0, ge, ALU.mult), reads=[B_tqn, self.B_nrm], writes=[B_tab])
        P.op("dve", lambda e: e.tensor_tensor(b, x1, go, ALU.mult), reads=[B_tqn, self.B_nrm], writes=[B_tab])
        m = [T["m"][:, i, 0:nh * 32].rearrange("p (h d) -> p h d", h=nh) for i in range(4)]
        P.op("dve", lambda e: e.tensor_tensor(m[0], a, cosb, ALU.mult), reads=[B_tab, B_cs], writes=[B_tm])
        P.op("dve", lambda e: e.tensor_tensor(m[1], b, sinb, ALU.mult), reads=[B_tab, B_cs], writes=[B_tm])
        P.op("dve", lambda e: e.tensor_tensor(m[2], a, sinb, ALU.mult), reads=[B_tab, B_cs], writes=[B_tm])
        P.op("dve", lambda e: e.tensor_tensor(m[3], b, cosb, ALU.mult), reads=[B_tab, B_cs], writes=[B_tm])
        d3 = dst.rearrange("p (h d) -> p h d", h=nh)
        P.op("dve", lambda e: e.tensor_tensor(d3[:, :, 0:32], m[0], m[1], ALU.subtract), reads=[B_tm], writes=[B_dst])
        P.op("dve", lambda e: e.tensor_tensor(d3[:, :, 32:64], m[2], m[3], ALU.add), reads=[B_tm], writes=[B_dst])

    def load_cs(self, t):
        i = t % 2
        self.P.dma("sp", lambda e: e.dma_start(out=self.cst[i][:], in_=self.cs_in[:, t, :]), writes=[self.B_cst[i]])
        return self.cst[i], self.B_cst[i]

    def kv_phase(self, l, src, src_is_f32, B_src, pending):
        P = self.P
        w = self.W[l]
        P.dma("sp", lambda e: e.dma_start(out=self.wkv[:], in_=w["wkv_s"].ap()), reads=[w["B_wkv_s"]], writes=[self.B_wkv])
        def stage1(t):
            slot = t % 2
            self.load_xT(src, src_is_f32, B_src, t, self.xTt[slot][:], self.B_xTt[slot], slot)
            for _ in range(3):
                if pending:
                    pending.pop(0)()
            xT = self.xTt[slot]
            bk = 4 + slot
            pk = self.ps[:, bk, :]

            def mm(e, xT=xT, pk=pk):
                for c in range(8):
                    ins = e.matmul(pk, lhsT=xT[:, c, :], rhs=self.wkv[:, c, :], start=(c == 0), stop=(c == 7))
                return ins
            P.op("pe", mm, reads=[self.B_xTt[slot], self.B_wkv], writes=[self.B_ps[bk]])

        def stage2a(t):
            slot = t % 2
            bk = 4 + slot
            pk = self.ps[:, bk, :]
            kvb, B_kvb, B_kvb2 = self.kvb[slot], self.B_kvb[slot], self.B_kvb2[slot]
            P.op("act", lambda e: e.activation(out=kvb[:, 128:256], in_=pk[:, 256:384], func=AF.Copy),
                 reads=[self.B_ps[bk]], writes=[B_kvb2])
            P.op("act", lambda e: e.activation(out=self.VA[:, t, :, 0:64],
                                               in_=pk[:, 128:256].rearrange("p (h d) -> p h d", h=2), func=AF.Copy),
                 reads=[self.B_ps[bk], self.B_vones], writes=[self.B_KA[t]])
            sl = self.bslot(t)
            if sl is not None:
                P.op("act", lambda e: e.activation(out=self.VB[:, sl, :, 0:64],
                                                   in_=pk[:, 384:512].rearrange("p (h d) -> p h d", h=2), func=AF.Copy),
                     reads=[self.B_ps[bk], self.B_vones], writes=[self.B_KB[t]])
            cs, B_cs = self.load_cs(t)
            self.norm_rope(pk[:, 0:128], self.B_ps[bk], 2, 64, cs, B_cs, 1.0, kvb[:, 0:128], B_kvb, T=self.tk[slot])

        def stage2b(t):
            slot = t % 2
            kvb, B_kvb, B_kvb2 = self.kvb[slot], self.B_kvb[slot], self.B_kvb2[slot]
            sl = self.bslot(t)
            bt = 2 + slot
            pt = self.psb(bt)

            def trk(e):
                e.transpose(pt[:, 0, :], kvb[:, 0:128], self.ident_b[:])
                return e.transpose(pt[:, 1, :], kvb[:, 128:256], self.ident_b[:])
            P.op("pe", trk, reads=[B_kvb, B_kvb2, self.B_idb], writes=[self.B_ps[bt]])
            P.op("dve", lambda e: e.tensor_copy(self.KAT[:, t * 128:(t + 1) * 128], pt[:, 0, :]),
                 reads=[self.B_ps[bt]], writes=[self.B_KA[t]])
            if sl is not None:
                P.op("dve", lambda e: e.tensor_copy(self.KBT[:, sl * 128:(sl + 1) * 128], pt[:, 1, :]),
                     reads=[self.B_ps[bt]], writes=[self.B_KB[t]])

        stage1(0)
        for t in range(NT):
            stage2a(t)
            if t + 1 < NT:
                stage1(t + 1)
            stage2b(t)

    def att_block(self, l, src, src_is_f32, B_src, tiles, col0, ncol, x1_dst, B_x1):
        P = self.P
        w = self.W[l]
        nt = len(tiles)
        ntok = nt * 128
        for i, t in enumerate(tiles):
            self.load_xT(src, src_is_f32, B_src, t, self.xTb[:, :, i * 128:(i + 1) * 128], self.B_xTb, i % 2)
        for half in range(2):
            P.dma("sp", lambda e, half=half: e.dma_start(out=self.wq[:], in_=w["wq_s"].ap()[:, :, half * 512:(half + 1) * 512]),
                  reads=[w["B_wq_s"]], writes=[self.B_wq])
            for i, t in enumerate(tiles):
                bk = 4 + i
                pq = self.ps[:, bk, :]

                def mm(e, i=i, pq=pq):
                    for c in range(8):
                        ins = e.matmul(pq, lhsT=self.xTb[:, c, i * 128:(i + 1) * 128], rhs=self.wq[:, c, :],
                                       start=(c == 0), stop=(c == 7))
                    return ins
                P.op("pe", mm, reads=[self.B_xTb, self.B_wq], writes=[self.B_ps[bk]])
            for i, t in enumerate(tiles):
                bk = 4 + i
                pq = self.ps[:, bk, :]
                qb, B_qb = self.qbf[i % 2], self.B_qbf[i % 2]
                if half == 0:
                    cs, B_cs = self.load_cs(t)
                    self.norm_rope(pq, self.B_ps[bk], 8, 0, cs, B_cs, 0.125, qb[:, 0:512], B_qb)
                else:
                    P.op("act", lambda e, qb=qb, pq=pq: e.activation(out=qb[:, 0:512], in_=pq, func=AF.Copy),
                         reads=[self.B_ps[bk]], writes=[B_qb])
                bt = 2 + (i % 2)
                pt = self.psb(bt)

                def trq(e, qb=qb, pt=pt):
                    for c in range(4):
                        ins = e.transpose(pt[:, c, :], qb[:, c * 128:(c + 1) * 128], self.ident_b[:])
                    return ins
                P.op("pe", trq, reads=[B_qb, self.B_idb], writes=[self.B_ps[bt]])
                P.op("dve", lambda e, i=i, half=half, pt=pt: e.tensor_copy(
                    self.QT[:, half * 4:(half + 1) * 4, i * 128:(i + 1) * 128], pt[:, 0:4, :]),
                    reads=[self.B_ps[bt]], writes=[self.B_QT])
        for c in range(4):
            accb = (4, 5) if c % 2 == 0 else (6, 7)

            def qk(kt, c=c):
                sb0 = 2 * (kt % 2)

                def f(e):
                    e.matmul(self.ps[:, sb0, 0:ntok], lhsT=self.KAT[0:64, kt * 128:(kt + 1) * 128],
                             rhs=self.QT[0:64, c, 0:ntok], start=True, stop=True)
                    return e.matmul(self.ps[:, sb0 + 1, 0:ntok], lhsT=self.KAT[64:128, kt * 128:(kt + 1) * 128],
                                    rhs=self.QT[64:128, c, 0:ntok], start=True, stop=True)
                P.op("pe", f, reads=[self.B_KA[kt], self.B_QT], writes=[self.B_ps[sb0], self.B_ps[sb0 + 1]])

            def ex(kt):
                sb0 = 2 * (kt % 2)
                pt = self.PT[kt % 2]
                P.op("act", lambda e: e.activation(out=pt[:, :, 0:ntok], in_=self.ps[:, sb0:sb0 + 2, 0:ntok], func=AF.Exp),
                     reads=[self.B_ps[sb0], self.B_ps[sb0 + 1]], writes=[self.B_PT[kt % 2]])

            def pv(kt, accb=accb):
                pt = self.PT[kt % 2]

                def f(e):
                    e.matmul(self.ps[:, accb[0], 0:ntok], lhsT=self.VA[:, kt, 0, :], rhs=pt[:, 0, 0:ntok],
                             start=(kt == 0), stop=(kt == NT - 1))
                    return e.matmul(self.ps[:, accb[1], 0:ntok], lhsT=self.VA[:, kt, 1, :], rhs=pt[:, 1, 0:ntok],
                                    start=(kt == 0), stop=(kt == NT - 1))
                P.op("pe", f, reads=[self.B_KA[kt], self.B_PT[kt % 2]], writes=[self.B_ps[accb[0]], self.B_ps[accb[1]]])

            qk(0)
            qk(1)
            for kt in range(NT):
                ex(kt)
                pv(kt)
                if kt + 2 < NT:
                    qk(kt + 2)
            for kvh in range(2):
                self.finalize_head(accb[kvh], kvh, None, self.yT[kvh * 64:(kvh + 1) * 64, c, 0:ntok], ntok, kvh)
        def b_main(i, t):
            accb = (4, 5) if i % 2 == 0 else (6, 7)
            nbrs = [((t - 1) % NT, 0), (t, 1), ((t + 1) % NT, 2)]

            def qk(jj):
                tk, jidx = nbrs[jj]
                sb0 = 2 * (jj % 2)
                ks = self.bslot(tk)
                assert ks is not None

                def f(e):
                    e.matmul(self.ps[:, sb0, :], lhsT=self.KBT[0:64, ks * 128:(ks + 1) * 128],
                             rhs=self.QT[0:64, 4:8, i * 128:(i + 1) * 128], start=True, stop=True)
                    return e.matmul(self.ps[:, sb0 + 1, :], lhsT=self.KBT[64:128, ks * 128:(ks + 1) * 128],
                                    rhs=self.QT[64:128, 4:8, i * 128:(i + 1) * 128], start=True, stop=True)
                P.op("pe", f, reads=[self.B_KB[tk], self.B_QT], writes=[self.B_ps[sb0], self.B_ps[sb0 + 1]])

            def chain(jj):
                tk, jidx = nbrs[jj]
                sb0 = 2 * (jj % 2)
                st = self.stt[jj % 2]
                B_st = self.B_stt[jj % 2]
                ks = self.bslot(tk)
                for kvh in range(2):
                    P.op("dve", lambda e, kvh=kvh: e.scalar_tensor_tensor(
                        out=st[:, kvh, :], in0=self.ps[:, sb0 + kvh, :], scalar=0.125, in1=self.biasT[:, jidx, kvh, :],
                        op0=ALU.mult, op1=ALU.add),
                        reads=[self.B_ps[sb0 + kvh], self.B_biasT], writes=[B_st])
                jcol = None
                if jidx == 0 and t == 0:
                    jcol = 0
                elif jidx == 0 and t == 32:
                    jcol = 1
                elif jidx == 2 and t == 63:
                    jcol = 0
                elif jidx == 2 and t == 31:
                    jcol = 1
                if jcol is not None:
                    P.op("dve", lambda e: e.tensor_scalar(
                        st[:].rearrange("p a n -> p (a n)"), st[:].rearrange("p a n -> p (a n)"),
                        self.jm[:, jcol:jcol + 1], None, ALU.add),
                        reads=[B_st, self.B_jm], writes=[B_st])
                pt = self.PT[jj % 2]
                P.op("act", lambda e: e.activation(out=pt[:], in_=st[:], func=AF.Exp),
                     reads=[B_st], writes=[self.B_PT[jj % 2]])

                def g(e):
                    e.matmul(self.ps[0:65, accb[0], :], lhsT=self.VB[:, ks, 0, :], rhs=pt[:, 0, :],
                             start=(jj == 0), stop=(jj == 2))
                    return e.matmul(self.ps[0:65, accb[1], :], lhsT=self.VB[:, ks, 1, :], rhs=pt[:, 1, :],
                                    start=(jj == 0), stop=(jj == 2))
                P.op("pe", g, reads=[self.B_KB[tk], self.B_PT[jj % 2]], writes=[self.B_ps[accb[0]], self.B_ps[accb[1]]])

            qk(0)
            qk(1)
            chain(0)
            qk(2)
            chain(1)
            chain(2)

        def b_fin(i):
            accb = (4, 5) if i % 2 == 0 else (6, 7)
            for kvh in range(2):
                self.finalize_head_b(accb[kvh], kvh, self.yT[kvh * 64:(kvh + 1) * 64, 4:8, i * 128:(i + 1) * 128])

        for i, t in enumerate(tiles):
            b_main(i, t)
            if i >= 1:
                b_fin(i - 1)
        b_fin(nt - 1)
        self.out_stage(l, col0, ncol)
        self.layer_norm(0, col0, ncol, lambda m: x1_dst[:, m, :], B_x1, BF16)

    def finalize_head(self, bank, kvh, sink_kvh, y_dst, n, slot):
        P = self.P
        dsb, B_dsb = self.dsb[slot], self.B_dsb[slot]
        if sink_kvh is not None:
            P.op("dve", lambda e: e.tensor_tensor(dsb[0:64, 0:n], self.ps[64:128, bank, 0:n], self.esrow[64:128, sink_kvh, 0:n], ALU.add),
                 reads=[self.B_ps[bank], self.B_esrow], writes=[B_dsb])
        else:
            P.op("act", lambda e: e.activation(out=dsb[0:64, 0:n], in_=self.ps[64:128, bank, 0:n], func=AF.Copy),
                 reads=[self.B_ps[bank]], writes=[B_dsb])
        if sink_kvh is not None:
            P.op("act", lambda e: e.activation(out=dsb[0:64, 0:n], in_=dsb[0:64, 0:n], func=AF.Ln), reads=[B_dsb], writes=[B_dsb])
            P.op("act", lambda e: e.activation(out=dsb[0:64, 0:n], in_=dsb[0:64, 0:n], func=AF.Exp, scale=-1.0),
                 reads=[B_dsb], writes=[B_dsb])
        else:
            P.op("dve", lambda e: e.reciprocal(dsb[0:64, 0:n], dsb[0:64, 0:n]), reads=[B_dsb], writes=[B_dsb])
        if len(y_dst.shape) == 3:
            in0 = self.ps[0:64, bank, 0:n].rearrange("p (a n) -> p a n", a=4)
            in1 = dsb[0:64, 0:n].rearrange("p (a n) -> p a n", a=4)
        else:
            in0 = self.ps[0:64, bank, 0:n]
            in1 = dsb[0:64, 0:n]
        P.op("dve", lambda e: e.tensor_tensor(y_dst, in0, in1, ALU.mult),
             reads=[self.B_ps[bank], B_dsb], writes=[self.B_yT])

    def finalize_head_b(self, bank, kvh, y_dst):
        P = self.P
        n = 512
        dsb, B_dsb = self.dsb[kvh], self.B_dsb[kvh]
        P.op("dve", lambda e: e.tensor_tensor(dsb[64:65, 0:n], self.ps[64:65, bank, 0:n], self.esrow[64:65, kvh, 0:n], ALU.add),
             reads=[self.B_ps[bank], self.B_esrow], writes=[B_dsb])
        P.op("act", lambda e: e.activation(out=dsb[64:65, 0:n], in_=dsb[64:65, 0:n], func=AF.Ln), reads=[B_dsb], writes=[B_dsb])
        P.op("act", lambda e: e.activation(out=dsb[64:65, 0:n], in_=dsb[64:65, 0:n], func=AF.Exp, scale=-1.0),
             reads=[B_dsb], writes=[B_dsb])
        bb = 2 + kvh
        P.op("pe", lambda e: e.matmul(self.ps[0:64, bb, 0:n], lhsT=self.onesf[64:65, 0:64], rhs=dsb[64:65, 0:n],
                                      start=True, stop=True),
             reads=[B_dsb, self.B_onesf], writes=[self.B_ps[bb]])
        P.op("act", lambda e: e.activation(out=dsb[0:64, 0:n], in_=self.ps[0:64, bb, 0:n], func=AF.Copy),
             reads=[self.B_ps[bb]], writes=[B_dsb])
        in0 = self.ps[0:64, bank, 0:n].rearrange("p (a n) -> p a n", a=4)
        in1 = dsb[0:64, 0:n].rearrange("p (a n) -> p a n", a=4)
        P.op("dve", lambda e: e.tensor_tensor(y_dst, in0, in1, ALU.mult),
             reads=[self.B_ps[bank], B_dsb], writes=[self.B_yT])

    def rsqrt_act(self, dst, src_ap, B_src, B_dst, scale, eps_col, ncol):
        P = self.P
        P.op("act", lambda e: e.activation(out=dst, in_=src_ap, func=AF.Ln, scale=scale, bias=self.epsr[:, eps_col:eps_col + 1]),
             reads=[B_src, self.B_eps], writes=[B_dst])
        P.op("act", lambda e: e.activation(out=dst, in_=dst, func=AF.Exp, scale=-0.5), reads=[B_dst], writes=[B_dst])

    def out_stage(self, l, col0, ncol):
        P = self.P
        w = self.W[l]
        cs = slice(col0, col0 + ncol)
        for g in range(2):
            bank = g
            for cc in range(4):
                c = g * 4 + cc
                sq, B_sq = self.sqy[cc % 2], self.B_sqy[cc % 2]
                P.op("act", lambda e, sq=sq, c=c: e.activation(out=sq[:, 0:ncol], in_=self.yT[:, c, cs], func=AF.Square),
                     reads=[self.B_yT], writes=[B_sq])
                P.op("pe", lambda e, sq=sq, cc=cc, bank=bank: e.matmul(self.ps[:, bank, 0:ncol], lhsT=self.ones_b[:], rhs=sq[:, 0:ncol],
                                                                    start=(cc == 0), stop=(cc == 3)),
                     reads=[B_sq, self.B_ones], writes=[self.B_ps[bank]])
            rr, B_rr = self.rr[g], self.B_rr[g]
            self.rsqrt_act(rr[:, 0:ncol], self.ps[:, bank, 0:ncol], self.B_ps[bank], B_rr, 1.0 / 512.0, 0, ncol)
            for cc in range(4):
                c = g * 4 + cc
                P.op("dve", lambda e, c=c, rr=rr: e.tensor_tensor(self.yT[:, c, cs], self.yT[:, c, cs], rr[:, 0:ncol], ALU.mult),
                     reads=[self.B_yT, B_rr], writes=[self.B_yT])
        for m in range(8):
            wo, B_wo = self.wo[m % 2], self.B_wo[m % 2]
            P.dma("sp", lambda e, wo=wo, m=m: e.dma_start(out=wo[:], in_=w["wout_s"].ap()[m]),
                  reads=[w["B_wout_s"]], writes=[B_wo])
            ba = 2 + (m % 2)

            def mm(e, wo=wo, ba=ba, m=m):
                for c in range(8):
                    ins = e.matmul(self.ps[:, ba, 0:ncol], lhsT=wo[:, c, :], rhs=self.yT[:, c, cs],
                                   start=(c == 0), stop=(c == 7))
                return ins
            P.op("pe", mm, reads=[B_wo, self.B_yT], writes=[self.B_ps[ba]])
            P.op("dve", lambda e, ba=ba, m=m: e.scalar_tensor_tensor(out=self.z[:, m, 0:ncol], in0=self.xTb[:, m, cs], scalar=ALPHA,
                                                                     in1=self.ps[:, ba, 0:ncol], op0=ALU.mult, op1=ALU.add),
                 reads=[self.B_xTb, self.B_ps[ba]], writes=[self.B_z[m]])

    def layer_norm(self, which, col0_unused, ncol, dst_fn, B_dst, out_dt):
        P = self.P
        gcol = 0 if which == 0 else 16
        bcol = gcol + 8
        for m in range(8):
            zb, B_zb = self.zb[m % 2], self.B_zb[m % 2]
            zq, B_zq = self.zq[m % 2], self.B_zq[m % 2]
            P.op("act", lambda e, zb=zb, m=m: e.activation(out=zb[:, 0:ncol], in_=self.z[:, m, 0:ncol], func=AF.Copy),
                 reads=[self.B_z[m]], writes=[B_zb])
            P.op("act", lambda e, zq=zq, m=m: e.activation(out=zq[:, 0:ncol], in_=self.z[:, m, 0:ncol], func=AF.Square),
                 reads=[self.B_z[m]], writes=[B_zq])
            P.op("pe", lambda e, zb=zb, m=m: e.matmul(self.ps[:, 0, 0:ncol], lhsT=self.ones_b[:], rhs=zb[:, 0:ncol],
                                                      start=(m == 0), stop=(m == 7)),
                 reads=[B_zb, self.B_ones], writes=[self.B_ps[0]])
            P.op("pe", lambda e, zq=zq, m=m: e.matmul(self.ps[:, 1, 0:ncol], lhsT=self.ones_b[:], rhs=zq[:, 0:ncol],
                                                      start=(m == 0), stop=(m == 7)),
                 reads=[B_zq, self.B_ones], writes=[self.B_ps[1]])
        mean, rstd = self.mean, self.rstd
        P.op("dve", lambda e: e.tensor_scalar(mean[:, 0:ncol], self.ps[:, 0, 0:ncol], 1.0 / D, None, ALU.mult),
             reads=[self.B_ps[0]], writes=[self.B_mean])
        P.op("dve", lambda e: e.tensor_tensor(rstd[:, 0:ncol], mean[:, 0:ncol], mean[:, 0:ncol], ALU.mult),
             reads=[self.B_mean], writes=[self.B_rstd])
        P.op("dve", lambda e: e.scalar_tensor_tensor(out=rstd[:, 0:ncol], in0=self.ps[:, 1, 0:ncol], scalar=1.0 / D,
                                                     in1=rstd[:, 0:ncol], op0=ALU.mult, op1=ALU.subtract),
             reads=[self.B_ps[1], self.B_rstd], writes=[self.B_rstd])
        self.rsqrt_act(rstd[:, 0:ncol], rstd[:, 0:ncol], self.B_rstd, self.B_rstd, 1.0, 3, ncol)
        for m in range(8):
            ta, B_ta = self.tmpa[m % 2], self.B_tmpa[m % 2]
            P.op("dve", lambda e, ta=ta, m=m: e.tensor_tensor(ta[:, 0:ncol], self.z[:, m, 0:ncol], mean[:, 0:ncol], ALU.subtract),
                 reads=[self.B_z[m], self.B_mean], writes=[B_ta])
            P.op("dve", lambda e, ta=ta: e.tensor_tensor(ta[:, 0:ncol], ta[:, 0:ncol], rstd[:, 0:ncol], ALU.mult),
                 reads=[B_ta, self.B_rstd], writes=[B_ta])
            dst = dst_fn(m)
            Bd = B_dst(m) if callable(B_dst) else B_dst
            P.op("act", lambda e, ta=ta, m=m, dst=dst: e.activation(out=dst, in_=ta[:, 0:ncol], func=AF.Identity,
                                                                    scale=self.lnp[:, gcol + m:gcol + m + 1],
                                                                    bias=self.lnp[:, bcol + m:bcol + m + 1]),
                 reads=[B_ta, self.B_lnp], writes=[Bd])

    def ffn_block(self, l, k, X, B_X, hl_ap, B_hl, hr_ap, B_hr, first, last, dst, dst_dt, B_dstbuf, row0):
        P = self.P
        w = self.W[l]
        HC, B_HC = self.HC[k % 2], self.B_HC[k % 2]
        P.op("dve", lambda e: e.tensor_copy(HC[:, :, 0:1], hl_ap), reads=[B_hl], writes=[B_HC])
        P.op("dve", lambda e: e.tensor_copy(HC[:, :, 1:2], hr_ap), reads=[B_hr], writes=[B_HC])
        for m in range(8):
            P.op("act", lambda e, m=m: e.activation(out=self.z[:, m, :], in_=X[:, m, :], func=AF.Copy, scale=ALPHA),
                 reads=[B_X], writes=[self.B_z[m]])
        for hh in range(2):
            for cc in range(11):
                c = hh * 11 + cc
                wg, B_wg = self.wgu[c % 3], self.B_wgu[c % 3]
                P.dma("sp", lambda e, wg=wg, c=c: e.dma_start(out=wg[:], in_=w["wgu_s"].ap()[c]),
                      reads=[w["B_wgu_s"]], writes=[B_wg])
                gb = (0, 1, 4)[c % 3]
                ub = (2, 3, 5)[c % 3]
                hcol = (c % 3) * 2

                def mm(e, wg=wg, gb=gb, ub=ub, hcol=hcol):
                    for kc in range(8):
                        e.matmul(self.ps[:, gb, :], lhsT=wg[:, kc, 0:128], rhs=X[:, kc, :], start=(kc == 0), stop=(kc == 7))
                    for kc in range(8):
                        e.matmul(self.ps[:, 7, hcol:hcol + 2], lhsT=wg[:, kc, 0:128], rhs=HC[:, kc, :], start=(kc == 0), stop=(kc == 7))
                    for kc in range(8):
                        ins = e.matmul(self.ps[:, ub, :], lhsT=wg[:, kc, 128:256], rhs=X[:, kc, :], start=(kc == 0), stop=(kc == 7))
                    return ins
                P.op("pe", mm, reads=[B_wg, B_X, B_HC], writes=[self.B_ps[gb], self.B_ps[ub], self.B_ps[7]])
                gc, B_gc = self.gcs[c % 2], self.B_gcs[c % 2]
                G = self.ps[:, gb, :]
                Gh = self.ps[:, 7, hcol:hcol + 2]
                w0 = self.cvp[:, c, 0:1]
                w1 = self.cvp[:, c, 1:2]
                w2 = self.cvp[:, c, 2:3]
                cb = self.cvp[:, c, 3:4]
                w0e = self.cvm[:, c, 0:1] if first else w0
                w2e = self.cvm[:, c, 1:2] if last else w2
                ed, B_ed = self.ed[c % 3], self.B_ed[c % 3]
                P.op("act", lambda e, ed=ed, Gh=Gh: e.activation(out=ed[:], in_=Gh, func=AF.Copy),
                     reads=[self.B_ps[7]], writes=[B_ed])
                P.op("act", lambda e, gc=gc, G=G, w1=w1, cb=cb: e.activation(out=gc[:], in_=G, func=AF.Identity, scale=w1, bias=cb),
                     reads=[self.B_ps[gb], self.B_cvp], writes=[B_gc])
                P.op("dve", lambda e, gc=gc, G=G, w0=w0: e.scalar_tensor_tensor(out=gc[:, 1:512], in0=G[:, 0:511], scalar=w0,
                                                                                 in1=gc[:, 1:512], op0=ALU.mult, op1=ALU.add),
                     reads=[self.B_ps[gb], B_gc, self.B_cvp], writes=[B_gc])
                P.op("dve", lambda e, gc=gc, G=G, w2=w2: e.scalar_tensor_tensor(out=gc[:, 0:511], in0=G[:, 1:512], scalar=w2,
                                                                                 in1=gc[:, 0:511], op0=ALU.mult, op1=ALU.add),
                     reads=[self.B_ps[gb], B_gc, self.B_cvp], writes=[B_gc])
                P.op("dve", lambda e, gc=gc, ed=ed, w0e=w0e: e.scalar_tensor_tensor(out=gc[:, 0:1], in0=ed[:, 0:1], scalar=w0e,
                                                                                     in1=gc[:, 0:1], op0=ALU.mult, op1=ALU.add),
                     reads=[B_ed, B_gc, self.B_cvp, self.B_cvm], writes=[B_gc])
                P.op("dve", lambda e, gc=gc, ed=ed, w2e=w2e: e.scalar_tensor_tensor(out=gc[:, 511:512], in0=ed[:, 1:2], scalar=w2e,
                                                                                     in1=gc[:, 511:512], op0=ALU.mult, op1=ALU.add),
                     reads=[B_ed, B_gc, self.B_cvp, self.B_cvm], writes=[B_gc])
                if cc >= 1:
                    self._ffn_tail(c - 1, cc - 1)
            self._ffn_tail(hh * 11 + 10, 10)
            di = 0
            for grp in ((0, 1, 2), (3, 4, 5), (6, 7)):
                ng = len(grp)
                for cc in range(11):
                    c = hh * 11 + cc
                    wd, B_wd = self.wdp[di % 8], self.B_wdp[di % 8]
                    di += 1
                    P.dma("sp", lambda e, wd=wd, c=c, grp=grp, ng=ng: e.dma_start(
                        out=wd[:, 0:ng * 128], in_=w["wd_s"].ap()[c, :, grp[0] * 128:(grp[0] + ng) * 128]),
                        reads=[w["B_wd_s"]], writes=[B_wd])

                    def mm(e, wd=wd, cc=cc, ng=ng):
                        for mi in range(ng):
                            ins = e.matmul(self.ps[:, 4 + mi, :], lhsT=wd[:, mi * 128:(mi + 1) * 128], rhs=self.hT[:, cc, :],
                                           start=(cc == 0), stop=(cc == 10))
                        return ins
                    P.op("pe", mm, reads=[B_wd, self.B_hT[cc]], writes=[self.B_ps[4 + mi] for mi in range(ng)])
                for mi, m in enumerate(grp):
                    eng = "dve" if mi % 2 == 0 else "dve"
                    P.op(eng, lambda e, mi=mi, m=m: e.tensor_tensor(self.z[:, m, :], self.ps[:, 4 + mi, :], self.z[:, m, :], ALU.add),
                         reads=[self.B_ps[4 + mi], self.B_z[m]], writes=[self.B_z[m]])
        is_f32 = (dst_dt == F32)

        def emit_out(m, y2, B_y2):
            bank = 2 + (m % 2)
            if is_f32:
                pv = self.ps[:, bank, :].rearrange("p (a n) -> p a n", a=4)
                idn, B_idn = self.ident_f, self.B_idf
            else:
                pv = self.psb(bank)[:, 0:4, :]
                idn, B_idn = self.ident_b, self.B_idb

            def tr(e):
                for i in range(4):
                    ins = e.transpose(pv[:, i, :], y2[:, i * 128:(i + 1) * 128], idn[:])
                return ins
            P.op("pe", tr, reads=[B_y2, B_idn], writes=[self.B_ps[bank]])
            if is_f32:
                ot, B_ot = self.ot[m % 2], self.B_ot[m % 2]
                otv = ot[:]
            else:
                ot, B_ot = self.ot[m % 2], self.B_ot[m % 2]
                otv = ot[:].rearrange("p a n -> p (a n)").bitcast(BF16)[:, 0:512].rearrange("p (a n) -> p a n", a=4)
            P.op("act", lambda e: e.activation(out=otv, in_=pv, func=AF.Copy), reads=[self.B_ps[bank]], writes=[B_ot])
            dview = dst[row0:row0 + 512, m * 128:(m + 1) * 128].rearrange("(a p) n -> p a n", p=128)
            P.dma("sp", lambda e: e.dma_start(out=dview, in_=otv), reads=[B_ot], dwrites=[B_dstbuf], owner=B_ot,
                  is_out=True)

        self._ln_out_queue = []

        def dst_fn(m):
            y2 = self.y2[m % 2]
            if is_f32:
                return y2[:]
            return y2[:].bitcast(BF16)[:, 0:512]

        self.layer_norm_with_out(1, 512, dst_fn, lambda m: self.B_y2[m % 2], emit_out, is_f32)

    def _ffn_tail(self, c, cc):
        P = self.P
        ub = (2, 3, 5)[c % 3]
        gc, B_gc = self.gcs[c % 2], self.B_gcs[c % 2]
        tg, B_tg = self.tgs[c % 2], self.B_tgs[c % 2]
        P.op("act", lambda e: e.activation(out=tg[:], in_=gc[:], func=AF.Gelu_apprx_tanh), reads=[B_gc], writes=[B_tg])
        P.op("dve", lambda e: e.tensor_tensor(self.hT[:, cc, :], self.ps[:, ub, :], tg[:], ALU.mult),
             reads=[self.B_ps[ub], B_tg], writes=[self.B_hT[cc]])

    def layer_norm_with_out(self, which, ncol, dst_fn, B_dst_fn, emit_out, is_f32):
        P = self.P
        gcol = 0 if which == 0 else 16
        bcol = gcol + 8
        for m in range(8):
            zb, B_zb = self.zb[m % 2], self.B_zb[m % 2]
            zq, B_zq = self.zq[m % 2], self.B_zq[m % 2]
            P.op("act", lambda e, zb=zb, m=m: e.activation(out=zb[:, 0:ncol], in_=self.z[:, m, 0:ncol], func=AF.Copy),
                 reads=[self.B_z[m]], writes=[B_zb])
            P.op("act", lambda e, zq=zq, m=m: e.activation(out=zq[:, 0:ncol], in_=self.z[:, m, 0:ncol], func=AF.Square),
                 reads=[self.B_z[m]], writes=[B_zq])
            P.op("pe", lambda e, zb=zb, m=m: e.matmul(self.ps[:, 0, 0:ncol], lhsT=self.ones_b[:], rhs=zb[:, 0:ncol],
                                                      start=(m == 0), stop=(m == 7)),
                 reads=[B_zb, self.B_ones], writes=[self.B_ps[0]])
            P.op("pe", lambda e, zq=zq, m=m: e.matmul(self.ps[:, 1, 0:ncol], lhsT=self.ones_b[:], rhs=zq[:, 0:ncol],
                                                      start=(m == 0), stop=(m == 7)),
                 reads=[B_zq, self.B_ones], writes=[self.B_ps[1]])
        mean, rstd = self.mean, self.rstd
        P.op("dve", lambda e: e.tensor_scalar(mean[:, 0:ncol], self.ps[:, 0, 0:ncol], 1.0 / D, None, ALU.mult),
             reads=[self.B_ps[0]], writes=[self.B_mean])
        P.op("dve", lambda e: e.tensor_tensor(rstd[:, 0:ncol], mean[:, 0:ncol], mean[:, 0:ncol], ALU.mult),
             reads=[self.B_mean], writes=[self.B_rstd])
        P.op("dve", lambda e: e.scalar_tensor_tensor(out=rstd[:, 0:ncol], in0=self.ps[:, 1, 0:ncol], scalar=1.0 / D,
                                                     in1=rstd[:, 0:ncol], op0=ALU.mult, op1=ALU.subtract),
             reads=[self.B_ps[1], self.B_rstd], writes=[self.B_rstd])
        self.rsqrt_act(rstd[:, 0:ncol], rstd[:, 0:ncol], self.B_rstd, self.B_rstd, 1.0, 3, ncol)
        for m in range(8):
            ta, B_ta = self.tmpa[m % 2], self.B_tmpa[m % 2]
            P.op("dve", lambda e, ta=ta, m=m: e.tensor_tensor(ta[:, 0:ncol], self.z[:, m, 0:ncol], mean[:, 0:ncol], ALU.subtract),
                 reads=[self.B_z[m], self.B_mean], writes=[B_ta])
            P.op("dve", lambda e, ta=ta: e.tensor_tensor(ta[:, 0:ncol], ta[:, 0:ncol], rstd[:, 0:ncol], ALU.mult),
                 reads=[B_ta, self.B_rstd], writes=[B_ta])
            dst = dst_fn(m)
            Bd = B_dst_fn(m)
            P.op("act", lambda e, ta=ta, m=m, dst=dst: e.activation(out=dst, in_=ta[:, 0:ncol], func=AF.Identity,
                                                                    scale=self.lnp[:, gcol + m:gcol + m + 1],
                                                                    bias=self.lnp[:, bcol + m:bcol + m + 1]),
                 reads=[B_ta, self.B_lnp], writes=[Bd])
            emit_out(m, dst, Bd)

    def half_layer(self, l, own_off, src, src_is_f32, B_src, dst, dst_dt, B_dstbuf, pending):
        P = self.P
        self.own_off = own_off
        self.setup_layer(l, own_off)
        self.kv_phase(l, src, src_is_f32, B_src, pending)
        while pending:
            pending.pop(0)()
        tl = (own_off - 1) % NT
        tr_ = (own_off + 32) % NT
        self.att_block(l, src, src_is_f32, B_src, [tl, tr_], 127, 2, self.XH, self.B_XH)
        for k in range(8):
            tiles = [own_off + 4 * k + i for i in range(4)]
            XB, B_XB = self.XB[k % 2], self.B_XB[k % 2]
            self.att_block(l, src, src_is_f32, B_src, tiles, 0, 512, XB, B_XB)
            P.op("dve", lambda e, XB=XB, k=k: e.tensor_copy(self.LC[:, :, k:k + 1], XB[:, :, 511:512]),
                 reads=[B_XB], writes=[self.B_LC[k]])
            if k >= 1:
                self._ffn(l, k - 1, dst, dst_dt, B_dstbuf)
        self._ffn(l, 7, dst, dst_dt, B_dstbuf)

    def _ffn(self, l, k, dst, dst_dt, B_dstbuf):
        X, B_X = self.XB[k % 2], self.B_XB[k % 2]
        if k == 0:
            hl, B_hl = self.XH[:, :, 0:1], self.B_XH
        else:
            hl, B_hl = self.LC[:, :, k - 1:k], self.B_LC[k - 1]
        if k == 7:
            hr, B_hr = self.XH[:, :, 1:2], self.B_XH
        else:
            hr, B_hr = self.XB[(k + 1) % 2][:, :, 0:1], self.B_XB[(k + 1) % 2]
        self.fence(self.att_bufs + self.ffn_bufs)
        self.ffn_block(l, k, X, B_X, hl, B_hl, hr, B_hr, k == 0, k == 7, dst, dst_dt, B_dstbuf, k * 512)
        self.fence(self.att_bufs + self.ffn_bufs)

    def build(self):
        self.setup_consts()
        if not self.fused:
            pending = self.convert_weights(0)
            for _ in range(8):
                pending.pop(0)()
            self.half_layer(0, 0, self.x_in, True, None, self.out, F32, self.B_out, pending)
        else:
            pending = self.convert_weights(0) + self.convert_weights(1)
            for _ in range(8):
                pending.pop(0)()
            xm = self.xmid.ap()
            self.half_layer(0, 32, self.x_in, True, None, xm[S // 2:S, :], BF16, self.B_xmid, pending)
            self.half_layer(0, 0, self.x_in, True, None, xm[0:S // 2, :], BF16, self.B_xmid, pending)
            self.half_layer(1, 0, xm, False, self.B_xmid, self.out, F32, self.B_out, pending)
        self.P.emit()
        self.st.close()
        return self.nc


HPERM = [0, 4, 1, 5, 2, 6, 3, 7]


def _t5_bucket(rel):
    half = 16
    max_exact = 8
    bucket = np.where(rel > 0, half, 0)
    rp = np.abs(rel)
    rpf = np.maximum(rp, 1).astype(np.float32)
    large = max_exact + (np.log(rpf / np.float32(max_exact)) / np.float32(math.log(128 / max_exact))
                         * np.float32(half - max_exact)).astype(np.int32)
    large = np.minimum(large, half - 1)
    return bucket + np.where(rp < max_exact, rp, large)


def _bias_table(rel_bias):
    k = np.arange(128)[:, None, None]
    j = np.arange(3)[None, :, None]
    q = np.arange(128)[None, None, :]
    rel = (j - 1) * 128 + k - q
    idx = _t5_bucket(rel)
    tab = np.asarray(rel_bias, np.float32)[idx]
    tab = np.where((np.abs(rel) <= 128)[..., None], tab, np.float32(NEG))
    tab = np.ascontiguousarray(tab.transpose(0, 1, 3, 2))
    return tab.reshape(128, 3 * 8 * 128).astype(np.float32)


def _rope_table(half):
    tok = (np.arange(S) + half * (S // 2)) % S
    row = (tok // 64).astype(np.float32)
    col = (tok % 64).astype(np.float32)
    inv = (np.float32(10000.0) ** (-np.arange(0, 32, 2, dtype=np.float32) / np.float32(32))).astype(np.float32)
    ang = np.concatenate([row[:, None] * inv, col[:, None] * inv], axis=-1).astype(np.float32)
    cs = np.concatenate([np.cos(ang), np.sin(ang)], axis=-1).astype(np.float32)
    return np.ascontiguousarray(cs.reshape(NT, 128, 64).transpose(1, 0, 2))


def _layer_params(inp, l):
    f = lambda a: np.ascontiguousarray(np.asarray(a, np.float32))
    w_in = np.asarray(inp["w_in"][l], np.float32)
    qa = w_in[:, 0:512].reshape(D, 8, 64)[:, HPERM, :].reshape(D, 512)
    qb = w_in[:, 768:1280].reshape(D, 8, 64)[:, HPERM, :].reshape(D, 512)
    wq = np.concatenate([qa, qb], axis=1)
    wkv = np.concatenate([w_in[:, 512:640], w_in[:, 640:768], w_in[:, 1280:1408], w_in[:, 1408:1536]], axis=1)
    w_out = np.asarray(inp["w_out"][l], np.float32)
    wo = np.concatenate([w_out[0:512].reshape(8, 64, D)[HPERM].reshape(512, D),
                         w_out[512:1024].reshape(8, 64, D)[HPERM].reshape(512, D)], axis=0)
    ga = np.asarray(inp["out_norm_a"][l], np.float32).reshape(8, 64)[HPERM].reshape(512)
    gb = np.asarray(inp["out_norm_b"][l], np.float32).reshape(8, 64)[HPERM].reshape(512)
    gout = np.concatenate([ga, gb]).reshape(8, 128).T
    qn = np.asarray(inp["q_norm"][l], np.float32)
    kn = np.asarray(inp["k_norm"][l], np.float32)
    nrm = np.concatenate([qn[0::2], qn[1::2], kn[0::2], kn[1::2]])[None, :]
    fm = lambda v: np.asarray(v, np.float32).reshape(8, 128).T
    lnp = np.concatenate([fm(inp["ln1_g"][l]), fm(inp["ln1_b"][l]), fm(inp["ln2_g"][l]), fm(inp["ln2_b"][l])], axis=1)
    cw = np.asarray(inp["conv_w"][l], np.float32)
    cb = np.asarray(inp["conv_b"][l], np.float32)
    cv = np.stack([cw[0], cw[1], cw[2], cb], axis=-1).reshape(NFC, 128, 4).transpose(1, 0, 2).reshape(128, NFC * 4)
    sink = np.asarray(inp["sink"][l], np.float32)[None, :]
    return dict(wq=f(wq), wkv=f(wkv), wout=f(wo), wg=f(inp["w_gate"][l]), wu=f(inp["w_up"][l]), wd=f(inp["w_down"][l]),
                nrm=f(nrm), gout=f(gout), lnp=f(lnp), cvp=f(cv), sink=f(sink))


def _core_consts(inp, half):
    jm = np.zeros((128, 4), np.float32)
    if half == 0:
        jm[:, 0] = NEG; jm[:, 1] = 0.0; jm[:, 2] = 0.0; jm[:, 3] = 1.0
    else:
        jm[:, 0] = 0.0; jm[:, 1] = NEG; jm[:, 2] = 1.0; jm[:, 3] = 0.0
    return dict(biasT=_bias_table(inp["rel_bias"]), cs=_rope_table(half), jm=jm, ident=np.eye(128, dtype=np.float32))


def _layout_x(xb, half):
    if half == 0:
        return np.ascontiguousarray(xb)
    return np.ascontiguousarray(np.concatenate([xb[S // 2:], xb[:S // 2]], axis=0))


_NC_CACHE = {}


def _get_nc(n_layers, fused):
    key = (n_layers, fused)
    if key not in _NC_CACHE:
        _NC_CACHE[key] = Builder(n_layers, fused).build()
    return _NC_CACHE[key]


FUSED = True


def kernel(**inp):
    x = np.asarray(inp["x"], np.float32)
    B = x.shape[0]
    lp = [_layer_params(inp, l) for l in range(2)]
    cc = [_core_consts(inp, h) for h in range(2)]
    if FUSED:
        nc = _get_nc(2, True)
        in_maps = []
        for core in range(N_CORES):
            b, h = core // 2, core % 2
            m = {"xsrc": _layout_x(x[b], h)}
            for l in range(2):
                for k, v in lp[l].items():
                    m[f"{k}{l}"] = v
            m.update(cc[h])
            in_maps.append(m)
        res = run_bass_kernel_spmd(nc, in_maps, core_ids=list(range(N_CORES)))
        out = np.empty((B, S, D), np.float32)
        for core in range(N_CORES):
            b, h = core // 2, core % 2
            out[b, h * (S // 2):(h + 1) * (S // 2)] = res.results[core]["out"]
        return out
    nc = _get_nc(1, False)
    cur = x
    for l in range(2):
        in_maps = []
        for core in range(N_CORES):
            b, h = core // 2, core % 2
            m = {"xsrc": _layout_x(cur[b], h)}
            for k, v in lp[l].items():
                m[f"{k}0"] = v
            m.update(cc[h])
            in_maps.append(m)
        res = run_bass_kernel_spmd(nc, in_maps, core_ids=list(range(N_CORES)))
        nxt = np.empty((B, S, D), np.float32)
        for core in range(N_CORES):
            b, h = core // 2, core % 2
            nxt[b, h * (S // 2):(h + 1) * (S // 2)] = res.results[core]["out"]
        cur = nxt
    return cur
```

```python
import math
from contextlib import ExitStack
import numpy as np
import concourse.bass as bass
import concourse.mybir as mybir
from concourse.bass_utils import run_bass_kernel_spmd

F32 = mybir.dt.float32
BF16 = mybir.dt.bfloat16
AF = mybir.ActivationFunctionType
ALU = mybir.AluOpType
AX = mybir.AxisListType

D = 1024
S = 8192
NT = 64
NB = 4
DFF = 2816
NFC = 22
ALPHA = 4.0 ** 0.25
RMS_EPS = 1e-6
LN_EPS = 1e-5
NEG = -30000.0
N_CORES = 8


class Buf:
    __slots__ = ("name", "w", "r", "dsem", "dcnt")

    def __init__(self, name):
        self.name = name
        self.w = []
        self.r = []
        self.dsem = None
        self.dcnt = 0


class Prog:
    CE = ("pe", "act", "dve", "pool")
    ENG = ("pe", "act", "dve", "pool", "sp")

    def __init__(self, nc, stack):
        self.nc = nc
        self.stack = stack
        self.ops = {e: [] for e in self.ENG}
        self.cnt = {e: 0 for e in self.CE}
        self.esem = {e: stack.enter_context(nc.semaphore("s_" + e)) for e in self.CE}
        self.seen = {e: {} for e in self.ENG}
        self.nsem = 4
        self.out_tokens = []

    def _mk_sem(self, name):
        self.nsem += 1
        return self.stack.enter_context(self.nc.semaphore(name))

    def _waits(self, eng, reads, writes, dwrites=()):
        deps = []
        for b in reads:
            deps.extend(b.w)
        for b in writes:
            deps.extend(b.w)
            deps.extend(b.r)
        for b in dwrites:
            deps.extend(b.r)
        best = {}
        for (s, v) in deps:
            k = id(s)
            if k not in best or best[k][1] < v:
                best[k] = (s, v)
        waits = []
        own = self.esem.get(eng) if eng == "pe" else None
        for k, (s, v) in best.items():
            if s is own:
                continue
            if self.seen[eng].get(k, 0) >= v:
                continue
            self.seen[eng][k] = v
            waits.append((s, v))
        return waits

    @staticmethod
    def _compact(lst):
        best = {}
        for (s, v) in lst:
            if id(s) not in best or best[id(s)][1] < v:
                best[id(s)] = (s, v)
        return list(best.values())

    def _commit(self, tok, reads, writes, dwrites=()):
        for b in dwrites:
            b.w.append(tok)
            if len(b.w) > 24:
                b.w = self._compact(b.w)
        for b in reads:
            b.r.append(tok)
            if len(b.r) > 24:
                best = {}
                for (s, v) in b.r:
                    if id(s) not in best or best[id(s)][1] < v:
                        best[id(s)] = (s, v)
                b.r = list(best.values())
        for b in writes:
            b.w = [tok]
            b.r = []

    def op(self, eng, fn, reads=(), writes=()):
        waits = self._waits(eng, reads, writes)
        self.cnt[eng] += 1
        tok = (self.esem[eng], self.cnt[eng])
        self.ops[eng].append((fn, waits, (self.esem[eng], 1)))
        self._commit(tok, reads, writes)
        return tok

    def dma(self, q, fn, reads=(), writes=(), owner=None, is_out=False, dwrites=()):
        if owner is None:
            owner = writes[0] if writes else (dwrites[0] if dwrites else reads[0])
        if owner.dsem is None:
            owner.dsem = self._mk_sem("d_" + owner.name)
        waits = self._waits(q, reads, writes, dwrites)
        owner.dcnt += 16
        tok = (owner.dsem, owner.dcnt)
        self.ops[q].append((fn, waits, (owner.dsem, 16)))
        self._commit(tok, reads, writes, dwrites)
        if is_out:
            self.out_tokens.append(tok)
        return tok

    def emit(self):
        nc = self.nc
        ws = []
        for (s, v) in self.out_tokens:
            k = id(s)
            if self.seen["sp"].get(k, 0) >= v:
                continue
            self.seen["sp"][k] = v
            ws.append((s, v))
        if ws:
            self.ops["sp"].append((None, ws, None))
        ops = self.ops

        def replay(e, lst):
            for (fn, waits, inc) in lst:
                for (s, v) in waits:
                    e.wait_ge(s, v)
                if fn is not None:
                    ins = fn(e)
                    ins.then_inc(inc[0], inc[1])

        with nc.Block() as block:
            @block.tensor
            def _(e):
                replay(e, ops["pe"])

            @block.scalar
            def _(e):
                replay(e, ops["act"])

            @block.vector
            def _(e):
                replay(e, ops["dve"])

            @block.gpsimd
            def _(e):
                replay(e, ops["pool"])

            @block.sync
            def _(e):
                replay(e, ops["sp"])


class Builder:
    def __init__(self, n_layers, fused, dbg=False):
        self.fused = fused
        self.n_layers = n_layers
        self.dbg = dbg
        self.nc = bass.Bass("TRN2", target_bir_lowering=False)
        self.st = ExitStack()
        self.P = Prog(self.nc, self.st)
        self.sb_bytes = 0
        self._declare_io()
        self._alloc()

    def sb(self, name, shape, dt):
        n = 1
        for s in shape[1:]:
            n *= s
        self.sb_bytes += n * (2 if dt == BF16 else 4)
        return self.st.enter_context(self.nc.sbuf_tensor(name, list(shape), dt))

    def din(self, name, shape, dt=F32):
        return self.nc.dram_tensor(name, list(shape), dt, kind="ExternalInput").ap()

    def _declare_io(self):
        nc = self.nc
        L = self.n_layers
        self.x_in = self.din("xsrc", [S, D])
        self.W = []
        for l in range(L):
            w = dict(
                wq=self.din(f"wq{l}", [D, D]), wkv=self.din(f"wkv{l}", [D, 512]),
                wout=self.din(f"wout{l}", [D, D]), wg=self.din(f"wg{l}", [D, DFF]),
                wu=self.din(f"wu{l}", [D, DFF]), wd=self.din(f"wd{l}", [DFF, D]),
                nrm=self.din(f"nrm{l}", [1, 128]), gout=self.din(f"gout{l}", [128, 8]),
                lnp=self.din(f"lnp{l}", [128, 32]), cvp=self.din(f"cvp{l}", [128, 88]),
                sink=self.din(f"sink{l}", [1, 8]),
            )
            w["wq_s"] = nc.dram_tensor(f"wq_s{l}", [128, 8, D], BF16)
            w["wkv_s"] = nc.dram_tensor(f"wkv_s{l}", [128, 8, 512], BF16)
            w["wout_s"] = nc.dram_tensor(f"wout_s{l}", [8, 128, 8, 128], BF16)
            w["wgu_s"] = nc.dram_tensor(f"wgu_s{l}", [NFC, 128, 8, 256], BF16)
            w["wd_s"] = nc.dram_tensor(f"wd_s{l}", [NFC, 128, D], BF16)
            for k in ("wq_s", "wkv_s", "wout_s", "wgu_s", "wd_s"):
                w["B_" + k] = Buf(f"{k}{l}")
            self.W.append(w)
        self.biasT_in = self.din("biasT", [128, 3072])
        self.cs_in = self.din("cs", [128, NT, 64])
        self.jm_in = self.din("jm", [128, 4])
        self.ident_in = self.din("ident", [128, 128])
        self.out = nc.dram_tensor("out", [S // 2, D], F32, kind="ExternalOutput").ap()
        if self.fused:
            self.xmid = nc.dram_tensor("xmid", [S, D], BF16)
            self.B_xmid = Buf("xmid")
        self.B_out = Buf("out")
        self.dbg_out = {}

    def _alloc(self):
        nc, sb = self.nc, self.sb
        self.KAT = sb("KAT", [128, S], BF16)
        self.VA = sb("VA", [128, NT, 2, 128], BF16)
        self.KBT = sb("KBT", [128, 36 * 128], BF16)
        self.VB = sb("VB", [128, 36, 2, 65], BF16)
        self.B_KA = [Buf(f"KA{t}") for t in range(NT)]
        self.B_KB = [Buf(f"KB{t}") for t in range(NT)]
        self.B_vones = Buf("vones")
        self.ident_f = sb("ident_f", [128, 128], F32); self.B_idf = Buf("ident_f")
        self.ident_b = sb("ident_b", [128, 128], BF16); self.B_idb = Buf("ident_b")
        self.ones_b = sb("ones_b", [128, 128], BF16); self.B_ones = Buf("ones_b")
        self.zeros = sb("zeros", [128, 128], F32); self.B_zeros = Buf("zeros")
        self.onesf = sb("onesf", [128, 64], F32); self.B_onesf = Buf("onesf")
        self.epsr = sb("epsr", [128, 4], F32); self.B_eps = Buf("epsr")
        self.biasT = sb("biasT_sb", [128, 3, 2, 512], BF16); self.B_biasT = Buf("biasT")
        self.jm = sb("jm_sb", [128, 4], F32); self.B_jm = Buf("jm")
        self.nrm = sb("nrm_sb", [128, 128], F32); self.B_nrm = Buf("nrm")
        self.gout = [sb(f"gout_sb{l}", [128, 8], F32) for l in range(self.n_layers)]
        self.B_gout = [Buf(f"gout{l}") for l in range(self.n_layers)]
        self.lnp = sb("lnp_sb", [128, 32], F32); self.B_lnp = Buf("lnp")
        self.cvp = sb("cvp_sb", [128, NFC, 4], F32); self.B_cvp = Buf("cvp")
        self.cvm = sb("cvm_sb", [128, NFC, 2], F32); self.B_cvm = Buf("cvm")
        self.esink = sb("esink", [128, 8], F32); self.B_esink = Buf("esink")
        self.esrow = sb("esrow", [128, 2, 512], F32); self.B_esrow = Buf("esrow")
        self.cst = [sb(f"cst{i}", [128, 64], F32) for i in range(2)]; self.B_cst = [Buf(f"cst{i}") for i in range(2)]
        self.xt = [sb(f"xt{i}", [128, D], BF16) for i in range(2)]; self.B_xt = [Buf(f"xt{i}") for i in range(2)]
        self.xTb = sb("xTb", [128, 8, 512], BF16); self.B_xTb = Buf("xTb")
        self.xTt = [sb(f"xTt{i}", [128, 8, 128], BF16) for i in range(3)]; self.B_xTt = [Buf(f"xTt{i}") for i in range(3)]
        self.wq = sb("wq_sb", [128, 8, 512], BF16); self.B_wq = Buf("wq")
        self.wkv = self.wq; self.B_wkv = self.B_wq
        self.fscr = sb("fscr", [128, 8], F32)
        self.t_sq = sb("t_sq", [128, 512], F32); self.B_tsq = Buf("t_sq")
        self.t_qn = sb("t_qn", [128, 512], F32); self.B_tqn = Buf("t_qn")
        self.t_ab = sb("t_ab", [128, 2, 256], F32); self.B_tab = Buf("t_ab")
        self.t_m = sb("t_m", [128, 4, 256], F32); self.B_tm = Buf("t_m")
        self.t_ss = sb("t_ss", [128, 16], F32); self.B_tss = Buf("t_ss")
        self.t_rs = sb("t_rs", [128, 16], F32); self.B_trs = Buf("t_rs")
        self.tq = dict(sq=self.t_sq, qn=self.t_qn, ab=self.t_ab, m=self.t_m, ss=self.t_ss, rs=self.t_rs,
                       B=[self.B_tsq, self.B_tqn, self.B_tab, self.B_tm, self.B_tss, self.B_trs])
        self.tk = []
        for i in range(2):
            self.tk.append(dict(sq=sb(f"k_sq{i}", [128, 128], F32), qn=sb(f"k_qn{i}", [128, 128], F32),
                                ab=sb(f"k_ab{i}", [128, 2, 64], F32), m=sb(f"k_m{i}", [128, 4, 64], F32),
                                ss=sb(f"k_ss{i}", [128, 4], F32), rs=sb(f"k_rs{i}", [128, 4], F32),
                                B=[Buf(f"k_t{i}_{j}") for j in range(6)]))
        self.qbf = [sb(f"qbf{i}", [128, 512], BF16) for i in range(2)]; self.B_qbf = [Buf(f"qbf{i}") for i in range(2)]
        self.kvb = [sb(f"kvb{i}", [128, 256], BF16) for i in range(2)]; self.B_kvb = [Buf(f"kvb{i}") for i in range(2)]
        self.B_kvb2 = [Buf(f"kvbB{i}") for i in range(2)]
        A1 = sb("A1", [128, 6144], BF16)
        self.QT = A1[:, 0:4096].rearrange("p (a n) -> p a n", a=8); self.B_QT = Buf("QT")
        self.PT = [A1[:, 4096 + i * 1024:5120 + i * 1024].rearrange("p (a n) -> p a n", a=2) for i in range(2)]
        self.B_PT = [Buf(f"PT{i}") for i in range(2)]
        self.hT = A1[:, 0:5632].rearrange("p (a n) -> p a n", a=11); self.B_hT = [Buf(f"hT{i}") for i in range(11)]
        A2 = sb("A2", [128, 10240], BF16)
        f32v = lambda a, b: A2[:, a:b].bitcast(F32)
        self.yT = A2[:, 0:4096].rearrange("p (a n) -> p a n", a=8); self.B_yT = Buf("yT")
        self.stt = [f32v(4096 + i * 2048, 6144 + i * 2048).rearrange("p (a n) -> p a n", a=2) for i in range(2)]
        self.B_stt = [Buf(f"stt{i}") for i in range(2)]
        self.dsb = [f32v(8192 + i * 1024, 9216 + i * 1024) for i in range(2)]; self.B_dsb = [Buf(f"dsb{i}") for i in range(2)]
        self.rr = self.dsb; self.B_rr = self.B_dsb
        self.wgu = [A2[:, i * 2048:(i + 1) * 2048].rearrange("p (a n) -> p a n", a=8) for i in range(3)]
        self.B_wgu = [Buf(f"wgu{i}") for i in range(3)]
        self.gcs = [f32v(6144 + i * 1024, 7168 + i * 1024) for i in range(2)]; self.B_gcs = [Buf(f"gcs{i}") for i in range(2)]
        self.tgs = [f32v(8192 + i * 1024, 9216 + i * 1024) for i in range(2)]; self.B_tgs = [Buf(f"tgs{i}") for i in range(2)]
        self.y2 = self.gcs; self.B_y2 = self.B_gcs
        self.ot = [t.rearrange("p (a n) -> p a n", a=4) for t in self.tgs]; self.B_ot = self.B_tgs
        self.att_bufs = [self.B_QT] + self.B_PT + [self.B_yT] + self.B_stt + self.B_dsb
        self.ffn_bufs = self.B_hT + self.B_wgu + self.B_gcs + self.B_tgs
        self.wo = [sb(f"wo{i}", [128, 8, 128], BF16) for i in range(4)]; self.B_wo = [Buf(f"wo{i}") for i in range(4)]
        self.z = sb("z", [128, 8, 512], F32); self.B_z = [Buf(f"z{m}") for m in range(8)]
        self.zb = [sb(f"zb{i}", [128, 512], BF16) for i in range(2)]; self.B_zb = [Buf(f"zb{i}") for i in range(2)]
        self.zq = [sb(f"zq{i}", [128, 512], BF16) for i in range(2)]; self.B_zq = [Buf(f"zq{i}") for i in range(2)]
        self.sqy = self.zq; self.B_sqy = self.B_zq
        self.mean = self.t_ab[:].rearrange("p a n -> p (a n)"); self.B_mean = self.B_tab
        self.rstd = self.t_m[:, 0:2, :].rearrange("p a n -> p (a n)"); self.B_rstd = self.B_tm
        self.tmpa = [self.t_sq, self.t_qn]; self.B_tmpa = [self.B_tsq, self.B_tqn]
        self.XB = [sb(f"XB{i}", [128, 8, 512], BF16) for i in range(2)]; self.B_XB = [Buf(f"XB{i}") for i in range(2)]
        self.XH = sb("XH", [128, 8, 2], BF16); self.B_XH = Buf("XH")
        self.LC = sb("LC", [128, 8, 8], BF16); self.B_LC = [Buf(f"LC{k}") for k in range(8)]
        self.HC = [sb(f"HC{i}", [128, 8, 2], BF16) for i in range(2)]; self.B_HC = [Buf(f"HC{i}") for i in range(2)]
        self.ed = [sb(f"ed{i}", [128, 2], F32) for i in range(3)]; self.B_ed = [Buf(f"ed{i}") for i in range(3)]
        self.wdp = [sb(f"wdp{i}", [128, 384], BF16) for i in range(8)]; self.B_wdp = [Buf(f"wdp{i}") for i in range(8)]
        self.cvt = [self.z[:, 2 * i:2 * i + 2, :].rearrange("p a n -> p (a n)") for i in range(2)]
        self.B_cvt = [[self.B_z[2 * i], self.B_z[2 * i + 1]] for i in range(2)]
        self.cvo = [self.z[:, 4 + i, :].bitcast(BF16) for i in range(2)]
        self.B_cvo = [[self.B_z[4 + i]] for i in range(2)]
        self.ps = self.st.enter_context(nc.psum_tensor("ps", [128, 8, 512], F32))
        self.B_ps = [Buf(f"ps{i}") for i in range(8)]
        self.B_gh = [Buf(f"gh{i}") for i in range(3)]

    def bslot(self, tk):
        sl = (tk - (self.own_off - 2)) % NT
        return sl if sl < 36 else None

    def fence(self, bufs):
        self.P.op("pool", lambda e: e.memset(self.fscr[:], 0.0), writes=list(bufs))

    def psb(self, b):
        return self.ps[:, b, :].bitcast(BF16).rearrange("p (a n) -> p a n", a=8)

    def setup_consts(self):
        P = self.P
        P.dma("sp", lambda e: e.dma_start(out=self.ident_f[:], in_=self.ident_in), writes=[self.B_idf])
        P.op("act", lambda e: e.activation(out=self.ident_b[:], in_=self.ident_f[:], func=AF.Copy),
             reads=[self.B_idf], writes=[self.B_idb])
        P.op("pool", lambda e: e.memset(self.ones_b[:], 1.0), writes=[self.B_ones])
        P.op("pool", lambda e: e.memset(self.zeros[:], 0.0), writes=[self.B_zeros])
        P.op("pool", lambda e: e.memset(self.onesf[:], 1.0), writes=[self.B_onesf])
        P.op("pool", lambda e: e.memset(self.epsr[:, 0:1], RMS_EPS), writes=[self.B_eps])
        P.op("pool", lambda e: e.memset(self.epsr[:, 1:2], math.log(0.125)), writes=[self.B_eps])
        P.op("pool", lambda e: e.memset(self.epsr[:, 2:3], 0.0), writes=[self.B_eps])
        P.op("pool", lambda e: e.memset(self.epsr[:, 3:4], LN_EPS), writes=[self.B_eps])
        P.op("pool", lambda e: e.memset(self.VA[:, :, :, 64:128], 1.0), writes=[self.B_vones])
        P.op("pool", lambda e: e.memset(self.VB[:, :, :, 64:65], 1.0), writes=[self.B_vones])
        P.dma("pool", lambda e: e.dma_start(out=self.biasT[:].rearrange("p a b n -> p (a b n)"), in_=self.biasT_in),
              writes=[self.B_biasT])
        P.dma("sp", lambda e: e.dma_start(out=self.jm[:], in_=self.jm_in), writes=[self.B_jm])

    def setup_layer(self, l, own_off):
        P = self.P
        w = self.W[l]
        P.dma("sp", lambda e: e.dma_start(out=self.nrm[:], in_=w["nrm"].partition_broadcast(128).rearrange("p a n -> p (a n)")), writes=[self.B_nrm])
        P.dma("sp", lambda e: e.dma_start(out=self.lnp[:], in_=w["lnp"]), writes=[self.B_lnp])
        P.dma("sp", lambda e: e.dma_start(out=self.cvp[:].rearrange("p c k -> p (c k)"), in_=w["cvp"]), writes=[self.B_cvp])
        P.dma("sp", lambda e: e.dma_start(out=self.esink[:], in_=w["sink"].partition_broadcast(128).rearrange("p a n -> p (a n)")), writes=[self.B_esink])
        P.op("act", lambda e: e.activation(out=self.esink[:], in_=self.esink[:], func=AF.Exp),
             reads=[self.B_esink], writes=[self.B_esink])
        for kvh in range(2):
            for c in range(4):
                h = kvh * 4 + c
                P.op("dve", lambda e, kvh=kvh, c=c, h=h: e.tensor_scalar(
                    self.esrow[:, kvh, c * 128:(c + 1) * 128], self.zeros[:, :],
                    self.esink[:, h:h + 1], None, ALU.add),
                    reads=[self.B_esink, self.B_zeros], writes=[self.B_esrow])
        jl = 2 if own_off == 0 else 3
        jr = 3 if own_off == 0 else 2
        P.op("dve", lambda e: e.tensor_scalar(self.cvm[:, :, 0], self.cvp[:, :, 0], self.jm[:, jl:jl + 1], None, ALU.mult),
             reads=[self.B_cvp, self.B_jm], writes=[self.B_cvm])
        P.op("dve", lambda e: e.tensor_scalar(self.cvm[:, :, 1], self.cvp[:, :, 2], self.jm[:, jr:jr + 1], None, ALU.mult),
             reads=[self.B_cvp, self.B_jm], writes=[self.B_cvm])

    def convert_weights(self, l):
        P = self.P
        w = self.W[l]
        steps = []

        def cast_dma(dst_ap, src_ap, B):
            P.dma("pool", lambda e: e.dma_start(out=dst_ap, in_=src_ap), dwrites=[B])

        for kc in range(8):
            steps.append(lambda kc=kc: cast_dma(w["wkv_s"].ap()[:, kc, :], w["wkv"][kc * 128:(kc + 1) * 128, :], w["B_wkv_s"]))
        for kc in range(8):
            steps.append(lambda kc=kc: cast_dma(w["wq_s"].ap()[:, kc, :], w["wq"][kc * 128:(kc + 1) * 128, :], w["B_wq_s"]))

        def wout_step(c):
            i = c % 2
            if c == 0:
                P.dma("sp", lambda e: e.dma_start(out=self.gout[l][:], in_=w["gout"]), writes=[self.B_gout[l]])
            P.dma("sp", lambda e: e.dma_start(out=self.cvt[i], in_=w["wout"][c * 128:(c + 1) * 128, :]),
                  writes=self.B_cvt[i])
            P.op("dve", lambda e: e.tensor_scalar(self.cvo[i], self.cvt[i], self.gout[l][:, c:c + 1], None, ALU.mult),
                 reads=self.B_cvt[i] + [self.B_gout[l]], writes=self.B_cvo[i])
            P.dma("sp", lambda e: e.dma_start(out=w["wout_s"].ap()[:, :, c, :].rearrange("m p n -> p m n"),
                                              in_=self.cvo[i].rearrange("p (m n) -> p m n", m=8)),
                  reads=self.B_cvo[i], dwrites=[w["B_wout_s"]], owner=w["B_wout_s"])
        for c in range(8):
            steps.append(lambda c=c: wout_step(c))
        for c in range(NFC):
            def gu(c=c):
                cast_dma(w["wgu_s"].ap()[c, :, :, 0:128],
                         w["wg"][:, c * 128:(c + 1) * 128].rearrange("(kc p) n -> p kc n", p=128), w["B_wgu_s"])
                cast_dma(w["wgu_s"].ap()[c, :, :, 128:256],
                         w["wu"][:, c * 128:(c + 1) * 128].rearrange("(kc p) n -> p kc n", p=128), w["B_wgu_s"])
            steps.append(gu)
        for c in range(NFC):
            steps.append(lambda c=c: cast_dma(w["wd_s"].ap()[c], w["wd"][c * 128:(c + 1) * 128, :], w["B_wd_s"]))
        return steps

    def load_xT(self, src, src_is_f32, B_src, t, dst_ap, B_dst, slot):
        P = self.P
        xt, B_xt = self.xt[slot], self.B_xt[slot]
        rd = [B_src] if B_src is not None else []
        if src_is_f32:
            P.dma("pool", lambda e: e.dma_start(out=xt[:], in_=src[t * 128:(t + 1) * 128, :]), reads=rd, writes=[B_xt])
        else:
            P.dma("sp", lambda e: e.dma_start(out=xt[:], in_=src[t * 128:(t + 1) * 128, :]), reads=rd, writes=[B_xt])
        bank = 6 + slot
        pv = self.psb(bank)

        def tr(e):
            for c in range(8):
                ins = e.transpose(pv[:, c, :], xt[:, c * 128:(c + 1) * 128], self.ident_b[:])
            return ins
        P.op("pe", tr, reads=[B_xt, self.B_idb], writes=[self.B_ps[bank]])
        P.op("act", lambda e: e.activation(out=dst_ap, in_=pv, func=AF.Copy), reads=[self.B_ps[bank]], writes=[B_dst])

    def norm_rope(self, src, B_src, nh, goff, cs, B_cs, scale, dst, B_dst, T=None):
        P = self.P
        W = nh * 64
        if T is None:
            T = self.tq
        B_tsq, B_tqn, B_tab, B_tm, B_tss, B_trs = T["B"]
        sq = T["sq"][:, 0:W]
        P.op("act", lambda e: e.activation(out=sq, in_=src, func=AF.Square), reads=[B_src], writes=[B_tsq])
        ss = T["ss"][:, 0:nh]
        P.op("dve", lambda e: e.tensor_reduce(out=ss, in_=sq.rearrange("p (h d) -> p h d", h=nh), axis=AX.X, op=ALU.add),
             reads=[B_tsq], writes=[B_tss])
        P.op("act", lambda e: e.activation(out=ss, in_=ss, func=AF.Ln, scale=1.0 / 64.0, bias=self.epsr[:, 0:1]),
             reads=[B_tss, self.B_eps], writes=[B_tss])
        rs = T["rs"][:, 0:nh]
        bcol = 1 if scale != 1.0 else 2
        P.op("act", lambda e: e.activation(out=rs, in_=ss, func=AF.Exp, scale=-0.5, bias=self.epsr[:, bcol:bcol + 1]),
             reads=[B_tss, self.B_eps], writes=[B_trs])
        qn = T["qn"][:, 0:W].rearrange("p (h d) -> p h d", h=nh)
        P.op("dve", lambda e: e.tensor_tensor(qn, src.rearrange("p (h d) -> p h d", h=nh),
                                              rs.unsqueeze(2).to_broadcast([128, nh, 64]), ALU.mult),
             reads=[B_src, B_trs], writes=[B_tqn])
        x0 = qn[:, :, 0::2]
        x1 = qn[:, :, 1::2]
        ge = self.nrm[:, goff:goff + 32].unsqueeze(1).to_broadcast([128, nh, 32])
        go = self.nrm[:, goff + 32:goff + 64].unsqueeze(1).to_broadcast([128, nh, 32])
        cosb = cs[:, 0:32].unsqueeze(1).to_broadcast([128, nh, 32])
        sinb = cs[:, 32:64].unsqueeze(1).to_broadcast([128, nh, 32])
        a = T["ab"][:, 0, 0:nh * 32].rearrange("p (h d) -> p h d", h=nh)
        b = T["ab"][:, 1, 0:nh * 32].rearrange("p (h d) -> p h d", h=nh)
        P.op("dve", lambda e: e.tensor_tensor(a, x0, ge, ALU.mult), reads=[B_tqn, self.B_nrm], writes=[B_tab])
        P.op("dve", lambda e: e.tensor_tensor(b, x1, go, ALU.mult), reads=[B_tqn, self.B_nrm], writes=[B_tab])
        m = [T["m"][:, i, 0:nh * 32].rearrange("p (h d) -> p h d", h=nh) for i in range(4)]
        P.op("dve", lambda e: e.tensor_tensor(m[0], a, cosb, ALU.mult), reads=[B_tab, B_cs], writes=[B_tm])
        P.op("dve", lambda e: e.tensor_tensor(m[1], b, sinb, ALU.mult), reads=[B_tab, B_cs], writes=[B_tm])
        P.op("dve", lambda e: e.tensor_tensor(m[2], a, sinb, ALU.mult), reads=[B_tab, B_cs], writes=[B_tm])
        P.op("dve", lambda e: e.tensor_tensor(m[3], b, cosb, ALU.mult), reads=[B_tab, B_cs], writes=[B_tm])
        d3 = dst.rearrange("p (h d) -> p h d", h=nh)
        P.op("dve", lambda e: e.tensor_tensor(d3[:, :, 0:32], m[0], m[1], ALU.subtract), reads=[B_tm], writes=[B_dst])
        P.op("dve", lambda e: e.tensor_tensor(d3[:, :, 32:64], m[2], m[3], ALU.add), reads=[B_tm], writes=[B_dst])

    def load_cs(self, t):
        i = t % 2
        self.P.dma("sp", lambda e: e.dma_start(out=self.cst[i][:], in_=self.cs_in[:, t, :]), writes=[self.B_cst[i]])
        return self.cst[i], self.B_cst[i]

    def kv_phase(self, l, src, src_is_f32, B_src, pending):
        P = self.P
        w = self.W[l]
        P.dma("sp", lambda e: e.dma_start(out=self.wkv[:], in_=w["wkv_s"].ap()), reads=[w["B_wkv_s"]], writes=[self.B_wkv])
        def stage1a(t):
            self.load_xT(src, src_is_f32, B_src, t, self.xTt[t % 3][:], self.B_xTt[t % 3], t % 2)
            for _ in range(3):
                if pending:
                    pending.pop(0)()

        def stage1b(t):
            slot = t % 2
            xT = self.xTt[t % 3]
            bk = 4 + slot
            pk = self.ps[:, bk, :]

            def mm(e, xT=xT, pk=pk):
                for c in range(8):
                    ins = e.matmul(pk, lhsT=xT[:, c, :], rhs=self.wkv[:, c, :], start=(c == 0), stop=(c == 7))
                return ins
            P.op("pe", mm, reads=[self.B_xTt[t % 3], self.B_wkv], writes=[self.B_ps[bk]])

        def stage2a(t):
            slot = t % 2
            bk = 4 + slot
            pk = self.ps[:, bk, :]
            kvb, B_kvb, B_kvb2 = self.kvb[slot], self.B_kvb[slot], self.B_kvb2[slot]
            P.op("act", lambda e: e.activation(out=kvb[:, 128:256], in_=pk[:, 256:384], func=AF.Copy),
                 reads=[self.B_ps[bk]], writes=[B_kvb2])
            P.op("act", lambda e: e.activation(out=self.VA[:, t, :, 0:64],
                                               in_=pk[:, 128:256].rearrange("p (h d) -> p h d", h=2), func=AF.Copy),
                 reads=[self.B_ps[bk], self.B_vones], writes=[self.B_KA[t]])
            sl = self.bslot(t)
            if sl is not None:
                P.op("act", lambda e: e.activation(out=self.VB[:, sl, :, 0:64],
                                                   in_=pk[:, 384:512].rearrange("p (h d) -> p h d", h=2), func=AF.Copy),
                     reads=[self.B_ps[bk], self.B_vones], writes=[self.B_KB[t]])
            cs, B_cs = self.load_cs(t)
            self.norm_rope(pk[:, 0:128], self.B_ps[bk], 2, 64, cs, B_cs, 1.0, kvb[:, 0:128], B_kvb, T=self.tk[slot])

        def stage2b(t):
            slot = t % 2
            kvb, B_kvb, B_kvb2 = self.kvb[slot], self.B_kvb[slot], self.B_kvb2[slot]
            sl = self.bslot(t)
            bt = 2 + slot
            pt = self.psb(bt)

            def trk(e):
                e.transpose(pt[:, 0, :], kvb[:, 0:128], self.ident_b[:])
                return e.transpose(pt[:, 1, :], kvb[:, 128:256], self.ident_b[:])
            P.op("pe", trk, reads=[B_kvb, B_kvb2, self.B_idb], writes=[self.B_ps[bt]])
            P.op("dve", lambda e: e.tensor_copy(self.KAT[:, t * 128:(t + 1) * 128], pt[:, 0, :]),
                 reads=[self.B_ps[bt]], writes=[self.B_KA[t]])
            if sl is not None:
                P.op("dve", lambda e: e.tensor_copy(self.KBT[:, sl * 128:(sl + 1) * 128], pt[:, 1, :]),
                     reads=[self.B_ps[bt]], writes=[self.B_KB[t]])

        stage1a(0)
        stage1a(1)
        stage1b(0)
        for t in range(NT):
            stage2a(t)
            if t + 2 < NT:
                stage1a(t + 2)
            if t + 1 < NT:
                stage1b(t + 1)
            stage2b(t)

    def att_block(self, l, src, src_is_f32, B_src, tiles, col0, ncol, x1_dst, B_x1):
        P = self.P
        w = self.W[l]
        nt = len(tiles)
        ntok = nt * 128
        for i, t in enumerate(tiles):
            self.load_xT(src, src_is_f32, B_src, t, self.xTb[:, :, i * 128:(i + 1) * 128], self.B_xTb, i % 2)
        for half in range(2):
            P.dma("sp", lambda e, half=half: e.dma_start(out=self.wq[:], in_=w["wq_s"].ap()[:, :, half * 512:(half + 1) * 512]),
                  reads=[w["B_wq_s"]], writes=[self.B_wq])
            for i, t in enumerate(tiles):
                bk = 4 + i
                pq = self.ps[:, bk, :]

                def mm(e, i=i, pq=pq):
                    for c in range(8):
                        ins = e.matmul(pq, lhsT=self.xTb[:, c, i * 128:(i + 1) * 128], rhs=self.wq[:, c, :],
                                       start=(c == 0), stop=(c == 7))
                    return ins
                P.op("pe", mm, reads=[self.B_xTb, self.B_wq], writes=[self.B_ps[bk]])
            for i, t in enumerate(tiles):
                bk = 4 + i
                pq = self.ps[:, bk, :]
                qb, B_qb = self.qbf[i % 2], self.B_qbf[i % 2]
                if half == 0:
                    cs, B_cs = self.load_cs(t)
                    self.norm_rope(pq, self.B_ps[bk], 8, 0, cs, B_cs, 0.125, qb[:, 0:512], B_qb)
                else:
                    P.op("act", lambda e, qb=qb, pq=pq: e.activation(out=qb[:, 0:512], in_=pq, func=AF.Copy),
                         reads=[self.B_ps[bk]], writes=[B_qb])
                bt = 2 + (i % 2)
                pt = self.psb(bt)

                def trq(e, qb=qb, pt=pt):
                    for c in range(4):
                        ins = e.transpose(pt[:, c, :], qb[:, c * 128:(c + 1) * 128], self.ident_b[:])
                    return ins
                P.op("pe", trq, reads=[B_qb, self.B_idb], writes=[self.B_ps[bt]])
                P.op("dve", lambda e, i=i, half=half, pt=pt: e.tensor_copy(
                    self.QT[:, half * 4:(half + 1) * 4, i * 128:(i + 1) * 128], pt[:, 0:4, :]),
                    reads=[self.B_ps[bt]], writes=[self.B_QT])
        for c in range(4):
            accb = (4, 5) if c % 2 == 0 else (6, 7)

            def qk(kt, c=c):
                sb0 = 2 * (kt % 2)

                def f(e):
                    e.matmul(self.ps[:, sb0, 0:ntok], lhsT=self.KAT[0:64, kt * 128:(kt + 1) * 128],
                             rhs=self.QT[0:64, c, 0:ntok], start=True, stop=True)
                    return e.matmul(self.ps[:, sb0 + 1, 0:ntok], lhsT=self.KAT[64:128, kt * 128:(kt + 1) * 128],
                                    rhs=self.QT[64:128, c, 0:ntok], start=True, stop=True)
                P.op("pe", f, reads=[self.B_KA[kt], self.B_QT], writes=[self.B_ps[sb0], self.B_ps[sb0 + 1]])

            def ex(kt):
                sb0 = 2 * (kt % 2)
                pt = self.PT[kt % 2]
                P.op("act", lambda e: e.activation(out=pt[:, :, 0:ntok], in_=self.ps[:, sb0:sb0 + 2, 0:ntok], func=AF.Exp),
                     reads=[self.B_ps[sb0], self.B_ps[sb0 + 1]], writes=[self.B_PT[kt % 2]])

            def pv(kt, accb=accb):
                pt = self.PT[kt % 2]

                def f(e):
                    e.matmul(self.ps[:, accb[0], 0:ntok], lhsT=self.VA[:, kt, 0, :], rhs=pt[:, 0, 0:ntok],
                             start=(kt == 0), stop=(kt == NT - 1))
                    return e.matmul(self.ps[:, accb[1], 0:ntok], lhsT=self.VA[:, kt, 1, :], rhs=pt[:, 1, 0:ntok],
                                    start=(kt == 0), stop=(kt == NT - 1))
                P.op("pe", f, reads=[self.B_KA[kt], self.B_PT[kt % 2]], writes=[self.B_ps[accb[0]], self.B_ps[accb[1]]])

            qk(0)
            qk(1)
            for kt in range(NT):
                ex(kt)
                pv(kt)
                if kt + 2 < NT:
                    qk(kt + 2)
            for kvh in range(2):
                self.finalize_head(accb[kvh], kvh, None, self.yT[kvh * 64:(kvh + 1) * 64, c, 0:ntok], ntok, kvh)
        def b_main(i, t):
            accb = (4, 5) if i % 2 == 0 else (6, 7)
            nbrs = [((t - 1) % NT, 0), (t, 1), ((t + 1) % NT, 2)]

            def qk(jj):
                tk, jidx = nbrs[jj]
                sb0 = 2 * (jj % 2)
                ks = self.bslot(tk)
                assert ks is not None

                def f(e):
                    e.matmul(self.ps[:, sb0, :], lhsT=self.KBT[0:64, ks * 128:(ks + 1) * 128],
                             rhs=self.QT[0:64, 4:8, i * 128:(i + 1) * 128], start=True, stop=True)
                    return e.matmul(self.ps[:, sb0 + 1, :], lhsT=self.KBT[64:128, ks * 128:(ks + 1) * 128],
                                    rhs=self.QT[64:128, 4:8, i * 128:(i + 1) * 128], start=True, stop=True)
                P.op("pe", f, reads=[self.B_KB[tk], self.B_QT], writes=[self.B_ps[sb0], self.B_ps[sb0 + 1]])

            def chain(jj):
                tk, jidx = nbrs[jj]
                sb0 = 2 * (jj % 2)
                st = self.stt[jj % 2]
                B_st = self.B_stt[jj % 2]
                ks = self.bslot(tk)
                for kvh in range(2):
                    P.op("dve", lambda e, kvh=kvh: e.scalar_tensor_tensor(
                        out=st[:, kvh, :], in0=self.ps[:, sb0 + kvh, :], scalar=0.125, in1=self.biasT[:, jidx, kvh, :],
                        op0=ALU.mult, op1=ALU.add),
                        reads=[self.B_ps[sb0 + kvh], self.B_biasT], writes=[B_st])
                jcol = None
                if jidx == 0 and t == 0:
                    jcol = 0
                elif jidx == 0 and t == 32:
                    jcol = 1
                elif jidx == 2 and t == 63:
                    jcol = 0
                elif jidx == 2 and t == 31:
                    jcol = 1
                if jcol is not None:
                    P.op("dve", lambda e: e.tensor_scalar(
                        st[:].rearrange("p a n -> p (a n)"), st[:].rearrange("p a n -> p (a n)"),
                        self.jm[:, jcol:jcol + 1], None, ALU.add),
                        reads=[B_st, self.B_jm], writes=[B_st])
                pt = self.PT[jj % 2]
                P.op("act", lambda e: e.activation(out=pt[:], in_=st[:], func=AF.Exp),
                     reads=[B_st], writes=[self.B_PT[jj % 2]])

                def g(e):
                    e.matmul(self.ps[0:65, accb[0], :], lhsT=self.VB[:, ks, 0, :], rhs=pt[:, 0, :],
                             start=(jj == 0), stop=(jj == 2))
                    return e.matmul(self.ps[0:65, accb[1], :], lhsT=self.VB[:, ks, 1, :], rhs=pt[:, 1, :],
                                    start=(jj == 0), stop=(jj == 2))
                P.op("pe", g, reads=[self.B_KB[tk], self.B_PT[jj % 2]], writes=[self.B_ps[accb[0]], self.B_ps[accb[1]]])

            qk(0)
            qk(1)
            chain(0)
            qk(2)
            chain(1)
            chain(2)

        def b_fin(i):
            accb = (4, 5) if i % 2 == 0 else (6, 7)
            for kvh in range(2):
                self.finalize_head_b(accb[kvh], kvh, self.yT[kvh * 64:(kvh + 1) * 64, 4:8, i * 128:(i + 1) * 128])

        for i, t in enumerate(tiles):
            b_main(i, t)
            if i >= 1:
                b_fin(i - 1)
        b_fin(nt - 1)
        self.out_stage(l, col0, ncol)
        self.layer_norm(0, col0, ncol, lambda m: x1_dst[:, m, :], B_x1, BF16)

    def finalize_head(self, bank, kvh, sink_kvh, y_dst, n, slot):
        P = self.P
        dsb, B_dsb = self.dsb[slot], self.B_dsb[slot]
        if sink_kvh is not None:
            P.op("dve", lambda e: e.tensor_tensor(dsb[0:64, 0:n], self.ps[64:128, bank, 0:n], self.esrow[64:128, sink_kvh, 0:n], ALU.add),
                 reads=[self.B_ps[bank], self.B_esrow], writes=[B_dsb])
        else:
            P.op("act", lambda e: e.activation(out=dsb[0:64, 0:n], in_=self.ps[64:128, bank, 0:n], func=AF.Copy),
                 reads=[self.B_ps[bank]], writes=[B_dsb])
        if sink_kvh is not None:
            P.op("act", lambda e: e.activation(out=dsb[0:64, 0:n], in_=dsb[0:64, 0:n], func=AF.Ln), reads=[B_dsb], writes=[B_dsb])
            P.op("act", lambda e: e.activation(out=dsb[0:64, 0:n], in_=dsb[0:64, 0:n], func=AF.Exp, scale=-1.0),
                 reads=[B_dsb], writes=[B_dsb])
        else:
            P.op("dve", lambda e: e.reciprocal(dsb[0:64, 0:n], dsb[0:64, 0:n]), reads=[B_dsb], writes=[B_dsb])
        if len(y_dst.shape) == 3:
            in0 = self.ps[0:64, bank, 0:n].rearrange("p (a n) -> p a n", a=4)
            in1 = dsb[0:64, 0:n].rearrange("p (a n) -> p a n", a=4)
        else:
            in0 = self.ps[0:64, bank, 0:n]
            in1 = dsb[0:64, 0:n]
        P.op("dve", lambda e: e.tensor_tensor(y_dst, in0, in1, ALU.mult),
             reads=[self.B_ps[bank], B_dsb], writes=[self.B_yT])

    def finalize_head_b(self, bank, kvh, y_dst):
        P = self.P
        n = 512
        dsb, B_dsb = self.dsb[kvh], self.B_dsb[kvh]
        P.op("dve", lambda e: e.tensor_tensor(dsb[64:65, 0:n], self.ps[64:65, bank, 0:n], self.esrow[64:65, kvh, 0:n], ALU.add),
             reads=[self.B_ps[bank], self.B_esrow], writes=[B_dsb])
        P.op("act", lambda e: e.activation(out=dsb[64:65, 0:n], in_=dsb[64:65, 0:n], func=AF.Ln), reads=[B_dsb], writes=[B_dsb])
        P.op("act", lambda e: e.activation(out=dsb[64:65, 0:n], in_=dsb[64:65, 0:n], func=AF.Exp, scale=-1.0),
             reads=[B_dsb], writes=[B_dsb])
        bb = 2 + kvh
        P.op("pe", lambda e: e.matmul(self.ps[0:64, bb, 0:n], lhsT=self.onesf[64:65, 0:64], rhs=dsb[64:65, 0:n],
                                      start=True, stop=True),
             reads=[B_dsb, self.B_onesf], writes=[self.B_ps[bb]])
        P.op("act", lambda e: e.activation(out=dsb[0:64, 0:n], in_=self.ps[0:64, bb, 0:n], func=AF.Copy),
             reads=[self.B_ps[bb]], writes=[B_dsb])
        in0 = self.ps[0:64, bank, 0:n].rearrange("p (a n) -> p a n", a=4)
        in1 = dsb[0:64, 0:n].rearrange("p (a n) -> p a n", a=4)
        P.op("dve", lambda e: e.tensor_tensor(y_dst, in0, in1, ALU.mult),
             reads=[self.B_ps[bank], B_dsb], writes=[self.B_yT])

    def rsqrt_act(self, dst, src_ap, B_src, B_dst, scale, eps_col, ncol):
        P = self.P
        P.op("act", lambda e: e.activation(out=dst, in_=src_ap, func=AF.Ln, scale=scale, bias=self.epsr[:, eps_col:eps_col + 1]),
             reads=[B_src, self.B_eps], writes=[B_dst])
        P.op("act", lambda e: e.activation(out=dst, in_=dst, func=AF.Exp, scale=-0.5), reads=[B_dst], writes=[B_dst])

    def out_stage(self, l, col0, ncol):
        P = self.P
        w = self.W[l]
        cs = slice(col0, col0 + ncol)
        for g in range(2):
            bank = g
            for cc in range(4):
                c = g * 4 + cc
                sq, B_sq = self.sqy[cc % 2], self.B_sqy[cc % 2]
                P.op("act", lambda e, sq=sq, c=c: e.activation(out=sq[:, 0:ncol], in_=self.yT[:, c, cs], func=AF.Square),
                     reads=[self.B_yT], writes=[B_sq])
                P.op("pe", lambda e, sq=sq, cc=cc, bank=bank: e.matmul(self.ps[:, bank, 0:ncol], lhsT=self.ones_b[:], rhs=sq[:, 0:ncol],
                                                                    start=(cc == 0), stop=(cc == 3)),
                     reads=[B_sq, self.B_ones], writes=[self.B_ps[bank]])
            rr, B_rr = self.rr[g], self.B_rr[g]
            self.rsqrt_act(rr[:, 0:ncol], self.ps[:, bank, 0:ncol], self.B_ps[bank], B_rr, 1.0 / 512.0, 0, ncol)
            for cc in range(4):
                c = g * 4 + cc
                P.op("dve", lambda e, c=c, rr=rr: e.tensor_tensor(self.yT[:, c, cs], self.yT[:, c, cs], rr[:, 0:ncol], ALU.mult),
                     reads=[self.B_yT, B_rr], writes=[self.B_yT])
        for m in range(8):
            wo, B_wo = self.wo[m % 4], self.B_wo[m % 4]
            P.dma("sp", lambda e, wo=wo, m=m: e.dma_start(out=wo[:], in_=w["wout_s"].ap()[m]),
                  reads=[w["B_wout_s"]], writes=[B_wo])
            ba = 2 + (m % 2)

            def mm(e, wo=wo, ba=ba, m=m):
                for c in range(8):
                    ins = e.matmul(self.ps[:, ba, 0:ncol], lhsT=wo[:, c, :], rhs=self.yT[:, c, cs],
                                   start=(c == 0), stop=(c == 7))
                return ins
            P.op("pe", mm, reads=[B_wo, self.B_yT], writes=[self.B_ps[ba]])
            P.op("dve", lambda e, ba=ba, m=m: e.scalar_tensor_tensor(out=self.z[:, m, 0:ncol], in0=self.xTb[:, m, cs], scalar=ALPHA,
                                                                     in1=self.ps[:, ba, 0:ncol], op0=ALU.mult, op1=ALU.add),
                 reads=[self.B_xTb, self.B_ps[ba]], writes=[self.B_z[m]])

    def layer_norm(self, which, col0_unused, ncol, dst_fn, B_dst, out_dt):
        P = self.P
        gcol = 0 if which == 0 else 16
        bcol = gcol + 8
        for m in range(8):
            zb, B_zb = self.zb[m % 2], self.B_zb[m % 2]
            zq, B_zq = self.zq[m % 2], self.B_zq[m % 2]
            P.op("act", lambda e, zb=zb, m=m: e.activation(out=zb[:, 0:ncol], in_=self.z[:, m, 0:ncol], func=AF.Copy),
                 reads=[self.B_z[m]], writes=[B_zb])
            P.op("act", lambda e, zq=zq, m=m: e.activation(out=zq[:, 0:ncol], in_=self.z[:, m, 0:ncol], func=AF.Square),
                 reads=[self.B_z[m]], writes=[B_zq])
            P.op("pe", lambda e, zb=zb, m=m: e.matmul(self.ps[:, 0, 0:ncol], lhsT=self.ones_b[:], rhs=zb[:, 0:ncol],
                                                      start=(m == 0), stop=(m == 7)),
                 reads=[B_zb, self.B_ones], writes=[self.B_ps[0]])
            P.op("pe", lambda e, zq=zq, m=m: e.matmul(self.ps[:, 1, 0:ncol], lhsT=self.ones_b[:], rhs=zq[:, 0:ncol],
                                                      start=(m == 0), stop=(m == 7)),
                 reads=[B_zq, self.B_ones], writes=[self.B_ps[1]])
        mean, rstd = self.mean, self.rstd
        P.op("dve", lambda e: e.tensor_scalar(mean[:, 0:ncol], self.ps[:, 0, 0:ncol], 1.0 / D, None, ALU.mult),
             reads=[self.B_ps[0]], writes=[self.B_mean])
        P.op("dve", lambda e: e.tensor_tensor(rstd[:, 0:ncol], mean[:, 0:ncol], mean[:, 0:ncol], ALU.mult),
             reads=[self.B_mean], writes=[self.B_rstd])
        P.op("dve", lambda e: e.scalar_tensor_tensor(out=rstd[:, 0:ncol], in0=self.ps[:, 1, 0:ncol], scalar=1.0 / D,
                                                     in1=rstd[:, 0:ncol], op0=ALU.mult, op1=ALU.subtract),
             reads=[self.B_ps[1], self.B_rstd], writes=[self.B_rstd])
        self.rsqrt_act(rstd[:, 0:ncol], rstd[:, 0:ncol], self.B_rstd, self.B_rstd, 1.0, 3, ncol)
        for m in range(8):
            ta, B_ta = self.tmpa[m % 2], self.B_tmpa[m % 2]
            P.op("dve", lambda e, ta=ta, m=m: e.tensor_tensor(ta[:, 0:ncol], self.z[:, m, 0:ncol], mean[:, 0:ncol], ALU.subtract),
                 reads=[self.B_z[m], self.B_mean], writes=[B_ta])
            P.op("dve", lambda e, ta=ta: e.tensor_tensor(ta[:, 0:ncol], ta[:, 0:ncol], rstd[:, 0:ncol], ALU.mult),
                 reads=[B_ta, self.B_rstd], writes=[B_ta])
            dst = dst_fn(m)
            Bd = B_dst(m) if callable(B_dst) else B_dst
            P.op("act", lambda e, ta=ta, m=m, dst=dst: e.activation(out=dst, in_=ta[:, 0:ncol], func=AF.Identity,
                                                                    scale=self.lnp[:, gcol + m:gcol + m + 1],
                                                                    bias=self.lnp[:, bcol + m:bcol + m + 1]),
                 reads=[B_ta, self.B_lnp], writes=[Bd])

    def ffn_block(self, l, k, X, B_X, hl_ap, B_hl, hr_ap, B_hr, first, last, dst, dst_dt, B_dstbuf, row0):
        P = self.P
        w = self.W[l]
        HC, B_HC = self.HC[k % 2], self.B_HC[k % 2]
        P.op("dve", lambda e: e.tensor_copy(HC[:, :, 0:1], hl_ap), reads=[B_hl], writes=[B_HC])
        P.op("dve", lambda e: e.tensor_copy(HC[:, :, 1:2], hr_ap), reads=[B_hr], writes=[B_HC])
        for m in range(8):
            P.op("act", lambda e, m=m: e.activation(out=self.z[:, m, :], in_=X[:, m, :], func=AF.Copy, scale=ALPHA),
                 reads=[B_X], writes=[self.B_z[m]])
        for hh in range(2):
            for cc in range(11):
                c = hh * 11 + cc
                wg, B_wg = self.wgu[c % 3], self.B_wgu[c % 3]
                P.dma("sp", lambda e, wg=wg, c=c: e.dma_start(out=wg[:], in_=w["wgu_s"].ap()[c]),
                      reads=[w["B_wgu_s"]], writes=[B_wg])
                gb = (0, 1, 4)[c % 3]
                ub = (2, 3, 5)[c % 3]
                hcol = (c % 3) * 2

                def mm(e, wg=wg, gb=gb, ub=ub, hcol=hcol):
                    for kc in range(8):
                        e.matmul(self.ps[:, gb, :], lhsT=wg[:, kc, 0:128], rhs=X[:, kc, :], start=(kc == 0), stop=(kc == 7))
                    for kc in range(8):
                        e.matmul(self.ps[:, 7, hcol:hcol + 2], lhsT=wg[:, kc, 0:128], rhs=HC[:, kc, :], start=(kc == 0), stop=(kc == 7))
                    for kc in range(8):
                        ins = e.matmul(self.ps[:, ub, :], lhsT=wg[:, kc, 128:256], rhs=X[:, kc, :], start=(kc == 0), stop=(kc == 7))
                    return ins
                P.op("pe", mm, reads=[B_wg, B_X, B_HC], writes=[self.B_ps[gb], self.B_ps[ub], self.B_ps[7]])
                gc, B_gc = self.gcs[c % 2], self.B_gcs[c % 2]
                G = self.ps[:, gb, :]
                Gh = self.ps[:, 7, hcol:hcol + 2]
                w0 = self.cvp[:, c, 0:1]
                w1 = self.cvp[:, c, 1:2]
                w2 = self.cvp[:, c, 2:3]
                cb = self.cvp[:, c, 3:4]
                w0e = self.cvm[:, c, 0:1] if first else w0
                w2e = self.cvm[:, c, 1:2] if last else w2
                ed, B_ed = self.ed[c % 3], self.B_ed[c % 3]
                P.op("act", lambda e, ed=ed, Gh=Gh: e.activation(out=ed[:], in_=Gh, func=AF.Copy),
                     reads=[self.B_ps[7]], writes=[B_ed])
                P.op("act", lambda e, gc=gc, G=G, w1=w1, cb=cb: e.activation(out=gc[:], in_=G, func=AF.Identity, scale=w1, bias=cb),
                     reads=[self.B_ps[gb], self.B_cvp], writes=[B_gc])
                P.op("dve", lambda e, gc=gc, G=G, w0=w0: e.scalar_tensor_tensor(out=gc[:, 1:512], in0=G[:, 0:511], scalar=w0,
                                                                                 in1=gc[:, 1:512], op0=ALU.mult, op1=ALU.add),
                     reads=[self.B_ps[gb], B_gc, self.B_cvp], writes=[B_gc])
                P.op("dve", lambda e, gc=gc, G=G, w2=w2: e.scalar_tensor_tensor(out=gc[:, 0:511], in0=G[:, 1:512], scalar=w2,
                                                                                 in1=gc[:, 0:511], op0=ALU.mult, op1=ALU.add),
                     reads=[self.B_ps[gb], B_gc, self.B_cvp], writes=[B_gc])
                P.op("dve", lambda e, gc=gc, ed=ed, w0e=w0e: e.scalar_tensor_tensor(out=gc[:, 0:1], in0=ed[:, 0:1], scalar=w0e,
                                                                                     in1=gc[:, 0:1], op0=ALU.mult, op1=ALU.add),
                     reads=[B_ed, B_gc, self.B_cvp, self.B_cvm], writes=[B_gc])
                P.op("dve", lambda e, gc=gc, ed=ed, w2e=w2e: e.scalar_tensor_tensor(out=gc[:, 511:512], in0=ed[:, 1:2], scalar=w2e,
                                                                                     in1=gc[:, 511:512], op0=ALU.mult, op1=ALU.add),
                     reads=[B_ed, B_gc, self.B_cvp, self.B_cvm], writes=[B_gc])
                if cc >= 1:
                    self._ffn_tail(c - 1, cc - 1)
            self._ffn_tail(hh * 11 + 10, 10)
            di = 0
            for grp in ((0, 1, 2), (3, 4, 5), (6, 7)):
                ng = len(grp)
                for cc in range(11):
                    c = hh * 11 + cc
                    wd, B_wd = self.wdp[di % 8], self.B_wdp[di % 8]
                    di += 1
                    P.dma("sp", lambda e, wd=wd, c=c, grp=grp, ng=ng: e.dma_start(
                        out=wd[:, 0:ng * 128], in_=w["wd_s"].ap()[c, :, grp[0] * 128:(grp[0] + ng) * 128]),
                        reads=[w["B_wd_s"]], writes=[B_wd])

                    def mm(e, wd=wd, cc=cc, ng=ng):
                        for mi in range(ng):
                            ins = e.matmul(self.ps[:, 4 + mi, :], lhsT=wd[:, mi * 128:(mi + 1) * 128], rhs=self.hT[:, cc, :],
                                           start=(cc == 0), stop=(cc == 10))
                        return ins
                    P.op("pe", mm, reads=[B_wd, self.B_hT[cc]], writes=[self.B_ps[4 + mi] for mi in range(ng)])
                for mi, m in enumerate(grp):
                    eng = "dve" if mi % 2 == 0 else "dve"
                    P.op(eng, lambda e, mi=mi, m=m: e.tensor_tensor(self.z[:, m, :], self.ps[:, 4 + mi, :], self.z[:, m, :], ALU.add),
                         reads=[self.B_ps[4 + mi], self.B_z[m]], writes=[self.B_z[m]])
        is_f32 = (dst_dt == F32)

        def emit_out(m, y2, B_y2):
            bank = 2 + (m % 2)
            if is_f32:
                pv = self.ps[:, bank, :].rearrange("p (a n) -> p a n", a=4)
                idn, B_idn = self.ident_f, self.B_idf
            else:
                pv = self.psb(bank)[:, 0:4, :]
                idn, B_idn = self.ident_b, self.B_idb

            def tr(e):
                for i in range(4):
                    ins = e.transpose(pv[:, i, :], y2[:, i * 128:(i + 1) * 128], idn[:])
                return ins
            P.op("pe", tr, reads=[B_y2, B_idn], writes=[self.B_ps[bank]])
            if is_f32:
                ot, B_ot = self.ot[m % 2], self.B_ot[m % 2]
                otv = ot[:]
            else:
                ot, B_ot = self.ot[m % 2], self.B_ot[m % 2]
                otv = ot[:].rearrange("p a n -> p (a n)").bitcast(BF16)[:, 0:512].rearrange("p (a n) -> p a n", a=4)
            P.op("act", lambda e: e.activation(out=otv, in_=pv, func=AF.Copy), reads=[self.B_ps[bank]], writes=[B_ot])
            dview = dst[row0:row0 + 512, m * 128:(m + 1) * 128].rearrange("(a p) n -> p a n", p=128)
            P.dma("sp", lambda e: e.dma_start(out=dview, in_=otv), reads=[B_ot], dwrites=[B_dstbuf], owner=B_ot,
                  is_out=True)

        self._ln_out_queue = []

        def dst_fn(m):
            y2 = self.y2[m % 2]
            if is_f32:
                return y2[:]
            return y2[:].bitcast(BF16)[:, 0:512]

        self.layer_norm_with_out(1, 512, dst_fn, lambda m: self.B_y2[m % 2], emit_out, is_f32)

    def _ffn_tail(self, c, cc):
        P = self.P
        ub = (2, 3, 5)[c % 3]
        gc, B_gc = self.gcs[c % 2], self.B_gcs[c % 2]
        tg, B_tg = self.tgs[c % 2], self.B_tgs[c % 2]
        P.op("act", lambda e: e.activation(out=tg[:], in_=gc[:], func=AF.Gelu_apprx_tanh), reads=[B_gc], writes=[B_tg])
        P.op("dve", lambda e: e.tensor_tensor(self.hT[:, cc, :], self.ps[:, ub, :], tg[:], ALU.mult),
             reads=[self.B_ps[ub], B_tg], writes=[self.B_hT[cc]])

    def layer_norm_with_out(self, which, ncol, dst_fn, B_dst_fn, emit_out, is_f32):
        P = self.P
        gcol = 0 if which == 0 else 16
        bcol = gcol + 8
        for m in range(8):
            zb, B_zb = self.zb[m % 2], self.B_zb[m % 2]
            zq, B_zq = self.zq[m % 2], self.B_zq[m % 2]
            P.op("act", lambda e, zb=zb, m=m: e.activation(out=zb[:, 0:ncol], in_=self.z[:, m, 0:ncol], func=AF.Copy),
                 reads=[self.B_z[m]], writes=[B_zb])
            P.op("act", lambda e, zq=zq, m=m: e.activation(out=zq[:, 0:ncol], in_=self.z[:, m, 0:ncol], func=AF.Square),
                 reads=[self.B_z[m]], writes=[B_zq])
            P.op("pe", lambda e, zb=zb, m=m: e.matmul(self.ps[:, 0, 0:ncol], lhsT=self.ones_b[:], rhs=zb[:, 0:ncol],
                                                      start=(m == 0), stop=(m == 7)),
                 reads=[B_zb, self.B_ones], writes=[self.B_ps[0]])
            P.op("pe", lambda e, zq=zq, m=m: e.matmul(self.ps[:, 1, 0:ncol], lhsT=self.ones_b[:], rhs=zq[:, 0:ncol],
                                                      start=(m == 0), stop=(m == 7)),
                 reads=[B_zq, self.B_ones], writes=[self.B_ps[1]])
        mean, rstd = self.mean, self.rstd
        P.op("dve", lambda e: e.tensor_scalar(mean[:, 0:ncol], self.ps[:, 0, 0:ncol], 1.0 / D, None, ALU.mult),
             reads=[self.B_ps[0]], writes=[self.B_mean])
        P.op("dve", lambda e: e.tensor_tensor(rstd[:, 0:ncol], mean[:, 0:ncol], mean[:, 0:ncol], ALU.mult),
             reads=[self.B_mean], writes=[self.B_rstd])
        P.op("dve", lambda e: e.scalar_tensor_tensor(out=rstd[:, 0:ncol], in0=self.ps[:, 1, 0:ncol], scalar=1.0 / D,
                                                     in1=rstd[:, 0:ncol], op0=ALU.mult, op1=ALU.subtract),
             reads=[self.B_ps[1], self.B_rstd], writes=[self.B_rstd])
        self.rsqrt_act(rstd[:, 0:ncol], rstd[:, 0:ncol], self.B_rstd, self.B_rstd, 1.0, 3, ncol)
        for m in range(8):
            ta, B_ta = self.tmpa[m % 2], self.B_tmpa[m % 2]
            P.op("dve", lambda e, ta=ta, m=m: e.tensor_tensor(ta[:, 0:ncol], self.z[:, m, 0:ncol], mean[:, 0:ncol], ALU.subtract),
                 reads=[self.B_z[m], self.B_mean], writes=[B_ta])
            P.op("dve", lambda e, ta=ta: e.tensor_tensor(ta[:, 0:ncol], ta[:, 0:ncol], rstd[:, 0:ncol], ALU.mult),
                 reads=[B_ta, self.B_rstd], writes=[B_ta])
            dst = dst_fn(m)
            Bd = B_dst_fn(m)
            P.op("act", lambda e, ta=ta, m=m, dst=dst: e.activation(out=dst, in_=ta[:, 0:ncol], func=AF.Identity,
                                                                    scale=self.lnp[:, gcol + m:gcol + m + 1],
                                                                    bias=self.lnp[:, bcol + m:bcol + m + 1]),
                 reads=[B_ta, self.B_lnp], writes=[Bd])
            emit_out(m, dst, Bd)

    def half_layer(self, l, own_off, src, src_is_f32, B_src, dst, dst_dt, B_dstbuf, pending):
        P = self.P
        self.own_off = own_off
        self.setup_layer(l, own_off)
        self.kv_phase(l, src, src_is_f32, B_src, pending)
        while pending:
            pending.pop(0)()
        tl = (own_off - 1) % NT
        tr_ = (own_off + 32) % NT
        self.att_block(l, src, src_is_f32, B_src, [tl, tr_], 127, 2, self.XH, self.B_XH)
        for k in range(8):
            tiles = [own_off + 4 * k + i for i in range(4)]
            XB, B_XB = self.XB[k % 2], self.B_XB[k % 2]
            self.att_block(l, src, src_is_f32, B_src, tiles, 0, 512, XB, B_XB)
            P.op("dve", lambda e, XB=XB, k=k: e.tensor_copy(self.LC[:, :, k:k + 1], XB[:, :, 511:512]),
                 reads=[B_XB], writes=[self.B_LC[k]])
            if k >= 1:
                self._ffn(l, k - 1, dst, dst_dt, B_dstbuf)
        self._ffn(l, 7, dst, dst_dt, B_dstbuf)

    def _ffn(self, l, k, dst, dst_dt, B_dstbuf):
        X, B_X = self.XB[k % 2], self.B_XB[k % 2]
        if k == 0:
            hl, B_hl = self.XH[:, :, 0:1], self.B_XH
        else:
            hl, B_hl = self.LC[:, :, k - 1:k], self.B_LC[k - 1]
        if k == 7:
            hr, B_hr = self.XH[:, :, 1:2], self.B_XH
        else:
            hr, B_hr = self.XB[(k + 1) % 2][:, :, 0:1], self.B_XB[(k + 1) % 2]
        self.fence(self.att_bufs + self.ffn_bufs)
        self.ffn_block(l, k, X, B_X, hl, B_hl, hr, B_hr, k == 0, k == 7, dst, dst_dt, B_dstbuf, k * 512)
        self.fence(self.att_bufs + self.ffn_bufs)

    def build(self):
        self.setup_consts()
        if not self.fused:
            pending = self.convert_weights(0)
            for _ in range(8):
                pending.pop(0)()
            self.half_layer(0, 0, self.x_in, True, None, self.out, F32, self.B_out, pending)
        else:
            pending = self.convert_weights(0) + self.convert_weights(1)
            for _ in range(8):
                pending.pop(0)()
            xm = self.xmid.ap()
            self.half_layer(0, 32, self.x_in, True, None, xm[S // 2:S, :], BF16, self.B_xmid, pending)
            self.half_layer(0, 0, self.x_in, True, None, xm[0:S // 2, :], BF16, self.B_xmid, pending)
            self.half_layer(1, 0, xm, False, self.B_xmid, self.out, F32, self.B_out, pending)
        self.P.emit()
        self.st.close()
        return self.nc


HPERM = [0, 4, 1, 5, 2, 6, 3, 7]


def _t5_bucket(rel):
    half = 16
    max_exact = 8
    bucket = np.where(rel > 0, half, 0)
    rp = np.abs(rel)
    rpf = np.maximum(rp, 1).astype(np.float32)
    large = max_exact + (np.log(rpf / np.float32(max_exact)) / np.float32(math.log(128 / max_exact))
                         * np.float32(half - max_exact)).astype(np.int32)
    large = np.minimum(large, half - 1)
    return bucket + np.where(rp < max_exact, rp, large)


def _bias_table(rel_bias):
    k = np.arange(128)[:, None, None]
    j = np.arange(3)[None, :, None]
    q = np.arange(128)[None, None, :]
    rel = (j - 1) * 128 + k - q
    idx = _t5_bucket(rel)
    tab = np.asarray(rel_bias, np.float32)[idx]
    tab = np.where((np.abs(rel) <= 128)[..., None], tab, np.float32(NEG))
    tab = np.ascontiguousarray(tab.transpose(0, 1, 3, 2))
    return tab.reshape(128, 3 * 8 * 128).astype(np.float32)


def _rope_table(half):
    tok = (np.arange(S) + half * (S // 2)) % S
    row = (tok // 64).astype(np.float32)
    col = (tok % 64).astype(np.float32)
    inv = (np.float32(10000.0) ** (-np.arange(0, 32, 2, dtype=np.float32) / np.float32(32))).astype(np.float32)
    ang = np.concatenate([row[:, None] * inv, col[:, None] * inv], axis=-1).astype(np.float32)
    cs = np.concatenate([np.cos(ang), np.sin(ang)], axis=-1).astype(np.float32)
    return np.ascontiguousarray(cs.reshape(NT, 128, 64).transpose(1, 0, 2))


def _layer_params(inp, l):
    f = lambda a: np.ascontiguousarray(np.asarray(a, np.float32))
    w_in = np.asarray(inp["w_in"][l], np.float32)
    qa = w_in[:, 0:512].reshape(D, 8, 64)[:, HPERM, :].reshape(D, 512)
    qb = w_in[:, 768:1280].reshape(D, 8, 64)[:, HPERM, :].reshape(D, 512)
    wq = np.concatenate([qa, qb], axis=1)
    wkv = np.concatenate([w_in[:, 512:640], w_in[:, 640:768], w_in[:, 1280:1408], w_in[:, 1408:1536]], axis=1)
    w_out = np.asarray(inp["w_out"][l], np.float32)
    wo = np.concatenate([w_out[0:512].reshape(8, 64, D)[HPERM].reshape(512, D),
                         w_out[512:1024].reshape(8, 64, D)[HPERM].reshape(512, D)], axis=0)
    ga = np.asarray(inp["out_norm_a"][l], np.float32).reshape(8, 64)[HPERM].reshape(512)
    gb = np.asarray(inp["out_norm_b"][l], np.float32).reshape(8, 64)[HPERM].reshape(512)
    gout = np.concatenate([ga, gb]).reshape(8, 128).T
    qn = np.asarray(inp["q_norm"][l], np.float32)
    kn = np.asarray(inp["k_norm"][l], np.float32)
    nrm = np.concatenate([qn[0::2], qn[1::2], kn[0::2], kn[1::2]])[None, :]
    fm = lambda v: np.asarray(v, np.float32).reshape(8, 128).T
    lnp = np.concatenate([fm(inp["ln1_g"][l]), fm(inp["ln1_b"][l]), fm(inp["ln2_g"][l]), fm(inp["ln2_b"][l])], axis=1)
    cw = np.asarray(inp["conv_w"][l], np.float32)
    cb = np.asarray(inp["conv_b"][l], np.float32)
    cv = np.stack([cw[0], cw[1], cw[2], cb], axis=-1).reshape(NFC, 128, 4).transpose(1, 0, 2).reshape(128, NFC * 4)
    sink = np.asarray(inp["sink"][l], np.float32)[None, :]
    return dict(wq=f(wq), wkv=f(wkv), wout=f(wo), wg=f(inp["w_gate"][l]), wu=f(inp["w_up"][l]), wd=f(inp["w_down"][l]),
                nrm=f(nrm), gout=f(gout), lnp=f(lnp), cvp=f(cv), sink=f(sink))


def _core_consts(inp, half):
    jm = np.zeros((128, 4), np.float32)
    if half == 0:
        jm[:, 0] = NEG; jm[:, 1] = 0.0; jm[:, 2] = 0.0; jm[:, 3] = 1.0
    else:
        jm[:, 0] = 0.0; jm[:, 1] = NEG; jm[:, 2] = 1.0; jm[:, 3] = 0.0
    return dict(biasT=_bias_table(inp["rel_bias"]), cs=_rope_table(half), jm=jm, ident=np.eye(128, dtype=np.float32))


def _layout_x(xb, half):
    if half == 0:
        return np.ascontiguousarray(xb)
    return np.ascontiguousarray(np.concatenate([xb[S // 2:], xb[:S // 2]], axis=0))


_NC_CACHE = {}


def _get_nc(n_layers, fused):
    key = (n_layers, fused)
    if key not in _NC_CACHE:
        _NC_CACHE[key] = Builder(n_layers, fused).build()
    return _NC_CACHE[key]


FUSED = True


def kernel(**inp):
    x = np.asarray(inp["x"], np.float32)
    B = x.shape[0]
    lp = [_layer_params(inp, l) for l in range(2)]
    cc = [_core_consts(inp, h) for h in range(2)]
    if FUSED:
        nc = _get_nc(2, True)
        in_maps = []
        for core in range(N_CORES):
            b, h = core // 2, core % 2
            m = {"xsrc": _layout_x(x[b], h)}
            for l in range(2):
                for k, v in lp[l].items():
                    m[f"{k}{l}"] = v
            m.update(cc[h])
            in_maps.append(m)
        res = run_bass_kernel_spmd(nc, in_maps, core_ids=list(range(N_CORES)))
        out = np.empty((B, S, D), np.float32)
        for core in range(N_CORES):
            b, h = core // 2, core % 2
            out[b, h * (S // 2):(h + 1) * (S // 2)] = res.results[core]["out"]
        return out
    nc = _get_nc(1, False)
    cur = x
    for l in range(2):
        in_maps = []
        for core in range(N_CORES):
            b, h = core // 2, core % 2
            m = {"xsrc": _layout_x(cur[b], h)}
            for k, v in lp[l].items():
                m[f"{k}0"] = v
            m.update(cc[h])
            in_maps.append(m)
        res = run_bass_kernel_spmd(nc, in_maps, core_ids=list(range(N_CORES)))
        nxt = np.empty((B, S, D), np.float32)
        for core in range(N_CORES):
            b, h = core // 2, core % 2
            nxt[b, h * (S // 2):(h + 1) * (S // 2)] = res.results[core]["out"]
        cur = nxt
    return cur
```

```python
import math
from contextlib import ExitStack
import numpy as np
import concourse.bass as bass
import concourse.mybir as mybir
from concourse.bass_utils import run_bass_kernel_spmd

F32 = mybir.dt.float32
BF16 = mybir.dt.bfloat16
AF = mybir.ActivationFunctionType
ALU = mybir.AluOpType
AX = mybir.AxisListType

D = 1024
S = 8192
NT = 64
NB = 4
DFF = 2816
NFC = 22
ALPHA = 4.0 ** 0.25
RMS_EPS = 1e-6
LN_EPS = 1e-5
NEG = -30000.0
N_CORES = 8


class Buf:
    __slots__ = ("name", "w", "r", "dsem", "dcnt")

    def __init__(self, name):
        self.name = name
        self.w = []
        self.r = []
        self.dsem = None
        self.dcnt = 0


class Prog:
    CE = ("pe", "act", "dve", "pool")
    ENG = ("pe", "act", "dve", "pool", "sp")

    def __init__(self, nc, stack):
        self.nc = nc
        self.stack = stack
        self.ops = {e: [] for e in self.ENG}
        self.cnt = {e: 0 for e in self.CE}
        self.esem = {e: stack.enter_context(nc.semaphore("s_" + e)) for e in self.CE}
        self.seen = {e: {} for e in self.ENG}
        self.nsem = 4
        self.out_tokens = []

    def _mk_sem(self, name):
        self.nsem += 1
        return self.stack.enter_context(self.nc.semaphore(name))

    def _waits(self, eng, reads, writes, dwrites=()):
        deps = []
        for b in reads:
            deps.extend(b.w)
        for b in writes:
            deps.extend(b.w)
            deps.extend(b.r)
        for b in dwrites:
            deps.extend(b.r)
        best = {}
        for (s, v) in deps:
            k = id(s)
            if k not in best or best[k][1] < v:
                best[k] = (s, v)
        waits = []
        own = self.esem.get(eng) if eng == "pe" else None
        for k, (s, v) in best.items():
            if s is own:
                continue
            if self.seen[eng].get(k, 0) >= v:
                continue
            self.seen[eng][k] = v
            waits.append((s, v))
        return waits

    @staticmethod
    def _compact(lst):
        best = {}
        for (s, v) in lst:
            if id(s) not in best or best[id(s)][1] < v:
                best[id(s)] = (s, v)
        return list(best.values())

    def _commit(self, tok, reads, writes, dwrites=()):
        for b in dwrites:
            b.w.append(tok)
            if len(b.w) > 24:
                b.w = self._compact(b.w)
        for b in reads:
            b.r.append(tok)
            if len(b.r) > 24:
                best = {}
                for (s, v) in b.r:
                    if id(s) not in best or best[id(s)][1] < v:
                        best[id(s)] = (s, v)
                b.r = list(best.values())
        for b in writes:
            b.w = [tok]
            b.r = []

    def op(self, eng, fn, reads=(), writes=()):
        waits = self._waits(eng, reads, writes)
        self.cnt[eng] += 1
        tok = (self.esem[eng], self.cnt[eng])
        self.ops[eng].append((fn, waits, (self.esem[eng], 1)))
        self._commit(tok, reads, writes)
        return tok

    def dma(self, q, fn, reads=(), writes=(), owner=None, is_out=False, dwrites=()):
        if owner is None:
            owner = writes[0] if writes else (dwrites[0] if dwrites else reads[0])
        if owner.dsem is None:
            owner.dsem = self._mk_sem("d_" + owner.name)
        waits = self._waits(q, reads, writes, dwrites)
        owner.dcnt += 16
        tok = (owner.dsem, owner.dcnt)
        self.ops[q].append((fn, waits, (owner.dsem, 16)))
        self._commit(tok, reads, writes, dwrites)
        if is_out:
            self.out_tokens.append(tok)
        return tok

    def emit(self):
        nc = self.nc
        ws = []
        for (s, v) in self.out_tokens:
            k = id(s)
            if self.seen["sp"].get(k, 0) >= v:
                continue
            self.seen["sp"][k] = v
            ws.append((s, v))
        if ws:
            self.ops["sp"].append((None, ws, None))
        ops = self.ops

        def replay(e, lst):
            for (fn, waits, inc) in lst:
                for (s, v) in waits:
                    e.wait_ge(s, v)
                if fn is not None:
                    ins = fn(e)
                    ins.then_inc(inc[0], inc[1])

        with nc.Block() as block:
            @block.tensor
            def _(e):
                replay(e, ops["pe"])

            @block.scalar
            def _(e):
                replay(e, ops["act"])

            @block.vector
            def _(e):
                replay(e, ops["dve"])

            @block.gpsimd
            def _(e):
                replay(e, ops["pool"])

            @block.sync
            def _(e):
                replay(e, ops["sp"])


class Builder:
    def __init__(self, n_layers, fused, dbg=False):
        self.fused = fused
        self.n_layers = n_layers
        self.dbg = dbg
        self.nc = bass.Bass("TRN2", target_bir_lowering=False)
        self.st = ExitStack()
        self.P = Prog(self.nc, self.st)
        self.sb_bytes = 0
        self._declare_io()
        self._alloc()

    def sb(self, name, shape, dt):
        n = 1
        for s in shape[1:]:
            n *= s
        self.sb_bytes += n * (2 if dt == BF16 else 4)
        return self.st.enter_context(self.nc.sbuf_tensor(name, list(shape), dt))

    def din(self, name, shape, dt=F32):
        return self.nc.dram_tensor(name, list(shape), dt, kind="ExternalInput").ap()

    def _declare_io(self):
        nc = self.nc
        L = self.n_layers
        self.x_in = self.din("xsrc", [S, D])
        self.W = []
        for l in range(L):
            w = dict(
                wq=self.din(f"wq{l}", [D, D]), wkv=self.din(f"wkv{l}", [D, 512]),
                wout=self.din(f"wout{l}", [D, D]), wg=self.din(f"wg{l}", [D, DFF]),
                wu=self.din(f"wu{l}", [D, DFF]), wd=self.din(f"wd{l}", [DFF, D]),
                nrm=self.din(f"nrm{l}", [1, 128]), gout=self.din(f"gout{l}", [128, 8]),
                lnp=self.din(f"lnp{l}", [128, 32]), cvp=self.din(f"cvp{l}", [128, 88]),
                sink=self.din(f"sink{l}", [1, 8]),
            )
            w["wq_s"] = nc.dram_tensor(f"wq_s{l}", [128, 8, D], BF16)
            w["wkv_s"] = nc.dram_tensor(f"wkv_s{l}", [128, 8, 512], BF16)
            w["wout_s"] = nc.dram_tensor(f"wout_s{l}", [8, 128, 8, 128], BF16)
            w["wgu_s"] = nc.dram_tensor(f"wgu_s{l}", [NFC, 128, 8, 256], BF16)
            w["wd_s"] = nc.dram_tensor(f"wd_s{l}", [NFC, 128, D], BF16)
            for k in ("wq_s", "wkv_s", "wout_s", "wgu_s", "wd_s"):
                w["B_" + k] = Buf(f"{k}{l}")
            self.W.append(w)
        self.biasT_in = self.din("biasT", [128, 3072])
        self.cs_in = self.din("cs", [128, NT, 64])
        self.jm_in = self.din("jm", [128, 4])
        self.ident_in = self.din("ident", [128, 128])
        self.out = nc.dram_tensor("out", [S // 2, D], F32, kind="ExternalOutput").ap()
        if self.fused:
            self.xmid = nc.dram_tensor("xmid", [S, D], BF16)
            self.B_xmid = Buf("xmid")
        self.B_out = Buf("out")
        self.dbg_out = {}

    def _alloc(self):
        nc, sb = self.nc, self.sb
        self.KAT = sb("KAT", [128, S], BF16)
        self.VA = sb("VA", [128, NT, 2, 128], BF16)
        self.KBT = sb("KBT", [128, 36 * 128], BF16)
        self.VB = sb("VB", [128, 36, 2, 65], BF16)
        self.B_KA = [Buf(f"KA{t}") for t in range(NT)]
        self.B_KB = [Buf(f"KB{t}") for t in range(NT)]
        self.B_vones = Buf("vones")
        self.ident_f = sb("ident_f", [128, 128], F32); self.B_idf = Buf("ident_f")
        self.ident_b = sb("ident_b", [128, 128], BF16); self.B_idb = Buf("ident_b")
        self.ones_b = sb("ones_b", [128, 128], BF16); self.B_ones = Buf("ones_b")
        self.zeros = sb("zeros", [128, 128], F32); self.B_zeros = Buf("zeros")
        self.onesf = sb("onesf", [128, 64], F32); self.B_onesf = Buf("onesf")
        self.epsr = sb("epsr", [128, 4], F32); self.B_eps = Buf("epsr")
        self.biasT = sb("biasT_sb", [128, 3, 2, 512], BF16); self.B_biasT = Buf("biasT")
        self.jm = sb("jm_sb", [128, 4], F32); self.B_jm = Buf("jm")
        self.nrm = sb("nrm_sb", [128, 128], F32); self.B_nrm = Buf("nrm")
        self.gout = [sb(f"gout_sb{l}", [128, 8], F32) for l in range(self.n_layers)]
        self.B_gout = [Buf(f"gout{l}") for l in range(self.n_layers)]
        self.lnp = sb("lnp_sb", [128, 32], F32); self.B_lnp = Buf("lnp")
        self.cvp = sb("cvp_sb", [128, NFC, 4], F32); self.B_cvp = Buf("cvp")
        self.cvm = sb("cvm_sb", [128, NFC, 2], F32); self.B_cvm = Buf("cvm")
        self.esink = sb("esink", [128, 8], F32); self.B_esink = Buf("esink")
        self.esrow = sb("esrow", [128, 2, 512], F32); self.B_esrow = Buf("esrow")
        self.cst = [sb(f"cst{i}", [128, 64], F32) for i in range(2)]; self.B_cst = [Buf(f"cst{i}") for i in range(2)]
        self.xt = [sb(f"xt{i}", [128, D], BF16) for i in range(2)]; self.B_xt = [Buf(f"xt{i}") for i in range(2)]
        self.xTb = sb("xTb", [128, 8, 512], BF16); self.B_xTb = Buf("xTb")
        self.xTt = [sb(f"xTt{i}", [128, 8, 128], BF16) for i in range(3)]; self.B_xTt = [Buf(f"xTt{i}") for i in range(3)]
        self.wq = sb("wq_sb", [128, 8, 512], BF16); self.B_wq = Buf("wq")
        self.wkv = self.wq; self.B_wkv = self.B_wq
        self.fscr = sb("fscr", [128, 8], F32)
        self.t_sq = sb("t_sq", [128, 512], F32); self.B_tsq = Buf("t_sq")
        self.t_qn = sb("t_qn", [128, 512], F32); self.B_tqn = Buf("t_qn")
        self.t_ab = sb("t_ab", [128, 2, 256], F32); self.B_tab = Buf("t_ab")
        self.t_m = sb("t_m", [128, 4, 256], F32); self.B_tm = Buf("t_m")
        self.t_ss = sb("t_ss", [128, 16], F32); self.B_tss = Buf("t_ss")
        self.t_rs = sb("t_rs", [128, 16], F32); self.B_trs = Buf("t_rs")
        self.tq = dict(sq=self.t_sq, qn=self.t_qn, ab=self.t_ab, m=self.t_m, ss=self.t_ss, rs=self.t_rs,
                       B=[self.B_tsq, self.B_tqn, self.B_tab, self.B_tm, self.B_tss, self.B_trs])
        self.tk = []
        for i in range(2):
            self.tk.append(dict(sq=sb(f"k_sq{i}", [128, 128], F32), qn=sb(f"k_qn{i}", [128, 128], F32),
                                ab=sb(f"k_ab{i}", [128, 2, 64], F32), m=sb(f"k_m{i}", [128, 4, 64], F32),
                                ss=sb(f"k_ss{i}", [128, 4], F32), rs=sb(f"k_rs{i}", [128, 4], F32),
                                B=[Buf(f"k_t{i}_{j}") for j in range(6)]))
        self.qbf = [sb(f"qbf{i}", [128, 512], BF16) for i in range(2)]; self.B_qbf = [Buf(f"qbf{i}") for i in range(2)]
        self.kvb = [sb(f"kvb{i}", [128, 256], BF16) for i in range(2)]; self.B_kvb = [Buf(f"kvb{i}") for i in range(2)]
        self.B_kvb2 = [Buf(f"kvbB{i}") for i in range(2)]
        A1 = sb("A1", [128, 6144], BF16)
        self.QT = A1[:, 0:4096].rearrange("p (a n) -> p a n", a=8); self.B_QT = Buf("QT")
        self.PT = [A1[:, 4096 + i * 1024:5120 + i * 1024].rearrange("p (a n) -> p a n", a=2) for i in range(2)]
        self.B_PT = [Buf(f"PT{i}") for i in range(2)]
        self.hT = A1[:, 0:5632].rearrange("p (a n) -> p a n", a=11); self.B_hT = [Buf(f"hT{i}") for i in range(11)]
        A2 = sb("A2", [128, 10240], BF16)
        f32v = lambda a, b: A2[:, a:b].bitcast(F32)
        self.yT = A2[:, 0:4096].rearrange("p (a n) -> p a n", a=8); self.B_yT = Buf("yT")
        self.stt = [f32v(4096 + i * 2048, 6144 + i * 2048).rearrange("p (a n) -> p a n", a=2) for i in range(2)]
        self.B_stt = [Buf(f"stt{i}") for i in range(2)]
        self.dsb = [f32v(8192 + i * 1024, 9216 + i * 1024) for i in range(2)]; self.B_dsb = [Buf(f"dsb{i}") for i in range(2)]
        self.rr = self.dsb; self.B_rr = self.B_dsb
        self.wgu = [A2[:, i * 2048:(i + 1) * 2048].rearrange("p (a n) -> p a n", a=8) for i in range(3)]
        self.B_wgu = [Buf(f"wgu{i}") for i in range(3)]
        self.gcs = [f32v(6144 + i * 1024, 7168 + i * 1024) for i in range(2)]; self.B_gcs = [Buf(f"gcs{i}") for i in range(2)]
        self.tgs = [f32v(8192 + i * 1024, 9216 + i * 1024) for i in range(2)]; self.B_tgs = [Buf(f"tgs{i}") for i in range(2)]
        self.y2 = self.gcs; self.B_y2 = self.B_gcs
        self.ot = [t.rearrange("p (a n) -> p a n", a=4) for t in self.tgs]; self.B_ot = self.B_tgs
        self.att_bufs = [self.B_QT] + self.B_PT + [self.B_yT] + self.B_stt + self.B_dsb
        self.ffn_bufs = self.B_hT + self.B_wgu + self.B_gcs + self.B_tgs
        self.wo = [sb(f"wo{i}", [128, 8, 128], BF16) for i in range(4)]; self.B_wo = [Buf(f"wo{i}") for i in range(4)]
        self.z = sb("z", [128, 8, 512], F32); self.B_z = [Buf(f"z{m}") for m in range(8)]
        self.zb = [sb(f"zb{i}", [128, 512], BF16) for i in range(2)]; self.B_zb = [Buf(f"zb{i}") for i in range(2)]
        self.zq = [sb(f"zq{i}", [128, 512], BF16) for i in range(2)]; self.B_zq = [Buf(f"zq{i}") for i in range(2)]
        self.sqy = self.zq; self.B_sqy = self.B_zq
        self.mean = self.t_ab[:].rearrange("p a n -> p (a n)"); self.B_mean = self.B_tab
        self.rstd = self.t_m[:, 0:2, :].rearrange("p a n -> p (a n)"); self.B_rstd = self.B_tm
        self.tmpa = [self.t_sq, self.t_qn]; self.B_tmpa = [self.B_tsq, self.B_tqn]
        self.XB = [sb(f"XB{i}", [128, 8, 512], BF16) for i in range(2)]; self.B_XB = [Buf(f"XB{i}") for i in range(2)]
        self.XH = sb("XH", [128, 8, 2], BF16); self.B_XH = Buf("XH")
        self.LC = sb("LC", [128, 8, 8], BF16); self.B_LC = [Buf(f"LC{k}") for k in range(8)]
        self.HC = [sb(f"HC{i}", [128, 8, 2], BF16) for i in range(2)]; self.B_HC = [Buf(f"HC{i}") for i in range(2)]
        self.ed = [sb(f"ed{i}", [128, 2], F32) for i in range(3)]; self.B_ed = [Buf(f"ed{i}") for i in range(3)]
        self.wdp = [sb(f"wdp{i}", [128, 384], BF16) for i in range(8)]; self.B_wdp = [Buf(f"wdp{i}") for i in range(8)]
        self.cvt = [self.z[:, 2 * i:2 * i + 2, :].rearrange("p a n -> p (a n)") for i in range(2)]
        self.B_cvt = [[self.B_z[2 * i], self.B_z[2 * i + 1]] for i in range(2)]
        self.cvo = [self.z[:, 4 + i, :].bitcast(BF16) for i in range(2)]
        self.B_cvo = [[self.B_z[4 + i]] for i in range(2)]
        self.ps = self.st.enter_context(nc.psum_tensor("ps", [128, 8, 512], F32))
        self.B_ps = [Buf(f"ps{i}") for i in range(8)]
        self.B_gh = [Buf(f"gh{i}") for i in range(3)]

    def bslot(self, tk):
        sl = (tk - (self.own_off - 2)) % NT
        return sl if sl < 36 else None

    def fence(self, bufs):
        self.P.op("pool", lambda e: e.memset(self.fscr[:], 0.0), writes=list(bufs))

    def psb(self, b):
        return self.ps[:, b, :].bitcast(BF16).rearrange("p (a n) -> p a n", a=8)

    def setup_consts(self):
        P = self.P
        P.dma("sp", lambda e: e.dma_start(out=self.ident_f[:], in_=self.ident_in), writes=[self.B_idf])
        P.op("act", lambda e: e.activation(out=self.ident_b[:], in_=self.ident_f[:], func=AF.Copy),
             reads=[self.B_idf], writes=[self.B_idb])
        P.op("pool", lambda e: e.memset(self.ones_b[:], 1.0), writes=[self.B_ones])
        P.op("pool", lambda e: e.memset(self.zeros[:], 0.0), writes=[self.B_zeros])
        P.op("pool", lambda e: e.memset(self.onesf[:], 1.0), writes=[self.B_onesf])
        P.op("pool", lambda e: e.memset(self.epsr[:, 0:1], RMS_EPS), writes=[self.B_eps])
        P.op("pool", lambda e: e.memset(self.epsr[:, 1:2], math.log(0.125)), writes=[self.B_eps])
        P.op("pool", lambda e: e.memset(self.epsr[:, 2:3], 0.0), writes=[self.B_eps])
        P.op("pool", lambda e: e.memset(self.epsr[:, 3:4], LN_EPS), writes=[self.B_eps])
        P.op("pool", lambda e: e.memset(self.VA[:, :, :, 64:128], 1.0), writes=[self.B_vones])
        P.op("pool", lambda e: e.memset(self.VB[:, :, :, 64:65], 1.0), writes=[self.B_vones])
        P.dma("pool", lambda e: e.dma_start(out=self.biasT[:].rearrange("p a b n -> p (a b n)"), in_=self.biasT_in),
              writes=[self.B_biasT])
        P.dma("sp", lambda e: e.dma_start(out=self.jm[:], in_=self.jm_in), writes=[self.B_jm])

    def setup_layer(self, l, own_off):
        P = self.P
        w = self.W[l]
        P.dma("sp", lambda e: e.dma_start(out=self.nrm[:], in_=w["nrm"].partition_broadcast(128).rearrange("p a n -> p (a n)")), writes=[self.B_nrm])
        P.dma("sp", lambda e: e.dma_start(out=self.lnp[:], in_=w["lnp"]), writes=[self.B_lnp])
        P.dma("sp", lambda e: e.dma_start(out=self.cvp[:].rearrange("p c k -> p (c k)"), in_=w["cvp"]), writes=[self.B_cvp])
        P.dma("sp", lambda e: e.dma_start(out=self.esink[:], in_=w["sink"].partition_broadcast(128).rearrange("p a n -> p (a n)")), writes=[self.B_esink])
        P.op("act", lambda e: e.activation(out=self.esink[:], in_=self.esink[:], func=AF.Exp),
             reads=[self.B_esink], writes=[self.B_esink])
        for kvh in range(2):
            for c in range(4):
                h = kvh * 4 + c
                P.op("dve", lambda e, kvh=kvh, c=c, h=h: e.tensor_scalar(
                    self.esrow[:, kvh, c * 128:(c + 1) * 128], self.zeros[:, :],
                    self.esink[:, h:h + 1], None, ALU.add),
                    reads=[self.B_esink, self.B_zeros], writes=[self.B_esrow])
        jl = 2 if own_off == 0 else 3
        jr = 3 if own_off == 0 else 2
        P.op("dve", lambda e: e.tensor_scalar(self.cvm[:, :, 0], self.cvp[:, :, 0], self.jm[:, jl:jl + 1], None, ALU.mult),
             reads=[self.B_cvp, self.B_jm], writes=[self.B_cvm])
        P.op("dve", lambda e: e.tensor_scalar(self.cvm[:, :, 1], self.cvp[:, :, 2], self.jm[:, jr:jr + 1], None, ALU.mult),
             reads=[self.B_cvp, self.B_jm], writes=[self.B_cvm])

    def convert_weights(self, l):
        P = self.P
        w = self.W[l]
        steps = []

        def cast_dma(dst_ap, src_ap, B):
            P.dma("pool", lambda e: e.dma_start(out=dst_ap, in_=src_ap), dwrites=[B])

        for kc in range(8):
            steps.append(lambda kc=kc: cast_dma(w["wkv_s"].ap()[:, kc, :], w["wkv"][kc * 128:(kc + 1) * 128, :], w["B_wkv_s"]))
        for kc in range(8):
            steps.append(lambda kc=kc: cast_dma(w["wq_s"].ap()[:, kc, :], w["wq"][kc * 128:(kc + 1) * 128, :], w["B_wq_s"]))

        def wout_step(c):
            i = c % 2
            if c == 0:
                P.dma("sp", lambda e: e.dma_start(out=self.gout[l][:], in_=w["gout"]), writes=[self.B_gout[l]])
            P.dma("sp", lambda e: e.dma_start(out=self.cvt[i], in_=w["wout"][c * 128:(c + 1) * 128, :]),
                  writes=self.B_cvt[i])
            P.op("dve", lambda e: e.tensor_scalar(self.cvo[i], self.cvt[i], self.gout[l][:, c:c + 1], None, ALU.mult),
                 reads=self.B_cvt[i] + [self.B_gout[l]], writes=self.B_cvo[i])
            P.dma("sp", lambda e: e.dma_start(out=w["wout_s"].ap()[:, :, c, :].rearrange("m p n -> p m n"),
                                              in_=self.cvo[i].rearrange("p (m n) -> p m n", m=8)),
                  reads=self.B_cvo[i], dwrites=[w["B_wout_s"]], owner=w["B_wout_s"])
        for c in range(8):
            steps.append(lambda c=c: wout_step(c))
        for c in range(NFC):
            def gu(c=c):
                cast_dma(w["wgu_s"].ap()[c, :, :, 0:128],
                         w["wg"][:, c * 128:(c + 1) * 128].rearrange("(kc p) n -> p kc n", p=128), w["B_wgu_s"])
                cast_dma(w["wgu_s"].ap()[c, :, :, 128:256],
                         w["wu"][:, c * 128:(c + 1) * 128].rearrange("(kc p) n -> p kc n", p=128), w["B_wgu_s"])
            steps.append(gu)
        for c in range(NFC):
            steps.append(lambda c=c: cast_dma(w["wd_s"].ap()[c], w["wd"][c * 128:(c + 1) * 128, :], w["B_wd_s"]))
        return steps

    def load_xT(self, src, src_is_f32, B_src, t, dst_ap, B_dst, slot):
        P = self.P
        xt, B_xt = self.xt[slot], self.B_xt[slot]
        rd = [B_src] if B_src is not None else []
        if src_is_f32:
            P.dma("pool", lambda e: e.dma_start(out=xt[:], in_=src[t * 128:(t + 1) * 128, :]), reads=rd, writes=[B_xt])
        else:
            P.dma("sp", lambda e: e.dma_start(out=xt[:], in_=src[t * 128:(t + 1) * 128, :]), reads=rd, writes=[B_xt])
        bank = 6 + slot
        pv = self.psb(bank)

        def tr(e):
            for c in range(8):
                ins = e.transpose(pv[:, c, :], xt[:, c * 128:(c + 1) * 128], self.ident_b[:])
            return ins
        P.op("pe", tr, reads=[B_xt, self.B_idb], writes=[self.B_ps[bank]])
        P.op("act", lambda e: e.activation(out=dst_ap, in_=pv, func=AF.Copy), reads=[self.B_ps[bank]], writes=[B_dst])

    def norm_rope(self, src, B_src, nh, goff, cs, B_cs, scale, dst, B_dst, T=None):
        P = self.P
        W = nh * 64
        if T is None:
            T = self.tq
        B_tsq, B_tqn, B_tab, B_tm, B_tss, B_trs = T["B"]
        sq = T["sq"][:, 0:W]
        P.op("act", lambda e: e.activation(out=sq, in_=src, func=AF.Square), reads=[B_src], writes=[B_tsq])
        ss = T["ss"][:, 0:nh]
        P.op("dve", lambda e: e.tensor_reduce(out=ss, in_=sq.rearrange("p (h d) -> p h d", h=nh), axis=AX.X, op=ALU.add),
             reads=[B_tsq], writes=[B_tss])
        P.op("act", lambda e: e.activation(out=ss, in_=ss, func=AF.Ln, scale=1.0 / 64.0, bias=self.epsr[:, 0:1]),
             reads=[B_tss, self.B_eps], writes=[B_tss])
        rs = T["rs"][:, 0:nh]
        bcol = 1 if scale != 1.0 else 2
        P.op("act", lambda e: e.activation(out=rs, in_=ss, func=AF.Exp, scale=-0.5, bias=self.epsr[:, bcol:bcol + 1]),
             reads=[B_tss, self.B_eps], writes=[B_trs])
        qn = T["qn"][:, 0:W].rearrange("p (h d) -> p h d", h=nh)
        P.op("dve", lambda e: e.tensor_tensor(qn, src.rearrange("p (h d) -> p h d", h=nh),
                                              rs.unsqueeze(2).to_broadcast([128, nh, 64]), ALU.mult),
             reads=[B_src, B_trs], writes=[B_tqn])
        x0 = qn[:, :, 0::2]
        x1 = qn[:, :, 1::2]
        ge = self.nrm[:, goff:goff + 32].unsqueeze(1).to_broadcast([128, nh, 32])
        go = self.nrm[:, goff + 32:goff + 64].unsqueeze(1).to_broadcast([128, nh, 32])
        cosb = cs[:, 0:32].unsqueeze(1).to_broadcast([128, nh, 32])
        sinb = cs[:, 32:64].unsqueeze(1).to_broadcast([128, nh, 32])
        a = T["ab"][:, 0, 0:nh * 32].rearrange("p (h d) -> p h d", h=nh)
        b = T["ab"][:, 1, 0:nh * 32].rearrange("p (h d) -> p h d", h=nh)
        P.op("dve", lambda e: e.tensor_tensor(a, x0, ge, ALU.mult), reads=[B_tqn, self.B_nrm], writes=[B_tab])
        P.op("dve", lambda e: e.tensor_tensor(b, x1, go, ALU.mult), reads=[B_tqn, self.B_nrm], writes=[B_tab])
        m = [T["m"][:, i, 0:nh * 32].rearrange("p (h d) -> p h d", h=nh) for i in range(4)]
        P.op("dve", lambda e: e.tensor_tensor(m[0], a, cosb, ALU.mult), reads=[B_tab, B_cs], writes=[B_tm])
        P.op("dve", lambda e: e.tensor_tensor(m[1], b, sinb, ALU.mult), reads=[B_tab, B_cs], writes=[B_tm])
        P.op("dve", lambda e: e.tensor_tensor(m[2], a, sinb, ALU.mult), reads=[B_tab, B_cs], writes=[B_tm])
        P.op("dve", lambda e: e.tensor_tensor(m[3], b, cosb, ALU.mult), reads=[B_tab, B_cs], writes=[B_tm])
        d3 = dst.rearrange("p (h d) -> p h d", h=nh)
        P.op("dve", lambda e: e.tensor_tensor(d3[:, :, 0:32], m[0], m[1], ALU.subtract), reads=[B_tm], writes=[B_dst])
        P.op("dve", lambda e: e.tensor_tensor(d3[:, :, 32:64], m[2], m[3], ALU.add), reads=[B_tm], writes=[B_dst])

    def load_cs(self, t):
        i = t % 2
        self.P.dma("sp", lambda e: e.dma_start(out=self.cst[i][:], in_=self.cs_in[:, t, :]), writes=[self.B_cst[i]])
        return self.cst[i], self.B_cst[i]

    def kv_phase(self, l, src, src_is_f32, B_src, pending):
        P = self.P
        w = self.W[l]
        P.dma("sp", lambda e: e.dma_start(out=self.wkv[:], in_=w["wkv_s"].ap()), reads=[w["B_wkv_s"]], writes=[self.B_wkv])
        def stage1a(t):
            self.load_xT(src, src_is_f32, B_src, t, self.xTt[t % 3][:], self.B_xTt[t % 3], t % 2)
            for _ in range(3):
                if pending:
                    pending.pop(0)()

        def stage1b(t):
            slot = t % 2
            xT = self.xTt[t % 3]
            bk = 4 + slot
            pk = self.ps[:, bk, :]

            def mm(e, xT=xT, pk=pk):
                for c in range(8):
                    ins = e.matmul(pk, lhsT=xT[:, c, :], rhs=self.wkv[:, c, :], start=(c == 0), stop=(c == 7))
                return ins
            P.op("pe", mm, reads=[self.B_xTt[t % 3], self.B_wkv], writes=[self.B_ps[bk]])

        def stage2a(t):
            slot = t % 2
            bk = 4 + slot
            pk = self.ps[:, bk, :]
            kvb, B_kvb, B_kvb2 = self.kvb[slot], self.B_kvb[slot], self.B_kvb2[slot]
            P.op("act", lambda e: e.activation(out=kvb[:, 128:256], in_=pk[:, 256:384], func=AF.Copy),
                 reads=[self.B_ps[bk]], writes=[B_kvb2])
            P.op("act", lambda e: e.activation(out=self.VA[:, t, :, 0:64],
                                               in_=pk[:, 128:256].rearrange("p (h d) -> p h d", h=2), func=AF.Copy),
                 reads=[self.B_ps[bk], self.B_vones], writes=[self.B_KA[t]])
            sl = self.bslot(t)
            if sl is not None:
                P.op("act", lambda e: e.activation(out=self.VB[:, sl, :, 0:64],
                                                   in_=pk[:, 384:512].rearrange("p (h d) -> p h d", h=2), func=AF.Copy),
                     reads=[self.B_ps[bk], self.B_vones], writes=[self.B_KB[t]])
            cs, B_cs = self.load_cs(t)
            self.norm_rope(pk[:, 0:128], self.B_ps[bk], 2, 64, cs, B_cs, 1.0, kvb[:, 0:128], B_kvb, T=self.tk[slot])

        def stage2b(t):
            slot = t % 2
            kvb, B_kvb, B_kvb2 = self.kvb[slot], self.B_kvb[slot], self.B_kvb2[slot]
            sl = self.bslot(t)
            bt = 2 + slot
            pt = self.psb(bt)

            def trk(e):
                e.transpose(pt[:, 0, :], kvb[:, 0:128], self.ident_b[:])
                return e.transpose(pt[:, 1, :], kvb[:, 128:256], self.ident_b[:])
            P.op("pe", trk, reads=[B_kvb, B_kvb2, self.B_idb], writes=[self.B_ps[bt]])
            P.op("dve", lambda e: e.tensor_copy(self.KAT[:, t * 128:(t + 1) * 128], pt[:, 0, :]),
                 reads=[self.B_ps[bt]], writes=[self.B_KA[t]])
            if sl is not None:
                P.op("dve", lambda e: e.tensor_copy(self.KBT[:, sl * 128:(sl + 1) * 128], pt[:, 1, :]),
                     reads=[self.B_ps[bt]], writes=[self.B_KB[t]])

        stage1a(0)
        stage1a(1)
        stage1b(0)
        for t in range(NT):
            stage2a(t)
            if t + 2 < NT:
                stage1a(t + 2)
            if t + 1 < NT:
                stage1b(t + 1)
            stage2b(t)

    def att_block(self, l, src, src_is_f32, B_src, tiles, col0, ncol, x1_dst, B_x1):
        P = self.P
        w = self.W[l]
        nt = len(tiles)
        ntok = nt * 128
        for i, t in enumerate(tiles):
            self.load_xT(src, src_is_f32, B_src, t, self.xTb[:, :, i * 128:(i + 1) * 128], self.B_xTb, i % 2)
        for half in range(2):
            P.dma("sp", lambda e, half=half: e.dma_start(out=self.wq[:], in_=w["wq_s"].ap()[:, :, half * 512:(half + 1) * 512]),
                  reads=[w["B_wq_s"]], writes=[self.B_wq])
            for i, t in enumerate(tiles):
                bk = 4 + i
                pq = self.ps[:, bk, :]

                def mm(e, i=i, pq=pq):
                    for c in range(8):
                        ins = e.matmul(pq, lhsT=self.xTb[:, c, i * 128:(i + 1) * 128], rhs=self.wq[:, c, :],
                                       start=(c == 0), stop=(c == 7))
                    return ins
                P.op("pe", mm, reads=[self.B_xTb, self.B_wq], writes=[self.B_ps[bk]])
            for i, t in enumerate(tiles):
                bk = 4 + i
                pq = self.ps[:, bk, :]
                qb, B_qb = self.qbf[i % 2], self.B_qbf[i % 2]
                if half == 0:
                    cs, B_cs = self.load_cs(t)
                    self.norm_rope(pq, self.B_ps[bk], 8, 0, cs, B_cs, 0.125, qb[:, 0:512], B_qb)
                else:
                    P.op("act", lambda e, qb=qb, pq=pq: e.activation(out=qb[:, 0:512], in_=pq, func=AF.Copy),
                         reads=[self.B_ps[bk]], writes=[B_qb])
                bt = 2 + (i % 2)
                pt = self.psb(bt)

                def trq(e, qb=qb, pt=pt):
                    for c in range(4):
                        ins = e.transpose(pt[:, c, :], qb[:, c * 128:(c + 1) * 128], self.ident_b[:])
                    return ins
                P.op("pe", trq, reads=[B_qb, self.B_idb], writes=[self.B_ps[bt]])
                P.op("dve", lambda e, i=i, half=half, pt=pt: e.tensor_copy(
                    self.QT[:, half * 4:(half + 1) * 4, i * 128:(i + 1) * 128], pt[:, 0:4, :]),
                    reads=[self.B_ps[bt]], writes=[self.B_QT])
        for c in range(4):
            accb = (4, 5) if c % 2 == 0 else (6, 7)

            def qk(kt, c=c):
                sb0 = 2 * (kt % 2)

                def f(e):
                    e.matmul(self.ps[:, sb0, 0:ntok], lhsT=self.KAT[0:64, kt * 128:(kt + 1) * 128],
                             rhs=self.QT[0:64, c, 0:ntok], start=True, stop=True)
                    return e.matmul(self.ps[:, sb0 + 1, 0:ntok], lhsT=self.KAT[64:128, kt * 128:(kt + 1) * 128],
                                    rhs=self.QT[64:128, c, 0:ntok], start=True, stop=True)
                P.op("pe", f, reads=[self.B_KA[kt], self.B_QT], writes=[self.B_ps[sb0], self.B_ps[sb0 + 1]])

            def ex(kt):
                sb0 = 2 * (kt % 2)
                pt = self.PT[kt % 2]
                P.op("act", lambda e: e.activation(out=pt[:, :, 0:ntok], in_=self.ps[:, sb0:sb0 + 2, 0:ntok], func=AF.Exp),
                     reads=[self.B_ps[sb0], self.B_ps[sb0 + 1]], writes=[self.B_PT[kt % 2]])

            def pv(kt, accb=accb):
                pt = self.PT[kt % 2]

                def f(e):
                    e.matmul(self.ps[:, accb[0], 0:ntok], lhsT=self.VA[:, kt, 0, :], rhs=pt[:, 0, 0:ntok],
                             start=(kt == 0), stop=(kt == NT - 1))
                    return e.matmul(self.ps[:, accb[1], 0:ntok], lhsT=self.VA[:, kt, 1, :], rhs=pt[:, 1, 0:ntok],
                                    start=(kt == 0), stop=(kt == NT - 1))
                P.op("pe", f, reads=[self.B_KA[kt], self.B_PT[kt % 2]], writes=[self.B_ps[accb[0]], self.B_ps[accb[1]]])

            qk(0)
            qk(1)
            for kt in range(NT):
                ex(kt)
                pv(kt)
                if kt + 2 < NT:
                    qk(kt + 2)
            for kvh in range(2):
                self.finalize_head(accb[kvh], kvh, None, self.yT[kvh * 64:(kvh + 1) * 64, c, 0:ntok], ntok, kvh)
        def b_main(i, t):
            accb = (4, 5) if i % 2 == 0 else (6, 7)
            nbrs = [((t - 1) % NT, 0), (t, 1), ((t + 1) % NT, 2)]

            def qk(jj):
                tk, jidx = nbrs[jj]
                sb0 = 2 * (jj % 2)
                ks = self.bslot(tk)
                assert ks is not None

                def f(e):
                    e.matmul(self.ps[:, sb0, :], lhsT=self.KBT[0:64, ks * 128:(ks + 1) * 128],
                             rhs=self.QT[0:64, 4:8, i * 128:(i + 1) * 128], start=True, stop=True)
                    return e.matmul(self.ps[:, sb0 + 1, :], lhsT=self.KBT[64:128, ks * 128:(ks + 1) * 128],
                                    rhs=self.QT[64:128, 4:8, i * 128:(i + 1) * 128], start=True, stop=True)
                P.op("pe", f, reads=[self.B_KB[tk], self.B_QT], writes=[self.B_ps[sb0], self.B_ps[sb0 + 1]])

            def chain(jj):
                tk, jidx = nbrs[jj]
                sb0 = 2 * (jj % 2)
                st = self.stt[jj % 2]
                B_st = self.B_stt[jj % 2]
                ks = self.bslot(tk)
                for kvh in range(2):
                    P.op("dve", lambda e, kvh=kvh: e.scalar_tensor_tensor(
                        out=st[:, kvh, :], in0=self.ps[:, sb0 + kvh, :], scalar=0.125, in1=self.biasT[:, jidx, kvh, :],
                        op0=ALU.mult, op1=ALU.add),
                        reads=[self.B_ps[sb0 + kvh], self.B_biasT], writes=[B_st])
                jcol = None
                if jidx == 0 and t == 0:
                    jcol = 0
                elif jidx == 0 and t == 32:
                    jcol = 1
                elif jidx == 2 and t == 63:
                    jcol = 0
                elif jidx == 2 and t == 31:
                    jcol = 1
                if jcol is not None:
                    P.op("dve", lambda e: e.tensor_scalar(
                        st[:].rearrange("p a n -> p (a n)"), st[:].rearrange("p a n -> p (a n)"),
                        self.jm[:, jcol:jcol + 1], None, ALU.add),
                        reads=[B_st, self.B_jm], writes=[B_st])
                pt = self.PT[jj % 2]
                P.op("act", lambda e: e.activation(out=pt[:], in_=st[:], func=AF.Exp),
                     reads=[B_st], writes=[self.B_PT[jj % 2]])

                def g(e):
                    e.matmul(self.ps[0:65, accb[0], :], lhsT=self.VB[:, ks, 0, :], rhs=pt[:, 0, :],
                             start=(jj == 0), stop=(jj == 2))
                    return e.matmul(self.ps[0:65, accb[1], :], lhsT=self.VB[:, ks, 1, :], rhs=pt[:, 1, :],
                                    start=(jj == 0), stop=(jj == 2))
                P.op("pe", g, reads=[self.B_KB[tk], self.B_PT[jj % 2]], writes=[self.B_ps[accb[0]], self.B_ps[accb[1]]])

            qk(0)
            qk(1)
            chain(0)
            qk(2)
            chain(1)
            chain(2)

        def b_fin(i):
            accb = (4, 5) if i % 2 == 0 else (6, 7)
            self.finalize_b_pair(accb, [self.yT[kvh * 64:(kvh + 1) * 64, 4:8, i * 128:(i + 1) * 128] for kvh in range(2)])

        for i, t in enumerate(tiles):
            b_main(i, t)
            if i >= 1:
                b_fin(i - 1)
        b_fin(nt - 1)
        self.out_stage(l, col0, ncol)
        self.layer_norm(0, col0, ncol, lambda m: x1_dst[:, m, :], B_x1, BF16)

    def finalize_head(self, bank, kvh, sink_kvh, y_dst, n, slot):
        P = self.P
        dsb, B_dsb = self.dsb[slot], self.B_dsb[slot]
        if sink_kvh is not None:
            P.op("dve", lambda e: e.tensor_tensor(dsb[0:64, 0:n], self.ps[64:128, bank, 0:n], self.esrow[64:128, sink_kvh, 0:n], ALU.add),
                 reads=[self.B_ps[bank], self.B_esrow], writes=[B_dsb])
        else:
            P.op("act", lambda e: e.activation(out=dsb[0:64, 0:n], in_=self.ps[64:128, bank, 0:n], func=AF.Copy),
                 reads=[self.B_ps[bank]], writes=[B_dsb])
        if sink_kvh is not None:
            P.op("act", lambda e: e.activation(out=dsb[0:64, 0:n], in_=dsb[0:64, 0:n], func=AF.Ln), reads=[B_dsb], writes=[B_dsb])
            P.op("act", lambda e: e.activation(out=dsb[0:64, 0:n], in_=dsb[0:64, 0:n], func=AF.Exp, scale=-1.0),
                 reads=[B_dsb], writes=[B_dsb])
        else:
            P.op("dve", lambda e: e.reciprocal(dsb[0:64, 0:n], dsb[0:64, 0:n]), reads=[B_dsb], writes=[B_dsb])
        if len(y_dst.shape) == 3:
            in0 = self.ps[0:64, bank, 0:n].rearrange("p (a n) -> p a n", a=4)
            in1 = dsb[0:64, 0:n].rearrange("p (a n) -> p a n", a=4)
        else:
            in0 = self.ps[0:64, bank, 0:n]
            in1 = dsb[0:64, 0:n]
        P.op("dve", lambda e: e.tensor_tensor(y_dst, in0, in1, ALU.mult),
             reads=[self.B_ps[bank], B_dsb], writes=[self.B_yT])

    def finalize_b_pair(self, banks, y_dsts):
        P = self.P
        n = 512
        for kvh in range(2):
            dsb, B_dsb, bank = self.dsb[kvh], self.B_dsb[kvh], banks[kvh]
            P.op("dve", lambda e, dsb=dsb, bank=bank, kvh=kvh: e.tensor_tensor(
                dsb[64:65, 0:n], self.ps[64:65, bank, 0:n], self.esrow[64:65, kvh, 0:n], ALU.add),
                reads=[self.B_ps[bank], self.B_esrow], writes=[B_dsb])
        for kvh in range(2):
            dsb, B_dsb = self.dsb[kvh], self.B_dsb[kvh]
            P.op("act", lambda e, dsb=dsb: e.activation(out=dsb[64:65, 0:n], in_=dsb[64:65, 0:n], func=AF.Ln),
                 reads=[B_dsb], writes=[B_dsb])
            P.op("act", lambda e, dsb=dsb: e.activation(out=dsb[64:65, 0:n], in_=dsb[64:65, 0:n], func=AF.Exp, scale=-1.0),
                 reads=[B_dsb], writes=[B_dsb])
        for kvh in range(2):
            dsb, B_dsb = self.dsb[kvh], self.B_dsb[kvh]
            bb = 2 + kvh
            P.op("pe", lambda e, dsb=dsb, bb=bb: e.matmul(self.ps[0:64, bb, 0:n], lhsT=self.onesf[64:65, 0:64], rhs=dsb[64:65, 0:n],
                                                        start=True, stop=True),
                 reads=[B_dsb, self.B_onesf], writes=[self.B_ps[bb]])
        for kvh in range(2):
            dsb, B_dsb = self.dsb[kvh], self.B_dsb[kvh]
            bb = 2 + kvh
            P.op("act", lambda e, dsb=dsb, bb=bb: e.activation(out=dsb[0:64, 0:n], in_=self.ps[0:64, bb, 0:n], func=AF.Copy),
                 reads=[self.B_ps[bb]], writes=[B_dsb])
        for kvh in range(2):
            dsb, B_dsb, bank = self.dsb[kvh], self.B_dsb[kvh], banks[kvh]
            in0 = self.ps[0:64, bank, 0:n].rearrange("p (a n) -> p a n", a=4)
            in1 = dsb[0:64, 0:n].rearrange("p (a n) -> p a n", a=4)
            P.op("dve", lambda e, y=y_dsts[kvh], in0=in0, in1=in1: e.tensor_tensor(y, in0, in1, ALU.mult),
                 reads=[self.B_ps[bank], B_dsb], writes=[self.B_yT])

    def rsqrt_act(self, dst, src_ap, B_src, B_dst, scale, eps_col, ncol):
        P = self.P
        P.op("act", lambda e: e.activation(out=dst, in_=src_ap, func=AF.Ln, scale=scale, bias=self.epsr[:, eps_col:eps_col + 1]),
             reads=[B_src, self.B_eps], writes=[B_dst])
        P.op("act", lambda e: e.activation(out=dst, in_=dst, func=AF.Exp, scale=-0.5), reads=[B_dst], writes=[B_dst])

    def out_stage(self, l, col0, ncol):
        P = self.P
        w = self.W[l]
        cs = slice(col0, col0 + ncol)
        for g in range(2):
            bank = g
            for cc in range(4):
                c = g * 4 + cc
                sq, B_sq = self.sqy[cc % 2], self.B_sqy[cc % 2]
                P.op("act", lambda e, sq=sq, c=c: e.activation(out=sq[:, 0:ncol], in_=self.yT[:, c, cs], func=AF.Square),
                     reads=[self.B_yT], writes=[B_sq])
                P.op("pe", lambda e, sq=sq, cc=cc, bank=bank: e.matmul(self.ps[:, bank, 0:ncol], lhsT=self.ones_b[:], rhs=sq[:, 0:ncol],
                                                                    start=(cc == 0), stop=(cc == 3)),
                     reads=[B_sq, self.B_ones], writes=[self.B_ps[bank]])
            rr, B_rr = self.rr[g], self.B_rr[g]
            self.rsqrt_act(rr[:, 0:ncol], self.ps[:, bank, 0:ncol], self.B_ps[bank], B_rr, 1.0 / 512.0, 0, ncol)
            for cc in range(4):
                c = g * 4 + cc
                P.op("dve", lambda e, c=c, rr=rr: e.tensor_tensor(self.yT[:, c, cs], self.yT[:, c, cs], rr[:, 0:ncol], ALU.mult),
                     reads=[self.B_yT, B_rr], writes=[self.B_yT])
        for m in range(8):
            wo, B_wo = self.wo[m % 4], self.B_wo[m % 4]
            P.dma("sp", lambda e, wo=wo, m=m: e.dma_start(out=wo[:], in_=w["wout_s"].ap()[m]),
                  reads=[w["B_wout_s"]], writes=[B_wo])
            ba = 2 + (m % 2)

            def mm(e, wo=wo, ba=ba, m=m):
                for c in range(8):
                    ins = e.matmul(self.ps[:, ba, 0:ncol], lhsT=wo[:, c, :], rhs=self.yT[:, c, cs],
                                   start=(c == 0), stop=(c == 7))
                return ins
            P.op("pe", mm, reads=[B_wo, self.B_yT], writes=[self.B_ps[ba]])
            P.op("dve", lambda e, ba=ba, m=m: e.scalar_tensor_tensor(out=self.z[:, m, 0:ncol], in0=self.xTb[:, m, cs], scalar=ALPHA,
                                                                     in1=self.ps[:, ba, 0:ncol], op0=ALU.mult, op1=ALU.add),
                 reads=[self.B_xTb, self.B_ps[ba]], writes=[self.B_z[m]])

    def layer_norm(self, which, col0_unused, ncol, dst_fn, B_dst, out_dt):
        P = self.P
        gcol = 0 if which == 0 else 16
        bcol = gcol + 8
        for m in range(8):
            zb, B_zb = self.zb[m % 2], self.B_zb[m % 2]
            zq, B_zq = self.zq[m % 2], self.B_zq[m % 2]
            P.op("act", lambda e, zb=zb, m=m: e.activation(out=zb[:, 0:ncol], in_=self.z[:, m, 0:ncol], func=AF.Copy),
                 reads=[self.B_z[m]], writes=[B_zb])
            P.op("act", lambda e, zq=zq, m=m: e.activation(out=zq[:, 0:ncol], in_=self.z[:, m, 0:ncol], func=AF.Square),
                 reads=[self.B_z[m]], writes=[B_zq])
            P.op("pe", lambda e, zb=zb, m=m: e.matmul(self.ps[:, 0, 0:ncol], lhsT=self.ones_b[:], rhs=zb[:, 0:ncol],
                                                      start=(m == 0), stop=(m == 7)),
                 reads=[B_zb, self.B_ones], writes=[self.B_ps[0]])
            P.op("pe", lambda e, zq=zq, m=m: e.matmul(self.ps[:, 1, 0:ncol], lhsT=self.ones_b[:], rhs=zq[:, 0:ncol],
                                                      start=(m == 0), stop=(m == 7)),
                 reads=[B_zq, self.B_ones], writes=[self.B_ps[1]])
        mean, rstd = self.mean, self.rstd
        P.op("dve", lambda e: e.tensor_scalar(mean[:, 0:ncol], self.ps[:, 0, 0:ncol], 1.0 / D, None, ALU.mult),
             reads=[self.B_ps[0]], writes=[self.B_mean])
        P.op("dve", lambda e: e.tensor_tensor(rstd[:, 0:ncol], mean[:, 0:ncol], mean[:, 0:ncol], ALU.mult),
             reads=[self.B_mean], writes=[self.B_rstd])
        P.op("dve", lambda e: e.scalar_tensor_tensor(out=rstd[:, 0:ncol], in0=self.ps[:, 1, 0:ncol], scalar=1.0 / D,
                                                     in1=rstd[:, 0:ncol], op0=ALU.mult, op1=ALU.subtract),
             reads=[self.B_ps[1], self.B_rstd], writes=[self.B_rstd])
        self.rsqrt_act(rstd[:, 0:ncol], rstd[:, 0:ncol], self.B_rstd, self.B_rstd, 1.0, 3, ncol)
        for m in range(8):
            ta, B_ta = self.tmpa[m % 2], self.B_tmpa[m % 2]
            P.op("dve", lambda e, ta=ta, m=m: e.tensor_tensor(ta[:, 0:ncol], self.z[:, m, 0:ncol], mean[:, 0:ncol], ALU.subtract),
                 reads=[self.B_z[m], self.B_mean], writes=[B_ta])
            P.op("dve", lambda e, ta=ta: e.tensor_tensor(ta[:, 0:ncol], ta[:, 0:ncol], rstd[:, 0:ncol], ALU.mult),
                 reads=[B_ta, self.B_rstd], writes=[B_ta])
            dst = dst_fn(m)
            Bd = B_dst(m) if callable(B_dst) else B_dst
            P.op("act", lambda e, ta=ta, m=m, dst=dst: e.activation(out=dst, in_=ta[:, 0:ncol], func=AF.Identity,
                                                                    scale=self.lnp[:, gcol + m:gcol + m + 1],
                                                                    bias=self.lnp[:, bcol + m:bcol + m + 1]),
                 reads=[B_ta, self.B_lnp], writes=[Bd])

    def ffn_block(self, l, k, X, B_X, hl_ap, B_hl, hr_ap, B_hr, first, last, dst, dst_dt, B_dstbuf, row0):
        P = self.P
        w = self.W[l]
        HC, B_HC = self.HC[k % 2], self.B_HC[k % 2]
        P.op("dve", lambda e: e.tensor_copy(HC[:, :, 0:1], hl_ap), reads=[B_hl], writes=[B_HC])
        P.op("dve", lambda e: e.tensor_copy(HC[:, :, 1:2], hr_ap), reads=[B_hr], writes=[B_HC])
        for m in range(8):
            P.op("act", lambda e, m=m: e.activation(out=self.z[:, m, :], in_=X[:, m, :], func=AF.Copy, scale=ALPHA),
                 reads=[B_X], writes=[self.B_z[m]])
        for hh in range(2):
            for cc in range(11):
                c = hh * 11 + cc
                wg, B_wg = self.wgu[c % 3], self.B_wgu[c % 3]
                P.dma("sp", lambda e, wg=wg, c=c: e.dma_start(out=wg[:], in_=w["wgu_s"].ap()[c]),
                      reads=[w["B_wgu_s"]], writes=[B_wg])
                gb = (0, 1, 4)[c % 3]
                ub = (2, 3, 5)[c % 3]
                hcol = (c % 3) * 2

                def mm(e, wg=wg, gb=gb, ub=ub, hcol=hcol):
                    for kc in range(8):
                        e.matmul(self.ps[:, gb, :], lhsT=wg[:, kc, 0:128], rhs=X[:, kc, :], start=(kc == 0), stop=(kc == 7))
                    for kc in range(8):
                        e.matmul(self.ps[:, 7, hcol:hcol + 2], lhsT=wg[:, kc, 0:128], rhs=HC[:, kc, :], start=(kc == 0), stop=(kc == 7))
                    for kc in range(8):
                        ins = e.matmul(self.ps[:, ub, :], lhsT=wg[:, kc, 128:256], rhs=X[:, kc, :], start=(kc == 0), stop=(kc == 7))
                    return ins
                P.op("pe", mm, reads=[B_wg, B_X, B_HC], writes=[self.B_ps[gb], self.B_ps[ub], self.B_ps[7]])
                gc, B_gc = self.gcs[c % 2], self.B_gcs[c % 2]
                G = self.ps[:, gb, :]
                Gh = self.ps[:, 7, hcol:hcol + 2]
                w0 = self.cvp[:, c, 0:1]
                w1 = self.cvp[:, c, 1:2]
                w2 = self.cvp[:, c, 2:3]
                cb = self.cvp[:, c, 3:4]
                w0e = self.cvm[:, c, 0:1] if first else w0
                w2e = self.cvm[:, c, 1:2] if last else w2
                ed, B_ed = self.ed[c % 3], self.B_ed[c % 3]
                P.op("act", lambda e, ed=ed, Gh=Gh: e.activation(out=ed[:], in_=Gh, func=AF.Copy),
                     reads=[self.B_ps[7]], writes=[B_ed])
                P.op("act", lambda e, gc=gc, G=G, w1=w1, cb=cb: e.activation(out=gc[:], in_=G, func=AF.Identity, scale=w1, bias=cb),
                     reads=[self.B_ps[gb], self.B_cvp], writes=[B_gc])
                P.op("dve", lambda e, gc=gc, G=G, w0=w0: e.scalar_tensor_tensor(out=gc[:, 1:512], in0=G[:, 0:511], scalar=w0,
                                                                                 in1=gc[:, 1:512], op0=ALU.mult, op1=ALU.add),
                     reads=[self.B_ps[gb], B_gc, self.B_cvp], writes=[B_gc])
                P.op("dve", lambda e, gc=gc, G=G, w2=w2: e.scalar_tensor_tensor(out=gc[:, 0:511], in0=G[:, 1:512], scalar=w2,
                                                                                 in1=gc[:, 0:511], op0=ALU.mult, op1=ALU.add),
                     reads=[self.B_ps[gb], B_gc, self.B_cvp], writes=[B_gc])
                P.op("dve", lambda e, gc=gc, ed=ed, w0e=w0e: e.scalar_tensor_tensor(out=gc[:, 0:1], in0=ed[:, 0:1], scalar=w0e,
                                                                                     in1=gc[:, 0:1], op0=ALU.mult, op1=ALU.add),
                     reads=[B_ed, B_gc, self.B_cvp, self.B_cvm], writes=[B_gc])
                P.op("dve", lambda e, gc=gc, ed=ed, w2e=w2e: e.scalar_tensor_tensor(out=gc[:, 511:512], in0=ed[:, 1:2], scalar=w2e,
                                                                                     in1=gc[:, 511:512], op0=ALU.mult, op1=ALU.add),
                     reads=[B_ed, B_gc, self.B_cvp, self.B_cvm], writes=[B_gc])
                if cc >= 1:
                    self._ffn_tail(c - 1, cc - 1)
            self._ffn_tail(hh * 11 + 10, 10)
            di = 0
            for grp in ((0, 1, 2), (3, 4, 5), (6, 7)):
                ng = len(grp)
                for cc in range(11):
                    c = hh * 11 + cc
                    wd, B_wd = self.wdp[di % 8], self.B_wdp[di % 8]
                    di += 1
                    P.dma("sp", lambda e, wd=wd, c=c, grp=grp, ng=ng: e.dma_start(
                        out=wd[:, 0:ng * 128], in_=w["wd_s"].ap()[c, :, grp[0] * 128:(grp[0] + ng) * 128]),
                        reads=[w["B_wd_s"]], writes=[B_wd])

                    def mm(e, wd=wd, cc=cc, ng=ng):
                        for mi in range(ng):
                            ins = e.matmul(self.ps[:, 4 + mi, :], lhsT=wd[:, mi * 128:(mi + 1) * 128], rhs=self.hT[:, cc, :],
                                           start=(cc == 0), stop=(cc == 10))
                        return ins
                    P.op("pe", mm, reads=[B_wd, self.B_hT[cc]], writes=[self.B_ps[4 + mi] for mi in range(ng)])
                for mi, m in enumerate(grp):
                    eng = "dve" if mi % 2 == 0 else "dve"
                    P.op(eng, lambda e, mi=mi, m=m: e.tensor_tensor(self.z[:, m, :], self.ps[:, 4 + mi, :], self.z[:, m, :], ALU.add),
                         reads=[self.B_ps[4 + mi], self.B_z[m]], writes=[self.B_z[m]])
        is_f32 = (dst_dt == F32)

        def emit_out(m, y2, B_y2):
            bank = 2 + (m % 2)
            if is_f32:
                pv = self.ps[:, bank, :].rearrange("p (a n) -> p a n", a=4)
                idn, B_idn = self.ident_f, self.B_idf
            else:
                pv = self.psb(bank)[:, 0:4, :]
                idn, B_idn = self.ident_b, self.B_idb

            def tr(e):
                for i in range(4):
                    ins = e.transpose(pv[:, i, :], y2[:, i * 128:(i + 1) * 128], idn[:])
                return ins
            P.op("pe", tr, reads=[B_y2, B_idn], writes=[self.B_ps[bank]])
            if is_f32:
                ot, B_ot = self.ot[m % 2], self.B_ot[m % 2]
                otv = ot[:]
            else:
                ot, B_ot = self.ot[m % 2], self.B_ot[m % 2]
                otv = ot[:].rearrange("p a n -> p (a n)").bitcast(BF16)[:, 0:512].rearrange("p (a n) -> p a n", a=4)
            P.op("act", lambda e: e.activation(out=otv, in_=pv, func=AF.Copy), reads=[self.B_ps[bank]], writes=[B_ot])
            dview = dst[row0:row0 + 512, m * 128:(m + 1) * 128].rearrange("(a p) n -> p a n", p=128)
            P.dma("sp", lambda e: e.dma_start(out=dview, in_=otv), reads=[B_ot], dwrites=[B_dstbuf], owner=B_ot,
                  is_out=True)

        self._ln_out_queue = []

        def dst_fn(m):
            y2 = self.y2[m % 2]
            if is_f32:
                return y2[:]
            return y2[:].bitcast(BF16)[:, 0:512]

        self.layer_norm_with_out(1, 512, dst_fn, lambda m: self.B_y2[m % 2], emit_out, is_f32)

    def _ffn_tail(self, c, cc):
        P = self.P
        ub = (2, 3, 5)[c % 3]
        gc, B_gc = self.gcs[c % 2], self.B_gcs[c % 2]
        tg, B_tg = self.tgs[c % 2], self.B_tgs[c % 2]
        P.op("act", lambda e: e.activation(out=tg[:], in_=gc[:], func=AF.Gelu_apprx_tanh), reads=[B_gc], writes=[B_tg])
        P.op("dve", lambda e: e.tensor_tensor(self.hT[:, cc, :], self.ps[:, ub, :], tg[:], ALU.mult),
             reads=[self.B_ps[ub], B_tg], writes=[self.B_hT[cc]])

    def layer_norm_with_out(self, which, ncol, dst_fn, B_dst_fn, emit_out, is_f32):
        P = self.P
        gcol = 0 if which == 0 else 16
        bcol = gcol + 8
        for m in range(8):
            zb, B_zb = self.zb[m % 2], self.B_zb[m % 2]
            zq, B_zq = self.zq[m % 2], self.B_zq[m % 2]
            P.op("act", lambda e, zb=zb, m=m: e.activation(out=zb[:, 0:ncol], in_=self.z[:, m, 0:ncol], func=AF.Copy),
                 reads=[self.B_z[m]], writes=[B_zb])
            P.op("act", lambda e, zq=zq, m=m: e.activation(out=zq[:, 0:ncol], in_=self.z[:, m, 0:ncol], func=AF.Square),
                 reads=[self.B_z[m]], writes=[B_zq])
            P.op("pe", lambda e, zb=zb, m=m: e.matmul(self.ps[:, 0, 0:ncol], lhsT=self.ones_b[:], rhs=zb[:, 0:ncol],
                                                      start=(m == 0), stop=(m == 7)),
                 reads=[B_zb, self.B_ones], writes=[self.B_ps[0]])
            P.op("pe", lambda e, zq=zq, m=m: e.matmul(self.ps[:, 1, 0:ncol], lhsT=self.ones_b[:], rhs=zq[:, 0:ncol],
                                                      start=(m == 0), stop=(m == 7)),
                 reads=[B_zq, self.B_ones], writes=[self.B_ps[1]])
        mean, rstd = self.mean, self.rstd
        P.op("dve", lambda e: e.tensor_scalar(mean[:, 0:ncol], self.ps[:, 0, 0:ncol], 1.0 / D, None, ALU.mult),
             reads=[self.B_ps[0]], writes=[self.B_mean])
        P.op("dve", lambda e: e.tensor_tensor(rstd[:, 0:ncol], mean[:, 0:ncol], mean[:, 0:ncol], ALU.mult),
             reads=[self.B_mean], writes=[self.B_rstd])
        P.op("dve", lambda e: e.scalar_tensor_tensor(out=rstd[:, 0:ncol], in0=self.ps[:, 1, 0:ncol], scalar=1.0 / D,
                                                     in1=rstd[:, 0:ncol], op0=ALU.mult, op1=ALU.subtract),
             reads=[self.B_ps[1], self.B_rstd], writes=[self.B_rstd])
        self.rsqrt_act(rstd[:, 0:ncol], rstd[:, 0:ncol], self.B_rstd, self.B_rstd, 1.0, 3, ncol)
        for m in range(8):
            ta, B_ta = self.tmpa[m % 2], self.B_tmpa[m % 2]
            P.op("dve", lambda e, ta=ta, m=m: e.tensor_tensor(ta[:, 0:ncol], self.z[:, m, 0:ncol], mean[:, 0:ncol], ALU.subtract),
                 reads=[self.B_z[m], self.B_mean], writes=[B_ta])
            P.op("dve", lambda e, ta=ta: e.tensor_tensor(ta[:, 0:ncol], ta[:, 0:ncol], rstd[:, 0:ncol], ALU.mult),
                 reads=[B_ta, self.B_rstd], writes=[B_ta])
            dst = dst_fn(m)
            Bd = B_dst_fn(m)
            P.op("act", lambda e, ta=ta, m=m, dst=dst: e.activation(out=dst, in_=ta[:, 0:ncol], func=AF.Identity,
                                                                    scale=self.lnp[:, gcol + m:gcol + m + 1],
                                                                    bias=self.lnp[:, bcol + m:bcol + m + 1]),
                 reads=[B_ta, self.B_lnp], writes=[Bd])
            emit_out(m, dst, Bd)

    def half_layer(self, l, own_off, src, src_is_f32, B_src, dst, dst_dt, B_dstbuf, pending):
        P = self.P
        self.own_off = own_off
        self.setup_layer(l, own_off)
        self.kv_phase(l, src, src_is_f32, B_src, pending)
        while pending:
            pending.pop(0)()
        tl = (own_off - 1) % NT
        tr_ = (own_off + 32) % NT
        self.att_block(l, src, src_is_f32, B_src, [tl, tr_], 127, 2, self.XH, self.B_XH)
        for k in range(8):
            tiles = [own_off + 4 * k + i for i in range(4)]
            XB, B_XB = self.XB[k % 2], self.B_XB[k % 2]
            self.att_block(l, src, src_is_f32, B_src, tiles, 0, 512, XB, B_XB)
            P.op("dve", lambda e, XB=XB, k=k: e.tensor_copy(self.LC[:, :, k:k + 1], XB[:, :, 511:512]),
                 reads=[B_XB], writes=[self.B_LC[k]])
            if k >= 1:
                self._ffn(l, k - 1, dst, dst_dt, B_dstbuf)
        self._ffn(l, 7, dst, dst_dt, B_dstbuf)

    def _ffn(self, l, k, dst, dst_dt, B_dstbuf):
        X, B_X = self.XB[k % 2], self.B_XB[k % 2]
        if k == 0:
            hl, B_hl = self.XH[:, :, 0:1], self.B_XH
        else:
            hl, B_hl = self.LC[:, :, k - 1:k], self.B_LC[k - 1]
        if k == 7:
            hr, B_hr = self.XH[:, :, 1:2], self.B_XH
        else:
            hr, B_hr = self.XB[(k + 1) % 2][:, :, 0:1], self.B_XB[(k + 1) % 2]
        self.fence(self.att_bufs + self.ffn_bufs)
        self.ffn_block(l, k, X, B_X, hl, B_hl, hr, B_hr, k == 0, k == 7, dst, dst_dt, B_dstbuf, k * 512)
        self.fence(self.att_bufs + self.ffn_bufs)

    def build(self):
        self.setup_consts()
        if not self.fused:
            pending = self.convert_weights(0)
            for _ in range(8):
                pending.pop(0)()
            self.half_layer(0, 0, self.x_in, True, None, self.out, F32, self.B_out, pending)
        else:
            pending = self.convert_weights(0) + self.convert_weights(1)
            for _ in range(8):
                pending.pop(0)()
            xm = self.xmid.ap()
            self.half_layer(0, 32, self.x_in, True, None, xm[S // 2:S, :], BF16, self.B_xmid, pending)
            self.half_layer(0, 0, self.x_in, True, None, xm[0:S // 2, :], BF16, self.B_xmid, pending)
            self.half_layer(1, 0, xm, False, self.B_xmid, self.out, F32, self.B_out, pending)
        self.P.emit()
        self.st.close()
        return self.nc


HPERM = [0, 4, 1, 5, 2, 6, 3, 7]


def _t5_bucket(rel):
    half = 16
    max_exact = 8
    bucket = np.where(rel > 0, half, 0)
    rp = np.abs(rel)
    rpf = np.maximum(rp, 1).astype(np.float32)
    large = max_exact + (np.log(rpf / np.float32(max_exact)) / np.float32(math.log(128 / max_exact))
                         * np.float32(half - max_exact)).astype(np.int32)
    large = np.minimum(large, half - 1)
    return bucket + np.where(rp < max_exact, rp, large)


def _bias_table(rel_bias):
    k = np.arange(128)[:, None, None]
    j = np.arange(3)[None, :, None]
    q = np.arange(128)[None, None, :]
    rel = (j - 1) * 128 + k - q
    idx = _t5_bucket(rel)
    tab = np.asarray(rel_bias, np.float32)[idx]
    tab = np.where((np.abs(rel) <= 128)[..., None], tab, np.float32(NEG))
    tab = np.ascontiguousarray(tab.transpose(0, 1, 3, 2))
    return tab.reshape(128, 3 * 8 * 128).astype(np.float32)


def _rope_table(half):
    tok = (np.arange(S) + half * (S // 2)) % S
    row = (tok // 64).astype(np.float32)
    col = (tok % 64).astype(np.float32)
    inv = (np.float32(10000.0) ** (-np.arange(0, 32, 2, dtype=np.float32) / np.float32(32))).astype(np.float32)
    ang = np.concatenate([row[:, None] * inv, col[:, None] * inv], axis=-1).astype(np.float32)
    cs = np.concatenate([np.cos(ang), np.sin(ang)], axis=-1).astype(np.float32)
    return np.ascontiguousarray(cs.reshape(NT, 128, 64).transpose(1, 0, 2))


def _layer_params(inp, l):
    f = lambda a: np.ascontiguousarray(np.asarray(a, np.float32))
    w_in = np.asarray(inp["w_in"][l], np.float32)
    qa = w_in[:, 0:512].reshape(D, 8, 64)[:, HPERM, :].reshape(D, 512)
    qb = w_in[:, 768:1280].reshape(D, 8, 64)[:, HPERM, :].reshape(D, 512)
    wq = np.concatenate([qa, qb], axis=1)
    wkv = np.concatenate([w_in[:, 512:640], w_in[:, 640:768], w_in[:, 1280:1408], w_in[:, 1408:1536]], axis=1)
    w_out = np.asarray(inp["w_out"][l], np.float32)
    wo = np.concatenate([w_out[0:512].reshape(8, 64, D)[HPERM].reshape(512, D),
                         w_out[512:1024].reshape(8, 64, D)[HPERM].reshape(512, D)], axis=0)
    ga = np.asarray(inp["out_norm_a"][l], np.float32).reshape(8, 64)[HPERM].reshape(512)
    gb = np.asarray(inp["out_norm_b"][l], np.float32).reshape(8, 64)[HPERM].reshape(512)
    gout = np.concatenate([ga, gb]).reshape(8, 128).T
    qn = np.asarray(inp["q_norm"][l], np.float32)
    kn = np.asarray(inp["k_norm"][l], np.float32)
    nrm = np.concatenate([qn[0::2], qn[1::2], kn[0::2], kn[1::2]])[None, :]
    fm = lambda v: np.asarray(v, np.float32).reshape(8, 128).T
    lnp = np.concatenate([fm(inp["ln1_g"][l]), fm(inp["ln1_b"][l]), fm(inp["ln2_g"][l]), fm(inp["ln2_b"][l])], axis=1)
    cw = np.asarray(inp["conv_w"][l], np.float32)
    cb = np.asarray(inp["conv_b"][l], np.float32)
    cv = np.stack([cw[0], cw[1], cw[2], cb], axis=-1).reshape(NFC, 128, 4).transpose(1, 0, 2).reshape(128, NFC * 4)
    sink = np.asarray(inp["sink"][l], np.float32)[None, :]
    return dict(wq=f(wq), wkv=f(wkv), wout=f(wo), wg=f(inp["w_gate"][l]), wu=f(inp["w_up"][l]), wd=f(inp["w_down"][l]),
                nrm=f(nrm), gout=f(gout), lnp=f(lnp), cvp=f(cv), sink=f(sink))


def _core_consts(inp, half):
    jm = np.zeros((128, 4), np.float32)
    if half == 0:
        jm[:, 0] = NEG; jm[:, 1] = 0.0; jm[:, 2] = 0.0; jm[:, 3] = 1.0
    else:
        jm[:, 0] = 0.0; jm[:, 1] = NEG; jm[:, 2] = 1.0; jm[:, 3] = 0.0
    return dict(biasT=_bias_table(inp["rel_bias"]), cs=_rope_table(half), jm=jm, ident=np.eye(128, dtype=np.float32))


def _layout_x(xb, half):
    if half == 0:
        return np.ascontiguousarray(xb)
    return np.ascontiguousarray(np.concatenate([xb[S // 2:], xb[:S // 2]], axis=0))


_NC_CACHE = {}


def _get_nc(n_layers, fused):
    key = (n_layers, fused)
    if key not in _NC_CACHE:
        _NC_CACHE[key] = Builder(n_layers, fused).build()
    return _NC_CACHE[key]


FUSED = True


def kernel(**inp):
    x = np.asarray(inp["x"], np.float32)
    B = x.shape[0]
    lp = [_layer_params(inp, l) for l in range(2)]
    cc = [_core_consts(inp, h) for h in range(2)]
    if FUSED:
        nc = _get_nc(2, True)
        in_maps = []
        for core in range(N_CORES):
            b, h = core // 2, core % 2
            m = {"xsrc": _layout_x(x[b], h)}
            for l in range(2):
                for k, v in lp[l].items():
                    m[f"{k}{l}"] = v
            m.update(cc[h])
            in_maps.append(m)
        res = run_bass_kernel_spmd(nc, in_maps, core_ids=list(range(N_CORES)))
        out = np.empty((B, S, D), np.float32)
        for core in range(N_CORES):
            b, h = core // 2, core % 2
            out[b, h * (S // 2):(h + 1) * (S // 2)] = res.results[core]["out"]
        return out
    nc = _get_nc(1, False)
    cur = x
    for l in range(2):
        in_maps = []
        for core in range(N_CORES):
            b, h = core // 2, core % 2
            m = {"xsrc": _layout_x(cur[b], h)}
            for k, v in lp[l].items():
                m[f"{k}0"] = v
            m.update(cc[h])
            in_maps.append(m)
        res = run_bass_kernel_spmd(nc, in_maps, core_ids=list(range(N_CORES)))
        nxt = np.empty((B, S, D), np.float32)
        for core in range(N_CORES):
            b, h = core // 2, core % 2
            nxt[b, h * (S // 2):(h + 1) * (S // 2)] = res.results[core]["out"]
        cur = nxt
    return cur
```

```python
import math
from contextlib import ExitStack
import numpy as np
import concourse.bass as bass
import concourse.mybir as mybir
from concourse.bass_utils import run_bass_kernel_spmd

F32 = mybir.dt.float32
BF16 = mybir.dt.bfloat16
AF = mybir.ActivationFunctionType
ALU = mybir.AluOpType
AX = mybir.AxisListType

D = 1024
S = 8192
NT = 64
NB = 4
DFF = 2816
NFC = 22
ALPHA = 4.0 ** 0.25
RMS_EPS = 1e-6
LN_EPS = 1e-5
NEG = -30000.0
N_CORES = 8


class Buf:
    __slots__ = ("name", "w", "r", "dsem", "dcnt")

    def __init__(self, name):
        self.name = name
        self.w = []
        self.r = []
        self.dsem = None
        self.dcnt = 0


class Prog:
    CE = ("pe", "act", "dve", "pool")
    ENG = ("pe", "act", "dve", "pool", "sp")

    def __init__(self, nc, stack):
        self.nc = nc
        self.stack = stack
        self.ops = {e: [] for e in self.ENG}
        self.cnt = {e: 0 for e in self.CE}
        self.esem = {e: stack.enter_context(nc.semaphore("s_" + e)) for e in self.CE}
        self.seen = {e: {} for e in self.ENG}
        self.nsem = 4
        self.out_tokens = []

    def _mk_sem(self, name):
        self.nsem += 1
        return self.stack.enter_context(self.nc.semaphore(name))

    def _waits(self, eng, reads, writes, dwrites=()):
        deps = []
        for b in reads:
            deps.extend(b.w)
        for b in writes:
            deps.extend(b.w)
            deps.extend(b.r)
        for b in dwrites:
            deps.extend(b.r)
        best = {}
        for (s, v) in deps:
            k = id(s)
            if k not in best or best[k][1] < v:
                best[k] = (s, v)
        waits = []
        own = self.esem.get(eng) if eng == "pe" else None
        for k, (s, v) in best.items():
            if s is own:
                continue
            if self.seen[eng].get(k, 0) >= v:
                continue
            self.seen[eng][k] = v
            waits.append((s, v))
        return waits

    @staticmethod
    def _compact(lst):
        best = {}
        for (s, v) in lst:
            if id(s) not in best or best[id(s)][1] < v:
                best[id(s)] = (s, v)
        return list(best.values())

    def _commit(self, tok, reads, writes, dwrites=()):
        for b in dwrites:
            b.w.append(tok)
            if len(b.w) > 24:
                b.w = self._compact(b.w)
        for b in reads:
            b.r.append(tok)
            if len(b.r) > 24:
                best = {}
                for (s, v) in b.r:
                    if id(s) not in best or best[id(s)][1] < v:
                        best[id(s)] = (s, v)
                b.r = list(best.values())
        for b in writes:
            b.w = [tok]
            b.r = []

    def op(self, eng, fn, reads=(), writes=()):
        waits = self._waits(eng, reads, writes)
        self.cnt[eng] += 1
        tok = (self.esem[eng], self.cnt[eng])
        self.ops[eng].append((fn, waits, (self.esem[eng], 1)))
        self._commit(tok, reads, writes)
        return tok

    def dma(self, q, fn, reads=(), writes=(), owner=None, is_out=False, dwrites=()):
        if owner is None:
            owner = writes[0] if writes else (dwrites[0] if dwrites else reads[0])
        if owner.dsem is None:
            owner.dsem = self._mk_sem("d_" + owner.name)
        waits = self._waits(q, reads, writes, dwrites)
        owner.dcnt += 16
        tok = (owner.dsem, owner.dcnt)
        self.ops[q].append((fn, waits, (owner.dsem, 16)))
        self._commit(tok, reads, writes, dwrites)
        if is_out:
            self.out_tokens.append(tok)
        return tok

    def emit(self):
        nc = self.nc
        ws = []
        for (s, v) in self.out_tokens:
            k = id(s)
            if self.seen["sp"].get(k, 0) >= v:
                continue
            self.seen["sp"][k] = v
            ws.append((s, v))
        if ws:
            self.ops["sp"].append((None, ws, None))
        ops = self.ops

        def replay(e, lst):
            for (fn, waits, inc) in lst:
                for (s, v) in waits:
                    e.wait_ge(s, v)
                if fn is not None:
                    ins = fn(e)
                    ins.then_inc(inc[0], inc[1])

        with nc.Block() as block:
            @block.tensor
            def _(e):
                replay(e, ops["pe"])

            @block.scalar
            def _(e):
                replay(e, ops["act"])

            @block.vector
            def _(e):
                replay(e, ops["dve"])

            @block.gpsimd
            def _(e):
                replay(e, ops["pool"])

            @block.sync
            def _(e):
                replay(e, ops["sp"])


class Builder:
    def __init__(self, n_layers, fused, dbg=False):
        self.fused = fused
        self.n_layers = n_layers
        self.dbg = dbg
        self.nc = bass.Bass("TRN2", target_bir_lowering=False)
        self.st = ExitStack()
        self.P = Prog(self.nc, self.st)
        self.sb_bytes = 0
        self._declare_io()
        self._alloc()

    def sb(self, name, shape, dt):
        n = 1
        for s in shape[1:]:
            n *= s
        self.sb_bytes += n * (2 if dt == BF16 else 4)
        return self.st.enter_context(self.nc.sbuf_tensor(name, list(shape), dt))

    def din(self, name, shape, dt=F32):
        return self.nc.dram_tensor(name, list(shape), dt, kind="ExternalInput").ap()

    def _declare_io(self):
        nc = self.nc
        L = self.n_layers
        self.x_in = self.din("xsrc", [S, D])
        self.W = []
        for l in range(L):
            w = dict(
                wq=self.din(f"wq{l}", [D, D]), wkv=self.din(f"wkv{l}", [D, 512]),
                wout=self.din(f"wout{l}", [D, D]), wg=self.din(f"wg{l}", [D, DFF]),
                wu=self.din(f"wu{l}", [D, DFF]), wd=self.din(f"wd{l}", [DFF, D]),
                nrm=self.din(f"nrm{l}", [1, 128]), gout=self.din(f"gout{l}", [128, 8]),
                lnp=self.din(f"lnp{l}", [128, 32]), cvp=self.din(f"cvp{l}", [128, 88]),
                sink=self.din(f"sink{l}", [1, 8]),
            )
            w["wq_s"] = nc.dram_tensor(f"wq_s{l}", [128, 8, D], BF16)
            w["wkv_s"] = nc.dram_tensor(f"wkv_s{l}", [128, 8, 512], BF16)
            w["wout_s"] = nc.dram_tensor(f"wout_s{l}", [8, 128, 8, 128], BF16)
            w["wgu_s"] = nc.dram_tensor(f"wgu_s{l}", [NFC, 128, 8, 256], BF16)
            w["wd_s"] = nc.dram_tensor(f"wd_s{l}", [NFC, 128, D], BF16)
            for k in ("wq_s", "wkv_s", "wout_s", "wgu_s", "wd_s"):
                w["B_" + k] = Buf(f"{k}{l}")
            self.W.append(w)
        self.biasT_in = self.din("biasT", [128, 3072])
        self.cs_in = self.din("cs", [128, NT, 64])
        self.jm_in = self.din("jm", [128, 4])
        self.ident_in = self.din("ident", [128, 128])
        self.out = nc.dram_tensor("out", [S // 2, D], F32, kind="ExternalOutput").ap()
        if self.fused:
            self.xmid = nc.dram_tensor("xmid", [S, D], BF16)
            self.B_xmid = Buf("xmid")
        self.B_out = Buf("out")
        self.dbg_out = {}

    def _alloc(self):
        nc, sb = self.nc, self.sb
        self.KAT = sb("KAT", [128, S], BF16)
        self.VA = sb("VA", [128, NT, 2, 128], BF16)
        self.KBT = sb("KBT", [128, 36 * 128], BF16)
        self.VB = sb("VB", [128, 36, 2, 65], BF16)
        self.B_KA = [Buf(f"KA{t}") for t in range(NT)]
        self.B_KB = [Buf(f"KB{t}") for t in range(NT)]
        self.B_vones = Buf("vones")
        self.ident_f = sb("ident_f", [128, 128], F32); self.B_idf = Buf("ident_f")
        self.ident_b = sb("ident_b", [128, 128], BF16); self.B_idb = Buf("ident_b")
        self.ones_b = sb("ones_b", [128, 128], BF16); self.B_ones = Buf("ones_b")
        self.zeros = sb("zeros", [128, 128], F32); self.B_zeros = Buf("zeros")
        self.onesf = sb("onesf", [128, 64], F32); self.B_onesf = Buf("onesf")
        self.epsr = sb("epsr", [128, 4], F32); self.B_eps = Buf("epsr")
        self.biasT = sb("biasT_sb", [128, 3, 2, 512], BF16); self.B_biasT = Buf("biasT")
        self.jm = sb("jm_sb", [128, 4], F32); self.B_jm = Buf("jm")
        self.nrm = sb("nrm_sb", [128, 128], F32); self.B_nrm = Buf("nrm")
        self.gout = [sb(f"gout_sb{l}", [128, 8], F32) for l in range(self.n_layers)]
        self.B_gout = [Buf(f"gout{l}") for l in range(self.n_layers)]
        self.lnp = sb("lnp_sb", [128, 32], F32); self.B_lnp = Buf("lnp")
        self.cvp = sb("cvp_sb", [128, NFC, 4], F32); self.B_cvp = Buf("cvp")
        self.cvm = sb("cvm_sb", [128, NFC, 2], F32); self.B_cvm = Buf("cvm")
        self.esink = sb("esink", [128, 8], F32); self.B_esink = Buf("esink")
        self.esrow = sb("esrow", [128, 2, 512], F32); self.B_esrow = Buf("esrow")
        self.cst = [sb(f"cst{i}", [128, 64], F32) for i in range(2)]; self.B_cst = [Buf(f"cst{i}") for i in range(2)]
        self.xt = [sb(f"xt{i}", [128, D], BF16) for i in range(2)]; self.B_xt = [Buf(f"xt{i}") for i in range(2)]
        self.xTb = sb("xTb", [128, 8, 512], BF16); self.B_xTb = Buf("xTb")
        self.xTt = [sb(f"xTt{i}", [128, 8, 128], BF16) for i in range(3)]; self.B_xTt = [Buf(f"xTt{i}") for i in range(3)]
        self.wq = sb("wq_sb", [128, 8, 512], BF16); self.B_wq = Buf("wq")
        self.wkv = self.wq; self.B_wkv = self.B_wq
        self.fscr = sb("fscr", [128, 8], F32)
        self.t_sq = sb("t_sq", [128, 512], F32); self.B_tsq = Buf("t_sq")
        self.t_qn = sb("t_qn", [128, 512], F32); self.B_tqn = Buf("t_qn")
        self.t_ab = sb("t_ab", [128, 2, 256], F32); self.B_tab = Buf("t_ab")
        self.t_m = sb("t_m", [128, 4, 256], F32); self.B_tm = Buf("t_m")
        self.t_ss = sb("t_ss", [128, 16], F32); self.B_tss = Buf("t_ss")
        self.t_rs = sb("t_rs", [128, 16], F32); self.B_trs = Buf("t_rs")
        self.tq = dict(sq=self.t_sq, qn=self.t_qn, ab=self.t_ab, m=self.t_m, ss=self.t_ss, rs=self.t_rs,
                       B=[self.B_tsq, self.B_tqn, self.B_tab, self.B_tm, self.B_tss, self.B_trs])
        self.tk = []
        for i in range(2):
            self.tk.append(dict(sq=sb(f"k_sq{i}", [128, 128], F32), qn=sb(f"k_qn{i}", [128, 128], F32),
                                ab=sb(f"k_ab{i}", [128, 2, 64], F32), m=sb(f"k_m{i}", [128, 4, 64], F32),
                                ss=sb(f"k_ss{i}", [128, 4], F32), rs=sb(f"k_rs{i}", [128, 4], F32),
                                B=[Buf(f"k_t{i}_{j}") for j in range(6)]))
        self.qbf = [sb(f"qbf{i}", [128, 512], BF16) for i in range(2)]; self.B_qbf = [Buf(f"qbf{i}") for i in range(2)]
        self.kvb = [sb(f"kvb{i}", [128, 256], BF16) for i in range(2)]; self.B_kvb = [Buf(f"kvb{i}") for i in range(2)]
        self.B_kvb2 = [Buf(f"kvbB{i}") for i in range(2)]
        A1 = sb("A1", [128, 6144], BF16)
        self.QT = A1[:, 0:4096].rearrange("p (a n) -> p a n", a=8); self.B_QT = Buf("QT")
        self.PT = [A1[:, 4096 + i * 1024:5120 + i * 1024].rearrange("p (a n) -> p a n", a=2) for i in range(2)]
        self.B_PT = [Buf(f"PT{i}") for i in range(2)]
        self.hT = A1[:, 0:5632].rearrange("p (a n) -> p a n", a=11); self.B_hT = [Buf(f"hT{i}") for i in range(11)]
        A2 = sb("A2", [128, 10240], BF16)
        f32v = lambda a, b: A2[:, a:b].bitcast(F32)
        self.yT = A2[:, 0:4096].rearrange("p (a n) -> p a n", a=8); self.B_yT = Buf("yT")
        self.stt = [f32v(4096 + i * 2048, 6144 + i * 2048).rearrange("p (a n) -> p a n", a=2) for i in range(2)]
        self.B_stt = [Buf(f"stt{i}") for i in range(2)]
        self.dsb = [f32v(8192 + i * 1024, 9216 + i * 1024) for i in range(2)]; self.B_dsb = [Buf(f"dsb{i}") for i in range(2)]
        self.rr = self.dsb; self.B_rr = self.B_dsb
        self.wgu = [A2[:, i * 2048:(i + 1) * 2048].rearrange("p (a n) -> p a n", a=8) for i in range(3)]
        self.B_wgu = [Buf(f"wgu{i}") for i in range(3)]
        self.gcs = [f32v(6144 + i * 1024, 7168 + i * 1024) for i in range(2)]; self.B_gcs = [Buf(f"gcs{i}") for i in range(2)]
        self.tgs = [f32v(8192 + i * 1024, 9216 + i * 1024) for i in range(2)]; self.B_tgs = [Buf(f"tgs{i}") for i in range(2)]
        self.y2 = self.gcs; self.B_y2 = self.B_gcs
        self.ot = [t.rearrange("p (a n) -> p a n", a=4) for t in self.tgs]; self.B_ot = self.B_tgs
        self.att_bufs = [self.B_QT] + self.B_PT + [self.B_yT] + self.B_stt + self.B_dsb
        self.ffn_bufs = self.B_hT + self.B_wgu + self.B_gcs + self.B_tgs
        self.wo = [sb(f"wo{i}", [128, 8, 128], BF16) for i in range(4)]; self.B_wo = [Buf(f"wo{i}") for i in range(4)]
        self.z = sb("z", [128, 8, 512], F32); self.B_z = [Buf(f"z{m}") for m in range(8)]
        self.zb = [sb(f"zb{i}", [128, 512], BF16) for i in range(2)]; self.B_zb = [Buf(f"zb{i}") for i in range(2)]
        self.zq = [sb(f"zq{i}", [128, 512], BF16) for i in range(2)]; self.B_zq = [Buf(f"zq{i}") for i in range(2)]
        self.sqy = self.zq; self.B_sqy = self.B_zq
        self.mean = self.t_ab[:].rearrange("p a n -> p (a n)"); self.B_mean = self.B_tab
        self.rstd = self.t_m[:, 0:2, :].rearrange("p a n -> p (a n)"); self.B_rstd = self.B_tm
        self.tmpa = [self.t_sq, self.t_qn]; self.B_tmpa = [self.B_tsq, self.B_tqn]
        self.XB = [sb(f"XB{i}", [128, 8, 512], BF16) for i in range(2)]; self.B_XB = [Buf(f"XB{i}") for i in range(2)]
        self.XH = sb("XH", [128, 8, 2], BF16); self.B_XH = Buf("XH")
        self.LC = sb("LC", [128, 8, 8], BF16); self.B_LC = [Buf(f"LC{k}") for k in range(8)]
        self.HC = [sb(f"HC{i}", [128, 8, 2], BF16) for i in range(2)]; self.B_HC = [Buf(f"HC{i}") for i in range(2)]
        self.ed = [sb(f"ed{i}", [128, 2], F32) for i in range(3)]; self.B_ed = [Buf(f"ed{i}") for i in range(3)]
        self.wdp = [sb(f"wdp{i}", [128, 384], BF16) for i in range(8)]; self.B_wdp = [Buf(f"wdp{i}") for i in range(8)]
        self.cvt = [self.z[:, 2 * i:2 * i + 2, :].rearrange("p a n -> p (a n)") for i in range(2)]
        self.B_cvt = [[self.B_z[2 * i], self.B_z[2 * i + 1]] for i in range(2)]
        self.cvo = [self.z[:, 4 + i, :].bitcast(BF16) for i in range(2)]
        self.B_cvo = [[self.B_z[4 + i]] for i in range(2)]
        self.ps = self.st.enter_context(nc.psum_tensor("ps", [128, 8, 512], F32))
        self.B_ps = [Buf(f"ps{i}") for i in range(8)]
        self.B_gh = [Buf(f"gh{i}") for i in range(3)]

    def bslot(self, tk):
        sl = (tk - (self.own_off - 2)) % NT
        return sl if sl < 36 else None

    def fence(self, bufs):
        self.P.op("pool", lambda e: e.memset(self.fscr[:], 0.0), writes=list(bufs))

    def psb(self, b):
        return self.ps[:, b, :].bitcast(BF16).rearrange("p (a n) -> p a n", a=8)

    def setup_consts(self):
        P = self.P
        P.dma("sp", lambda e: e.dma_start(out=self.ident_f[:], in_=self.ident_in), writes=[self.B_idf])
        P.op("act", lambda e: e.activation(out=self.ident_b[:], in_=self.ident_f[:], func=AF.Copy),
             reads=[self.B_idf], writes=[self.B_idb])
        P.op("pool", lambda e: e.memset(self.ones_b[:], 1.0), writes=[self.B_ones])
        P.op("pool", lambda e: e.memset(self.zeros[:], 0.0), writes=[self.B_zeros])
        P.op("pool", lambda e: e.memset(self.onesf[:], 1.0), writes=[self.B_onesf])
        P.op("pool", lambda e: e.memset(self.epsr[:, 0:1], RMS_EPS), writes=[self.B_eps])
        P.op("pool", lambda e: e.memset(self.epsr[:, 1:2], math.log(0.125)), writes=[self.B_eps])
        P.op("pool", lambda e: e.memset(self.epsr[:, 2:3], 0.0), writes=[self.B_eps])
        P.op("pool", lambda e: e.memset(self.epsr[:, 3:4], LN_EPS), writes=[self.B_eps])
        P.op("pool", lambda e: e.memset(self.VA[:, :, :, 64:128], 1.0), writes=[self.B_vones])
        P.op("pool", lambda e: e.memset(self.VB[:, :, :, 64:65], 1.0), writes=[self.B_vones])
        P.dma("pool", lambda e: e.dma_start(out=self.biasT[:].rearrange("p a b n -> p (a b n)"), in_=self.biasT_in),
              writes=[self.B_biasT])
        P.dma("sp", lambda e: e.dma_start(out=self.jm[:], in_=self.jm_in), writes=[self.B_jm])

    def setup_layer(self, l, own_off):
        P = self.P
        w = self.W[l]
        P.dma("sp", lambda e: e.dma_start(out=self.nrm[:], in_=w["nrm"].partition_broadcast(128).rearrange("p a n -> p (a n)")), writes=[self.B_nrm])
        P.dma("sp", lambda e: e.dma_start(out=self.lnp[:], in_=w["lnp"]), writes=[self.B_lnp])
        P.dma("sp", lambda e: e.dma_start(out=self.cvp[:].rearrange("p c k -> p (c k)"), in_=w["cvp"]), writes=[self.B_cvp])
        P.dma("sp", lambda e: e.dma_start(out=self.esink[:], in_=w["sink"].partition_broadcast(128).rearrange("p a n -> p (a n)")), writes=[self.B_esink])
        P.op("act", lambda e: e.activation(out=self.esink[:], in_=self.esink[:], func=AF.Exp),
             reads=[self.B_esink], writes=[self.B_esink])
        for kvh in range(2):
            for c in range(4):
                h = kvh * 4 + c
                P.op("dve", lambda e, kvh=kvh, c=c, h=h: e.tensor_scalar(
                    self.esrow[:, kvh, c * 128:(c + 1) * 128], self.zeros[:, :],
                    self.esink[:, h:h + 1], None, ALU.add),
                    reads=[self.B_esink, self.B_zeros], writes=[self.B_esrow])
        jl = 2 if own_off == 0 else 3
        jr = 3 if own_off == 0 else 2
        P.op("dve", lambda e: e.tensor_scalar(self.cvm[:, :, 0], self.cvp[:, :, 0], self.jm[:, jl:jl + 1], None, ALU.mult),
             reads=[self.B_cvp, self.B_jm], writes=[self.B_cvm])
        P.op("dve", lambda e: e.tensor_scalar(self.cvm[:, :, 1], self.cvp[:, :, 2], self.jm[:, jr:jr + 1], None, ALU.mult),
             reads=[self.B_cvp, self.B_jm], writes=[self.B_cvm])

    def convert_weights(self, l):
        P = self.P
        w = self.W[l]
        steps = []

        def cast_dma(dst_ap, src_ap, B):
            P.dma("pool", lambda e: e.dma_start(out=dst_ap, in_=src_ap), dwrites=[B])

        for kc in range(8):
            steps.append(lambda kc=kc: cast_dma(w["wkv_s"].ap()[:, kc, :], w["wkv"][kc * 128:(kc + 1) * 128, :], w["B_wkv_s"]))
        for kc in range(8):
            steps.append(lambda kc=kc: cast_dma(w["wq_s"].ap()[:, kc, :], w["wq"][kc * 128:(kc + 1) * 128, :], w["B_wq_s"]))

        def wout_step(c):
            i = c % 2
            if c == 0:
                P.dma("sp", lambda e: e.dma_start(out=self.gout[l][:], in_=w["gout"]), writes=[self.B_gout[l]])
            P.dma("sp", lambda e: e.dma_start(out=self.cvt[i], in_=w["wout"][c * 128:(c + 1) * 128, :]),
                  writes=self.B_cvt[i])
            P.op("dve", lambda e: e.tensor_scalar(self.cvo[i], self.cvt[i], self.gout[l][:, c:c + 1], None, ALU.mult),
                 reads=self.B_cvt[i] + [self.B_gout[l]], writes=self.B_cvo[i])
            P.dma("sp", lambda e: e.dma_start(out=w["wout_s"].ap()[:, :, c, :].rearrange("m p n -> p m n"),
                                              in_=self.cvo[i].rearrange("p (m n) -> p m n", m=8)),
                  reads=self.B_cvo[i], dwrites=[w["B_wout_s"]], owner=w["B_wout_s"])
        for c in range(8):
            steps.append(lambda c=c: wout_step(c))
        for c in range(NFC):
            def gu(c=c):
                cast_dma(w["wgu_s"].ap()[c, :, :, 0:128],
                         w["wg"][:, c * 128:(c + 1) * 128].rearrange("(kc p) n -> p kc n", p=128), w["B_wgu_s"])
                cast_dma(w["wgu_s"].ap()[c, :, :, 128:256],
                         w["wu"][:, c * 128:(c + 1) * 128].rearrange("(kc p) n -> p kc n", p=128), w["B_wgu_s"])
            steps.append(gu)
        for c in range(NFC):
            steps.append(lambda c=c: cast_dma(w["wd_s"].ap()[c], w["wd"][c * 128:(c + 1) * 128, :], w["B_wd_s"]))
        return steps

    def load_xT(self, src, src_is_f32, B_src, t, dst_ap, B_dst, slot):
        P = self.P
        xt, B_xt = self.xt[slot], self.B_xt[slot]
        rd = [B_src] if B_src is not None else []
        if src_is_f32:
            P.dma("pool", lambda e: e.dma_start(out=xt[:], in_=src[t * 128:(t + 1) * 128, :]), reads=rd, writes=[B_xt])
        else:
            P.dma("sp", lambda e: e.dma_start(out=xt[:], in_=src[t * 128:(t + 1) * 128, :]), reads=rd, writes=[B_xt])
        bank = 6 + slot
        pv = self.psb(bank)

        def tr(e):
            for c in range(8):
                ins = e.transpose(pv[:, c, :], xt[:, c * 128:(c + 1) * 128], self.ident_b[:])
            return ins
        P.op("pe", tr, reads=[B_xt, self.B_idb], writes=[self.B_ps[bank]])
        P.op("act", lambda e: e.activation(out=dst_ap, in_=pv, func=AF.Copy), reads=[self.B_ps[bank]], writes=[B_dst])

    def norm_rope(self, src, B_src, nh, goff, cs, B_cs, scale, dst, B_dst, T=None):
        P = self.P
        W = nh * 64
        if T is None:
            T = self.tq
        B_tsq, B_tqn, B_tab, B_tm, B_tss, B_trs = T["B"]
        sq = T["sq"][:, 0:W]
        P.op("act", lambda e: e.activation(out=sq, in_=src, func=AF.Square), reads=[B_src], writes=[B_tsq])
        ss = T["ss"][:, 0:nh]
        P.op("dve", lambda e: e.tensor_reduce(out=ss, in_=sq.rearrange("p (h d) -> p h d", h=nh), axis=AX.X, op=ALU.add),
             reads=[B_tsq], writes=[B_tss])
        P.op("act", lambda e: e.activation(out=ss, in_=ss, func=AF.Ln, scale=1.0 / 64.0, bias=self.epsr[:, 0:1]),
             reads=[B_tss, self.B_eps], writes=[B_tss])
        rs = T["rs"][:, 0:nh]
        bcol = 1 if scale != 1.0 else 2
        P.op("act", lambda e: e.activation(out=rs, in_=ss, func=AF.Exp, scale=-0.5, bias=self.epsr[:, bcol:bcol + 1]),
             reads=[B_tss, self.B_eps], writes=[B_trs])
        qn = T["qn"][:, 0:W].rearrange("p (h d) -> p h d", h=nh)
        P.op("dve", lambda e: e.tensor_tensor(qn, src.rearrange("p (h d) -> p h d", h=nh),
                                              rs.unsqueeze(2).to_broadcast([128, nh, 64]), ALU.mult),
             reads=[B_src, B_trs], writes=[B_tqn])
        x0 = qn[:, :, 0::2]
        x1 = qn[:, :, 1::2]
        ge = self.nrm[:, goff:goff + 32].unsqueeze(1).to_broadcast([128, nh, 32])
        go = self.nrm[:, goff + 32:goff + 64].unsqueeze(1).to_broadcast([128, nh, 32])
        cosb = cs[:, 0:32].unsqueeze(1).to_broadcast([128, nh, 32])
        sinb = cs[:, 32:64].unsqueeze(1).to_broadcast([128, nh, 32])
        a = T["ab"][:, 0, 0:nh * 32].rearrange("p (h d) -> p h d", h=nh)
        b = T["ab"][:, 1, 0:nh * 32].rearrange("p (h d) -> p h d", h=nh)
        P.op("dve", lambda e: e.tensor_tensor(a, x0, ge, ALU.mult), reads=[B_tqn, self.B_nrm], writes=[B_tab])
        P.op("dve", lambda e: e.tensor_tensor(b, x1, go, ALU.mult), reads=[B_tqn, self.B_nrm], writes=[B_tab])
        m = [T["m"][:, i, 0:nh * 32].rearrange("p (h d) -> p h d", h=nh) for i in range(4)]
        P.op("dve", lambda e: e.tensor_tensor(m[0], a, cosb, ALU.mult), reads=[B_tab, B_cs], writes=[B_tm])
        P.op("dve", lambda e: e.tensor_tensor(m[1], b, sinb, ALU.mult), reads=[B_tab, B_cs], writes=[B_tm])
        P.op("dve", lambda e: e.tensor_tensor(m[2], a, sinb, ALU.mult), reads=[B_tab, B_cs], writes=[B_tm])
        P.op("dve", lambda e: e.tensor_tensor(m[3], b, cosb, ALU.mult), reads=[B_tab, B_cs], writes=[B_tm])
        d3 = dst.rearrange("p (h d) -> p h d", h=nh)
        P.op("dve", lambda e: e.tensor_tensor(d3[:, :, 0:32], m[0], m[1], ALU.subtract), reads=[B_tm], writes=[B_dst])
        P.op("dve", lambda e: e.tensor_tensor(d3[:, :, 32:64], m[2], m[3], ALU.add), reads=[B_tm], writes=[B_dst])

    def load_cs(self, t):
        i = t % 2
        self.P.dma("sp", lambda e: e.dma_start(out=self.cst[i][:], in_=self.cs_in[:, t, :]), writes=[self.B_cst[i]])
        return self.cst[i], self.B_cst[i]

    def kv_phase(self, l, src, src_is_f32, B_src, pending):
        P = self.P
        w = self.W[l]
        P.dma("sp", lambda e: e.dma_start(out=self.wkv[:], in_=w["wkv_s"].ap()), reads=[w["B_wkv_s"]], writes=[self.B_wkv])
        def stage1a(t):
            self.load_xT(src, src_is_f32, B_src, t, self.xTt[t % 3][:], self.B_xTt[t % 3], t % 2)
            for _ in range(3):
                if pending:
                    pending.pop(0)()

        def stage1b(t):
            slot = t % 2
            xT = self.xTt[t % 3]
            bk = 4 + slot
            pk = self.ps[:, bk, :]

            def mm(e, xT=xT, pk=pk):
                for c in range(8):
                    ins = e.matmul(pk, lhsT=xT[:, c, :], rhs=self.wkv[:, c, :], start=(c == 0), stop=(c == 7))
                return ins
            P.op("pe", mm, reads=[self.B_xTt[t % 3], self.B_wkv], writes=[self.B_ps[bk]])

        def stage2a(t):
            slot = t % 2
            bk = 4 + slot
            pk = self.ps[:, bk, :]
            kvb, B_kvb, B_kvb2 = self.kvb[slot], self.B_kvb[slot], self.B_kvb2[slot]
            P.op("act", lambda e: e.activation(out=kvb[:, 128:256], in_=pk[:, 256:384], func=AF.Copy),
                 reads=[self.B_ps[bk]], writes=[B_kvb2])
            P.op("act", lambda e: e.activation(out=self.VA[:, t, :, 0:64],
                                               in_=pk[:, 128:256].rearrange("p (h d) -> p h d", h=2), func=AF.Copy),
                 reads=[self.B_ps[bk], self.B_vones], writes=[self.B_KA[t]])
            sl = self.bslot(t)
            if sl is not None:
                P.op("act", lambda e: e.activation(out=self.VB[:, sl, :, 0:64],
                                                   in_=pk[:, 384:512].rearrange("p (h d) -> p h d", h=2), func=AF.Copy),
                     reads=[self.B_ps[bk], self.B_vones], writes=[self.B_KB[t]])
            cs, B_cs = self.load_cs(t)
            self.norm_rope(pk[:, 0:128], self.B_ps[bk], 2, 64, cs, B_cs, 1.0, kvb[:, 0:128], B_kvb, T=self.tk[slot])

        def stage2b(t):
            slot = t % 2
            kvb, B_kvb, B_kvb2 = self.kvb[slot], self.B_kvb[slot], self.B_kvb2[slot]
            sl = self.bslot(t)
            bt = 2 + slot
            pt = self.psb(bt)

            def trk(e):
                e.transpose(pt[:, 0, :], kvb[:, 0:128], self.ident_b[:])
                return e.transpose(pt[:, 1, :], kvb[:, 128:256], self.ident_b[:])
            P.op("pe", trk, reads=[B_kvb, B_kvb2, self.B_idb], writes=[self.B_ps[bt]])
            P.op("dve", lambda e: e.tensor_copy(self.KAT[:, t * 128:(t + 1) * 128], pt[:, 0, :]),
                 reads=[self.B_ps[bt]], writes=[self.B_KA[t]])
            if sl is not None:
                P.op("dve", lambda e: e.tensor_copy(self.KBT[:, sl * 128:(sl + 1) * 128], pt[:, 1, :]),
                     reads=[self.B_ps[bt]], writes=[self.B_KB[t]])

        stage1a(0)
        stage1a(1)
        stage1b(0)
        for t in range(NT):
            stage2a(t)
            if t + 2 < NT:
                stage1a(t + 2)
            if t + 1 < NT:
                stage1b(t + 1)
            stage2b(t)

    def att_block(self, l, src, src_is_f32, B_src, tiles, col0, ncol, x1_dst, B_x1):
        P = self.P
        w = self.W[l]
        nt = len(tiles)
        ntok = nt * 128
        for i, t in enumerate(tiles):
            self.load_xT(src, src_is_f32, B_src, t, self.xTb[:, :, i * 128:(i + 1) * 128], self.B_xTb, i % 2)
        for half in range(2):
            P.dma("sp", lambda e, half=half: e.dma_start(out=self.wq[:], in_=w["wq_s"].ap()[:, :, half * 512:(half + 1) * 512]),
                  reads=[w["B_wq_s"]], writes=[self.B_wq])
            for i, t in enumerate(tiles):
                bk = 4 + i
                pq = self.ps[:, bk, :]

                def mm(e, i=i, pq=pq):
                    for c in range(8):
                        ins = e.matmul(pq, lhsT=self.xTb[:, c, i * 128:(i + 1) * 128], rhs=self.wq[:, c, :],
                                       start=(c == 0), stop=(c == 7))
                    return ins
                P.op("pe", mm, reads=[self.B_xTb, self.B_wq], writes=[self.B_ps[bk]])
            for i, t in enumerate(tiles):
                bk = 4 + i
                pq = self.ps[:, bk, :]
                qb, B_qb = self.qbf[i % 2], self.B_qbf[i % 2]
                if half == 0:
                    cs, B_cs = self.load_cs(t)
                    self.norm_rope(pq, self.B_ps[bk], 8, 0, cs, B_cs, 0.125, qb[:, 0:512], B_qb)
                else:
                    P.op("act", lambda e, qb=qb, pq=pq: e.activation(out=qb[:, 0:512], in_=pq, func=AF.Copy),
                         reads=[self.B_ps[bk]], writes=[B_qb])
                bt = 2 + (i % 2)
                pt = self.psb(bt)

                def trq(e, qb=qb, pt=pt):
                    for c in range(4):
                        ins = e.transpose(pt[:, c, :], qb[:, c * 128:(c + 1) * 128], self.ident_b[:])
                    return ins
                P.op("pe", trq, reads=[B_qb, self.B_idb], writes=[self.B_ps[bt]])
                P.op("dve", lambda e, i=i, half=half, pt=pt: e.tensor_copy(
                    self.QT[:, half * 4:(half + 1) * 4, i * 128:(i + 1) * 128], pt[:, 0:4, :]),
                    reads=[self.B_ps[bt]], writes=[self.B_QT])
        for c in range(4):
            accb = (4, 5) if c % 2 == 0 else (6, 7)

            def qk(kt, c=c):
                sb0 = 2 * (kt % 2)

                def f(e):
                    e.matmul(self.ps[:, sb0, 0:ntok], lhsT=self.KAT[0:64, kt * 128:(kt + 1) * 128],
                             rhs=self.QT[0:64, c, 0:ntok], start=True, stop=True)
                    return e.matmul(self.ps[:, sb0 + 1, 0:ntok], lhsT=self.KAT[64:128, kt * 128:(kt + 1) * 128],
                                    rhs=self.QT[64:128, c, 0:ntok], start=True, stop=True)
                P.op("pe", f, reads=[self.B_KA[kt], self.B_QT], writes=[self.B_ps[sb0], self.B_ps[sb0 + 1]])

            def ex(kt):
                sb0 = 2 * (kt % 2)
                pt = self.PT[kt % 2]
                P.op("act", lambda e: e.activation(out=pt[:, :, 0:ntok], in_=self.ps[:, sb0:sb0 + 2, 0:ntok], func=AF.Exp),
                     reads=[self.B_ps[sb0], self.B_ps[sb0 + 1]], writes=[self.B_PT[kt % 2]])

            def pv(kt, accb=accb):
                pt = self.PT[kt % 2]

                def f(e):
                    e.matmul(self.ps[:, accb[0], 0:ntok], lhsT=self.VA[:, kt, 0, :], rhs=pt[:, 0, 0:ntok],
                             start=(kt == 0), stop=(kt == NT - 1))
                    return e.matmul(self.ps[:, accb[1], 0:ntok], lhsT=self.VA[:, kt, 1, :], rhs=pt[:, 1, 0:ntok],
                                    start=(kt == 0), stop=(kt == NT - 1))
                P.op("pe", f, reads=[self.B_KA[kt], self.B_PT[kt % 2]], writes=[self.B_ps[accb[0]], self.B_ps[accb[1]]])

            qk(0)
            qk(1)
            for kt in range(NT):
                ex(kt)
                pv(kt)
                if kt + 2 < NT:
                    qk(kt + 2)
            for kvh in range(2):
                self.finalize_head(accb[kvh], kvh, None, self.yT[kvh * 64:(kvh + 1) * 64, c, 0:ntok], ntok, kvh)
        def b_main(i, t):
            accb = (4, 5) if i % 2 == 0 else (6, 7)
            nbrs = [((t - 1) % NT, 0), (t, 1), ((t + 1) % NT, 2)]

            def qk(jj):
                tk, jidx = nbrs[jj]
                sb0 = 2 * (jj % 2)
                ks = self.bslot(tk)
                assert ks is not None

                def f(e):
                    e.matmul(self.ps[:, sb0, :], lhsT=self.KBT[0:64, ks * 128:(ks + 1) * 128],
                             rhs=self.QT[0:64, 4:8, i * 128:(i + 1) * 128], start=True, stop=True)
                    return e.matmul(self.ps[:, sb0 + 1, :], lhsT=self.KBT[64:128, ks * 128:(ks + 1) * 128],
                                    rhs=self.QT[64:128, 4:8, i * 128:(i + 1) * 128], start=True, stop=True)
                P.op("pe", f, reads=[self.B_KB[tk], self.B_QT], writes=[self.B_ps[sb0], self.B_ps[sb0 + 1]])

            def chain(jj):
                tk, jidx = nbrs[jj]
                sb0 = 2 * (jj % 2)
                st = self.stt[jj % 2]
                B_st = self.B_stt[jj % 2]
                ks = self.bslot(tk)
                for kvh in range(2):
                    P.op("dve", lambda e, kvh=kvh: e.scalar_tensor_tensor(
                        out=st[:, kvh, :], in0=self.ps[:, sb0 + kvh, :], scalar=0.125, in1=self.biasT[:, jidx, kvh, :],
                        op0=ALU.mult, op1=ALU.add),
                        reads=[self.B_ps[sb0 + kvh], self.B_biasT], writes=[B_st])
                jcol = None
                if jidx == 0 and t == 0:
                    jcol = 0
                elif jidx == 0 and t == 32:
                    jcol = 1
                elif jidx == 2 and t == 63:
                    jcol = 0
                elif jidx == 2 and t == 31:
                    jcol = 1
                if jcol is not None:
                    P.op("dve", lambda e: e.tensor_scalar(
                        st[:].rearrange("p a n -> p (a n)"), st[:].rearrange("p a n -> p (a n)"),
                        self.jm[:, jcol:jcol + 1], None, ALU.add),
                        reads=[B_st, self.B_jm], writes=[B_st])
                pt = self.PT[jj % 2]
                P.op("act", lambda e: e.activation(out=pt[:], in_=st[:], func=AF.Exp),
                     reads=[B_st], writes=[self.B_PT[jj % 2]])

                def g(e):
                    e.matmul(self.ps[0:65, accb[0], :], lhsT=self.VB[:, ks, 0, :], rhs=pt[:, 0, :],
                             start=(jj == 0), stop=(jj == 2))
                    return e.matmul(self.ps[0:65, accb[1], :], lhsT=self.VB[:, ks, 1, :], rhs=pt[:, 1, :],
                                    start=(jj == 0), stop=(jj == 2))
                P.op("pe", g, reads=[self.B_KB[tk], self.B_PT[jj % 2]], writes=[self.B_ps[accb[0]], self.B_ps[accb[1]]])

            qk(0)
            qk(1)
            chain(0)
            qk(2)
            chain(1)
            chain(2)

        def b_fin(i):
            accb = (4, 5) if i % 2 == 0 else (6, 7)
            self.finalize_b_pair(accb, [self.yT[kvh * 64:(kvh + 1) * 64, 4:8, i * 128:(i + 1) * 128] for kvh in range(2)])

        for i, t in enumerate(tiles):
            b_main(i, t)
            if i >= 1:
                b_fin(i - 1)
        b_fin(nt - 1)
        self.out_stage(l, col0, ncol)
        self.layer_norm(0, col0, ncol, lambda m: x1_dst[:, m, :], B_x1, BF16)

    def finalize_head(self, bank, kvh, sink_kvh, y_dst, n, slot):
        P = self.P
        dsb, B_dsb = self.dsb[slot], self.B_dsb[slot]
        if sink_kvh is not None:
            P.op("dve", lambda e: e.tensor_tensor(dsb[0:64, 0:n], self.ps[64:128, bank, 0:n], self.esrow[64:128, sink_kvh, 0:n], ALU.add),
                 reads=[self.B_ps[bank], self.B_esrow], writes=[B_dsb])
        else:
            P.op("dve", lambda e: e.tensor_copy(dsb[0:64, 0:n], self.ps[64:128, bank, 0:n]),
                 reads=[self.B_ps[bank]], writes=[B_dsb])
        if sink_kvh is not None:
            P.op("act", lambda e: e.activation(out=dsb[0:64, 0:n], in_=dsb[0:64, 0:n], func=AF.Ln), reads=[B_dsb], writes=[B_dsb])
            P.op("act", lambda e: e.activation(out=dsb[0:64, 0:n], in_=dsb[0:64, 0:n], func=AF.Exp, scale=-1.0),
                 reads=[B_dsb], writes=[B_dsb])
        else:
            P.op("dve", lambda e: e.reciprocal(dsb[0:64, 0:n], dsb[0:64, 0:n]), reads=[B_dsb], writes=[B_dsb])
        if len(y_dst.shape) == 3:
            in0 = self.ps[0:64, bank, 0:n].rearrange("p (a n) -> p a n", a=4)
            in1 = dsb[0:64, 0:n].rearrange("p (a n) -> p a n", a=4)
        else:
            in0 = self.ps[0:64, bank, 0:n]
            in1 = dsb[0:64, 0:n]
        P.op("dve", lambda e: e.tensor_tensor(y_dst, in0, in1, ALU.mult),
             reads=[self.B_ps[bank], B_dsb], writes=[self.B_yT])

    def finalize_b_pair(self, banks, y_dsts):
        P = self.P
        n = 512
        for kvh in range(2):
            dsb, B_dsb, bank = self.dsb[kvh], self.B_dsb[kvh], banks[kvh]
            P.op("dve", lambda e, dsb=dsb, bank=bank, kvh=kvh: e.tensor_tensor(
                dsb[64:65, 0:n], self.ps[64:65, bank, 0:n], self.esrow[64:65, kvh, 0:n], ALU.add),
                reads=[self.B_ps[bank], self.B_esrow], writes=[B_dsb])
        for kvh in range(2):
            dsb, B_dsb = self.dsb[kvh], self.B_dsb[kvh]
            P.op("act", lambda e, dsb=dsb: e.activation(out=dsb[64:65, 0:n], in_=dsb[64:65, 0:n], func=AF.Ln),
                 reads=[B_dsb], writes=[B_dsb])
            P.op("act", lambda e, dsb=dsb: e.activation(out=dsb[64:65, 0:n], in_=dsb[64:65, 0:n], func=AF.Exp, scale=-1.0),
                 reads=[B_dsb], writes=[B_dsb])
        for kvh in range(2):
            dsb, B_dsb = self.dsb[kvh], self.B_dsb[kvh]
            bb = 2 + kvh
            P.op("pe", lambda e, dsb=dsb, bb=bb: e.matmul(self.ps[0:64, bb, 0:n], lhsT=self.onesf[64:65, 0:64], rhs=dsb[64:65, 0:n],
                                                        start=True, stop=True),
                 reads=[B_dsb, self.B_onesf], writes=[self.B_ps[bb]])
        for kvh in range(2):
            dsb, B_dsb = self.dsb[kvh], self.B_dsb[kvh]
            bb = 2 + kvh
            P.op("act", lambda e, dsb=dsb, bb=bb: e.activation(out=dsb[0:64, 0:n], in_=self.ps[0:64, bb, 0:n], func=AF.Copy),
                 reads=[self.B_ps[bb]], writes=[B_dsb])
        for kvh in range(2):
            dsb, B_dsb, bank = self.dsb[kvh], self.B_dsb[kvh], banks[kvh]
            in0 = self.ps[0:64, bank, 0:n].rearrange("p (a n) -> p a n", a=4)
            in1 = dsb[0:64, 0:n].rearrange("p (a n) -> p a n", a=4)
            P.op("dve", lambda e, y=y_dsts[kvh], in0=in0, in1=in1: e.tensor_tensor(y, in0, in1, ALU.mult),
                 reads=[self.B_ps[bank], B_dsb], writes=[self.B_yT])

    def rsqrt_act(self, dst, src_ap, B_src, B_dst, scale, eps_col, ncol):
        P = self.P
        P.op("act", lambda e: e.activation(out=dst, in_=src_ap, func=AF.Ln, scale=scale, bias=self.epsr[:, eps_col:eps_col + 1]),
             reads=[B_src, self.B_eps], writes=[B_dst])
        P.op("act", lambda e: e.activation(out=dst, in_=dst, func=AF.Exp, scale=-0.5), reads=[B_dst], writes=[B_dst])

    def out_stage(self, l, col0, ncol):
        P = self.P
        w = self.W[l]
        cs = slice(col0, col0 + ncol)
        for g in range(2):
            bank = g
            for cc in range(4):
                c = g * 4 + cc
                sq, B_sq = self.sqy[cc % 2], self.B_sqy[cc % 2]
                P.op("act", lambda e, sq=sq, c=c: e.activation(out=sq[:, 0:ncol], in_=self.yT[:, c, cs], func=AF.Square),
                     reads=[self.B_yT], writes=[B_sq])
                P.op("pe", lambda e, sq=sq, cc=cc, bank=bank: e.matmul(self.ps[:, bank, 0:ncol], lhsT=self.ones_b[:], rhs=sq[:, 0:ncol],
                                                                    start=(cc == 0), stop=(cc == 3)),
                     reads=[B_sq, self.B_ones], writes=[self.B_ps[bank]])
            rr, B_rr = self.rr[g], self.B_rr[g]
            self.rsqrt_act(rr[:, 0:ncol], self.ps[:, bank, 0:ncol], self.B_ps[bank], B_rr, 1.0 / 512.0, 0, ncol)
            for cc in range(4):
                c = g * 4 + cc
                P.op("dve", lambda e, c=c, rr=rr: e.tensor_tensor(self.yT[:, c, cs], self.yT[:, c, cs], rr[:, 0:ncol], ALU.mult),
                     reads=[self.B_yT, B_rr], writes=[self.B_yT])
        for m in range(8):
            wo, B_wo = self.wo[m % 4], self.B_wo[m % 4]
            P.dma("sp", lambda e, wo=wo, m=m: e.dma_start(out=wo[:], in_=w["wout_s"].ap()[m]),
                  reads=[w["B_wout_s"]], writes=[B_wo])
            ba = 2 + (m % 2)

            def mm(e, wo=wo, ba=ba, m=m):
                for c in range(8):
                    ins = e.matmul(self.ps[:, ba, 0:ncol], lhsT=wo[:, c, :], rhs=self.yT[:, c, cs],
                                   start=(c == 0), stop=(c == 7))
                return ins
            P.op("pe", mm, reads=[B_wo, self.B_yT], writes=[self.B_ps[ba]])
            P.op("dve", lambda e, ba=ba, m=m: e.scalar_tensor_tensor(out=self.z[:, m, 0:ncol], in0=self.xTb[:, m, cs], scalar=ALPHA,
                                                                     in1=self.ps[:, ba, 0:ncol], op0=ALU.mult, op1=ALU.add),
                 reads=[self.B_xTb, self.B_ps[ba]], writes=[self.B_z[m]])

    def layer_norm(self, which, col0_unused, ncol, dst_fn, B_dst, out_dt):
        P = self.P
        gcol = 0 if which == 0 else 16
        bcol = gcol + 8
        for m in range(8):
            zb, B_zb = self.zb[m % 2], self.B_zb[m % 2]
            zq, B_zq = self.zq[m % 2], self.B_zq[m % 2]
            P.op("act", lambda e, zb=zb, m=m: e.activation(out=zb[:, 0:ncol], in_=self.z[:, m, 0:ncol], func=AF.Copy),
                 reads=[self.B_z[m]], writes=[B_zb])
            P.op("act", lambda e, zq=zq, m=m: e.activation(out=zq[:, 0:ncol], in_=self.z[:, m, 0:ncol], func=AF.Square),
                 reads=[self.B_z[m]], writes=[B_zq])
            P.op("pe", lambda e, zb=zb, m=m: e.matmul(self.ps[:, 0, 0:ncol], lhsT=self.ones_b[:], rhs=zb[:, 0:ncol],
                                                      start=(m == 0), stop=(m == 7)),
                 reads=[B_zb, self.B_ones], writes=[self.B_ps[0]])
            P.op("pe", lambda e, zq=zq, m=m: e.matmul(self.ps[:, 1, 0:ncol], lhsT=self.ones_b[:], rhs=zq[:, 0:ncol],
                                                      start=(m == 0), stop=(m == 7)),
                 reads=[B_zq, self.B_ones], writes=[self.B_ps[1]])
        mean, rstd = self.mean, self.rstd
        P.op("dve", lambda e: e.tensor_scalar(mean[:, 0:ncol], self.ps[:, 0, 0:ncol], 1.0 / D, None, ALU.mult),
             reads=[self.B_ps[0]], writes=[self.B_mean])
        P.op("dve", lambda e: e.tensor_tensor(rstd[:, 0:ncol], mean[:, 0:ncol], mean[:, 0:ncol], ALU.mult),
             reads=[self.B_mean], writes=[self.B_rstd])
        P.op("dve", lambda e: e.scalar_tensor_tensor(out=rstd[:, 0:ncol], in0=self.ps[:, 1, 0:ncol], scalar=1.0 / D,
                                                     in1=rstd[:, 0:ncol], op0=ALU.mult, op1=ALU.subtract),
             reads=[self.B_ps[1], self.B_rstd], writes=[self.B_rstd])
        self.rsqrt_act(rstd[:, 0:ncol], rstd[:, 0:ncol], self.B_rstd, self.B_rstd, 1.0, 3, ncol)
        for m in range(8):
            ta, B_ta = self.tmpa[m % 2], self.B_tmpa[m % 2]
            P.op("dve", lambda e, ta=ta, m=m: e.tensor_tensor(ta[:, 0:ncol], self.z[:, m, 0:ncol], mean[:, 0:ncol], ALU.subtract),
                 reads=[self.B_z[m], self.B_mean], writes=[B_ta])
            P.op("dve", lambda e, ta=ta: e.tensor_tensor(ta[:, 0:ncol], ta[:, 0:ncol], rstd[:, 0:ncol], ALU.mult),
                 reads=[B_ta, self.B_rstd], writes=[B_ta])
            dst = dst_fn(m)
            Bd = B_dst(m) if callable(B_dst) else B_dst
            P.op("act", lambda e, ta=ta, m=m, dst=dst: e.activation(out=dst, in_=ta[:, 0:ncol], func=AF.Identity,
                                                                    scale=self.lnp[:, gcol + m:gcol + m + 1],
                                                                    bias=self.lnp[:, bcol + m:bcol + m + 1]),
                 reads=[B_ta, self.B_lnp], writes=[Bd])

    def ffn_block(self, l, k, X, B_X, hl_ap, B_hl, hr_ap, B_hr, first, last, dst, dst_dt, B_dstbuf, row0):
        P = self.P
        w = self.W[l]
        HC, B_HC = self.HC[k % 2], self.B_HC[k % 2]
        P.op("dve", lambda e: e.tensor_copy(HC[:, :, 0:1], hl_ap), reads=[B_hl], writes=[B_HC])
        P.op("dve", lambda e: e.tensor_copy(HC[:, :, 1:2], hr_ap), reads=[B_hr], writes=[B_HC])
        for m in range(8):
            P.op("act", lambda e, m=m: e.activation(out=self.z[:, m, :], in_=X[:, m, :], func=AF.Copy, scale=ALPHA),
                 reads=[B_X], writes=[self.B_z[m]])
        for hh in range(2):
            for cc in range(11):
                c = hh * 11 + cc
                wg, B_wg = self.wgu[c % 3], self.B_wgu[c % 3]
                P.dma("sp", lambda e, wg=wg, c=c: e.dma_start(out=wg[:], in_=w["wgu_s"].ap()[c]),
                      reads=[w["B_wgu_s"]], writes=[B_wg])
                gb = (0, 1, 4)[c % 3]
                ub = (2, 3, 5)[c % 3]
                hcol = (c % 3) * 2

                def mm(e, wg=wg, gb=gb, ub=ub, hcol=hcol):
                    for kc in range(8):
                        e.matmul(self.ps[:, gb, :], lhsT=wg[:, kc, 0:128], rhs=X[:, kc, :], start=(kc == 0), stop=(kc == 7))
                    for kc in range(8):
                        e.matmul(self.ps[:, 7, hcol:hcol + 2], lhsT=wg[:, kc, 0:128], rhs=HC[:, kc, :], start=(kc == 0), stop=(kc == 7))
                    for kc in range(8):
                        ins = e.matmul(self.ps[:, ub, :], lhsT=wg[:, kc, 128:256], rhs=X[:, kc, :], start=(kc == 0), stop=(kc == 7))
                    return ins
                P.op("pe", mm, reads=[B_wg, B_X, B_HC], writes=[self.B_ps[gb], self.B_ps[ub], self.B_ps[7]])
                gc, B_gc = self.gcs[c % 2], self.B_gcs[c % 2]
                G = self.ps[:, gb, :]
                Gh = self.ps[:, 7, hcol:hcol + 2]
                w0 = self.cvp[:, c, 0:1]
                w1 = self.cvp[:, c, 1:2]
                w2 = self.cvp[:, c, 2:3]
                cb = self.cvp[:, c, 3:4]
                w0e = self.cvm[:, c, 0:1] if first else w0
                w2e = self.cvm[:, c, 1:2] if last else w2
                ed, B_ed = self.ed[c % 3], self.B_ed[c % 3]
                P.op("act", lambda e, ed=ed, Gh=Gh: e.activation(out=ed[:], in_=Gh, func=AF.Copy),
                     reads=[self.B_ps[7]], writes=[B_ed])
                P.op("act", lambda e, gc=gc, G=G, w1=w1, cb=cb: e.activation(out=gc[:], in_=G, func=AF.Identity, scale=w1, bias=cb),
                     reads=[self.B_ps[gb], self.B_cvp], writes=[B_gc])
                P.op("dve", lambda e, gc=gc, G=G, w0=w0: e.scalar_tensor_tensor(out=gc[:, 1:512], in0=G[:, 0:511], scalar=w0,
                                                                                 in1=gc[:, 1:512], op0=ALU.mult, op1=ALU.add),
                     reads=[self.B_ps[gb], B_gc, self.B_cvp], writes=[B_gc])
                P.op("dve", lambda e, gc=gc, G=G, w2=w2: e.scalar_tensor_tensor(out=gc[:, 0:511], in0=G[:, 1:512], scalar=w2,
                                                                                 in1=gc[:, 0:511], op0=ALU.mult, op1=ALU.add),
                     reads=[self.B_ps[gb], B_gc, self.B_cvp], writes=[B_gc])
                P.op("dve", lambda e, gc=gc, ed=ed, w0e=w0e: e.scalar_tensor_tensor(out=gc[:, 0:1], in0=ed[:, 0:1], scalar=w0e,
                                                                                     in1=gc[:, 0:1], op0=ALU.mult, op1=ALU.add),
                     reads=[B_ed, B_gc, self.B_cvp, self.B_cvm], writes=[B_gc])
                P.op("dve", lambda e, gc=gc, ed=ed, w2e=w2e: e.scalar_tensor_tensor(out=gc[:, 511:512], in0=ed[:, 1:2], scalar=w2e,
                                                                                     in1=gc[:, 511:512], op0=ALU.mult, op1=ALU.add),
                     reads=[B_ed, B_gc, self.B_cvp, self.B_cvm], writes=[B_gc])
                if cc >= 1:
                    self._ffn_tail(c - 1, cc - 1)
            self._ffn_tail(hh * 11 + 10, 10)
            di = 0
            for gi, grp in enumerate(((0, 1, 2), (3, 4, 5), (6, 7))):
                ng = len(grp)
                ab = 4 if (hh * 3 + gi) % 2 == 0 else 0
                for cc in range(11):
                    c = hh * 11 + cc
                    wd, B_wd = self.wdp[di % 8], self.B_wdp[di % 8]
                    di += 1
                    P.dma("sp", lambda e, wd=wd, c=c, grp=grp, ng=ng: e.dma_start(
                        out=wd[:, 0:ng * 128], in_=w["wd_s"].ap()[c, :, grp[0] * 128:(grp[0] + ng) * 128]),
                        reads=[w["B_wd_s"]], writes=[B_wd])

                    def mm(e, wd=wd, cc=cc, ng=ng, ab=ab):
                        for mi in range(ng):
                            ins = e.matmul(self.ps[:, ab + mi, :], lhsT=wd[:, mi * 128:(mi + 1) * 128], rhs=self.hT[:, cc, :],
                                           start=(cc == 0), stop=(cc == 10))
                        return ins
                    P.op("pe", mm, reads=[B_wd, self.B_hT[cc]], writes=[self.B_ps[ab + mi] for mi in range(ng)])
                for mi, m in enumerate(grp):
                    P.op("dve", lambda e, mi=mi, m=m, ab=ab: e.tensor_tensor(self.z[:, m, :], self.ps[:, ab + mi, :], self.z[:, m, :], ALU.add),
                         reads=[self.B_ps[ab + mi], self.B_z[m]], writes=[self.B_z[m]])
        is_f32 = (dst_dt == F32)

        def emit_out(m, y2, B_y2):
            bank = 2 + (m % 2)
            if is_f32:
                pv = self.ps[:, bank, :].rearrange("p (a n) -> p a n", a=4)
                idn, B_idn = self.ident_f, self.B_idf
            else:
                pv = self.psb(bank)[:, 0:4, :]
                idn, B_idn = self.ident_b, self.B_idb

            def tr(e):
                for i in range(4):
                    ins = e.transpose(pv[:, i, :], y2[:, i * 128:(i + 1) * 128], idn[:])
                return ins
            P.op("pe", tr, reads=[B_y2, B_idn], writes=[self.B_ps[bank]])
            if is_f32:
                ot, B_ot = self.ot[m % 2], self.B_ot[m % 2]
                otv = ot[:]
            else:
                ot, B_ot = self.ot[m % 2], self.B_ot[m % 2]
                otv = ot[:].rearrange("p a n -> p (a n)").bitcast(BF16)[:, 0:512].rearrange("p (a n) -> p a n", a=4)
            P.op("act", lambda e: e.activation(out=otv, in_=pv, func=AF.Copy), reads=[self.B_ps[bank]], writes=[B_ot])
            dview = dst[row0:row0 + 512, m * 128:(m + 1) * 128].rearrange("(a p) n -> p a n", p=128)
            P.dma("sp", lambda e: e.dma_start(out=dview, in_=otv), reads=[B_ot], dwrites=[B_dstbuf], owner=B_ot,
                  is_out=True)

        self._ln_out_queue = []

        def dst_fn(m):
            y2 = self.y2[m % 2]
            if is_f32:
                return y2[:]
            return y2[:].bitcast(BF16)[:, 0:512]

        self.layer_norm_with_out(1, 512, dst_fn, lambda m: self.B_y2[m % 2], emit_out, is_f32)

    def _ffn_tail(self, c, cc):
        P = self.P
        ub = (2, 3, 5)[c % 3]
        gc, B_gc = self.gcs[c % 2], self.B_gcs[c % 2]
        tg, B_tg = self.tgs[c % 2], self.B_tgs[c % 2]
        P.op("act", lambda e: e.activation(out=tg[:], in_=gc[:], func=AF.Gelu_apprx_tanh), reads=[B_gc], writes=[B_tg])
        P.op("dve", lambda e: e.tensor_tensor(self.hT[:, cc, :], self.ps[:, ub, :], tg[:], ALU.mult),
             reads=[self.B_ps[ub], B_tg], writes=[self.B_hT[cc]])

    def layer_norm_with_out(self, which, ncol, dst_fn, B_dst_fn, emit_out, is_f32):
        P = self.P
        gcol = 0 if which == 0 else 16
        bcol = gcol + 8
        for m in range(8):
            zb, B_zb = self.zb[m % 2], self.B_zb[m % 2]
            zq, B_zq = self.zq[m % 2], self.B_zq[m % 2]
            P.op("act", lambda e, zb=zb, m=m: e.activation(out=zb[:, 0:ncol], in_=self.z[:, m, 0:ncol], func=AF.Copy),
                 reads=[self.B_z[m]], writes=[B_zb])
            P.op("act", lambda e, zq=zq, m=m: e.activation(out=zq[:, 0:ncol], in_=self.z[:, m, 0:ncol], func=AF.Square),
                 reads=[self.B_z[m]], writes=[B_zq])
            P.op("pe", lambda e, zb=zb, m=m: e.matmul(self.ps[:, 0, 0:ncol], lhsT=self.ones_b[:], rhs=zb[:, 0:ncol],
                                                      start=(m == 0), stop=(m == 7)),
                 reads=[B_zb, self.B_ones], writes=[self.B_ps[0]])
            P.op("pe", lambda e, zq=zq, m=m: e.matmul(self.ps[:, 1, 0:ncol], lhsT=self.ones_b[:], rhs=zq[:, 0:ncol],
                                                      start=(m == 0), stop=(m == 7)),
                 reads=[B_zq, self.B_ones], writes=[self.B_ps[1]])
        mean, rstd = self.mean, self.rstd
        P.op("dve", lambda e: e.tensor_scalar(mean[:, 0:ncol], self.ps[:, 0, 0:ncol], 1.0 / D, None, ALU.mult),
             reads=[self.B_ps[0]], writes=[self.B_mean])
        P.op("dve", lambda e: e.tensor_tensor(rstd[:, 0:ncol], mean[:, 0:ncol], mean[:, 0:ncol], ALU.mult),
             reads=[self.B_mean], writes=[self.B_rstd])
        P.op("dve", lambda e: e.scalar_tensor_tensor(out=rstd[:, 0:ncol], in0=self.ps[:, 1, 0:ncol], scalar=1.0 / D,
                                                     in1=rstd[:, 0:ncol], op0=ALU.mult, op1=ALU.subtract),
             reads=[self.B_ps[1], self.B_rstd], writes=[self.B_rstd])
        self.rsqrt_act(rstd[:, 0:ncol], rstd[:, 0:ncol], self.B_rstd, self.B_rstd, 1.0, 3, ncol)
        for m in range(8):
            ta, B_ta = self.tmpa[m % 2], self.B_tmpa[m % 2]
            P.op("dve", lambda e, ta=ta, m=m: e.tensor_tensor(ta[:, 0:ncol], self.z[:, m, 0:ncol], mean[:, 0:ncol], ALU.subtract),
                 reads=[self.B_z[m], self.B_mean], writes=[B_ta])
            P.op("dve", lambda e, ta=ta: e.tensor_tensor(ta[:, 0:ncol], ta[:, 0:ncol], rstd[:, 0:ncol], ALU.mult),
                 reads=[B_ta, self.B_rstd], writes=[B_ta])
            dst = dst_fn(m)
            Bd = B_dst_fn(m)
            P.op("act", lambda e, ta=ta, m=m, dst=dst: e.activation(out=dst, in_=ta[:, 0:ncol], func=AF.Identity,
                                                                    scale=self.lnp[:, gcol + m:gcol + m + 1],
                                                                    bias=self.lnp[:, bcol + m:bcol + m + 1]),
                 reads=[B_ta, self.B_lnp], writes=[Bd])
            emit_out(m, dst, Bd)

    def half_layer(self, l, own_off, src, src_is_f32, B_src, dst, dst_dt, B_dstbuf, pending):
        P = self.P
        self.own_off = own_off
        self.setup_layer(l, own_off)
        self.kv_phase(l, src, src_is_f32, B_src, pending)
        while pending:
            pending.pop(0)()
        tl = (own_off - 1) % NT
        tr_ = (own_off + 32) % NT
        self.att_block(l, src, src_is_f32, B_src, [tl, tr_], 127, 2, self.XH, self.B_XH)
        for k in range(8):
            tiles = [own_off + 4 * k + i for i in range(4)]
            XB, B_XB = self.XB[k % 2], self.B_XB[k % 2]
            self.att_block(l, src, src_is_f32, B_src, tiles, 0, 512, XB, B_XB)
            P.op("dve", lambda e, XB=XB, k=k: e.tensor_copy(self.LC[:, :, k:k + 1], XB[:, :, 511:512]),
                 reads=[B_XB], writes=[self.B_LC[k]])
            if k >= 1:
                self._ffn(l, k - 1, dst, dst_dt, B_dstbuf)
        self._ffn(l, 7, dst, dst_dt, B_dstbuf)

    def _ffn(self, l, k, dst, dst_dt, B_dstbuf):
        X, B_X = self.XB[k % 2], self.B_XB[k % 2]
        if k == 0:
            hl, B_hl = self.XH[:, :, 0:1], self.B_XH
        else:
            hl, B_hl = self.LC[:, :, k - 1:k], self.B_LC[k - 1]
        if k == 7:
            hr, B_hr = self.XH[:, :, 1:2], self.B_XH
        else:
            hr, B_hr = self.XB[(k + 1) % 2][:, :, 0:1], self.B_XB[(k + 1) % 2]
        self.fence(self.att_bufs + self.ffn_bufs)
        self.ffn_block(l, k, X, B_X, hl, B_hl, hr, B_hr, k == 0, k == 7, dst, dst_dt, B_dstbuf, k * 512)
        self.fence(self.att_bufs + self.ffn_bufs)

    def build(self):
        self.setup_consts()
        if not self.fused:
            pending = self.convert_weights(0)
            for _ in range(8):
                pending.pop(0)()
            self.half_layer(0, 0, self.x_in, True, None, self.out, F32, self.B_out, pending)
        else:
            pending = self.convert_weights(0) + self.convert_weights(1)
            for _ in range(8):
                pending.pop(0)()
            xm = self.xmid.ap()
            self.half_layer(0, 32, self.x_in, True, None, xm[S // 2:S, :], BF16, self.B_xmid, pending)
            self.half_layer(0, 0, self.x_in, True, None, xm[0:S // 2, :], BF16, self.B_xmid, pending)
            self.half_layer(1, 0, xm, False, self.B_xmid, self.out, F32, self.B_out, pending)
        self.P.emit()
        self.st.close()
        return self.nc


HPERM = [0, 4, 1, 5, 2, 6, 3, 7]


def _t5_bucket(rel):
    half = 16
    max_exact = 8
    bucket = np.where(rel > 0, half, 0)
    rp = np.abs(rel)
    rpf = np.maximum(rp, 1).astype(np.float32)
    large = max_exact + (np.log(rpf / np.float32(max_exact)) / np.float32(math.log(128 / max_exact))
                         * np.float32(half - max_exact)).astype(np.int32)
    large = np.minimum(large, half - 1)
    return bucket + np.where(rp < max_exact, rp, large)


def _bias_table(rel_bias):
    k = np.arange(128)[:, None, None]
    j = np.arange(3)[None, :, None]
    q = np.arange(128)[None, None, :]
    rel = (j - 1) * 128 + k - q
    idx = _t5_bucket(rel)
    tab = np.asarray(rel_bias, np.float32)[idx]
    tab = np.where((np.abs(rel) <= 128)[..., None], tab, np.float32(NEG))
    tab = np.ascontiguousarray(tab.transpose(0, 1, 3, 2))
    return tab.reshape(128, 3 * 8 * 128).astype(np.float32)


def _rope_table(half):
    tok = (np.arange(S) + half * (S // 2)) % S
    row = (tok // 64).astype(np.float32)
    col = (tok % 64).astype(np.float32)
    inv = (np.float32(10000.0) ** (-np.arange(0, 32, 2, dtype=np.float32) / np.float32(32))).astype(np.float32)
    ang = np.concatenate([row[:, None] * inv, col[:, None] * inv], axis=-1).astype(np.float32)
    cs = np.concatenate([np.cos(ang), np.sin(ang)], axis=-1).astype(np.float32)
    return np.ascontiguousarray(cs.reshape(NT, 128, 64).transpose(1, 0, 2))


def _layer_params(inp, l):
    f = lambda a: np.ascontiguousarray(np.asarray(a, np.float32))
    w_in = np.asarray(inp["w_in"][l], np.float32)
    qa = w_in[:, 0:512].reshape(D, 8, 64)[:, HPERM, :].reshape(D, 512)
    qb = w_in[:, 768:1280].reshape(D, 8, 64)[:, HPERM, :].reshape(D, 512)
    wq = np.concatenate([qa, qb], axis=1)
    wkv = np.concatenate([w_in[:, 512:640], w_in[:, 640:768], w_in[:, 1280:1408], w_in[:, 1408:1536]], axis=1)
    w_out = np.asarray(inp["w_out"][l], np.float32)
    wo = np.concatenate([w_out[0:512].reshape(8, 64, D)[HPERM].reshape(512, D),
                         w_out[512:1024].reshape(8, 64, D)[HPERM].reshape(512, D)], axis=0)
    ga = np.asarray(inp["out_norm_a"][l], np.float32).reshape(8, 64)[HPERM].reshape(512)
    gb = np.asarray(inp["out_norm_b"][l], np.float32).reshape(8, 64)[HPERM].reshape(512)
    gout = np.concatenate([ga, gb]).reshape(8, 128).T
    qn = np.asarray(inp["q_norm"][l], np.float32)
    kn = np.asarray(inp["k_norm"][l], np.float32)
    nrm = np.concatenate([qn[0::2], qn[1::2], kn[0::2], kn[1::2]])[None, :]
    fm = lambda v: np.asarray(v, np.float32).reshape(8, 128).T
    lnp = np.concatenate([fm(inp["ln1_g"][l]), fm(inp["ln1_b"][l]), fm(inp["ln2_g"][l]), fm(inp["ln2_b"][l])], axis=1)
    cw = np.asarray(inp["conv_w"][l], np.float32)
    cb = np.asarray(inp["conv_b"][l], np.float32)
    cv = np.stack([cw[0], cw[1], cw[2], cb], axis=-1).reshape(NFC, 128, 4).transpose(1, 0, 2).reshape(128, NFC * 4)
    sink = np.asarray(inp["sink"][l], np.float32)[None, :]
    return dict(wq=f(wq), wkv=f(wkv), wout=f(wo), wg=f(inp["w_gate"][l]), wu=f(inp["w_up"][l]), wd=f(inp["w_down"][l]),
                nrm=f(nrm), gout=f(gout), lnp=f(lnp), cvp=f(cv), sink=f(sink))


def _core_consts(inp, half):
    jm = np.zeros((128, 4), np.float32)
    if half == 0:
        jm[:, 0] = NEG; jm[:, 1] = 0.0; jm[:, 2] = 0.0; jm[:, 3] = 1.0
    else:
        jm[:, 0] = 0.0; jm[:, 1] = NEG; jm[:, 2] = 1.0; jm[:, 3] = 0.0
    return dict(biasT=_bias_table(inp["rel_bias"]), cs=_rope_table(half), jm=jm, ident=np.eye(128, dtype=np.float32))


def _layout_x(xb, half):
    if half == 0:
        return np.ascontiguousarray(xb)
    return np.ascontiguousarray(np.concatenate([xb[S // 2:], xb[:S // 2]], axis=0))


_NC_CACHE = {}


def _get_nc(n_layers, fused):
    key = (n_layers, fused)
    if key not in _NC_CACHE:
        _NC_CACHE[key] = Builder(n_layers, fused).build()
    return _NC_CACHE[key]


FUSED = True


def kernel(**inp):
    x = np.asarray(inp["x"], np.float32)
    B = x.shape[0]
    lp = [_layer_params(inp, l) for l in range(2)]
    cc = [_core_consts(inp, h) for h in range(2)]
    if FUSED:
        nc = _get_nc(2, True)
        in_maps = []
        for core in range(N_CORES):
            b, h = core // 2, core % 2
            m = {"xsrc": _layout_x(x[b], h)}
            for l in range(2):
                for k, v in lp[l].items():
                    m[f"{k}{l}"] = v
            m.update(cc[h])
            in_maps.append(m)
        res = run_bass_kernel_spmd(nc, in_maps, core_ids=list(range(N_CORES)))
        out = np.empty((B, S, D), np.float32)
        for core in range(N_CORES):
            b, h = core // 2, core % 2
            out[b, h * (S // 2):(h + 1) * (S // 2)] = res.results[core]["out"]
        return out
    nc = _get_nc(1, False)
    cur = x
    for l in range(2):
        in_maps = []
        for core in range(N_CORES):
            b, h = core // 2, core % 2
            m = {"xsrc": _layout_x(cur[b], h)}
            for k, v in lp[l].items():
                m[f"{k}0"] = v
            m.update(cc[h])
            in_maps.append(m)
        res = run_bass_kernel_spmd(nc, in_maps, core_ids=list(range(N_CORES)))
        nxt = np.empty((B, S, D), np.float32)
        for core in range(N_CORES):
            b, h = core // 2, core % 2
            nxt[b, h * (S // 2):(h + 1) * (S // 2)] = res.results[core]["out"]
        cur = nxt
    return cur
```

```python
import math
from contextlib import ExitStack
import numpy as np
import concourse.bass as bass
import concourse.mybir as mybir
from concourse.bass_utils import run_bass_kernel_spmd

F32 = mybir.dt.float32
BF16 = mybir.dt.bfloat16
AF = mybir.ActivationFunctionType
ALU = mybir.AluOpType
AX = mybir.AxisListType

D = 1024
S = 8192
NT = 64
NB = 4
DFF = 2816
NFC = 22
ALPHA = 4.0 ** 0.25
RMS_EPS = 1e-6
LN_EPS = 1e-5
NEG = -30000.0
N_CORES = 8


class Buf:
    __slots__ = ("name", "w", "r", "dsem", "dcnt")

    def __init__(self, name):
        self.name = name
        self.w = []
        self.r = []
        self.dsem = None
        self.dcnt = 0


class Prog:
    CE = ("pe", "act", "dve", "pool")
    ENG = ("pe", "act", "dve", "pool", "sp")

    def __init__(self, nc, stack):
        self.nc = nc
        self.stack = stack
        self.ops = {e: [] for e in self.ENG}
        self.cnt = {e: 0 for e in self.CE}
        self.esem = {e: stack.enter_context(nc.semaphore("s_" + e)) for e in self.CE}
        self.seen = {e: {} for e in self.ENG}
        self.nsem = 4
        self.out_tokens = []

    def _mk_sem(self, name):
        self.nsem += 1
        return self.stack.enter_context(self.nc.semaphore(name))

    def _waits(self, eng, reads, writes, dwrites=()):
        deps = []
        for b in reads:
            deps.extend(b.w)
        for b in writes:
            deps.extend(b.w)
            deps.extend(b.r)
        for b in dwrites:
            deps.extend(b.r)
        best = {}
        for (s, v) in deps:
            k = id(s)
            if k not in best or best[k][1] < v:
                best[k] = (s, v)
        waits = []
        own = self.esem.get(eng) if eng == "pe" else None
        for k, (s, v) in best.items():
            if s is own:
                continue
            if self.seen[eng].get(k, 0) >= v:
                continue
            self.seen[eng][k] = v
            waits.append((s, v))
        return waits

    @staticmethod
    def _compact(lst):
        best = {}
        for (s, v) in lst:
            if id(s) not in best or best[id(s)][1] < v:
                best[id(s)] = (s, v)
        return list(best.values())

    def _commit(self, tok, reads, writes, dwrites=()):
        for b in dwrites:
            b.w.append(tok)
            if len(b.w) > 24:
                b.w = self._compact(b.w)
        for b in reads:
            b.r.append(tok)
            if len(b.r) > 24:
                best = {}
                for (s, v) in b.r:
                    if id(s) not in best or best[id(s)][1] < v:
                        best[id(s)] = (s, v)
                b.r = list(best.values())
        for b in writes:
            b.w = [tok]
            b.r = []

    def op(self, eng, fn, reads=(), writes=()):
        waits = self._waits(eng, reads, writes)
        self.cnt[eng] += 1
        tok = (self.esem[eng], self.cnt[eng])
        self.ops[eng].append((fn, waits, (self.esem[eng], 1)))
        self._commit(tok, reads, writes)
        return tok

    def dma(self, q, fn, reads=(), writes=(), owner=None, is_out=False, dwrites=()):
        if owner is None:
            owner = writes[0] if writes else (dwrites[0] if dwrites else reads[0])
        if owner.dsem is None:
            owner.dsem = self._mk_sem("d_" + owner.name)
        waits = self._waits(q, reads, writes, dwrites)
        owner.dcnt += 16
        tok = (owner.dsem, owner.dcnt)
        self.ops[q].append((fn, waits, (owner.dsem, 16)))
        self._commit(tok, reads, writes, dwrites)
        if is_out:
            self.out_tokens.append(tok)
        return tok

    def emit(self):
        nc = self.nc
        ws = []
        for (s, v) in self.out_tokens:
            k = id(s)
            if self.seen["sp"].get(k, 0) >= v:
                continue
            self.seen["sp"][k] = v
            ws.append((s, v))
        if ws:
            self.ops["sp"].append((None, ws, None))
        ops = self.ops

        def replay(e, lst):
            for (fn, waits, inc) in lst:
                for (s, v) in waits:
                    e.wait_ge(s, v)
                if fn is not None:
                    ins = fn(e)
                    ins.then_inc(inc[0], inc[1])

        with nc.Block() as block:
            @block.tensor
            def _(e):
                replay(e, ops["pe"])

            @block.scalar
            def _(e):
                replay(e, ops["act"])

            @block.vector
            def _(e):
                replay(e, ops["dve"])

            @block.gpsimd
            def _(e):
                replay(e, ops["pool"])

            @block.sync
            def _(e):
                replay(e, ops["sp"])


class Builder:
    def __init__(self, n_layers, fused, dbg=False):
        self.fused = fused
        self.n_layers = n_layers
        self.dbg = dbg
        self.nc = bass.Bass("TRN2", target_bir_lowering=False)
        self.st = ExitStack()
        self.P = Prog(self.nc, self.st)
        self.sb_bytes = 0
        self._declare_io()
        self._alloc()

    def sb(self, name, shape, dt):
        n = 1
        for s in shape[1:]:
            n *= s
        self.sb_bytes += n * (2 if dt == BF16 else 4)
        return self.st.enter_context(self.nc.sbuf_tensor(name, list(shape), dt))

    def din(self, name, shape, dt=F32):
        return self.nc.dram_tensor(name, list(shape), dt, kind="ExternalInput").ap()

    def _declare_io(self):
        nc = self.nc
        L = self.n_layers
        self.x_in = self.din("xsrc", [S, D])
        self.W = []
        for l in range(L):
            w = dict(
                wq=self.din(f"wq{l}", [D, D]), wkv=self.din(f"wkv{l}", [D, 512]),
                wout=self.din(f"wout{l}", [D, D]), wg=self.din(f"wg{l}", [D, DFF]),
                wu=self.din(f"wu{l}", [D, DFF]), wd=self.din(f"wd{l}", [DFF, D]),
                nrm=self.din(f"nrm{l}", [1, 128]), gout=self.din(f"gout{l}", [128, 8]),
                lnp=self.din(f"lnp{l}", [128, 32]), cvp=self.din(f"cvp{l}", [128, 88]),
                sink=self.din(f"sink{l}", [1, 8]),
            )
            w["wq_s"] = nc.dram_tensor(f"wq_s{l}", [128, 8, D], BF16)
            w["wkv_s"] = nc.dram_tensor(f"wkv_s{l}", [128, 8, 512], BF16)
            w["wout_s"] = nc.dram_tensor(f"wout_s{l}", [8, 128, 8, 128], BF16)
            w["wgu_s"] = nc.dram_tensor(f"wgu_s{l}", [NFC, 128, 8, 256], BF16)
            w["wd_s"] = nc.dram_tensor(f"wd_s{l}", [NFC, 128, D], BF16)
            for k in ("wq_s", "wkv_s", "wout_s", "wgu_s", "wd_s"):
                w["B_" + k] = Buf(f"{k}{l}")
            self.W.append(w)
        self.biasT_in = self.din("biasT", [128, 3072])
        self.cs_in = self.din("cs", [128, NT, 64])
        self.jm_in = self.din("jm", [128, 4])
        self.ident_in = self.din("ident", [128, 128])
        self.out = nc.dram_tensor("out", [S // 2, D], F32, kind="ExternalOutput").ap()
        if self.fused:
            self.xmid = nc.dram_tensor("xmid", [S, D], BF16)
            self.B_xmid = Buf("xmid")
        self.B_out = Buf("out")
        self.dbg_out = {}

    def _alloc(self):
        nc, sb = self.nc, self.sb
        self.KAT = sb("KAT", [128, S], BF16)
        self.VA = sb("VA", [128, NT, 2, 128], BF16)
        self.KBT = sb("KBT", [128, 36 * 128], BF16)
        self.VB = sb("VB", [128, 36, 2, 65], BF16)
        self.B_KA = [Buf(f"KA{t}") for t in range(NT)]
        self.B_KB = [Buf(f"KB{t}") for t in range(NT)]
        self.B_vones = Buf("vones")
        self.ident_f = sb("ident_f", [128, 128], F32); self.B_idf = Buf("ident_f")
        self.ident_b = sb("ident_b", [128, 128], BF16); self.B_idb = Buf("ident_b")
        self.ones_b = sb("ones_b", [128, 128], BF16); self.B_ones = Buf("ones_b")
        self.zeros = sb("zeros", [128, 128], F32); self.B_zeros = Buf("zeros")
        self.onesf = sb("onesf", [128, 64], F32); self.B_onesf = Buf("onesf")
        self.epsr = sb("epsr", [128, 4], F32); self.B_eps = Buf("epsr")
        self.biasT = sb("biasT_sb", [128, 3, 2, 512], BF16); self.B_biasT = Buf("biasT")
        self.jm = sb("jm_sb", [128, 4], F32); self.B_jm = Buf("jm")
        self.nrm = sb("nrm_sb", [128, 128], F32); self.B_nrm = Buf("nrm")
        self.gout = [sb(f"gout_sb{l}", [128, 8], F32) for l in range(self.n_layers)]
        self.B_gout = [Buf(f"gout{l}") for l in range(self.n_layers)]
        self.lnp = sb("lnp_sb", [128, 32], F32); self.B_lnp = Buf("lnp")
        self.cvp = sb("cvp_sb", [128, NFC, 4], F32); self.B_cvp = Buf("cvp")
        self.cvm = sb("cvm_sb", [128, NFC, 2], F32); self.B_cvm = Buf("cvm")
        self.esink = sb("esink", [128, 8], F32); self.B_esink = Buf("esink")
        self.esrow = sb("esrow", [128, 2, 512], F32); self.B_esrow = Buf("esrow")
        self.cst = [sb(f"cst{i}", [128, 64], F32) for i in range(2)]; self.B_cst = [Buf(f"cst{i}") for i in range(2)]
        self.xt = [sb(f"xt{i}", [128, D], BF16) for i in range(2)]; self.B_xt = [Buf(f"xt{i}") for i in range(2)]
        self.xTb = sb("xTb", [128, 8, 512], BF16); self.B_xTb = Buf("xTb")
        self.xTt = [sb(f"xTt{i}", [128, 8, 128], BF16) for i in range(3)]; self.B_xTt = [Buf(f"xTt{i}") for i in range(3)]
        self.wq = sb("wq_sb", [128, 8, 512], BF16); self.B_wq = Buf("wq")
        self.wkv = self.wq; self.B_wkv = self.B_wq
        self.fscr = sb("fscr", [128, 8], F32)
        self.t_sq = sb("t_sq", [128, 512], F32); self.B_tsq = Buf("t_sq")
        self.t_qn = sb("t_qn", [128, 512], F32); self.B_tqn = Buf("t_qn")
        self.t_ab = sb("t_ab", [128, 2, 256], F32); self.B_tab = Buf("t_ab")
        self.t_m = sb("t_m", [128, 4, 256], F32); self.B_tm = Buf("t_m")
        self.t_ss = sb("t_ss", [128, 16], F32); self.B_tss = Buf("t_ss")
        self.t_rs = sb("t_rs", [128, 16], F32); self.B_trs = Buf("t_rs")
        self.tq = dict(sq=self.t_sq, qn=self.t_qn, ab=self.t_ab, m=self.t_m, ss=self.t_ss, rs=self.t_rs,
                       B=[self.B_tsq, self.B_tqn, self.B_tab, self.B_tm, self.B_tss, self.B_trs])
        self.tk = []
        for i in range(2):
            self.tk.append(dict(sq=sb(f"k_sq{i}", [128, 128], F32), qn=sb(f"k_qn{i}", [128, 128], F32),
                                ab=sb(f"k_ab{i}", [128, 2, 64], F32), m=sb(f"k_m{i}", [128, 4, 64], F32),
                                ss=sb(f"k_ss{i}", [128, 4], F32), rs=sb(f"k_rs{i}", [128, 4], F32),
                                B=[Buf(f"k_t{i}_{j}") for j in range(6)]))
        self.qbf = [sb(f"qbf{i}", [128, 512], BF16) for i in range(2)]; self.B_qbf = [Buf(f"qbf{i}") for i in range(2)]
        self.kvb = [sb(f"kvb{i}", [128, 256], BF16) for i in range(2)]; self.B_kvb = [Buf(f"kvb{i}") for i in range(2)]
        self.B_kvb2 = [Buf(f"kvbB{i}") for i in range(2)]
        A1 = sb("A1", [128, 6144], BF16)
        self.QT = A1[:, 0:4096].rearrange("p (a n) -> p a n", a=8); self.B_QT = Buf("QT")
        self.PT = [A1[:, 4096 + i * 1024:5120 + i * 1024].rearrange("p (a n) -> p a n", a=2) for i in range(2)]
        self.B_PT = [Buf(f"PT{i}") for i in range(2)]
        self.hT = A1[:, 0:5632].rearrange("p (a n) -> p a n", a=11); self.B_hT = [Buf(f"hT{i}") for i in range(11)]
        A2 = sb("A2", [128, 10240], BF16)
        f32v = lambda a, b: A2[:, a:b].bitcast(F32)
        self.yT = A2[:, 0:4096].rearrange("p (a n) -> p a n", a=8); self.B_yT = Buf("yT")
        self.stt = [f32v(4096 + i * 2048, 6144 + i * 2048).rearrange("p (a n) -> p a n", a=2) for i in range(2)]
        self.B_stt = [Buf(f"stt{i}") for i in range(2)]
        self.dsb = [f32v(8192 + i * 1024, 9216 + i * 1024) for i in range(2)]; self.B_dsb = [Buf(f"dsb{i}") for i in range(2)]
        self.rr = self.dsb; self.B_rr = self.B_dsb
        self.wgu = [A2[:, i * 2048:(i + 1) * 2048].rearrange("p (a n) -> p a n", a=8) for i in range(3)]
        self.B_wgu = [Buf(f"wgu{i}") for i in range(3)]
        self.gcs = [f32v(6144 + i * 1024, 7168 + i * 1024) for i in range(2)]; self.B_gcs = [Buf(f"gcs{i}") for i in range(2)]
        self.tgs = [f32v(8192 + i * 1024, 9216 + i * 1024) for i in range(2)]; self.B_tgs = [Buf(f"tgs{i}") for i in range(2)]
        self.y2 = self.gcs; self.B_y2 = self.B_gcs
        self.ot = [t.rearrange("p (a n) -> p a n", a=4) for t in self.tgs]; self.B_ot = self.B_tgs
        self.att_bufs = [self.B_QT] + self.B_PT + [self.B_yT] + self.B_stt + self.B_dsb
        self.ffn_bufs = self.B_hT + self.B_wgu + self.B_gcs + self.B_tgs
        self.wo = [sb(f"wo{i}", [128, 8, 128], BF16) for i in range(4)]; self.B_wo = [Buf(f"wo{i}") for i in range(4)]
        self.z = sb("z", [128, 8, 512], F32); self.B_z = [Buf(f"z{m}") for m in range(8)]
        self.zb = [sb(f"zb{i}", [128, 512], BF16) for i in range(2)]; self.B_zb = [Buf(f"zb{i}") for i in range(2)]
        self.zq = [sb(f"zq{i}", [128, 512], BF16) for i in range(2)]; self.B_zq = [Buf(f"zq{i}") for i in range(2)]
        self.sqy = self.zq; self.B_sqy = self.B_zq
        self.mean = self.t_ab[:].rearrange("p a n -> p (a n)"); self.B_mean = self.B_tab
        self.rstd = self.t_m[:, 0:2, :].rearrange("p a n -> p (a n)"); self.B_rstd = self.B_tm
        self.tmpa = [self.t_sq, self.t_qn]; self.B_tmpa = [self.B_tsq, self.B_tqn]
        self.XB = [sb(f"XB{i}", [128, 8, 512], BF16) for i in range(2)]; self.B_XB = [Buf(f"XB{i}") for i in range(2)]
        self.XH = sb("XH", [128, 8, 2], BF16); self.B_XH = Buf("XH")
        self.LC = sb("LC", [128, 8, 8], BF16); self.B_LC = [Buf(f"LC{k}") for k in range(8)]
        self.HC = [sb(f"HC{i}", [128, 8, 2], BF16) for i in range(2)]; self.B_HC = [Buf(f"HC{i}") for i in range(2)]
        self.ed = [sb(f"ed{i}", [128, 2], F32) for i in range(3)]; self.B_ed = [Buf(f"ed{i}") for i in range(3)]
        self.wdp = [sb(f"wdp{i}", [128, 384], BF16) for i in range(8)]; self.B_wdp = [Buf(f"wdp{i}") for i in range(8)]
        self.cvt = [self.z[:, 2 * i:2 * i + 2, :].rearrange("p a n -> p (a n)") for i in range(2)]
        self.B_cvt = [[self.B_z[2 * i], self.B_z[2 * i + 1]] for i in range(2)]
        self.cvo = [self.z[:, 4 + i, :].bitcast(BF16) for i in range(2)]
        self.B_cvo = [[self.B_z[4 + i]] for i in range(2)]
        self.ps = self.st.enter_context(nc.psum_tensor("ps", [128, 8, 512], F32))
        self.B_ps = [Buf(f"ps{i}") for i in range(8)]
        self.B_gh = [Buf(f"gh{i}") for i in range(3)]

    def bslot(self, tk):
        sl = (tk - (self.own_off - 2)) % NT
        return sl if sl < 36 else None

    def fence(self, bufs):
        self.P.op("pool", lambda e: e.memset(self.fscr[:], 0.0), writes=list(bufs))

    def psb(self, b):
        return self.ps[:, b, :].bitcast(BF16).rearrange("p (a n) -> p a n", a=8)

    def setup_consts(self):
        P = self.P
        P.dma("sp", lambda e: e.dma_start(out=self.ident_f[:], in_=self.ident_in), writes=[self.B_idf])
        P.op("act", lambda e: e.activation(out=self.ident_b[:], in_=self.ident_f[:], func=AF.Copy),
             reads=[self.B_idf], writes=[self.B_idb])
        P.op("pool", lambda e: e.memset(self.ones_b[:], 1.0), writes=[self.B_ones])
        P.op("pool", lambda e: e.memset(self.zeros[:], 0.0), writes=[self.B_zeros])
        P.op("pool", lambda e: e.memset(self.onesf[:], 1.0), writes=[self.B_onesf])
        P.op("pool", lambda e: e.memset(self.epsr[:, 0:1], RMS_EPS), writes=[self.B_eps])
        P.op("pool", lambda e: e.memset(self.epsr[:, 1:2], math.log(0.125)), writes=[self.B_eps])
        P.op("pool", lambda e: e.memset(self.epsr[:, 2:3], 0.0), writes=[self.B_eps])
        P.op("pool", lambda e: e.memset(self.epsr[:, 3:4], LN_EPS), writes=[self.B_eps])
        P.op("pool", lambda e: e.memset(self.VA[:, :, :, 64:128], 1.0), writes=[self.B_vones])
        P.op("pool", lambda e: e.memset(self.VB[:, :, :, 64:65], 1.0), writes=[self.B_vones])
        P.dma("pool", lambda e: e.dma_start(out=self.biasT[:].rearrange("p a b n -> p (a b n)"), in_=self.biasT_in),
              writes=[self.B_biasT])
        P.dma("sp", lambda e: e.dma_start(out=self.jm[:], in_=self.jm_in), writes=[self.B_jm])

    def setup_layer(self, l, own_off):
        P = self.P
        w = self.W[l]
        P.dma("sp", lambda e: e.dma_start(out=self.nrm[:], in_=w["nrm"].partition_broadcast(128).rearrange("p a n -> p (a n)")), writes=[self.B_nrm])
        P.dma("sp", lambda e: e.dma_start(out=self.lnp[:], in_=w["lnp"]), writes=[self.B_lnp])
        P.dma("sp", lambda e: e.dma_start(out=self.cvp[:].rearrange("p c k -> p (c k)"), in_=w["cvp"]), writes=[self.B_cvp])
        P.dma("sp", lambda e: e.dma_start(out=self.esink[:], in_=w["sink"].partition_broadcast(128).rearrange("p a n -> p (a n)")), writes=[self.B_esink])
        P.op("act", lambda e: e.activation(out=self.esink[:], in_=self.esink[:], func=AF.Exp),
             reads=[self.B_esink], writes=[self.B_esink])
        for kvh in range(2):
            for c in range(4):
                h = kvh * 4 + c
                P.op("dve", lambda e, kvh=kvh, c=c, h=h: e.tensor_scalar(
                    self.esrow[:, kvh, c * 128:(c + 1) * 128], self.zeros[:, :],
                    self.esink[:, h:h + 1], None, ALU.add),
                    reads=[self.B_esink, self.B_zeros], writes=[self.B_esrow])
        jl = 2 if own_off == 0 else 3
        jr = 3 if own_off == 0 else 2
        P.op("dve", lambda e: e.tensor_scalar(self.cvm[:, :, 0], self.cvp[:, :, 0], self.jm[:, jl:jl + 1], None, ALU.mult),
             reads=[self.B_cvp, self.B_jm], writes=[self.B_cvm])
        P.op("dve", lambda e: e.tensor_scalar(self.cvm[:, :, 1], self.cvp[:, :, 2], self.jm[:, jr:jr + 1], None, ALU.mult),
             reads=[self.B_cvp, self.B_jm], writes=[self.B_cvm])

    def convert_weights(self, l):
        P = self.P
        w = self.W[l]
        steps = []

        def cast_dma(dst_ap, src_ap, B):
            P.dma("pool", lambda e: e.dma_start(out=dst_ap, in_=src_ap), dwrites=[B])

        for kc in range(8):
            steps.append(lambda kc=kc: cast_dma(w["wkv_s"].ap()[:, kc, :], w["wkv"][kc * 128:(kc + 1) * 128, :], w["B_wkv_s"]))
        for kc in range(8):
            steps.append(lambda kc=kc: cast_dma(w["wq_s"].ap()[:, kc, :], w["wq"][kc * 128:(kc + 1) * 128, :], w["B_wq_s"]))

        def wout_step(c):
            i = c % 2
            if c == 0:
                P.dma("sp", lambda e: e.dma_start(out=self.gout[l][:], in_=w["gout"]), writes=[self.B_gout[l]])
            P.dma("sp", lambda e: e.dma_start(out=self.cvt[i], in_=w["wout"][c * 128:(c + 1) * 128, :]),
                  writes=self.B_cvt[i])
            P.op("dve", lambda e: e.tensor_scalar(self.cvo[i], self.cvt[i], self.gout[l][:, c:c + 1], None, ALU.mult),
                 reads=self.B_cvt[i] + [self.B_gout[l]], writes=self.B_cvo[i])
            P.dma("sp", lambda e: e.dma_start(out=w["wout_s"].ap()[:, :, c, :].rearrange("m p n -> p m n"),
                                              in_=self.cvo[i].rearrange("p (m n) -> p m n", m=8)),
                  reads=self.B_cvo[i], dwrites=[w["B_wout_s"]], owner=w["B_wout_s"])
        for c in range(8):
            steps.append(lambda c=c: wout_step(c))
        for c in range(NFC):
            def gu(c=c):
                cast_dma(w["wgu_s"].ap()[c, :, :, 0:128],
                         w["wg"][:, c * 128:(c + 1) * 128].rearrange("(kc p) n -> p kc n", p=128), w["B_wgu_s"])
                cast_dma(w["wgu_s"].ap()[c, :, :, 128:256],
                         w["wu"][:, c * 128:(c + 1) * 128].rearrange("(kc p) n -> p kc n", p=128), w["B_wgu_s"])
            steps.append(gu)
        for c in range(NFC):
            steps.append(lambda c=c: cast_dma(w["wd_s"].ap()[c], w["wd"][c * 128:(c + 1) * 128, :], w["B_wd_s"]))
        return steps

    def load_xT(self, src, src_is_f32, B_src, t, dst_ap, B_dst, slot):
        P = self.P
        xt, B_xt = self.xt[slot], self.B_xt[slot]
        rd = [B_src] if B_src is not None else []
        if src_is_f32:
            P.dma("pool", lambda e: e.dma_start(out=xt[:], in_=src[t * 128:(t + 1) * 128, :]), reads=rd, writes=[B_xt])
        else:
            P.dma("sp", lambda e: e.dma_start(out=xt[:], in_=src[t * 128:(t + 1) * 128, :]), reads=rd, writes=[B_xt])
        bank = 6 + slot
        pv = self.psb(bank)

        def tr(e):
            for c in range(8):
                ins = e.transpose(pv[:, c, :], xt[:, c * 128:(c + 1) * 128], self.ident_b[:])
            return ins
        P.op("pe", tr, reads=[B_xt, self.B_idb], writes=[self.B_ps[bank]])
        P.op("act", lambda e: e.activation(out=dst_ap, in_=pv, func=AF.Copy), reads=[self.B_ps[bank]], writes=[B_dst])

    def norm_rope(self, src, B_src, nh, goff, cs, B_cs, scale, dst, B_dst, T=None):
        P = self.P
        W = nh * 64
        if T is None:
            T = self.tq
        B_tsq, B_tqn, B_tab, B_tm, B_tss, B_trs = T["B"]
        sq = T["sq"][:, 0:W]
        P.op("act", lambda e: e.activation(out=sq, in_=src, func=AF.Square), reads=[B_src], writes=[B_tsq])
        ss = T["ss"][:, 0:nh]
        P.op("dve", lambda e: e.tensor_reduce(out=ss, in_=sq.rearrange("p (h d) -> p h d", h=nh), axis=AX.X, op=ALU.add),
             reads=[B_tsq], writes=[B_tss])
        P.op("act", lambda e: e.activation(out=ss, in_=ss, func=AF.Ln, scale=1.0 / 64.0, bias=self.epsr[:, 0:1]),
             reads=[B_tss, self.B_eps], writes=[B_tss])
        rs = T["rs"][:, 0:nh]
        bcol = 1 if scale != 1.0 else 2
        P.op("act", lambda e: e.activation(out=rs, in_=ss, func=AF.Exp, scale=-0.5, bias=self.epsr[:, bcol:bcol + 1]),
             reads=[B_tss, self.B_eps], writes=[B_trs])
        qn = T["qn"][:, 0:W].rearrange("p (h d) -> p h d", h=nh)
        P.op("dve", lambda e: e.tensor_tensor(qn, src.rearrange("p (h d) -> p h d", h=nh),
                                              rs.unsqueeze(2).to_broadcast([128, nh, 64]), ALU.mult),
             reads=[B_src, B_trs], writes=[B_tqn])
        x0 = qn[:, :, 0::2]
        x1 = qn[:, :, 1::2]
        ge = self.nrm[:, goff:goff + 32].unsqueeze(1).to_broadcast([128, nh, 32])
        go = self.nrm[:, goff + 32:goff + 64].unsqueeze(1).to_broadcast([128, nh, 32])
        cosb = cs[:, 0:32].unsqueeze(1).to_broadcast([128, nh, 32])
        sinb = cs[:, 32:64].unsqueeze(1).to_broadcast([128, nh, 32])
        a = T["ab"][:, 0, 0:nh * 32].rearrange("p (h d) -> p h d", h=nh)
        b = T["ab"][:, 1, 0:nh * 32].rearrange("p (h d) -> p h d", h=nh)
        P.op("dve", lambda e: e.tensor_tensor(a, x0, ge, ALU.mult), reads=[B_tqn, self.B_nrm], writes=[B_tab])
        P.op("dve", lambda e: e.tensor_tensor(b, x1, go, ALU.mult), reads=[B_tqn, self.B_nrm], writes=[B_tab])
        m = [T["m"][:, i, 0:nh * 32].rearrange("p (h d) -> p h d", h=nh) for i in range(4)]
        P.op("dve", lambda e: e.tensor_tensor(m[0], a, cosb, ALU.mult), reads=[B_tab, B_cs], writes=[B_tm])
        P.op("dve", lambda e: e.tensor_tensor(m[1], b, sinb, ALU.mult), reads=[B_tab, B_cs], writes=[B_tm])
        P.op("dve", lambda e: e.tensor_tensor(m[2], a, sinb, ALU.mult), reads=[B_tab, B_cs], writes=[B_tm])
        P.op("dve", lambda e: e.tensor_tensor(m[3], b, cosb, ALU.mult), reads=[B_tab, B_cs], writes=[B_tm])
        d3 = dst.rearrange("p (h d) -> p h d", h=nh)
        P.op("dve", lambda e: e.tensor_tensor(d3[:, :, 0:32], m[0], m[1], ALU.subtract), reads=[B_tm], writes=[B_dst])
        P.op("dve", lambda e: e.tensor_tensor(d3[:, :, 32:64], m[2], m[3], ALU.add), reads=[B_tm], writes=[B_dst])

    def load_cs(self, t):
        i = t % 2
        self.P.dma("sp", lambda e: e.dma_start(out=self.cst[i][:], in_=self.cs_in[:, t, :]), writes=[self.B_cst[i]])
        return self.cst[i], self.B_cst[i]

    def kv_phase(self, l, src, src_is_f32, B_src, pending):
        P = self.P
        w = self.W[l]
        P.dma("sp", lambda e: e.dma_start(out=self.wkv[:], in_=w["wkv_s"].ap()), reads=[w["B_wkv_s"]], writes=[self.B_wkv])
        def stage1a(t):
            self.load_xT(src, src_is_f32, B_src, t, self.xTt[t % 3][:], self.B_xTt[t % 3], t % 2)
            for _ in range(3):
                if pending:
                    pending.pop(0)()

        def stage1b(t):
            slot = t % 2
            xT = self.xTt[t % 3]
            bk = 4 + slot
            pk = self.ps[:, bk, :]

            def mm(e, xT=xT, pk=pk):
                for c in range(8):
                    ins = e.matmul(pk, lhsT=xT[:, c, :], rhs=self.wkv[:, c, :], start=(c == 0), stop=(c == 7))
                return ins
            P.op("pe", mm, reads=[self.B_xTt[t % 3], self.B_wkv], writes=[self.B_ps[bk]])

        def stage2a(t):
            slot = t % 2
            bk = 4 + slot
            pk = self.ps[:, bk, :]
            kvb, B_kvb, B_kvb2 = self.kvb[slot], self.B_kvb[slot], self.B_kvb2[slot]
            P.op("act", lambda e: e.activation(out=kvb[:, 128:256], in_=pk[:, 256:384], func=AF.Copy),
                 reads=[self.B_ps[bk]], writes=[B_kvb2])
            P.op("act", lambda e: e.activation(out=self.VA[:, t, :, 0:64],
                                               in_=pk[:, 128:256].rearrange("p (h d) -> p h d", h=2), func=AF.Copy),
                 reads=[self.B_ps[bk], self.B_vones], writes=[self.B_KA[t]])
            sl = self.bslot(t)
            if sl is not None:
                P.op("act", lambda e: e.activation(out=self.VB[:, sl, :, 0:64],
                                                   in_=pk[:, 384:512].rearrange("p (h d) -> p h d", h=2), func=AF.Copy),
                     reads=[self.B_ps[bk], self.B_vones], writes=[self.B_KB[t]])
            cs, B_cs = self.load_cs(t)
            self.norm_rope(pk[:, 0:128], self.B_ps[bk], 2, 64, cs, B_cs, 1.0, kvb[:, 0:128], B_kvb, T=self.tk[slot])

        def stage2b(t):
            slot = t % 2
            kvb, B_kvb, B_kvb2 = self.kvb[slot], self.B_kvb[slot], self.B_kvb2[slot]
            sl = self.bslot(t)
            bt = 2 + slot
            pt = self.psb(bt)

            def trk(e):
                e.transpose(pt[:, 0, :], kvb[:, 0:128], self.ident_b[:])
                return e.transpose(pt[:, 1, :], kvb[:, 128:256], self.ident_b[:])
            P.op("pe", trk, reads=[B_kvb, B_kvb2, self.B_idb], writes=[self.B_ps[bt]])
            P.op("dve", lambda e: e.tensor_copy(self.KAT[:, t * 128:(t + 1) * 128], pt[:, 0, :]),
                 reads=[self.B_ps[bt]], writes=[self.B_KA[t]])
            if sl is not None:
                P.op("dve", lambda e: e.tensor_copy(self.KBT[:, sl * 128:(sl + 1) * 128], pt[:, 1, :]),
                     reads=[self.B_ps[bt]], writes=[self.B_KB[t]])

        stage1a(0)
        stage1a(1)
        stage1b(0)
        for t in range(NT):
            stage2a(t)
            if t + 2 < NT:
                stage1a(t + 2)
            if t + 1 < NT:
                stage1b(t + 1)
            stage2b(t)

    def att_block(self, l, src, src_is_f32, B_src, tiles, col0, ncol, x1_dst, B_x1):
        P = self.P
        w = self.W[l]
        nt = len(tiles)
        ntok = nt * 128
        for i, t in enumerate(tiles):
            self.load_xT(src, src_is_f32, B_src, t, self.xTb[:, :, i * 128:(i + 1) * 128], self.B_xTb, i % 2)
        for half in range(2):
            P.dma("sp", lambda e, half=half: e.dma_start(out=self.wq[:], in_=w["wq_s"].ap()[:, :, half * 512:(half + 1) * 512]),
                  reads=[w["B_wq_s"]], writes=[self.B_wq])
            for i, t in enumerate(tiles):
                bk = 4 + i
                pq = self.ps[:, bk, :]

                def mm(e, i=i, pq=pq):
                    for c in range(8):
                        ins = e.matmul(pq, lhsT=self.xTb[:, c, i * 128:(i + 1) * 128], rhs=self.wq[:, c, :],
                                       start=(c == 0), stop=(c == 7))
                    return ins
                P.op("pe", mm, reads=[self.B_xTb, self.B_wq], writes=[self.B_ps[bk]])
            for i, t in enumerate(tiles):
                bk = 4 + i
                pq = self.ps[:, bk, :]
                qb, B_qb = self.qbf[i % 2], self.B_qbf[i % 2]
                if half == 0:
                    cs, B_cs = self.load_cs(t)
                    self.norm_rope(pq, self.B_ps[bk], 8, 0, cs, B_cs, 0.125, qb[:, 0:512], B_qb)
                else:
                    P.op("act", lambda e, qb=qb, pq=pq: e.activation(out=qb[:, 0:512], in_=pq, func=AF.Copy),
                         reads=[self.B_ps[bk]], writes=[B_qb])
                bt = 2 + (i % 2)
                pt = self.psb(bt)

                def trq(e, qb=qb, pt=pt):
                    for c in range(4):
                        ins = e.transpose(pt[:, c, :], qb[:, c * 128:(c + 1) * 128], self.ident_b[:])
                    return ins
                P.op("pe", trq, reads=[B_qb, self.B_idb], writes=[self.B_ps[bt]])
                P.op("dve", lambda e, i=i, half=half, pt=pt: e.tensor_copy(
                    self.QT[:, half * 4:(half + 1) * 4, i * 128:(i + 1) * 128], pt[:, 0:4, :]),
                    reads=[self.B_ps[bt]], writes=[self.B_QT])
        a_lo, a_n = (0, ntok) if ncol == ntok else (col0, ncol)
        for c in range(4):
            accb = (4, 5) if c % 2 == 0 else (6, 7)

            def qk(kt, c=c):
                sb0 = 2 * (kt % 2)

                def f(e):
                    e.matmul(self.ps[:, sb0, 0:a_n], lhsT=self.KAT[0:64, kt * 128:(kt + 1) * 128],
                             rhs=self.QT[0:64, c, a_lo:a_lo + a_n], start=True, stop=True)
                    return e.matmul(self.ps[:, sb0 + 1, 0:a_n], lhsT=self.KAT[64:128, kt * 128:(kt + 1) * 128],
                                    rhs=self.QT[64:128, c, a_lo:a_lo + a_n], start=True, stop=True)
                P.op("pe", f, reads=[self.B_KA[kt], self.B_QT], writes=[self.B_ps[sb0], self.B_ps[sb0 + 1]])

            def ex(kt):
                sb0 = 2 * (kt % 2)
                pt = self.PT[kt % 2]
                P.op("act", lambda e: e.activation(out=pt[:, :, 0:a_n], in_=self.ps[:, sb0:sb0 + 2, 0:a_n], func=AF.Exp),
                     reads=[self.B_ps[sb0], self.B_ps[sb0 + 1]], writes=[self.B_PT[kt % 2]])

            def pv(kt, accb=accb):
                pt = self.PT[kt % 2]

                def f(e):
                    e.matmul(self.ps[:, accb[0], 0:a_n], lhsT=self.VA[:, kt, 0, :], rhs=pt[:, 0, 0:a_n],
                             start=(kt == 0), stop=(kt == NT - 1))
                    return e.matmul(self.ps[:, accb[1], 0:a_n], lhsT=self.VA[:, kt, 1, :], rhs=pt[:, 1, 0:a_n],
                                    start=(kt == 0), stop=(kt == NT - 1))
                P.op("pe", f, reads=[self.B_KA[kt], self.B_PT[kt % 2]], writes=[self.B_ps[accb[0]], self.B_ps[accb[1]]])

            qk(0)
            qk(1)
            for kt in range(NT):
                ex(kt)
                pv(kt)
                if kt + 2 < NT:
                    qk(kt + 2)
            for kvh in range(2):
                self.finalize_head(accb[kvh], kvh, None, self.yT[kvh * 64:(kvh + 1) * 64, c, a_lo:a_lo + a_n], a_n, kvh)
        def b_main(i, t):
            accb = (4, 5) if i % 2 == 0 else (6, 7)
            nbrs = [((t - 1) % NT, 0), (t, 1), ((t + 1) % NT, 2)]

            def qk(jj):
                tk, jidx = nbrs[jj]
                sb0 = 2 * (jj % 2)
                ks = self.bslot(tk)
                assert ks is not None

                def f(e):
                    e.matmul(self.ps[:, sb0, :], lhsT=self.KBT[0:64, ks * 128:(ks + 1) * 128],
                             rhs=self.QT[0:64, 4:8, i * 128:(i + 1) * 128], start=True, stop=True)
                    return e.matmul(self.ps[:, sb0 + 1, :], lhsT=self.KBT[64:128, ks * 128:(ks + 1) * 128],
                                    rhs=self.QT[64:128, 4:8, i * 128:(i + 1) * 128], start=True, stop=True)
                P.op("pe", f, reads=[self.B_KB[tk], self.B_QT], writes=[self.B_ps[sb0], self.B_ps[sb0 + 1]])

            def chain(jj):
                tk, jidx = nbrs[jj]
                sb0 = 2 * (jj % 2)
                st = self.stt[jj % 2]
                B_st = self.B_stt[jj % 2]
                ks = self.bslot(tk)
                for kvh in range(2):
                    P.op("dve", lambda e, kvh=kvh: e.scalar_tensor_tensor(
                        out=st[:, kvh, :], in0=self.ps[:, sb0 + kvh, :], scalar=0.125, in1=self.biasT[:, jidx, kvh, :],
                        op0=ALU.mult, op1=ALU.add),
                        reads=[self.B_ps[sb0 + kvh], self.B_biasT], writes=[B_st])
                jcol = None
                if jidx == 0 and t == 0:
                    jcol = 0
                elif jidx == 0 and t == 32:
                    jcol = 1
                elif jidx == 2 and t == 63:
                    jcol = 0
                elif jidx == 2 and t == 31:
                    jcol = 1
                if jcol is not None:
                    P.op("dve", lambda e: e.tensor_scalar(
                        st[:].rearrange("p a n -> p (a n)"), st[:].rearrange("p a n -> p (a n)"),
                        self.jm[:, jcol:jcol + 1], None, ALU.add),
                        reads=[B_st, self.B_jm], writes=[B_st])
                pt = self.PT[jj % 2]
                P.op("act", lambda e: e.activation(out=pt[:], in_=st[:], func=AF.Exp),
                     reads=[B_st], writes=[self.B_PT[jj % 2]])

                def g(e):
                    e.matmul(self.ps[0:65, accb[0], :], lhsT=self.VB[:, ks, 0, :], rhs=pt[:, 0, :],
                             start=(jj == 0), stop=(jj == 2))
                    return e.matmul(self.ps[0:65, accb[1], :], lhsT=self.VB[:, ks, 1, :], rhs=pt[:, 1, :],
                                    start=(jj == 0), stop=(jj == 2))
                P.op("pe", g, reads=[self.B_KB[tk], self.B_PT[jj % 2]], writes=[self.B_ps[accb[0]], self.B_ps[accb[1]]])

            qk(0)
            qk(1)
            chain(0)
            qk(2)
            chain(1)
            chain(2)

        def b_fin(i):
            accb = (4, 5) if i % 2 == 0 else (6, 7)
            self.finalize_b_pair(accb, [self.yT[kvh * 64:(kvh + 1) * 64, 4:8, i * 128:(i + 1) * 128] for kvh in range(2)])

        for i, t in enumerate(tiles):
            b_main(i, t)
            if i >= 1:
                b_fin(i - 1)
        b_fin(nt - 1)
        self.out_stage(l, col0, ncol)
        self.layer_norm(0, col0, ncol, lambda m: x1_dst[:, m, :], B_x1, BF16)

    def finalize_head(self, bank, kvh, sink_kvh, y_dst, n, slot):
        P = self.P
        dsb, B_dsb = self.dsb[slot], self.B_dsb[slot]
        if sink_kvh is not None:
            P.op("dve", lambda e: e.tensor_tensor(dsb[0:64, 0:n], self.ps[64:128, bank, 0:n], self.esrow[64:128, sink_kvh, 0:n], ALU.add),
                 reads=[self.B_ps[bank], self.B_esrow], writes=[B_dsb])
        else:
            P.op("dve", lambda e: e.tensor_copy(dsb[0:64, 0:n], self.ps[64:128, bank, 0:n]),
                 reads=[self.B_ps[bank]], writes=[B_dsb])
        if sink_kvh is not None:
            P.op("act", lambda e: e.activation(out=dsb[0:64, 0:n], in_=dsb[0:64, 0:n], func=AF.Ln), reads=[B_dsb], writes=[B_dsb])
            P.op("act", lambda e: e.activation(out=dsb[0:64, 0:n], in_=dsb[0:64, 0:n], func=AF.Exp, scale=-1.0),
                 reads=[B_dsb], writes=[B_dsb])
        else:
            P.op("dve", lambda e: e.reciprocal(dsb[0:64, 0:n], dsb[0:64, 0:n]), reads=[B_dsb], writes=[B_dsb])
        if len(y_dst.shape) == 3:
            in0 = self.ps[0:64, bank, 0:n].rearrange("p (a n) -> p a n", a=4)
            in1 = dsb[0:64, 0:n].rearrange("p (a n) -> p a n", a=4)
        else:
            in0 = self.ps[0:64, bank, 0:n]
            in1 = dsb[0:64, 0:n]
        P.op("dve", lambda e: e.tensor_tensor(y_dst, in0, in1, ALU.mult),
             reads=[self.B_ps[bank], B_dsb], writes=[self.B_yT])

    def finalize_b_pair(self, banks, y_dsts):
        P = self.P
        n = 512
        for kvh in range(2):
            dsb, B_dsb, bank = self.dsb[kvh], self.B_dsb[kvh], banks[kvh]
            P.op("dve", lambda e, dsb=dsb, bank=bank, kvh=kvh: e.tensor_tensor(
                dsb[64:65, 0:n], self.ps[64:65, bank, 0:n], self.esrow[64:65, kvh, 0:n], ALU.add),
                reads=[self.B_ps[bank], self.B_esrow], writes=[B_dsb])
        for kvh in range(2):
            dsb, B_dsb = self.dsb[kvh], self.B_dsb[kvh]
            P.op("act", lambda e, dsb=dsb: e.activation(out=dsb[64:65, 0:n], in_=dsb[64:65, 0:n], func=AF.Ln),
                 reads=[B_dsb], writes=[B_dsb])
            P.op("act", lambda e, dsb=dsb: e.activation(out=dsb[64:65, 0:n], in_=dsb[64:65, 0:n], func=AF.Exp, scale=-1.0),
                 reads=[B_dsb], writes=[B_dsb])
        for kvh in range(2):
            dsb, B_dsb = self.dsb[kvh], self.B_dsb[kvh]
            bb = 2 + kvh
            P.op("pe", lambda e, dsb=dsb, bb=bb: e.matmul(self.ps[0:64, bb, 0:n], lhsT=self.onesf[64:65, 0:64], rhs=dsb[64:65, 0:n],
                                                        start=True, stop=True),
                 reads=[B_dsb, self.B_onesf], writes=[self.B_ps[bb]])
        for kvh in range(2):
            dsb, B_dsb = self.dsb[kvh], self.B_dsb[kvh]
            bb = 2 + kvh
            P.op("act", lambda e, dsb=dsb, bb=bb: e.activation(out=dsb[0:64, 0:n], in_=self.ps[0:64, bb, 0:n], func=AF.Copy),
                 reads=[self.B_ps[bb]], writes=[B_dsb])
        for kvh in range(2):
            dsb, B_dsb, bank = self.dsb[kvh], self.B_dsb[kvh], banks[kvh]
            in0 = self.ps[0:64, bank, 0:n].rearrange("p (a n) -> p a n", a=4)
            in1 = dsb[0:64, 0:n].rearrange("p (a n) -> p a n", a=4)
            P.op("dve", lambda e, y=y_dsts[kvh], in0=in0, in1=in1: e.tensor_tensor(y, in0, in1, ALU.mult),
                 reads=[self.B_ps[bank], B_dsb], writes=[self.B_yT])

    def rsqrt_act(self, dst, src_ap, B_src, B_dst, scale, eps_col, ncol):
        P = self.P
        P.op("act", lambda e: e.activation(out=dst, in_=src_ap, func=AF.Ln, scale=scale, bias=self.epsr[:, eps_col:eps_col + 1]),
             reads=[B_src, self.B_eps], writes=[B_dst])
        P.op("act", lambda e: e.activation(out=dst, in_=dst, func=AF.Exp, scale=-0.5), reads=[B_dst], writes=[B_dst])

    def out_stage(self, l, col0, ncol):
        P = self.P
        w = self.W[l]
        cs = slice(col0, col0 + ncol)
        for g in range(2):
            bank = g
            for cc in range(4):
                c = g * 4 + cc
                sq, B_sq = self.sqy[cc % 2], self.B_sqy[cc % 2]
                P.op("act", lambda e, sq=sq, c=c: e.activation(out=sq[:, 0:ncol], in_=self.yT[:, c, cs], func=AF.Square),
                     reads=[self.B_yT], writes=[B_sq])
                P.op("pe", lambda e, sq=sq, cc=cc, bank=bank: e.matmul(self.ps[:, bank, 0:ncol], lhsT=self.ones_b[:], rhs=sq[:, 0:ncol],
                                                                    start=(cc == 0), stop=(cc == 3)),
                     reads=[B_sq, self.B_ones], writes=[self.B_ps[bank]])
            rr, B_rr = self.rr[g], self.B_rr[g]
            self.rsqrt_act(rr[:, 0:ncol], self.ps[:, bank, 0:ncol], self.B_ps[bank], B_rr, 1.0 / 512.0, 0, ncol)
            for cc in range(4):
                c = g * 4 + cc
                P.op("dve", lambda e, c=c, rr=rr: e.tensor_tensor(self.yT[:, c, cs], self.yT[:, c, cs], rr[:, 0:ncol], ALU.mult),
                     reads=[self.B_yT, B_rr], writes=[self.B_yT])
        for m in range(8):
            wo, B_wo = self.wo[m % 4], self.B_wo[m % 4]
            P.dma("sp", lambda e, wo=wo, m=m: e.dma_start(out=wo[:], in_=w["wout_s"].ap()[m]),
                  reads=[w["B_wout_s"]], writes=[B_wo])
            ba = 2 + (m % 2)

            def mm(e, wo=wo, ba=ba, m=m):
                for c in range(8):
                    ins = e.matmul(self.ps[:, ba, 0:ncol], lhsT=wo[:, c, :], rhs=self.yT[:, c, cs],
                                   start=(c == 0), stop=(c == 7))
                return ins
            P.op("pe", mm, reads=[B_wo, self.B_yT], writes=[self.B_ps[ba]])
            P.op("dve", lambda e, ba=ba, m=m: e.scalar_tensor_tensor(out=self.z[:, m, 0:ncol], in0=self.xTb[:, m, cs], scalar=ALPHA,
                                                                     in1=self.ps[:, ba, 0:ncol], op0=ALU.mult, op1=ALU.add),
                 reads=[self.B_xTb, self.B_ps[ba]], writes=[self.B_z[m]])

    def layer_norm(self, which, col0_unused, ncol, dst_fn, B_dst, out_dt):
        P = self.P
        gcol = 0 if which == 0 else 16
        bcol = gcol + 8
        for m in range(8):
            zb, B_zb = self.zb[m % 2], self.B_zb[m % 2]
            zq, B_zq = self.zq[m % 2], self.B_zq[m % 2]
            P.op("act", lambda e, zb=zb, m=m: e.activation(out=zb[:, 0:ncol], in_=self.z[:, m, 0:ncol], func=AF.Copy),
                 reads=[self.B_z[m]], writes=[B_zb])
            P.op("act", lambda e, zq=zq, m=m: e.activation(out=zq[:, 0:ncol], in_=self.z[:, m, 0:ncol], func=AF.Square),
                 reads=[self.B_z[m]], writes=[B_zq])
            P.op("pe", lambda e, zb=zb, m=m: e.matmul(self.ps[:, 0, 0:ncol], lhsT=self.ones_b[:], rhs=zb[:, 0:ncol],
                                                      start=(m == 0), stop=(m == 7)),
                 reads=[B_zb, self.B_ones], writes=[self.B_ps[0]])
            P.op("pe", lambda e, zq=zq, m=m: e.matmul(self.ps[:, 1, 0:ncol], lhsT=self.ones_b[:], rhs=zq[:, 0:ncol],
                                                      start=(m == 0), stop=(m == 7)),
                 reads=[B_zq, self.B_ones], writes=[self.B_ps[1]])
        mean, rstd = self.mean, self.rstd
        P.op("dve", lambda e: e.tensor_scalar(mean[:, 0:ncol], self.ps[:, 0, 0:ncol], 1.0 / D, None, ALU.mult),
             reads=[self.B_ps[0]], writes=[self.B_mean])
        P.op("dve", lambda e: e.tensor_tensor(rstd[:, 0:ncol], mean[:, 0:ncol], mean[:, 0:ncol], ALU.mult),
             reads=[self.B_mean], writes=[self.B_rstd])
        P.op("dve", lambda e: e.scalar_tensor_tensor(out=rstd[:, 0:ncol], in0=self.ps[:, 1, 0:ncol], scalar=1.0 / D,
                                                     in1=rstd[:, 0:ncol], op0=ALU.mult, op1=ALU.subtract),
             reads=[self.B_ps[1], self.B_rstd], writes=[self.B_rstd])
        self.rsqrt_act(rstd[:, 0:ncol], rstd[:, 0:ncol], self.B_rstd, self.B_rstd, 1.0, 3, ncol)
        for m in range(8):
            ta, B_ta = self.tmpa[m % 2], self.B_tmpa[m % 2]
            P.op("dve", lambda e, ta=ta, m=m: e.tensor_tensor(ta[:, 0:ncol], self.z[:, m, 0:ncol], mean[:, 0:ncol], ALU.subtract),
                 reads=[self.B_z[m], self.B_mean], writes=[B_ta])
            P.op("dve", lambda e, ta=ta: e.tensor_tensor(ta[:, 0:ncol], ta[:, 0:ncol], rstd[:, 0:ncol], ALU.mult),
                 reads=[B_ta, self.B_rstd], writes=[B_ta])
            dst = dst_fn(m)
            Bd = B_dst(m) if callable(B_dst) else B_dst
            P.op("act", lambda e, ta=ta, m=m, dst=dst: e.activation(out=dst, in_=ta[:, 0:ncol], func=AF.Identity,
                                                                    scale=self.lnp[:, gcol + m:gcol + m + 1],
                                                                    bias=self.lnp[:, bcol + m:bcol + m + 1]),
                 reads=[B_ta, self.B_lnp], writes=[Bd])

    def ffn_block(self, l, k, X, B_X, hl_ap, B_hl, hr_ap, B_hr, first, last, dst, dst_dt, B_dstbuf, row0):
        P = self.P
        w = self.W[l]
        HC, B_HC = self.HC[k % 2], self.B_HC[k % 2]
        P.op("dve", lambda e: e.tensor_copy(HC[:, :, 0:1], hl_ap), reads=[B_hl], writes=[B_HC])
        P.op("dve", lambda e: e.tensor_copy(HC[:, :, 1:2], hr_ap), reads=[B_hr], writes=[B_HC])
        for m in range(8):
            P.op("act", lambda e, m=m: e.activation(out=self.z[:, m, :], in_=X[:, m, :], func=AF.Copy, scale=ALPHA),
                 reads=[B_X], writes=[self.B_z[m]])
        for hh in range(2):
            for cc in range(11):
                c = hh * 11 + cc
                wg, B_wg = self.wgu[c % 3], self.B_wgu[c % 3]
                P.dma("sp", lambda e, wg=wg, c=c: e.dma_start(out=wg[:], in_=w["wgu_s"].ap()[c]),
                      reads=[w["B_wgu_s"]], writes=[B_wg])
                gb = (0, 1, 4)[c % 3]
                ub = (2, 3, 5)[c % 3]
                hcol = (c % 3) * 2

                def mm(e, wg=wg, gb=gb, ub=ub, hcol=hcol):
                    for kc in range(8):
                        e.matmul(self.ps[:, gb, :], lhsT=wg[:, kc, 0:128], rhs=X[:, kc, :], start=(kc == 0), stop=(kc == 7))
                    for kc in range(8):
                        e.matmul(self.ps[:, 7, hcol:hcol + 2], lhsT=wg[:, kc, 0:128], rhs=HC[:, kc, :], start=(kc == 0), stop=(kc == 7))
                    for kc in range(8):
                        ins = e.matmul(self.ps[:, ub, :], lhsT=wg[:, kc, 128:256], rhs=X[:, kc, :], start=(kc == 0), stop=(kc == 7))
                    return ins
                P.op("pe", mm, reads=[B_wg, B_X, B_HC], writes=[self.B_ps[gb], self.B_ps[ub], self.B_ps[7]])
                gc, B_gc = self.gcs[c % 2], self.B_gcs[c % 2]
                G = self.ps[:, gb, :]
                Gh = self.ps[:, 7, hcol:hcol + 2]
                w0 = self.cvp[:, c, 0:1]
                w1 = self.cvp[:, c, 1:2]
                w2 = self.cvp[:, c, 2:3]
                cb = self.cvp[:, c, 3:4]
                w0e = self.cvm[:, c, 0:1] if first else w0
                w2e = self.cvm[:, c, 1:2] if last else w2
                ed, B_ed = self.ed[c % 3], self.B_ed[c % 3]
                P.op("act", lambda e, ed=ed, Gh=Gh: e.activation(out=ed[:], in_=Gh, func=AF.Copy),
                     reads=[self.B_ps[7]], writes=[B_ed])
                P.op("act", lambda e, gc=gc, G=G, w1=w1, cb=cb: e.activation(out=gc[:], in_=G, func=AF.Identity, scale=w1, bias=cb),
                     reads=[self.B_ps[gb], self.B_cvp], writes=[B_gc])
                P.op("dve", lambda e, gc=gc, G=G, w0=w0: e.scalar_tensor_tensor(out=gc[:, 1:512], in0=G[:, 0:511], scalar=w0,
                                                                                 in1=gc[:, 1:512], op0=ALU.mult, op1=ALU.add),
                     reads=[self.B_ps[gb], B_gc, self.B_cvp], writes=[B_gc])
                P.op("dve", lambda e, gc=gc, G=G, w2=w2: e.scalar_tensor_tensor(out=gc[:, 0:511], in0=G[:, 1:512], scalar=w2,
                                                                                 in1=gc[:, 0:511], op0=ALU.mult, op1=ALU.add),
                     reads=[self.B_ps[gb], B_gc, self.B_cvp], writes=[B_gc])
                P.op("dve", lambda e, gc=gc, ed=ed, w0e=w0e: e.scalar_tensor_tensor(out=gc[:, 0:1], in0=ed[:, 0:1], scalar=w0e,
                                                                                     in1=gc[:, 0:1], op0=ALU.mult, op1=ALU.add),
                     reads=[B_ed, B_gc, self.B_cvp, self.B_cvm], writes=[B_gc])
                P.op("dve", lambda e, gc=gc, ed=ed, w2e=w2e: e.scalar_tensor_tensor(out=gc[:, 511:512], in0=ed[:, 1:2], scalar=w2e,
                                                                                     in1=gc[:, 511:512], op0=ALU.mult, op1=ALU.add),
                     reads=[B_ed, B_gc, self.B_cvp, self.B_cvm], writes=[B_gc])
                if cc >= 1:
                    self._ffn_tail(c - 1, cc - 1)
            self._ffn_tail(hh * 11 + 10, 10)
            di = 0
            for gi, grp in enumerate(((0, 1, 2), (3, 4, 5), (6, 7))):
                ng = len(grp)
                ab = 4 if (hh * 3 + gi) % 2 == 0 else 0
                for cc in range(11):
                    c = hh * 11 + cc
                    wd, B_wd = self.wdp[di % 8], self.B_wdp[di % 8]
                    di += 1
                    P.dma("sp", lambda e, wd=wd, c=c, grp=grp, ng=ng: e.dma_start(
                        out=wd[:, 0:ng * 128], in_=w["wd_s"].ap()[c, :, grp[0] * 128:(grp[0] + ng) * 128]),
                        reads=[w["B_wd_s"]], writes=[B_wd])

                    def mm(e, wd=wd, cc=cc, ng=ng, ab=ab):
                        for mi in range(ng):
                            ins = e.matmul(self.ps[:, ab + mi, :], lhsT=wd[:, mi * 128:(mi + 1) * 128], rhs=self.hT[:, cc, :],
                                           start=(cc == 0), stop=(cc == 10))
                        return ins
                    P.op("pe", mm, reads=[B_wd, self.B_hT[cc]], writes=[self.B_ps[ab + mi] for mi in range(ng)])
                for mi, m in enumerate(grp):
                    P.op("dve", lambda e, mi=mi, m=m, ab=ab: e.tensor_tensor(self.z[:, m, :], self.ps[:, ab + mi, :], self.z[:, m, :], ALU.add),
                         reads=[self.B_ps[ab + mi], self.B_z[m]], writes=[self.B_z[m]])
        is_f32 = (dst_dt == F32)

        def emit_out(m, y2, B_y2):
            bank = 2 + (m % 2)
            if is_f32:
                pv = self.ps[:, bank, :].rearrange("p (a n) -> p a n", a=4)
                idn, B_idn = self.ident_f, self.B_idf
            else:
                pv = self.psb(bank)[:, 0:4, :]
                idn, B_idn = self.ident_b, self.B_idb

            def tr(e):
                for i in range(4):
                    ins = e.transpose(pv[:, i, :], y2[:, i * 128:(i + 1) * 128], idn[:])
                return ins
            P.op("pe", tr, reads=[B_y2, B_idn], writes=[self.B_ps[bank]])
            if is_f32:
                ot, B_ot = self.ot[m % 2], self.B_ot[m % 2]
                otv = ot[:]
            else:
                ot, B_ot = self.ot[m % 2], self.B_ot[m % 2]
                otv = ot[:].rearrange("p a n -> p (a n)").bitcast(BF16)[:, 0:512].rearrange("p (a n) -> p a n", a=4)
            P.op("act", lambda e: e.activation(out=otv, in_=pv, func=AF.Copy), reads=[self.B_ps[bank]], writes=[B_ot])
            dview = dst[row0:row0 + 512, m * 128:(m + 1) * 128].rearrange("(a p) n -> p a n", p=128)
            P.dma("sp", lambda e: e.dma_start(out=dview, in_=otv), reads=[B_ot], dwrites=[B_dstbuf], owner=B_ot,
                  is_out=True)

        self._ln_out_queue = []

        def dst_fn(m):
            y2 = self.y2[m % 2]
            if is_f32:
                return y2[:]
            return y2[:].bitcast(BF16)[:, 0:512]

        self.layer_norm_with_out(1, 512, dst_fn, lambda m: self.B_y2[m % 2], emit_out, is_f32)

    def _ffn_tail(self, c, cc):
        P = self.P
        ub = (2, 3, 5)[c % 3]
        gc, B_gc = self.gcs[c % 2], self.B_gcs[c % 2]
        tg, B_tg = self.tgs[c % 2], self.B_tgs[c % 2]
        P.op("act", lambda e: e.activation(out=tg[:], in_=gc[:], func=AF.Gelu_apprx_tanh), reads=[B_gc], writes=[B_tg])
        P.op("dve", lambda e: e.tensor_tensor(self.hT[:, cc, :], self.ps[:, ub, :], tg[:], ALU.mult),
             reads=[self.B_ps[ub], B_tg], writes=[self.B_hT[cc]])

    def layer_norm_with_out(self, which, ncol, dst_fn, B_dst_fn, emit_out, is_f32):
        P = self.P
        gcol = 0 if which == 0 else 16
        bcol = gcol + 8
        for m in range(8):
            zb, B_zb = self.zb[m % 2], self.B_zb[m % 2]
            zq, B_zq = self.zq[m % 2], self.B_zq[m % 2]
            P.op("act", lambda e, zb=zb, m=m: e.activation(out=zb[:, 0:ncol], in_=self.z[:, m, 0:ncol], func=AF.Copy),
                 reads=[self.B_z[m]], writes=[B_zb])
            P.op("act", lambda e, zq=zq, m=m: e.activation(out=zq[:, 0:ncol], in_=self.z[:, m, 0:ncol], func=AF.Square),
                 reads=[self.B_z[m]], writes=[B_zq])
            P.op("pe", lambda e, zb=zb, m=m: e.matmul(self.ps[:, 0, 0:ncol], lhsT=self.ones_b[:], rhs=zb[:, 0:ncol],
                                                      start=(m == 0), stop=(m == 7)),
                 reads=[B_zb, self.B_ones], writes=[self.B_ps[0]])
            P.op("pe", lambda e, zq=zq, m=m: e.matmul(self.ps[:, 1, 0:ncol], lhsT=self.ones_b[:], rhs=zq[:, 0:ncol],
                                                      start=(m == 0), stop=(m == 7)),
                 reads=[B_zq, self.B_ones], writes=[self.B_ps[1]])
        mean, rstd = self.mean, self.rstd
        P.op("dve", lambda e: e.tensor_scalar(mean[:, 0:ncol], self.ps[:, 0, 0:ncol], 1.0 / D, None, ALU.mult),
             reads=[self.B_ps[0]], writes=[self.B_mean])
        P.op("dve", lambda e: e.tensor_tensor(rstd[:, 0:ncol], mean[:, 0:ncol], mean[:, 0:ncol], ALU.mult),
             reads=[self.B_mean], writes=[self.B_rstd])
        P.op("dve", lambda e: e.scalar_tensor_tensor(out=rstd[:, 0:ncol], in0=self.ps[:, 1, 0:ncol], scalar=1.0 / D,
                                                     in1=rstd[:, 0:ncol], op0=ALU.mult, op1=ALU.subtract),
             reads=[self.B_ps[1], self.B_rstd], writes=[self.B_rstd])
        self.rsqrt_act(rstd[:, 0:ncol], rstd[:, 0:ncol], self.B_rstd, self.B_rstd, 1.0, 3, ncol)
        for m in range(8):
            ta, B_ta = self.tmpa[m % 2], self.B_tmpa[m % 2]
            P.op("dve", lambda e, ta=ta, m=m: e.tensor_tensor(ta[:, 0:ncol], self.z[:, m, 0:ncol], mean[:, 0:ncol], ALU.subtract),
                 reads=[self.B_z[m], self.B_mean], writes=[B_ta])
            P.op("dve", lambda e, ta=ta: e.tensor_tensor(ta[:, 0:ncol], ta[:, 0:ncol], rstd[:, 0:ncol], ALU.mult),
                 reads=[B_ta, self.B_rstd], writes=[B_ta])
            dst = dst_fn(m)
            Bd = B_dst_fn(m)
            P.op("act", lambda e, ta=ta, m=m, dst=dst: e.activation(out=dst, in_=ta[:, 0:ncol], func=AF.Identity,
                                                                    scale=self.lnp[:, gcol + m:gcol + m + 1],
                                                                    bias=self.lnp[:, bcol + m:bcol + m + 1]),
                 reads=[B_ta, self.B_lnp], writes=[Bd])
            emit_out(m, dst, Bd)

    def half_layer(self, l, own_off, src, src_is_f32, B_src, dst, dst_dt, B_dstbuf, pending):
        P = self.P
        self.own_off = own_off
        self.setup_layer(l, own_off)
        self.kv_phase(l, src, src_is_f32, B_src, pending)
        while pending:
            pending.pop(0)()
        tl = (own_off - 1) % NT
        tr_ = (own_off + 32) % NT
        self.att_block(l, src, src_is_f32, B_src, [tl, tr_], 127, 2, self.XH, self.B_XH)
        for k in range(8):
            tiles = [own_off + 4 * k + i for i in range(4)]
            XB, B_XB = self.XB[k % 2], self.B_XB[k % 2]
            self.att_block(l, src, src_is_f32, B_src, tiles, 0, 512, XB, B_XB)
            P.op("dve", lambda e, XB=XB, k=k: e.tensor_copy(self.LC[:, :, k:k + 1], XB[:, :, 511:512]),
                 reads=[B_XB], writes=[self.B_LC[k]])
            if k >= 1:
                self._ffn(l, k - 1, dst, dst_dt, B_dstbuf)
        self._ffn(l, 7, dst, dst_dt, B_dstbuf)

    def _ffn(self, l, k, dst, dst_dt, B_dstbuf):
        X, B_X = self.XB[k % 2], self.B_XB[k % 2]
        if k == 0:
            hl, B_hl = self.XH[:, :, 0:1], self.B_XH
        else:
            hl, B_hl = self.LC[:, :, k - 1:k], self.B_LC[k - 1]
        if k == 7:
            hr, B_hr = self.XH[:, :, 1:2], self.B_XH
        else:
            hr, B_hr = self.XB[(k + 1) % 2][:, :, 0:1], self.B_XB[(k + 1) % 2]
        self.fence(self.att_bufs + self.ffn_bufs)
        self.ffn_block(l, k, X, B_X, hl, B_hl, hr, B_hr, k == 0, k == 7, dst, dst_dt, B_dstbuf, k * 512)
        self.fence(self.att_bufs + self.ffn_bufs)

    def build(self):
        self.setup_consts()
        if not self.fused:
            pending = self.convert_weights(0)
            for _ in range(8):
                pending.pop(0)()
            self.half_layer(0, 0, self.x_in, True, None, self.out, F32, self.B_out, pending)
        else:
            pending = self.convert_weights(0) + self.convert_weights(1)
            for _ in range(8):
                pending.pop(0)()
            xm = self.xmid.ap()
            self.half_layer(0, 32, self.x_in, True, None, xm[S // 2:S, :], BF16, self.B_xmid, pending)
            self.half_layer(0, 0, self.x_in, True, None, xm[0:S // 2, :], BF16, self.B_xmid, pending)
            self.half_layer(1, 0, xm, False, self.B_xmid, self.out, F32, self.B_out, pending)
        self.P.emit()
        self.st.close()
        return self.nc


HPERM = [0, 4, 1, 5, 2, 6, 3, 7]


def _t5_bucket(rel):
    half = 16
    max_exact = 8
    bucket = np.where(rel > 0, half, 0)
    rp = np.abs(rel)
    rpf = np.maximum(rp, 1).astype(np.float32)
    large = max_exact + (np.log(rpf / np.float32(max_exact)) / np.float32(math.log(128 / max_exact))
                         * np.float32(half - max_exact)).astype(np.int32)
    large = np.minimum(large, half - 1)
    return bucket + np.where(rp < max_exact, rp, large)


def _bias_table(rel_bias):
    k = np.arange(128)[:, None, None]
    j = np.arange(3)[None, :, None]
    q = np.arange(128)[None, None, :]
    rel = (j - 1) * 128 + k - q
    idx = _t5_bucket(rel)
    tab = np.asarray(rel_bias, np.float32)[idx]
    tab = np.where((np.abs(rel) <= 128)[..., None], tab, np.float32(NEG))
    tab = np.ascontiguousarray(tab.transpose(0, 1, 3, 2))
    return tab.reshape(128, 3 * 8 * 128).astype(np.float32)


def _rope_table(half):
    tok = (np.arange(S) + half * (S // 2)) % S
    row = (tok // 64).astype(np.float32)
    col = (tok % 64).astype(np.float32)
    inv = (np.float32(10000.0) ** (-np.arange(0, 32, 2, dtype=np.float32) / np.float32(32))).astype(np.float32)
    ang = np.concatenate([row[:, None] * inv, col[:, None] * inv], axis=-1).astype(np.float32)
    cs = np.concatenate([np.cos(ang), np.sin(ang)], axis=-1).astype(np.float32)
    return np.ascontiguousarray(cs.reshape(NT, 128, 64).transpose(1, 0, 2))


def _layer_params(inp, l):
    f = lambda a: np.ascontiguousarray(np.asarray(a, np.float32))
    w_in = np.asarray(inp["w_in"][l], np.float32)
    qa = w_in[:, 0:512].reshape(D, 8, 64)[:, HPERM, :].reshape(D, 512)
    qb = w_in[:, 768:1280].reshape(D, 8, 64)[:, HPERM, :].reshape(D, 512)
    wq = np.concatenate([qa, qb], axis=1)
    wkv = np.concatenate([w_in[:, 512:640], w_in[:, 640:768], w_in[:, 1280:1408], w_in[:, 1408:1536]], axis=1)
    w_out = np.asarray(inp["w_out"][l], np.float32)
    wo = np.concatenate([w_out[0:512].reshape(8, 64, D)[HPERM].reshape(512, D),
                         w_out[512:1024].reshape(8, 64, D)[HPERM].reshape(512, D)], axis=0)
    ga = np.asarray(inp["out_norm_a"][l], np.float32).reshape(8, 64)[HPERM].reshape(512)
    gb = np.asarray(inp["out_norm_b"][l], np.float32).reshape(8, 64)[HPERM].reshape(512)
    gout = np.concatenate([ga, gb]).reshape(8, 128).T
    qn = np.asarray(inp["q_norm"][l], np.float32)
    kn = np.asarray(inp["k_norm"][l], np.float32)
    nrm = np.concatenate([qn[0::2], qn[1::2], kn[0::2], kn[1::2]])[None, :]
    fm = lambda v: np.asarray(v, np.float32).reshape(8, 128).T
    lnp = np.concatenate([fm(inp["ln1_g"][l]), fm(inp["ln1_b"][l]), fm(inp["ln2_g"][l]), fm(inp["ln2_b"][l])], axis=1)
    cw = np.asarray(inp["conv_w"][l], np.float32)
    cb = np.asarray(inp["conv_b"][l], np.float32)
    cv = np.stack([cw[0], cw[1], cw[2], cb], axis=-1).reshape(NFC, 128, 4).transpose(1, 0, 2).reshape(128, NFC * 4)
    sink = np.asarray(inp["sink"][l], np.float32)[None, :]
    return dict(wq=f(wq), wkv=f(wkv), wout=f(wo), wg=f(inp["w_gate"][l]), wu=f(inp["w_up"][l]), wd=f(inp["w_down"][l]),
                nrm=f(nrm), gout=f(gout), lnp=f(lnp), cvp=f(cv), sink=f(sink))


def _core_consts(inp, half):
    jm = np.zeros((128, 4), np.float32)
    if half == 0:
        jm[:, 0] = NEG; jm[:, 1] = 0.0; jm[:, 2] = 0.0; jm[:, 3] = 1.0
    else:
        jm[:, 0] = 0.0; jm[:, 1] = NEG; jm[:, 2] = 1.0; jm[:, 3] = 0.0
    return dict(biasT=_bias_table(inp["rel_bias"]), cs=_rope_table(half), jm=jm, ident=np.eye(128, dtype=np.float32))


def _layout_x(xb, half):
    if half == 0:
        return np.ascontiguousarray(xb)
    return np.ascontiguousarray(np.concatenate([xb[S // 2:], xb[:S // 2]], axis=0))


_NC_CACHE = {}


def _get_nc(n_layers, fused):
    key = (n_layers, fused)
    if key not in _NC_CACHE:
        _NC_CACHE[key] = Builder(n_layers, fused).build()
    return _NC_CACHE[key]


FUSED = True


def kernel(**inp):
    x = np.asarray(inp["x"], np.float32)
    B = x.shape[0]
    lp = [_layer_params(inp, l) for l in range(2)]
    cc = [_core_consts(inp, h) for h in range(2)]
    if FUSED:
        nc = _get_nc(2, True)
        in_maps = []
        for core in range(N_CORES):
            b, h = core // 2, core % 2
            m = {"xsrc": _layout_x(x[b], h)}
            for l in range(2):
                for k, v in lp[l].items():
                    m[f"{k}{l}"] = v
            m.update(cc[h])
            in_maps.append(m)
        res = run_bass_kernel_spmd(nc, in_maps, core_ids=list(range(N_CORES)))
        out = np.empty((B, S, D), np.float32)
        for core in range(N_CORES):
            b, h = core // 2, core % 2
            out[b, h * (S // 2):(h + 1) * (S // 2)] = res.results[core]["out"]
        return out
    nc = _get_nc(1, False)
    cur = x
    for l in range(2):
        in_maps = []
        for core in range(N_CORES):
            b, h = core // 2, core % 2
            m = {"xsrc": _layout_x(cur[b], h)}
            for k, v in lp[l].items():
                m[f"{k}0"] = v
            m.update(cc[h])
            in_maps.append(m)
        res = run_bass_kernel_spmd(nc, in_maps, core_ids=list(range(N_CORES)))
        nxt = np.empty((B, S, D), np.float32)
        for core in range(N_CORES):
            b, h = core // 2, core % 2
            nxt[b, h * (S // 2):(h + 1) * (S // 2)] = res.results[core]["out"]
        cur = nxt
    return cur
```
